# Optimizing a Trainium2 kernel written in Bass

```python
import functools
import jax, jax.numpy as jnp
from jax import lax
import numpy as np

D_MODEL = 1024
BATCH = 8
SEQ = 2048
DEPTH = 2
DEC_BATCH = 32
DEC_SEQ = 4
PAST_LEN = 8192
PAGE_SIZE = 128

D_MIX = D_MODEL
HEAD_DIM_A = 64
C_A = D_MIX // 4
H_A = C_A // HEAD_DIM_A
HEAD_DIM_B = 64
C_B = D_MIX // 2
H_B = C_B // HEAD_DIM_B
C_CONV = D_MIX - C_A - C_B
CONV_W = 31
W_LORA = 64
A_LORA = 64
G_LORA = 128
RWKV_COLS = 3 * C_A + W_LORA + A_LORA + G_LORA
FOX_COLS = 3 * C_B + H_B
CONV_COLS = 2 * C_CONV
IN_COLS = RWKV_COLS + FOX_COLS + CONV_COLS
Q_BLOCK = 128
N_MEM = 256
H_X = 4
D_X = D_MODEL
HEAD_DIM_X = D_X // H_X
D_FF = 2816
FFN_CONV_W = 3
RMS_EPS = 1e-6
LN_EPS = 1e-5
GN_EPS = 64e-5

kernel_name = 'hymba_rwkv7_fox_conformer_decode_step'


def rms_norm(x, g, eps=RMS_EPS):
    xf = x.astype(jnp.float32)
    y = xf * lax.rsqrt(jnp.mean(xf * xf, axis=-1, keepdims=True) + eps)
    return (y * g.astype(jnp.float32)).astype(x.dtype)


def layer_norm(x, g, b, eps=LN_EPS):
    xf = x.astype(jnp.float32)
    xc = xf - jnp.mean(xf, axis=-1, keepdims=True)
    var = jnp.mean(xc * xc, axis=-1, keepdims=True)
    return (xc * lax.rsqrt(var + eps) * g.astype(jnp.float32) + b.astype(jnp.float32)).astype(x.dtype)


def causal_dwconv(x, buf, w, b):
    xp = jnp.concatenate([buf.astype(x.dtype), x], axis=1)
    y = lax.conv_general_dilated(xp, w[:, None, :].astype(x.dtype), window_strides=(1,), padding='VALID',
                                 dimension_numbers=('NWC', 'WIO', 'NWC'), feature_group_count=x.shape[-1])
    return y + b.astype(x.dtype), xp[:, xp.shape[1] - (w.shape[0] - 1):]


def rwkv7_mix(z, shift_buf, S0, mu, w0, w_up, a0, a_up, g_up, k_k, k_a, r_k, ln_g, ln_b):
    B, T, _ = z.shape
    f32 = jnp.float32
    z_prev = jnp.concatenate([shift_buf.astype(z.dtype), z[:, :-1]], axis=1)
    zs = z + (z_prev - z) * mu.astype(z.dtype)
    r, k, v, wd, ad, gd = jnp.split(
        zs, [C_A, 2 * C_A, 3 * C_A, 3 * C_A + W_LORA, 3 * C_A + W_LORA + A_LORA], axis=-1)
    w_log = -jax.nn.softplus(-(w0 + jnp.tanh(wd) @ w_up).astype(f32)) - 0.5
    decay = jnp.exp(-jnp.exp(w_log))
    a = jax.nn.sigmoid((a0 + ad @ a_up).astype(f32))
    g = jax.nn.sigmoid(gd) @ g_up
    kf = k.astype(f32)
    k_mod = kf * (1.0 + (a - 1.0) * k_a.astype(f32))
    heads = lambda t: t.astype(f32).reshape(B, T, H_A, HEAD_DIM_A)
    kk = heads(kf * k_k.astype(f32))
    kk = kk / jnp.maximum(jnp.sqrt(jnp.sum(kk * kk, axis=-1, keepdims=True)), 1e-12)
    rh, wh, kh, vh, ah = heads(r), heads(decay), heads(k_mod), heads(v), heads(a)

    def step(S, inp):
        r_t, w_t, k_t, v_t, kk_t, a_t = inp
        sa = jnp.einsum('bhvk,bhk->bhv', S, -kk_t)
        S = S * w_t[:, :, None, :] + sa[..., None] * (kk_t * a_t)[:, :, None, :] + v_t[..., None] * k_t[:, :, None, :]
        return S, jnp.einsum('bhvk,bhk->bhv', S, r_t)

    xs = tuple(jnp.moveaxis(t, 1, 0) for t in (rh, wh, kh, vh, kk, ah))
    S_T, ys = lax.scan(step, S0.astype(f32), xs)
    y = jnp.moveaxis(ys, 0, 1)
    yc = y - jnp.mean(y, axis=-1, keepdims=True)
    y = yc * lax.rsqrt(jnp.mean(yc * yc, axis=-1, keepdims=True) + GN_EPS)
    y = y.reshape(B, T, C_A) * ln_g.astype(f32) + ln_b.astype(f32)
    bonus = (jnp.sum(rh * kh * r_k.astype(f32), axis=-1, keepdims=True) * vh).reshape(B, T, C_A)
    out = ((y + bonus) * g.astype(f32)).astype(z.dtype)
    return out, S_T.astype(S0.dtype), z[:, -1:]


def fox_project(zb, q_norm, k_norm, b_f):
    B, T, _ = zb.shape
    q, k, v, fl = jnp.split(zb, [C_B, 2 * C_B, 3 * C_B], axis=-1)
    hd = lambda t: t.reshape(B, T, H_B, HEAD_DIM_B)
    q = rms_norm(hd(q), q_norm)
    k = rms_norm(hd(k), k_norm)
    logf = jax.nn.log_sigmoid((fl + b_f).astype(jnp.float32))
    return q, k, hd(v), logf


def fox_prompt(q, k, v, logf):
    B, T, H, D = q.shape
    scale = D ** -0.5
    cT = jnp.transpose(jnp.cumsum(logf, axis=1), (0, 2, 1))
    kpos = jnp.arange(T)

    def block(i):
        start = i * Q_BLOCK
        qb = lax.dynamic_slice_in_dim(q, start, Q_BLOCK, axis=1)
        cb = lax.dynamic_slice_in_dim(cT, start, Q_BLOCK, axis=2)
        s = jnp.einsum('bqhd,bkhd->bhqk', qb, k, preferred_element_type=jnp.float32) * scale
        s = s + cb[..., :, None] - cT[..., None, :]
        qpos = start + jnp.arange(Q_BLOCK)
        s = jnp.where(kpos[None, :] <= qpos[:, None], s, -jnp.inf)
        p = jax.nn.softmax(s, axis=-1)
        return jnp.einsum('bhqk,bkhd->bqhd', p.astype(v.dtype), v)

    o = lax.map(block, jnp.arange(T // Q_BLOCK))
    return jnp.moveaxis(o, 0, 1).reshape(B, T, H * D)


def fox_sample(q, k, v, logf, kp, vp, lfp):
    B, S, H, D = q.shape
    scale = D ** -0.5
    suffix = lax.cumsum(lfp, axis=1, reverse=True)
    excl = jnp.concatenate([suffix[:, 1:], jnp.zeros_like(suffix[:, :1])], axis=1)
    cT = jnp.transpose(jnp.cumsum(logf, axis=1), (0, 2, 1))
    exT = jnp.transpose(excl, (0, 2, 1))
    s_past = jnp.einsum('bqhd,bkhd->bhqk', q, kp, preferred_element_type=jnp.float32) * scale
    s_past = s_past + cT[..., :, None] + exT[..., None, :]
    s_new = jnp.einsum('bqhd,bkhd->bhqk', q, k, preferred_element_type=jnp.float32) * scale
    s_new = s_new + cT[..., :, None] - cT[..., None, :]
    idx = jnp.arange(S)
    s_new = jnp.where(idx[None, :] <= idx[:, None], s_new, -jnp.inf)
    p = jax.nn.softmax(jnp.concatenate([s_past, s_new], axis=-1), axis=-1)
    P = kp.shape[1]
    o = jnp.einsum('bhqk,bkhd->bqhd', p[..., :P].astype(vp.dtype), vp) + \
        jnp.einsum('bhqk,bkhd->bqhd', p[..., P:].astype(v.dtype), v)
    return o.reshape(B, S, H * D)


def conv_module(zc, buf, conv_w, conv_b, ln_g, ln_b):
    a, b = jnp.split(zc, 2, axis=-1)
    u = a * jax.nn.sigmoid(b)
    y, new_buf = causal_dwconv(u, buf, conv_w, conv_b)
    return jax.nn.silu(layer_norm(y, ln_g, ln_b)), new_buf


def mem_kv(mem, g_mem, w_xk, w_xv, xk_norm):
    B, M, _ = mem.shape
    mn = rms_norm(mem, g_mem)
    k = rms_norm((mn @ w_xk).reshape(B, M, H_X, HEAD_DIM_X), xk_norm)
    v = (mn @ w_xv).reshape(B, M, H_X, HEAD_DIM_X)
    return k, v


def cross_attend(x, mk, mv, g_x, w_xq, xq_norm, w_xo):
    B, T, _ = x.shape
    q = rms_norm((rms_norm(x, g_x) @ w_xq).reshape(B, T, H_X, HEAD_DIM_X), xq_norm)
    s = jnp.einsum('bqhd,bkhd->bhqk', q, mk, preferred_element_type=jnp.float32) * HEAD_DIM_X ** -0.5
    p = jax.nn.softmax(s, axis=-1)
    o = jnp.einsum('bhqk,bkhd->bqhd', p.astype(mv.dtype), mv).reshape(B, T, D_X)
    return o @ w_xo


def conv_ffn(x, buf, g, w_up, cw, cb, w_down):
    h = rms_norm(x, g) @ w_up
    h, new_buf = causal_dwconv(h, buf, cw, cb)
    gate, val = jnp.split(h, 2, axis=-1)
    return (jax.nn.silu(gate) * val) @ w_down, new_buf


def layer_block(x, Wl, S0, shift0, conv0, ffn0, mk, mv, fox_fn):
    z = rms_norm(x, Wl['g_mix']) @ Wl['w_in']
    za, zb, zc = jnp.split(z, [RWKV_COLS, RWKV_COLS + FOX_COLS], axis=-1)
    ya, S, shift = rwkv7_mix(za, shift0, S0, Wl['rwkv_mu'], Wl['rwkv_w0'], Wl['rwkv_w_up'], Wl['rwkv_a0'],
                             Wl['rwkv_a_up'], Wl['rwkv_g_up'], Wl['rwkv_k_k'], Wl['rwkv_k_a'], Wl['rwkv_r_k'],
                             Wl['rwkv_ln_g'], Wl['rwkv_ln_b'])
    q, k, v, logf = fox_project(zb, Wl['fox_q_norm'], Wl['fox_k_norm'], Wl['fox_b_f'])
    yb = fox_fn(q, k, v, logf)
    yc, conv_buf = conv_module(zc, conv0, Wl['conv_w'], Wl['conv_b'], Wl['conv_ln_g'], Wl['conv_ln_b'])
    x = x + jnp.concatenate([ya, yb.astype(x.dtype), yc], axis=-1) @ Wl['w_out']
    x = x + cross_attend(x, mk, mv, Wl['g_x'], Wl['w_xq'], Wl['xq_norm'], Wl['w_xo'])
    f, ffn_buf = conv_ffn(x, ffn0, Wl['g_ffn'], Wl['w_up'], Wl['ffn_conv_w'], Wl['ffn_conv_b'], Wl['w_down'])
    x = x + f
    return x, (k, v, logf, S, shift, conv_buf, ffn_buf)


def setup_inputs(seed: int = 0) -> dict:
    key = jax.random.key(seed)
    keys = jax.random.split(key, 64)
    counter = iter(range(64))
    f32 = jnp.float32

    def nrm(shape, scale=1.0):
        return jax.random.normal(keys[next(counter)], shape, f32) * scale

    L = DEPTH
    n_pages = PAST_LEN // PAGE_SIZE
    n_used = DEC_BATCH * n_pages
    n_phys = n_used + n_used // 4
    perm = jax.random.permutation(keys[next(counter)], n_phys)
    page_table = perm[:n_used].reshape(DEC_BATCH, n_pages).astype(jnp.int32)

    inp = {}
    inp['x_prompt'] = nrm((BATCH, SEQ, D_MODEL))
    inp['x_sample'] = nrm((DEC_BATCH, DEC_SEQ, D_MODEL))
    inp['mem_prompt'] = nrm((BATCH, N_MEM, D_MODEL))
    inp['cache_fox_k'] = nrm((L, n_phys, PAGE_SIZE, H_B, HEAD_DIM_B))
    inp['cache_fox_v'] = nrm((L, n_phys, PAGE_SIZE, H_B, HEAD_DIM_B))
    inp['cache_fox_logf'] = jax.nn.log_sigmoid(3.0 + nrm((L, n_phys, PAGE_SIZE, H_B), 0.5))
    inp['page_table'] = page_table
    inp['state_rwkv'] = nrm((L, DEC_BATCH, H_A, HEAD_DIM_A, HEAD_DIM_A), 0.5)
    inp['state_rwkv_shift'] = nrm((L, DEC_BATCH, 1, RWKV_COLS))
    inp['state_conv'] = nrm((L, DEC_BATCH, CONV_W - 1, C_CONV), 0.5)
    inp['state_ffn'] = nrm((L, DEC_BATCH, FFN_CONV_W - 1, 2 * D_FF))
    inp['cache_mem_k'] = nrm((L, DEC_BATCH, N_MEM, H_X, HEAD_DIM_X))
    inp['cache_mem_v'] = nrm((L, DEC_BATCH, N_MEM, H_X, HEAD_DIM_X))
    inp['g_mix'] = 1.0 + nrm((L, D_MODEL), 0.02)
    inp['w_in'] = nrm((L, D_MODEL, IN_COLS), D_MODEL ** -0.5)
    inp['rwkv_mu'] = jax.random.uniform(keys[next(counter)], (L, RWKV_COLS), f32)
    inp['rwkv_w0'] = nrm((L, C_A), 0.5)
    inp['rwkv_w_up'] = nrm((L, W_LORA, C_A), W_LORA ** -0.5)
    inp['rwkv_a0'] = nrm((L, C_A), 0.1)
    inp['rwkv_a_up'] = nrm((L, A_LORA, C_A), A_LORA ** -0.5)
    inp['rwkv_g_up'] = nrm((L, G_LORA, C_A), G_LORA ** -0.5)
    inp['rwkv_k_k'] = 1.0 + nrm((L, C_A), 0.1)
    inp['rwkv_k_a'] = 1.0 + nrm((L, C_A), 0.1)
    inp['rwkv_r_k'] = nrm((L, H_A, HEAD_DIM_A), 0.1)
    inp['rwkv_ln_g'] = 1.0 + nrm((L, C_A), 0.02)
    inp['rwkv_ln_b'] = nrm((L, C_A), 0.02)
    inp['fox_q_norm'] = 1.0 + nrm((L, HEAD_DIM_B), 0.02)
    inp['fox_k_norm'] = 1.0 + nrm((L, HEAD_DIM_B), 0.02)
    inp['fox_b_f'] = 3.0 + nrm((L, H_B), 0.5)
    inp['conv_w'] = nrm((L, CONV_W, C_CONV), CONV_W ** -0.5)
    inp['conv_b'] = nrm((L, C_CONV), 0.02)
    inp['conv_ln_g'] = 1.0 + nrm((L, C_CONV), 0.02)
    inp['conv_ln_b'] = nrm((L, C_CONV), 0.02)
    inp['w_out'] = nrm((L, D_MIX, D_MODEL), D_MIX ** -0.5)
    inp['g_x'] = 1.0 + nrm((L, D_MODEL), 0.02)
    inp['g_mem'] = 1.0 + nrm((L, D_MODEL), 0.02)
    inp['w_xq'] = nrm((L, D_MODEL, D_X), D_MODEL ** -0.5)
    inp['w_xk'] = nrm((L, D_MODEL, D_X), D_MODEL ** -0.5)
    inp['w_xv'] = nrm((L, D_MODEL, D_X), D_MODEL ** -0.5)
    inp['xq_norm'] = 1.0 + nrm((L, HEAD_DIM_X), 0.02)
    inp['xk_norm'] = 1.0 + nrm((L, HEAD_DIM_X), 0.02)
    inp['w_xo'] = nrm((L, D_X, D_MODEL), D_X ** -0.5)
    inp['g_ffn'] = 1.0 + nrm((L, D_MODEL), 0.02)
    inp['w_up'] = nrm((L, D_MODEL, 2 * D_FF), D_MODEL ** -0.5)
    inp['ffn_conv_w'] = nrm((L, FFN_CONV_W, 2 * D_FF), FFN_CONV_W ** -0.5)
    inp['ffn_conv_b'] = nrm((L, 2 * D_FF), 0.02)
    inp['w_down'] = nrm((L, D_FF, D_MODEL), D_FF ** -0.5)
    return inp


def reference(x_prompt, x_sample, mem_prompt, cache_fox_k, cache_fox_v, cache_fox_logf, page_table,
              state_rwkv, state_rwkv_shift, state_conv, state_ffn, cache_mem_k, cache_mem_v,
              g_mix, w_in, rwkv_mu, rwkv_w0, rwkv_w_up, rwkv_a0, rwkv_a_up, rwkv_g_up, rwkv_k_k, rwkv_k_a,
              rwkv_r_k, rwkv_ln_g, rwkv_ln_b, fox_q_norm, fox_k_norm, fox_b_f, conv_w, conv_b, conv_ln_g,
              conv_ln_b, w_out, g_x, g_mem, w_xq, w_xk, w_xv, xq_norm, xk_norm, w_xo, g_ffn, w_up,
              ffn_conv_w, ffn_conv_b, w_down):
    W = dict(g_mix=g_mix, w_in=w_in, rwkv_mu=rwkv_mu, rwkv_w0=rwkv_w0, rwkv_w_up=rwkv_w_up, rwkv_a0=rwkv_a0,
             rwkv_a_up=rwkv_a_up, rwkv_g_up=rwkv_g_up, rwkv_k_k=rwkv_k_k, rwkv_k_a=rwkv_k_a, rwkv_r_k=rwkv_r_k,
             rwkv_ln_g=rwkv_ln_g, rwkv_ln_b=rwkv_ln_b, fox_q_norm=fox_q_norm, fox_k_norm=fox_k_norm,
             fox_b_f=fox_b_f, conv_w=conv_w, conv_b=conv_b, conv_ln_g=conv_ln_g, conv_ln_b=conv_ln_b,
             w_out=w_out, g_x=g_x, g_mem=g_mem, w_xq=w_xq, w_xk=w_xk, w_xv=w_xv, xq_norm=xq_norm,
             xk_norm=xk_norm, w_xo=w_xo, g_ffn=g_ffn, w_up=w_up, ffn_conv_w=ffn_conv_w,
             ffn_conv_b=ffn_conv_b, w_down=w_down)

    B = x_prompt.shape[0]
    dt = x_prompt.dtype
    x = x_prompt
    p_st, p_mk, p_mv = [], [], []
    for l in range(DEPTH):
        Wl = {n: a[l] for n, a in W.items()}
        mk, mv = mem_kv(mem_prompt, Wl['g_mem'], Wl['w_xk'], Wl['w_xv'], Wl['xk_norm'])
        S0 = jnp.zeros((B, H_A, HEAD_DIM_A, HEAD_DIM_A), dt)
        shift0 = jnp.zeros((B, 1, RWKV_COLS), dt)
        conv0 = jnp.zeros((B, CONV_W - 1, C_CONV), dt)
        ffn0 = jnp.zeros((B, FFN_CONV_W - 1, 2 * D_FF), dt)
        x, st = layer_block(x, Wl, S0, shift0, conv0, ffn0, mk, mv, fox_prompt)
        p_st.append(st)
        p_mk.append(mk)
        p_mv.append(mv)
    y_prompt = x
    p_fox_k, p_fox_v, p_fox_logf, p_rwkv, p_rwkv_shift, p_conv, p_ffn = (jnp.stack(f) for f in zip(*p_st))
    p_mem_k = jnp.stack(p_mk)
    p_mem_v = jnp.stack(p_mv)

    DB = x_sample.shape[0]
    x = x_sample
    s_st = []
    for l in range(DEPTH):
        Wl = {n: a[l] for n, a in W.items()}
        kp = cache_fox_k[l][page_table].reshape(DB, -1, H_B, HEAD_DIM_B)
        vp = cache_fox_v[l][page_table].reshape(DB, -1, H_B, HEAD_DIM_B)
        lfp = cache_fox_logf[l][page_table].reshape(DB, -1, H_B)
        fox_fn = functools.partial(fox_sample, kp=kp, vp=vp, lfp=lfp)
        x, st = layer_block(x, Wl, state_rwkv[l], state_rwkv_shift[l], state_conv[l], state_ffn[l],
                            cache_mem_k[l], cache_mem_v[l], fox_fn)
        s_st.append(st)
    y_sample = x
    s_fox_k, s_fox_v, s_fox_logf, s_rwkv, s_rwkv_shift, s_conv, s_ffn = (jnp.stack(f) for f in zip(*s_st))

    return (y_prompt, y_sample, p_fox_k, p_fox_v, p_fox_logf, p_rwkv, p_rwkv_shift, p_conv, p_ffn,
            p_mem_k, p_mem_v, s_fox_k, s_fox_v, s_fox_logf, s_rwkv, s_rwkv_shift, s_conv, s_ffn)
```

```python
import contextlib
import os
RWSUB = int(os.environ.get('RWSUB', '99'))
RWX = int(os.environ.get('RWX', '99'))
ZI = int(os.environ.get('ZI', '0'))
SRW = int(os.environ.get('SRW', '7'))
XVE = int(os.environ.get('XVE', '0'))
import numpy as np
import concourse.bass as bass
import concourse.mybir as mybir
from concourse.bass_utils import run_bass_kernel_spmd

F32 = mybir.dt.float32
BF16 = mybir.dt.bfloat16
I32 = mybir.dt.int32
AF = mybir.ActivationFunctionType
ALU = mybir.AluOpType
AX = mybir.AxisListType

NCORES = 8
D = 1024
KC = 8
T = 2048
NT = 16
NS = 16
DEPTH = 2
H_B = 8
IN_COLS = 3080
FOX0 = 1024
CONV0 = 1024 + 1544
D_FF = 2816
RMS_EPS = 1e-6
SAME_ENGINE_SYNC = True


class Sched:
    def __init__(self, nc, st):
        self.nc = nc
        self.st = st
        self.E = {}
        for n, h in (("pe", nc.tensor), ("act", nc.scalar), ("dve", nc.vector), ("pool", nc.gpsimd), ("sp", nc.sync)):
            self.E[n] = dict(h=h, sem=st.enter_context(nc.semaphore("q_" + n)), cnt=0, known={}, prog=[], gen=0)
        self.dpool = {"sp": [dict(sem=st.enter_context(nc.semaphore(f"dqh{i}")), val=0, i=("h", i)) for i in range(28)],
                      "pool": [dict(sem=st.enter_context(nc.semaphore(f"dqs{i}")), val=0, i=("s", i)) for i in range(12)]}
        self.dsems = self.dpool["sp"] + self.dpool["pool"]
        self.di = {"sp": 0, "pool": 0}
        self.lw = {}
        self.rd = {}

    def _deps(self, reads, writes):
        d = []
        for k in reads:
            if k in self.lw:
                d.append(self.lw[k])
        for k in writes:
            if k in self.lw:
                d.append(self.lw[k])
            d += self.rd.get(k, [])
        return d

    def _wait(self, en, deps):
        e = self.E[en]
        for (sem, val, key) in deps:
            if key == ("e", en) and not SAME_ENGINE_SYNC:
                continue
            if key == ("e", "pe") and en == "pe":
                continue
            if e["known"].get(key, 0) >= val:
                continue
            e["known"][key] = val
            e["prog"].append(lambda h, sem=sem, val=val: h.wait_ge(sem, val))

    def op(self, en, fn, reads=(), writes=()):
        e = self.E[en]
        writes = list(writes) + [k for k in reads if isinstance(k, tuple) and k and k[0] == "ps" and k not in writes]
        self._wait(en, self._deps(reads, writes))
        e["cnt"] += 1
        c = e["cnt"]
        sem = e["sem"]
        e["prog"].append(lambda h, fn=fn, sem=sem: fn(h).then_inc(sem, 1))
        tok = (sem, c, ("e", en))
        for k in writes:
            self.lw[k] = tok
            self.rd[k] = []
        for k in reads:
            if k not in writes:
                self.rd.setdefault(k, []).append(tok)

    def dma(self, qn, out, in_, reads=(), writes=(), **kw):
        e = self.E[qn]
        pool_ = self.dpool[qn]
        d = pool_[self.di[qn]]
        self.di[qn] = (self.di[qn] + 1) % len(pool_)
        deps = self._deps(reads, writes)
        if d["val"] > 0:
            deps.append((d["sem"], d["val"], ("d", d["i"])))
        self._wait(qn, deps)
        d["val"] += 16
        sem, val = d["sem"], d["val"]
        e["prog"].append(lambda h, out=out, in_=in_, sem=sem, kw=kw: h.dma_start(out=out, in_=in_, **kw).then_inc(sem, 16))
        tok = (sem, val, ("d", d["i"]))
        for k in writes:
            self.lw[k] = tok
            self.rd[k] = []
        for k in reads:
            if k not in writes:
                self.rd.setdefault(k, []).append(tok)

    def idma(self, out, in_, idx_ap, reads=(), writes=()):
        qn = "pool"
        e = self.E[qn]
        pool_ = self.dpool[qn]
        d = pool_[self.di[qn]]
        self.di[qn] = (self.di[qn] + 1) % len(pool_)
        deps = self._deps(reads, writes)
        if d["val"] > 0:
            deps.append((d["sem"], d["val"], ("d", d["i"])))
        self._wait(qn, deps)
        d["val"] += 16
        sem, val = d["sem"], d["val"]
        e["prog"].append(lambda h, out=out, in_=in_, idx_ap=idx_ap, sem=sem: h.indirect_dma_start(
            out=out, out_offset=None, in_=in_, in_offset=bass.IndirectOffsetOnAxis(ap=idx_ap, axis=0)).then_inc(sem, 16))
        tok = (sem, val, ("d", d["i"]))
        for k in writes:
            self.lw[k] = tok
            self.rd[k] = []
        for k in reads:
            if k not in writes:
                self.rd.setdefault(k, []).append(tok)

    def flush(self):
        for en, e in self.E.items():
            deps = [(o["sem"], o["cnt"], ("e", on)) for on, o in self.E.items() if on != en and o["cnt"] > 0]
            deps += [(d["sem"], d["val"], ("d", d["i"])) for d in self.dsems if d["val"] > 0]
            self._wait(en, deps)
        with self.nc.Block() as blk:
            for en, dec in (("pe", blk.tensor), ("act", blk.scalar), ("dve", blk.vector), ("pool", blk.gpsimd), ("sp", blk.sync)):
                prog = self.E[en]["prog"]
                if prog:
                    def body(h, prog=prog):
                        for f in prog:
                            f(h)
                    dec(body)
                self.E[en]["prog"] = []
        self.lw.clear()
        self.rd.clear()
        for en, e in self.E.items():
            if e["cnt"] > 12000:
                e["gen"] += 1
                e["sem"] = self.st.enter_context(self.nc.semaphore(f"q_{en}_{e['gen']}"))
                e["cnt"] = 0
                for o in self.E.values():
                    o["known"].pop(("e", en), None)
        for d in self.dsems:
            if d["val"] > 12000:
                d["sem"] = self.st.enter_context(self.nc.semaphore(f"dq{d['i'][0]}{d['i'][1]}_{d['val']}"))
                d["val"] = 0
                for o in self.E.values():
                    o["known"].pop(("d", d["i"]), None)


def build(stage=99):
    nc = bass.Bass("TRN2", target_bir_lowering=False)
    st = contextlib.ExitStack()

    def din(name, shape, dt=F32):
        return nc.dram_tensor(name, list(shape), dt, kind="ExternalInput").ap()

    def dout(name, shape, dt=F32):
        return nc.dram_tensor(name, list(shape), dt, kind="ExternalOutput").ap()

    xp = din("xp", [T, D])
    xs = din("xs", [NS, D])
    mem = din("mem", [256, D])
    W = {}
    for name, shape in (("g_mix", [DEPTH, D]), ("w_in", [DEPTH, D, IN_COLS]), ("fox_q_norm", [DEPTH, 64]),
                        ("fox_k_norm", [DEPTH, 64]), ("fox_b_f", [DEPTH, 8]), ("conv_w", [DEPTH, 31, 256]), ("conv_b", [DEPTH, 256]),
                        ("conv_ln_g", [DEPTH, 256]), ("conv_ln_b", [DEPTH, 256]),
                        ("rwkv_mu", [DEPTH, 1024]), ("rwkv_w0", [DEPTH, 256]), ("rwkv_w_up", [DEPTH, 64, 256]), ("rwkv_a0", [DEPTH, 256]),
                        ("rwkv_a_up", [DEPTH, 64, 256]), ("rwkv_g_up", [DEPTH, 128, 256]), ("rwkv_k_k", [DEPTH, 256]), ("rwkv_k_a", [DEPTH, 256]),
                        ("rwkv_r_k", [DEPTH, 256]), ("rwkv_ln_g", [DEPTH, 256]), ("rwkv_ln_b", [DEPTH, 256]),
                        ("g_mem", [DEPTH, D]), ("w_xk", [DEPTH, D, D]), ("w_xv", [DEPTH, D, D]), ("xk_norm", [DEPTH, 256]),
                        ("w_out", [DEPTH, D, D]), ("g_x", [DEPTH, D]), ("w_xq", [DEPTH, D, D]), ("xq_norm", [DEPTH, 256]), ("w_xo", [DEPTH, D, D]),
                        ("g_ffn", [DEPTH, D]), ("w_up", [DEPTH, D, 2 * D_FF]), ("ffn_conv_w", [DEPTH, 3, 2 * D_FF]), ("ffn_conv_b", [DEPTH, 2 * D_FF]),
                        ("w_down", [DEPTH, D_FF, D])):
        W[name] = din(name, shape)
    o_pfk = dout("p_fox_k", [DEPTH, T, 512])
    o_pfv = dout("p_fox_v", [DEPTH, T, 512])
    o_pfl = dout("p_fox_logf", [DEPTH, T, 8])
    o_sfk = dout("s_fox_k", [DEPTH, NS, 512])
    o_sfv = dout("s_fox_v", [DEPTH, NS, 512])
    o_sfl = dout("s_fox_logf", [DEPTH, NS, 8])
    scv = din("scv", [DEPTH, 4, 30, 256])
    o_pcv = dout("p_conv", [DEPTH, 30, 256])
    o_scv = dout("s_conv", [DEPTH, 4, 30, 256])
    srw = din("srw", [DEPTH, 4, 4, 64, 64])
    ssh = din("ssh", [DEPTH, 4, 1024])
    o_prw = dout("p_rwkv", [DEPTH, 4, 64, 64])
    o_psh = dout("p_rwkv_shift", [DEPTH, 1024])
    o_srw = dout("s_rwkv", [DEPTH, 4, 4, 64, 64])
    o_ssh = dout("s_rwkv_shift", [DEPTH, 4, 1024])
    o_pmk = dout("p_mem_k", [DEPTH, 256, D])
    o_pff = dout("p_ffn", [DEPTH, 2, 2 * D_FF])
    NPHYS = 2560
    fk = [din(f"fk{l_}", [NPHYS * 128, 512]) for l_ in range(DEPTH)]
    fv = [din(f"fv{l_}", [NPHYS * 128, 512]) for l_ in range(DEPTH)]
    flf = [din(f"flf{l_}", [NPHYS, 1024]) for l_ in range(DEPTH)]
    ptab = din("ptab", [4, 64], I32)
    cmk = din("cmk", [DEPTH, 4, 256, D])
    cmv = din("cmv", [DEPTH, 4, 256, D])
    sff = din("sff", [DEPTH, 8, 2 * D_FF])
    o_sff = dout("s_ffn", [DEPTH, 8, 2 * D_FF])
    o_ys = dout("y_sample", [NS, D])
    o_yp = dout("y_prompt", [T, D])
    o_pmv = dout("p_mem_v", [DEPTH, 256, D])

    DBG = {}
    if stage < 99:
        DBG["yb"] = dout("dbg_yb", [T, 512])
        DBG["yc"] = dout("dbg_yc", [2, 128, T], BF16)
        DBG["ya"] = dout("dbg_ya", [2, 128, T], BF16)
        DBG["x1"] = dout("dbg_x1", [KC, 128, T])
        DBG["yas"] = dout("dbg_yas", [2, 128, NS], BF16)
        DBG["ybs"] = dout("dbg_ybs", [NS, 512])
        for k_ in ("x1s", "x2s", "x3s"):
            DBG[k_] = dout("dbg_" + k_, [KC, 128, NS])
        DBG["x2"] = dout("dbg_x2", [KC, 128, T])
        DBG["x3"] = dout("dbg_x3", [KC, 128, T])
    S = Sched(nc, st)
    uid = [0]

    def sb(name, shape, dt=F32, stack=None):
        uid[0] += 1
        return (stack or st).enter_context(nc.sbuf_tensor(f"{name}_{uid[0]}", list(shape), dt))

    PS = [st.enter_context(nc.psum_tensor(f"ps{i}", [128, 512], F32)) for i in range(8)]
    psi = [0]

    def psum():
        i = psi[0]
        psi[0] = (i + 1) % 6
        return PS[i], ("ps", i)

    ident = sb("ident", [128, 128])
    onesf = sb("onesf", [128, 128])
    onesb = sb("onesb", [128, 128], BF16)
    xT = sb("xT", [128, KC, T])
    xTs = sb("xTs", [128, KC, NS])

    S.op("pool", lambda h: h.memset(onesf[:], 1.0), writes=["onesf"])
    S.op("pool", lambda h: h.memset(onesb[:], 1.0), writes=["onesb"])
    S.op("pool", lambda h: h.affine_select(out=ident[:], in_=onesf[:], pattern=[[-1, 128]], compare_op=ALU.is_equal,
                                           fill=0.0, base=0, channel_multiplier=1), reads=["onesf"], writes=["ident"])

    triU = sb("triU", [128, 128])
    maskb = sb("maskb", [128, 128], BF16)
    S.op("pool", lambda h: h.affine_select(out=triU[:], in_=onesf[:], pattern=[[1, 128]], compare_op=ALU.is_ge,
                                           fill=0.0, base=0, channel_multiplier=-1), reads=["onesf"], writes=["triU"])
    S.op("pool", lambda h: h.tensor_copy(out=maskb[:], in_=triU[:]), reads=["triU"], writes=["maskb"])

    with contextlib.ExitStack() as ph:
        xtm = [sb(f"xtm{i}", [128, D], stack=ph) for i in range(2)]

        def load_T(src_rows, n, dstT, col0, bi):
            t_ = xtm[bi]
            S.dma("sp", t_[0:n, :], src_rows, writes=[("xtm", bi)])
            for half in range(2):
                ps, pk = psum()
                for q in range(4):
                    c = half * 4 + q
                    S.op("pe", lambda h, ps=ps, q=q, c=c, t_=t_: h.transpose(out=ps[:, q * 128:q * 128 + n], in_=t_[0:n, c * 128:(c + 1) * 128],
                                                                          identity=ident[0:n, 0:n]),
                         reads=[("xtm", bi), "ident"], writes=[pk])
                eng = "act" if half == 0 else "dve"
                src = ps[:, :].rearrange("p (q t) -> p q t", q=4)[:, :, 0:n]
                dst = dstT[:, half * 4:half * 4 + 4, col0:col0 + n]
                if eng == "act":
                    S.op("act", lambda h, src=src, dst=dst: h.activation(out=dst, in_=src, func=AF.Copy), reads=[pk], writes=[("xT", id(dstT), col0)])
                else:
                    S.op("dve", lambda h, src=src, dst=dst: h.tensor_copy(out=dst, in_=src), reads=[pk], writes=[("xT", id(dstT), col0, 1)])

        for tt in range(NT):
            load_T(xp[tt * 128:(tt + 1) * 128, :], 128, xT, tt * 128, tt % 2)
        load_T(xs[:, :], NS, xTs, 0, 0)
        S.flush()

    def load_cols(tile_, vec, nch, stack_key):
        S.dma("sp", tile_[:, 0:nch], vec.rearrange("(c p) -> p c", p=128), writes=[stack_key], allow_slow_non_contiguous=True)

    def load_w(wt, wd, col0, ncols, key):
        nch = wd.shape[0] // 128
        for c in range(nch):
            S.dma("pool", wt[:, c, 0:ncols], wd[c * 128:(c + 1) * 128, col0:col0 + ncols], writes=[(key, c)])

    def rmsnorm_fm(xsrc, col0, n, gcol, xn, xn_key, scr, src_keys=()):
        sq, rbc = scr["sq"], scr["rbc"]
        S.op("act", lambda h: h.activation(out=sq[:, :, 0:n], in_=xsrc[:, :, col0:col0 + n], func=AF.Square), reads=list(src_keys), writes=["sq"])
        ps, pk = psum()
        for c in range(KC):
            S.op("pe", lambda h, c=c: h.matmul(ps[:, 0:n], lhsT=onesb[:, :], rhs=sq[:, c, 0:n], start=(c == 0), stop=(c == KC - 1)),
                 reads=["sq", "onesb"], writes=[pk])
        S.op("act", lambda h: h.activation(out=rbc[:, 0:n], in_=ps[:, 0:n], func=AF.Sqrt, scale=1.0 / D, bias=scr["eps"][:, 0:1]),
             reads=[pk, "eps"], writes=["rbc"])
        S.op("dve", lambda h: h.reciprocal(out=rbc[:, 0:n], in_=rbc[:, 0:n]), reads=["rbc"], writes=["rbc"])
        for c in range(KC):
            S.op("dve", lambda h, c=c: h.scalar_tensor_tensor(out=xn[:, c, 0:n], in0=xsrc[:, c, col0:col0 + n], scalar=gcol[:, c:c + 1],
                                                             in1=rbc[:, 0:n], op0=ALU.mult, op1=ALU.mult),
                 reads=["rbc", "gcol"] + list(src_keys), writes=[(xn_key, c)])

    for l in range(0 if stage < 1 else (DEPTH if stage >= 50 else 1)):
        lay = contextlib.ExitStack()
        yT = sb("yT", [128, KC, T], BF16, lay)
        yTs = sb("yTs", [128, KC, NS], BF16, lay)
        mx = contextlib.ExitStack()
        vS = sb("vS", [NS, 512], F32, mx)
        kS = sb("kS", [NS, 512], F32, mx)
        qS = sb("qS", [NS, 512], F32, mx)
        LFk = sb("LFk", [128, NT + 1, 8], F32, mx)
        QT = sb("QT", [128, 4, T], BF16, mx)
        KT = sb("KT", [128, 4, T], BF16, mx)
        Vp = sb("Vp", [128, NT, 8, 65], BF16, mx)
        S.op("pool", lambda h: h.memset(Vp[:, :, :, 64:65], 1.0), writes=["Vp1"])
        with contextlib.ExitStack() as ph:
            wfox = sb("wfox", [128, KC, 1544], BF16, ph)
            gcol = sb("gcol", [128, KC], F32, ph)
            epsc = sb("epsc", [128, 1], F32, ph)
            gq = sb("gq", [128, 64], F32, ph)
            gk = sb("gk", [128, 64], F32, ph)
            bfb = sb("bfb", [128, 8], F32, ph)
            sq = sb("sq", [128, KC, 512], BF16, ph)
            rbc = sb("rbc", [128, 512], F32, ph)
            xn = sb("xn", [128, KC, 512], BF16, ph)
            tq = sb("tq", [128, 512], F32, ph)
            tk = sb("tk", [128, 512], F32, ph)
            tv = sb("tv", [128, 512], F32, ph)
            tsq = sb("tsq", [128, 512], F32, ph)
            ss = sb("ss", [128, 16], F32, ph)
            LF = LFk
            scr = dict(sq=sq, rbc=rbc, eps=epsc)

            load_w(wfox, W["w_in"][l], FOX0, 1544, "wfox")
            load_cols(gcol, W["g_mix"][l], KC, "gcol")
            S.op("pool", lambda h: h.memset(epsc[:], RMS_EPS), writes=["eps"])
            S.dma("sp", gq[:, :], W["fox_q_norm"][l].partition_broadcast(128), writes=["gq"])
            S.dma("sp", gk[:, :], W["fox_k_norm"][l].partition_broadcast(128), writes=["gk"])
            S.dma("sp", bfb[:, :], W["fox_b_f"][l].partition_broadcast(128), writes=["bfb"])
            S.op("dve", lambda h: h.tensor_scalar(out=gq[:, :], in0=gq[:, :], scalar1=0.125, scalar2=None, op0=ALU.mult), reads=["gq"], writes=["gq"])
            wkeys = [("wfox", c) for c in range(KC)]

            def headnorm(ps, pk, n, gt, gkey, dst, dkey, si):
                S.op("act", lambda h: h.activation(out=tsq[0:n, :], in_=ps[0:n, :], func=AF.Square), reads=[pk], writes=["tsq"])
                S.op("dve", lambda h: h.tensor_reduce(out=ss[0:n, si * 8:si * 8 + 8], in_=tsq[0:n, :].rearrange("p (h d) -> p h d", h=8), axis=AX.X, op=ALU.add),
                     reads=["tsq"], writes=[("ss", si)])
                S.op("act", lambda h: h.activation(out=ss[0:n, si * 8:si * 8 + 8], in_=ss[0:n, si * 8:si * 8 + 8], func=AF.Sqrt, scale=1.0 / 64, bias=epsc[0:n, 0:1]),
                     reads=[("ss", si), "eps"], writes=[("ss", si)])
                S.op("dve", lambda h: h.reciprocal(out=ss[0:n, si * 8:si * 8 + 8], in_=ss[0:n, si * 8:si * 8 + 8]), reads=[("ss", si)], writes=[("ss", si)])
                S.op("dve", lambda h: h.tensor_tensor(out=dst[0:n, :].rearrange("p (h d) -> p h d", h=8), in0=ps[0:n, :].rearrange("p (h d) -> p h d", h=8),
                                                      in1=ss[0:n, si * 8:si * 8 + 8].unsqueeze(2).broadcast_to([n, 8, 64]), op=ALU.mult),
                     reads=[pk, ("ss", si)], writes=[dkey])
                S.op("dve", lambda h: h.tensor_tensor(out=dst[0:n, :].rearrange("p (h d) -> p h d", h=8), in0=dst[0:n, :].rearrange("p (h d) -> p h d", h=8),
                                                      in1=gt[0:n, :].unsqueeze(1).broadcast_to([n, 8, 64]), op=ALU.mult),
                     reads=[dkey, gkey], writes=[dkey])

            def projA(xsrc, blk0, nblk, o_k, o_v, o_l, row0, lf_tile0):
                rmsnorm_fm(xsrc, blk0, nblk, gcol, xn, "xn", scr)
                xnk = [("xn", c) for c in range(KC)]
                for t0 in range(0, nblk, 128):
                    n = min(128, nblk - t0)
                    ti = lf_tile0 + t0 // 128
                    r0 = row0 + t0
                    pss = []
                    for gi, (c0, nc_) in enumerate(((0, 512), (512, 512), (1024, 512), (1536, 8))):
                        ps, pk = psum()
                        for c in range(KC):
                            S.op("pe", lambda h, ps=ps, c=c, c0=c0, nc_=nc_, t0=t0, n=n: h.matmul(ps[0:n, 0:nc_], lhsT=xn[:, c, t0:t0 + n], rhs=wfox[:, c, c0:c0 + nc_],
                                                                                                 start=(c == 0), stop=(c == KC - 1)),
                                 reads=xnk + wkeys, writes=[pk])
                        pss.append((ps, pk))
                    if stage >= 4:
                        headnorm(pss[0][0], pss[0][1], n, gq, "gq", tq, "tq", 0)
                        headnorm(pss[1][0], pss[1][1], n, gk, "gk", tk, "tk", 1)
                        S.dma("sp", o_k[r0:r0 + n, :], tk[0:n, :], reads=["tk"])
                    S.op("act", lambda h, ps=pss[2][0], n=n: h.activation(out=tv[0:n, :], in_=ps[0:n, :], func=AF.Copy), reads=[pss[2][1]], writes=["tv"])
                    S.dma("sp", o_v[r0:r0 + n, :], tv[0:n, :], reads=["tv"])
                    if n == 128:
                        for (src_t, skey, dstT, dkey) in ((tq, "tq", QT, "QT"), (tk, "tk", KT, "KT")):
                            ps, pk = psum()
                            for q in range(4):
                                S.op("pe", lambda h, ps=ps, q=q, src_t=src_t: h.transpose(out=ps[:, q * 128:(q + 1) * 128], in_=src_t[:, q * 128:(q + 1) * 128], identity=ident[:, :]),
                                     reads=[skey, "ident"], writes=[pk])
                            S.op("act", lambda h, ps=ps, dstT=dstT, r0=r0: h.activation(out=dstT[:, :, r0:r0 + 128], in_=ps[:, :].rearrange("p (q t) -> p q t", q=4), func=AF.Copy),
                                 reads=[pk], writes=[(dkey, ti)])
                        S.op("pool", lambda h, ti=ti: h.tensor_copy(out=Vp[:, ti, :, 0:64], in_=tv[:, :].rearrange("p (h d) -> p h d", h=8)), reads=["tv"], writes=[("Vp", ti)])
                    else:
                        S.op("pool", lambda h: h.tensor_copy(out=qS[:, :], in_=tq[0:NS, :]), reads=["tq"], writes=["qS"])
                        S.op("pool", lambda h: h.tensor_copy(out=kS[:, :], in_=tk[0:NS, :]), reads=["tk"], writes=["kS"])
                        S.op("pool", lambda h: h.tensor_copy(out=vS[:, :], in_=tv[0:NS, :]), reads=["tv"], writes=["vS"])
                    if stage < 5:
                        continue
                    lfv = LF[0:n, ti, :]
                    S.op("dve", lambda h, ps=pss[3][0], n=n, lfv=lfv: h.tensor_tensor(out=lfv, in0=ps[0:n, 0:8], in1=bfb[0:n, :], op=ALU.add),
                         reads=[pss[3][1], "bfb"], writes=[("LF", ti)])
                    S.op("act", lambda h, lfv=lfv: h.activation(out=lfv, in_=lfv, func=AF.Exp, scale=-1.0), reads=[("LF", ti)], writes=[("LF", ti)])
                    S.op("dve", lambda h, lfv=lfv: h.tensor_scalar(out=lfv, in0=lfv, scalar1=1.0, scalar2=None, op0=ALU.add), reads=[("LF", ti)], writes=[("LF", ti)])
                    S.op("act", lambda h, lfv=lfv: h.activation(out=lfv, in_=lfv, func=AF.Ln), reads=[("LF", ti)], writes=[("LF", ti)])
                    S.op("dve", lambda h, lfv=lfv: h.tensor_scalar(out=lfv, in0=lfv, scalar1=-1.0, scalar2=None, op0=ALU.mult), reads=[("LF", ti)], writes=[("LF", ti)])
                    S.dma("sp", o_l[r0:r0 + n, :], lfv, reads=[("LF", ti)])

            if stage == 2:
                rmsnorm_fm(xT, 0, 512, gcol, xn, "xn", scr)
            if stage >= 3:
                for b in range(4):
                    projA(xT, b * 512, 512, o_pfk[l], o_pfv[l], o_pfl[l], b * 512, b * 4)
                projA(xTs, 0, NS, o_sfk[l], o_sfv[l], o_sfl[l], 0, NT)
            S.flush()

        if stage >= 6:
            with contextlib.ExitStack() as ph:
                Cw = sb("Cw", [128, NT, 8], F32, ph)
                offs = sb("offs", [128, NT, 8], F32, ph)
                tot = sb("tot", [128, NT, 8], F32, ph)
                negC = sb("negC", [128, NT, 8], F32, ph)
                BI = sb("BI", [128, NT, NT, 8], F32, ph)
                PT = [sb(f"PT{i}", [128, 128], BF16, ph) for i in range(4)]
                ybt = [sb(f"ybt{i}", [128, 512], F32, ph) for i in range(2)]
                rden = sb("rden", [128, 8], F32, ph)
                lfflat = LFk[:, 0:NT, :].rearrange("p t h -> p (t h)")
                ps, pk = psum()
                S.op("pe", lambda h, ps=ps: h.matmul(ps[:, 0:128], lhsT=triU[:, :], rhs=lfflat, start=True, stop=True), reads=["triU"], writes=[pk])
                S.op("act", lambda h, ps=ps: h.activation(out=Cw[:, :, :].rearrange("p t h -> p (t h)"), in_=ps[:, 0:128], func=AF.Copy), reads=[pk], writes=["Cw"])
                ps, pk = psum()
                S.op("pe", lambda h, ps=ps: h.matmul(ps[:, 0:128], lhsT=onesf[:, :], rhs=lfflat, start=True, stop=True), reads=["onesf"], writes=[pk])
                S.op("act", lambda h, ps=ps: h.activation(out=tot[:, :, :].rearrange("p t h -> p (t h)"), in_=ps[:, 0:128], func=AF.Copy), reads=[pk], writes=["tot"])
                S.op("dve", lambda h: h.memset(offs[:, 0, :], 0.0), writes=["offs"])
                for i in range(1, NT):
                    S.op("dve", lambda h, i=i: h.tensor_tensor(out=offs[:, i, :], in0=offs[:, i - 1, :], in1=tot[:, i - 1, :], op=ALU.add), reads=["offs", "tot"], writes=["offs"])
                S.op("dve", lambda h: h.tensor_tensor(out=negC[:, :, :], in0=Cw[:, :, :], in1=offs[:, :, :], op=ALU.add), reads=["Cw", "offs"], writes=["negC"])
                S.op("dve", lambda h: h.tensor_scalar(out=negC[:, :, :], in0=negC[:, :, :], scalar1=-1.0, scalar2=None, op0=ALU.mult), reads=["negC"], writes=["negC"])
                for j in range(NT):
                    S.op("dve", lambda h, j=j: h.tensor_tensor(out=BI[:, j, 0:j + 1, :], in0=negC[:, 0:j + 1, :],
                                                               in1=offs[:, j:j + 1, :].broadcast_to([128, j + 1, 8]), op=ALU.add),
                         reads=["negC", "offs"], writes=[("BI", j)])
                pti = 0
                for j in range(NT):
                    yb_t = ybt[j % 2]
                    ykey = ("ybt", j % 2)
                    for h_ in range(8):
                        pair, pb = h_ // 2, (h_ % 2) * 64
                        acc = PS[6 + (h_ // 4)]
                        akey = ("ps", 6 + (h_ // 4))
                        ac0 = (h_ % 4) * 65
                        for i in range(j + 1):
                            ps, pk = psum()
                            S.op("pe", lambda h, ps=ps, pair=pair, pb=pb, i=i, j=j: h.matmul(ps[:, 0:128], lhsT=KT[pb:pb + 64, pair, i * 128:(i + 1) * 128],
                                                                                        rhs=QT[pb:pb + 64, pair, j * 128:(j + 1) * 128], start=True, stop=True),
                                 reads=[("KT", i), ("QT", j)], writes=[pk])
                            pt = PT[pti % 4]
                            ptk = ("PT", pti % 4)
                            pti += 1
                            S.op("act", lambda h, ps=ps, pt=pt, i=i, j=j, h_=h_: h.activation(out=pt[:, :], in_=ps[:, 0:128], func=AF.Exp, bias=BI[:, j, i, h_:h_ + 1]),
                                 reads=[pk, ("BI", j)], writes=[ptk])
                            if i == j:
                                S.op("pool", lambda h, pt=pt: h.tensor_tensor(out=pt[:, :], in0=pt[:, :], in1=maskb[:, :], op=ALU.mult), reads=[ptk, "maskb"], writes=[ptk])
                            S.op("pe", lambda h, pt=pt, acc=acc, ac0=ac0, i=i, j=j, h_=h_: h.matmul(acc[:, ac0:ac0 + 65], lhsT=pt[:, :], rhs=Vp[:, i, h_, :], start=(i == 0), stop=(i == j)),
                                 reads=[ptk, ("Vp", i), "Vp1"], writes=[akey])
                        S.op("dve", lambda h, acc=acc, ac0=ac0, h_=h_: h.reciprocal(out=rden[:, h_:h_ + 1], in_=acc[:, ac0 + 64:ac0 + 65]), reads=[akey], writes=[("rden", h_)])
                        S.op("dve", lambda h, acc=acc, ac0=ac0, h_=h_, yb_t=yb_t: h.tensor_scalar(out=yb_t[:, h_ * 64:(h_ + 1) * 64], in0=acc[:, ac0:ac0 + 64], scalar1=rden[:, h_:h_ + 1],
                                                                                               scalar2=None, op0=ALU.mult),
                             reads=[akey, ("rden", h_)], writes=[ykey])
                    if "yb" in DBG:
                        S.dma("sp", DBG["yb"][j * 128:(j + 1) * 128, :], yb_t[:, :], reads=[ykey])
                    ps, pk = psum()
                    for q in range(4):
                        S.op("pe", lambda h, ps=ps, q=q, yb_t=yb_t: h.transpose(out=ps[:, q * 128:(q + 1) * 128], in_=yb_t[:, q * 128:(q + 1) * 128], identity=ident[:, :]),
                             reads=[ykey, "ident"], writes=[pk])
                    S.op("act", lambda h, ps=ps, j=j: h.activation(out=yT[:, 2:6, j * 128:(j + 1) * 128], in_=ps[:, :].rearrange("p (q t) -> p q t", q=4), func=AF.Copy),
                         reads=[pk], writes=[("yT", "b", j)])
                S.flush()
        if stage >= 14:
            with contextlib.ExitStack() as ph:
                ptb_i = sb("ptb_i", [128, 64], I32, ph)
                ptf = sb("ptf", [128, 64], F32, ph)
                idx = sb("idx", [128, 64], I32, ph)
                pio_i = sb("pio_i", [128, 1], I32, ph)
                pio = sb("pio", [128, 1], F32, ph)
                idxp = sb("idxp", [64, 1], I32, ph)
                Kt = [sb(f"Kt{i}", [128, 512], F32, ph) for i in range(3)]
                Vt = [sb(f"Vt{i}", [128, 512], F32, ph) for i in range(3)]
                Vb = [sb(f"Vb{i}", [128, 512], BF16, ph) for i in range(2)]
                KTp = [sb(f"KTp{i}", [128, 4, 128], BF16, ph) for i in range(2)]
                sbt = [sb(f"sbt{i}", [128, 8, 4], F32, ph) for i in range(2)]
                PTt = [sb(f"PTt{i}", [128, 32], BF16, ph) for i in range(2)]
                lft = sb("lft", [64, 1024], F32, ph)
                Pfx = sb("Pfx", [64, 1024], F32, ph)
                tot = sb("tot", [64, 8], F32, ph)
                later = sb("later", [64, 8], F32, ph)
                MLt = sb("MLt", [64, 64], F32, ph)
                EXT = sb("EXT", [128, 8, 64], F32, ph)
                qpad = sb("qpad", [128, 4, 2, 16], BF16, ph)
                KTn = sb("KTn", [128, 4, 16], BF16, ph)
                Vn = sb("Vn", [16, 512], BF16, ph)
                E4 = sb("E4", [4, 16], F32, ph)
                E4t = sb("E4t", [4, 16], F32, ph)
                BT = sb("BT", [16, 16], F32, ph)
                negcT = sb("negcT", [16, 8], F32, ph)
                maskn = sb("maskn", [16, 4, 4], F32, ph)
                mtmp = sb("mtmp", [16, 4, 4], F32, ph)
                sbn = sb("sbn", [16, 8, 4], F32, ph)
                PTn = sb("PTn", [16, 32], BF16, ph)
                onorm = sb("onorm", [32, 512], F32, ph)
                rdn = sb("rdn", [32, 1], F32, ph)
                ybS = sb("ybS", [NS, 512], F32, ph)
                S.op("pool", lambda h: h.iota(pio_i[:, 0:1], [[0, 1]], base=0, channel_multiplier=1), writes=["pio_i"])
                S.op("dve", lambda h: h.tensor_copy(out=pio[:, :], in_=pio_i[:, :]), reads=["pio_i"], writes=["pio"])
                S.op("pool", lambda h: h.affine_select(out=MLt[:, :], in_=onesf[0:64, 0:64], pattern=[[-1, 64]], compare_op=ALU.is_ge, fill=0.0, base=-1, channel_multiplier=1), reads=["onesf"], writes=["MLt"])
                S.op("pool", lambda h: h.affine_select(out=E4t[:, :], in_=onesf[0:4, 0:16], pattern=[[1, 16]], compare_op=ALU.is_ge, fill=0.0, base=0, channel_multiplier=-4), reads=["onesf"], writes=["E4t"])
                S.op("pool", lambda h: h.affine_select(out=E4[:, :], in_=E4t[:, :], pattern=[[-1, 16]], compare_op=ALU.is_ge, fill=0.0, base=3, channel_multiplier=4), reads=["E4t"], writes=["E4"])
                ps, pk = psum()
                S.op("pe", lambda h, ps=ps: h.matmul(ps[0:16, 0:16], lhsT=E4[:, :], rhs=E4[:, :], start=True, stop=True), reads=["E4"], writes=[pk])
                S.op("dve", lambda h, ps=ps: h.tensor_tensor(out=BT[:, :], in0=ps[0:16, 0:16], in1=triU[0:16, 0:16], op=ALU.mult), reads=[pk, "triU"], writes=["BT"])
                ps, pk = psum()
                S.op("pe", lambda h, ps=ps: h.matmul(ps[0:16, 0:8], lhsT=BT[:, :], rhs=LFk[0:NS, NT, :], start=True, stop=True), reads=["BT"], writes=[pk])
                S.op("dve", lambda h, ps=ps: h.tensor_scalar(out=negcT[:, :], in0=ps[0:16, 0:8], scalar1=-1.0, scalar2=None, op0=ALU.mult), reads=[pk], writes=["negcT"])
                S.op("pool", lambda h: h.memset(mtmp[:].rearrange("p a b -> p (a b)"), 1.0), writes=["mt0"])
                S.op("pool", lambda h: h.affine_select(out=maskn[:, :, :], in_=mtmp[:, :, :], pattern=[[4, 4], [1, 4]], compare_op=ALU.is_ge, fill=0.0, base=0, channel_multiplier=-1), reads=["mt0"], writes=["mk1"])
                S.op("pool", lambda h: h.affine_select(out=mtmp[:, :, :], in_=maskn[:, :, :], pattern=[[-4, 4], [0, 4]], compare_op=ALU.is_ge, fill=0.0, base=0, channel_multiplier=1), reads=["mk1"], writes=["maskn"])
                MSK = mtmp
                S.op("pool", lambda h: h.memset(qpad[:].rearrange("p a b c -> p (a b c)"), 0.0), writes=["qp0"])
                ps, pk = psum()
                for pr in range(4):
                    S.op("pe", lambda h, ps=ps, pr=pr: h.transpose(out=ps[:, pr * 16:(pr + 1) * 16], in_=qS[0:NS, pr * 128:(pr + 1) * 128], identity=ident[0:NS, 0:NS]), reads=["ident"], writes=[pk])
                for hh in range(2):
                    S.op("act", lambda h, ps=ps, hh=hh: h.activation(out=qpad[hh * 64:(hh + 1) * 64, :, hh, :], in_=ps[hh * 64:(hh + 1) * 64, 0:64].rearrange("p (a t) -> p a t", a=4), func=AF.Copy), reads=[pk, "qp0"], writes=[("qpad", hh)])
                qpk = [("qpad", 0), ("qpad", 1)]
                ps, pk = psum()
                for pr in range(4):
                    S.op("pe", lambda h, ps=ps, pr=pr: h.transpose(out=ps[:, pr * 16:(pr + 1) * 16], in_=kS[0:NS, pr * 128:(pr + 1) * 128], identity=ident[0:NS, 0:NS]), reads=["ident"], writes=[pk])
                S.op("act", lambda h, ps=ps: h.activation(out=KTn[:, :, :], in_=ps[:, 0:64].rearrange("p (a t) -> p a t", a=4), func=AF.Copy), reads=[pk], writes=["KTn"])
                S.op("act", lambda h: h.activation(out=Vn[:, :], in_=vS[0:NS, :], func=AF.Copy), writes=["Vn"])
                O_, okey = PS[6], ("ps", 6)
                D_, dkey = PS[7], ("ps", 7)
                ki = 0
                for b in range(4):
                    S.dma("sp", ptb_i[:, :], ptab[b].partition_broadcast(128), writes=["ptb_i"])
                    S.dma("sp", idxp[:, :], ptab[b].rearrange("(j o) -> j o", o=1), writes=["idxp"])
                    S.op("dve", lambda h: h.tensor_copy(out=ptf[:, :], in_=ptb_i[:, :]), reads=["ptb_i"], writes=["ptf"])
                    S.op("dve", lambda h: h.tensor_scalar(out=ptf[:, :], in0=ptf[:, :], scalar1=128.0, scalar2=None, op0=ALU.mult), reads=["ptf"], writes=["ptf"])
                    S.op("dve", lambda h: h.tensor_scalar(out=ptf[:, :], in0=ptf[:, :], scalar1=pio[:, 0:1], scalar2=None, op0=ALU.add), reads=["ptf", "pio"], writes=["ptf"])
                    S.op("dve", lambda h: h.tensor_copy(out=idx[:, :], in_=ptf[:, :]), reads=["ptf"], writes=["idx"])
                    S.idma(lft[:, :], flf[l], idxp[:, 0:1], reads=["idxp"], writes=["lft"])
                    l3 = lft[:, :].rearrange("p (t h) -> p t h", h=8)
                    p3 = Pfx[:, :].rearrange("p (t h) -> p t h", h=8)
                    for h_ in range(8):
                        S.op("dve", lambda h, h_=h_: h.tensor_tensor_scan(out=p3[:, :, h_], data0=onesf[0:64, 0:128], data1=l3[:, :, h_], initial=0.0, op0=ALU.mult, op1=ALU.add), reads=["lft", "onesf"], writes=[("Pfx", h_)])
                    pfk_ = [("Pfx", h_) for h_ in range(8)]
                    S.op("dve", lambda h: h.tensor_copy(out=tot[:, :], in_=p3[:, 127, :]), reads=pfk_, writes=["tot"])
                    S.op("dve", lambda h: h.tensor_tensor(out=p3, in0=tot[:, :].unsqueeze(1).broadcast_to([64, 128, 8]), in1=p3, op=ALU.subtract), reads=pfk_ + ["tot"], writes=pfk_)
                    ps, pk = psum()
                    S.op("pe", lambda h, ps=ps: h.matmul(ps[0:64, 0:8], lhsT=MLt[:, :], rhs=tot[:, :], start=True, stop=True), reads=["MLt", "tot"], writes=[pk])
                    S.op("act", lambda h, ps=ps: h.activation(out=later[:, :], in_=ps[0:64, 0:8], func=AF.Copy), reads=[pk], writes=["later"])
                    S.op("dve", lambda h: h.tensor_tensor(out=p3, in0=p3, in1=later[:, :].unsqueeze(1).broadcast_to([64, 128, 8]), op=ALU.add), reads=pfk_ + ["later"], writes=pfk_)
                    for h4 in range(2):
                        ps, pk = psum()
                        for q in range(4):
                            S.op("pe", lambda h, ps=ps, q=q, h4=h4: h.transpose(out=ps[:, q * 64:(q + 1) * 64], in_=p3[:, :, h4 * 4 + q], identity=ident[0:64, 0:64]), reads=pfk_ + ["ident"], writes=[pk])
                        S.op("act", lambda h, ps=ps, h4=h4: h.activation(out=EXT[:, h4 * 4:h4 * 4 + 4, :], in_=ps[:, 0:256].rearrange("p (a j) -> p a j", a=4), func=AF.Copy), reads=[pk], writes=[("EXT", h4)])
                    extk = [("EXT", 0), ("EXT", 1)]
                    for j in range(64):
                        kt_, vt_ = Kt[ki % 3], Vt[ki % 3]
                        kk_, vk_ = ("Kt", ki % 3), ("Vt", ki % 3)
                        vb_, vbk = Vb[ki % 2], ("Vb", ki % 2)
                        ktp, ktk = KTp[ki % 2], ("KTp", ki % 2)
                        sb_, sbk = sbt[ki % 2], ("sbt", ki % 2)
                        pt_, ptk = PTt[ki % 2], ("PTt", ki % 2)
                        ki += 1
                        S.idma(kt_[:, :], fk[l], idx[:, j:j + 1], reads=["idx"], writes=[kk_])
                        S.idma(vt_[:, :], fv[l], idx[:, j:j + 1], reads=["idx"], writes=[vk_])
                        ps, pk = psum()
                        for q in range(4):
                            S.op("pe", lambda h, ps=ps, q=q, kt_=kt_: h.transpose(out=ps[:, q * 128:(q + 1) * 128], in_=kt_[:, q * 128:(q + 1) * 128], identity=ident[:, :]), reads=[kk_, "ident"], writes=[pk])
                        S.op("act", lambda h, ps=ps, ktp=ktp: h.activation(out=ktp[:, :, :], in_=ps[:, :].rearrange("p (a t) -> p a t", a=4), func=AF.Copy), reads=[pk], writes=[ktk])
                        S.op("pool", lambda h, vb_=vb_, vt_=vt_: h.tensor_copy(out=vb_[:, :], in_=vt_[:, :]), reads=[vk_], writes=[vbk])
                        ps, pk = psum()
                        for h_ in range(8):
                            S.op("pe", lambda h, ps=ps, h_=h_, ktp=ktp, b=b: h.matmul(ps[:, h_ * 4:(h_ + 1) * 4], lhsT=ktp[:, h_ // 2, :], rhs=qpad[:, h_ // 2, h_ % 2, 4 * b:4 * b + 4], start=True, stop=True), reads=[ktk] + qpk, writes=[pk])
                        S.op("dve", lambda h, ps=ps, sb_=sb_, j=j: h.tensor_tensor(out=sb_[:, :, :], in0=ps[:, 0:32].rearrange("p (a q) -> p a q", a=8), in1=EXT[:, :, j].unsqueeze(2).broadcast_to([128, 8, 4]), op=ALU.add), reads=[pk] + extk, writes=[sbk])
                        S.op("act", lambda h, sb_=sb_, pt_=pt_: h.activation(out=pt_[:, :], in_=sb_[:, :, :].rearrange("p a q -> p (a q)"), func=AF.Exp), reads=[sbk], writes=[ptk])
                        S.op("pe", lambda h, pt_=pt_, vb_=vb_, j=j: h.matmul(O_[0:32, :], lhsT=pt_[:, :], rhs=vb_[:, :], start=(j == 0), stop=False), reads=[ptk, vbk], writes=[okey])
                        S.op("pe", lambda h, pt_=pt_, j=j: h.matmul(D_[0:32, 0:1], lhsT=pt_[:, :], rhs=onesb[:, 0:1], start=(j == 0), stop=False), reads=[ptk, "onesb"], writes=[dkey])
                    ps, pk = psum()
                    for h_ in range(8):
                        S.op("pe", lambda h, ps=ps, h_=h_, b=b: h.matmul(ps[0:NS, h_ * 4:(h_ + 1) * 4], lhsT=KTn[:, h_ // 2, :], rhs=qpad[:, h_ // 2, h_ % 2, 4 * b:4 * b + 4], start=True, stop=True), reads=["KTn"] + qpk, writes=[pk])
                    S.op("dve", lambda h, ps=ps: h.tensor_tensor(out=sbn[:, :, :], in0=ps[0:NS, 0:32].rearrange("p (a q) -> p a q", a=8), in1=negcT[:, :].unsqueeze(2).broadcast_to([NS, 8, 4]), op=ALU.add), reads=[pk, "negcT"], writes=["sbn"])
                    S.op("act", lambda h: h.activation(out=sbn[:, :, :].rearrange("p a q -> p (a q)"), in_=sbn[:, :, :].rearrange("p a q -> p (a q)"), func=AF.Exp), reads=["sbn"], writes=["sbn"])
                    S.op("dve", lambda h, b=b: h.tensor_tensor(out=PTn[:, :].rearrange("p (a q) -> p a q", a=8), in0=sbn[:, :, :], in1=MSK[:, b, :].unsqueeze(1).broadcast_to([NS, 8, 4]), op=ALU.mult), reads=["sbn", "maskn"], writes=["PTn"])
                    S.op("pe", lambda h: h.matmul(O_[0:32, :], lhsT=PTn[:, :], rhs=Vn[:, :], start=False, stop=True), reads=["PTn", "Vn"], writes=[okey])
                    S.op("pe", lambda h: h.matmul(D_[0:32, 0:1], lhsT=PTn[:, :], rhs=onesb[0:NS, 0:1], start=False, stop=True), reads=["PTn", "onesb"], writes=[dkey])
                    S.op("dve", lambda h: h.reciprocal(out=rdn[:, :], in_=D_[0:32, 0:1]), reads=[dkey], writes=["rdn"])
                    S.op("dve", lambda h: h.tensor_scalar(out=onorm[:, :], in0=O_[0:32, :], scalar1=rdn[:, 0:1], scalar2=None, op0=ALU.mult), reads=[okey, "rdn"], writes=["onorm"])
                    for h_ in range(8):
                        S.dma("sp", ybS[4 * b:4 * b + 4, h_ * 64:(h_ + 1) * 64], onorm[h_ * 4:(h_ + 1) * 4, h_ * 64:(h_ + 1) * 64], reads=["onorm"], writes=[("ybS", b, h_)])
                ybk = [("ybS", b, h_) for b in range(4) for h_ in range(8)]
                ps, pk = psum()
                for q in range(4):
                    S.op("pe", lambda h, ps=ps, q=q: h.transpose(out=ps[:, q * 16:(q + 1) * 16], in_=ybS[0:NS, q * 128:(q + 1) * 128], identity=ident[0:NS, 0:NS]), reads=ybk + ["ident"], writes=[pk])
                S.op("act", lambda h, ps=ps: h.activation(out=yTs[:, 2:6, :], in_=ps[:, 0:64].rearrange("p (a t) -> p a t", a=4), func=AF.Copy), reads=[pk], writes=[("yTs", "b")])
                if "ybs" in DBG and l == 0:
                    S.dma("sp", DBG["ybs"][:, :], ybS[:, :], reads=ybk)
                S.flush()

        mx.close()

        if stage >= 7:
            with contextlib.ExitStack() as ph:
                wcv = sb("wcv", [128, KC, 512], BF16, ph)
                gcol = sb("gcol", [128, KC], F32, ph)
                epsc = sb("epsc", [128, 1], F32, ph)
                epsl = sb("epsl", [128, 1], F32, ph)
                cwT = sb("cwT", [128, 2, 31], F32, ph)
                cbc = sb("cbc", [128, 2], F32, ph)
                lng = sb("lng", [128, 2], F32, ph)
                lnb = sb("lnb", [128, 2], F32, ph)
                sq = sb("sq", [128, KC, 512], BF16, ph)
                rbc = sb("rbc", [128, 512], F32, ph)
                xn = sb("xn", [128, KC, 512], BF16, ph)
                sig = sb("sig", [128, 512], F32, ph)
                U = sb("U", [128, 2, 30 + T], F32, ph)
                Y = sb("Y", [128, 2, T], F32, ph)
                Us = sb("Us", [128, 2, 4, 34], F32, ph)
                Ys = sb("Ys", [128, 2, 4, 4], F32, ph)
                mbc = sb("mbc", [128, 512], F32, ph)
                sq2 = sb("sq2", [128, 2, 512], F32, ph)
                cvt = sb("cvt", [32, 4, 256], F32, ph)
                pco = sb("pco", [32, 4, 256], F32, ph)
                cvo = sb("cvo", [32, 4, 256], F32, ph)
                ycd = sb("ycd", [128, 256], F32, ph)
                scr = dict(sq=sq, rbc=rbc, eps=epsc)
                load_w(wcv, W["w_in"][l], CONV0, 512, "wcv")
                wkeys = [("wcv", c) for c in range(KC)]
                load_cols(gcol, W["g_mix"][l], KC, "gcol")
                S.op("pool", lambda h: h.memset(epsc[:], RMS_EPS), writes=["eps"])
                S.op("pool", lambda h: h.memset(epsl[:], 1e-5), writes=["epsl"])
                for c_ in range(2):
                    S.dma("sp", cwT[:, c_, :], W["conv_w"][l][:, c_ * 128:(c_ + 1) * 128].rearrange("j p -> p j"), writes=[("cwT", c_)], allow_slow_non_contiguous=True)
                load_cols(cbc, W["conv_b"][l], 2, "cbc")
                load_cols(lng, W["conv_ln_g"][l], 2, "lng")
                load_cols(lnb, W["conv_ln_b"][l], 2, "lnb")
                S.op("pool", lambda h: h.memset(U[:, :, 0:30], 0.0), writes=["Upad"])
                S.dma("sp", cvt[0:30, :, :], scv[l].rearrange("b t c -> t b c"), writes=["cvt"])
                for b in range(4):
                    ps, pk = psum()
                    for ch in range(2):
                        S.op("pe", lambda h, ps=ps, ch=ch, b=b: h.transpose(out=ps[:, ch * 32:ch * 32 + 30], in_=cvt[0:30, b, ch * 128:(ch + 1) * 128], identity=ident[0:30, 0:30]),
                             reads=["cvt", "ident"], writes=[pk])
                    S.op("act", lambda h, ps=ps, b=b: h.activation(out=Us[:, :, b, 0:30], in_=ps[:, 0:64].rearrange("p (c t) -> p c t", c=2)[:, :, 0:30], func=AF.Copy),
                         reads=[pk], writes=[("Us", b)])

                def glu_block(xsrc, blk0, n, dst_fn, dkey, vw=lambda a: a):
                    rmsnorm_fm(xsrc, blk0, n, gcol, xn, "xn", scr)
                    xnk = [("xn", c) for c in range(KC)]
                    pss = []
                    for oc in range(4):
                        ps, pk = psum()
                        for c in range(KC):
                            S.op("pe", lambda h, ps=ps, c=c, oc=oc: h.matmul(ps[:, 0:n], lhsT=wcv[:, c, oc * 128:(oc + 1) * 128], rhs=xn[:, c, 0:n], start=(c == 0), stop=(c == KC - 1)),
                                 reads=xnk + wkeys, writes=[pk])
                        pss.append((ps, pk))
                    for ch in range(2):
                        S.op("act", lambda h, ps=pss[2 + ch][0]: h.activation(out=sig[:, 0:n], in_=ps[:, 0:n], func=AF.Sigmoid), reads=[pss[2 + ch][1]], writes=["sig"])
                        S.op("dve", lambda h, ps=pss[ch][0], ch=ch: h.tensor_tensor(out=dst_fn(ch), in0=vw(ps[:, 0:n]), in1=vw(sig[:, 0:n]), op=ALU.mult),
                             reads=[pss[ch][1], "sig"], writes=[dkey])

                for b in range(4):
                    glu_block(xT, b * 512, 512, lambda ch, b=b: U[:, ch, 30 + b * 512:30 + (b + 1) * 512], ("U", b))
                glu_block(xTs, 0, NS, lambda ch: Us[:, ch, :, 30:34], "Usn", vw=lambda a: a.rearrange("p (b t) -> p b t", b=4))

                ukeys = [("U", b) for b in range(4)] + ["Upad"]
                for ch in range(2):
                    for j in range(31):
                        if j == 0:
                            S.op("dve", lambda h, ch=ch: h.tensor_scalar(out=Y[:, ch, :], in0=U[:, ch, 0:T], scalar1=cwT[:, ch, 0:1], scalar2=cbc[:, ch:ch + 1], op0=ALU.mult, op1=ALU.add),
                                 reads=ukeys + [("cwT", 0), ("cwT", 1), "cbc"], writes=[("Y", ch)])
                            S.op("dve", lambda h, ch=ch: h.tensor_scalar(out=Ys[:, ch, :, :], in0=Us[:, ch, :, 0:4], scalar1=cwT[:, ch, 0:1], scalar2=cbc[:, ch:ch + 1], op0=ALU.mult, op1=ALU.add),
                                 reads=[("Us", b) for b in range(4)] + ["Usn", ("cwT", 0), ("cwT", 1), "cbc"], writes=[("Ys", ch)])
                        else:
                            S.op("dve", lambda h, ch=ch, j=j: h.scalar_tensor_tensor(out=Y[:, ch, :], in0=U[:, ch, j:j + T], scalar=cwT[:, ch, j:j + 1], in1=Y[:, ch, :], op0=ALU.mult, op1=ALU.add),
                                 reads=[("Y", ch)], writes=[("Y", ch)])
                            S.op("dve", lambda h, ch=ch, j=j: h.scalar_tensor_tensor(out=Ys[:, ch, :, :], in0=Us[:, ch, :, j:j + 4], scalar=cwT[:, ch, j:j + 1], in1=Ys[:, ch, :, :], op0=ALU.mult, op1=ALU.add),
                                 reads=[("Ys", ch)], writes=[("Ys", ch)])

                def ln_silu(yv, n, outv, ykeys, okey):
                    ps, pk = psum()
                    for ch in range(2):
                        S.op("pe", lambda h, ps=ps, ch=ch: h.matmul(ps[:, 0:n], lhsT=onesf[:, :], rhs=yv(ch), start=(ch == 0), stop=(ch == 1)), reads=ykeys + ["onesf"], writes=[pk])
                    S.op("act", lambda h, ps=ps: h.activation(out=mbc[:, 0:n], in_=ps[:, 0:n], func=AF.Copy, scale=1.0 / 256), reads=[pk], writes=["mbc"])
                    for ch in range(2):
                        S.op("dve", lambda h, ch=ch: h.tensor_tensor(out=yv(ch), in0=yv(ch), in1=mbc[:, 0:n], op=ALU.subtract), reads=["mbc"] + ykeys, writes=ykeys)
                        S.op("act", lambda h, ch=ch: h.activation(out=sq2[:, ch, 0:n], in_=yv(ch), func=AF.Square), reads=ykeys, writes=[("sq2", ch)])
                    ps, pk = psum()
                    for ch in range(2):
                        S.op("pe", lambda h, ps=ps, ch=ch: h.matmul(ps[:, 0:n], lhsT=onesf[:, :], rhs=sq2[:, ch, 0:n], start=(ch == 0), stop=(ch == 1)), reads=[("sq2", 0), ("sq2", 1), "onesf"], writes=[pk])
                    S.op("act", lambda h, ps=ps: h.activation(out=mbc[:, 0:n], in_=ps[:, 0:n], func=AF.Sqrt, scale=1.0 / 256, bias=epsl[:, 0:1]), reads=[pk, "epsl"], writes=["mbc"])
                    S.op("dve", lambda h: h.reciprocal(out=mbc[:, 0:n], in_=mbc[:, 0:n]), reads=["mbc"], writes=["mbc"])
                    for ch in range(2):
                        S.op("dve", lambda h, ch=ch: h.scalar_tensor_tensor(out=yv(ch), in0=yv(ch), scalar=lng[:, ch:ch + 1], in1=mbc[:, 0:n], op0=ALU.mult, op1=ALU.mult),
                             reads=["mbc", "lng"] + ykeys, writes=ykeys)
                        S.op("act", lambda h, ch=ch: h.activation(out=outv(ch), in_=yv(ch), func=AF.Silu, bias=lnb[:, ch:ch + 1]), reads=ykeys + ["lnb"], writes=[okey])

                for b in range(4):
                    ln_silu(lambda ch, b=b: Y[:, ch, b * 512:(b + 1) * 512], 512, lambda ch, b=b: yT[:, 6 + ch, b * 512:(b + 1) * 512], [("Y", 0), ("Y", 1)], ("yT", "c", b))
                ln_silu(lambda ch: Ys[:, ch, :, :].rearrange("p b t -> p (b t)"), NS, lambda ch: yTs[:, 6 + ch, :], [("Ys", 0), ("Ys", 1)], ("yTs", "c"))

                ps, pk = psum()
                for ch in range(2):
                    S.op("pe", lambda h, ps=ps, ch=ch: h.transpose(out=ps[0:30, ch * 128:(ch + 1) * 128], in_=U[:, ch, T:T + 30], identity=ident[:, :]), reads=ukeys + ["ident"], writes=[pk])
                S.op("act", lambda h, ps=ps: h.activation(out=pco[0:30, 0, :], in_=ps[0:30, 0:256], func=AF.Copy), reads=[pk], writes=["pco"])
                S.dma("sp", o_pcv[l], pco[0:30, 0, :], reads=["pco"])
                for b in range(4):
                    ps, pk = psum()
                    for ch in range(2):
                        S.op("pe", lambda h, ps=ps, ch=ch, b=b: h.transpose(out=ps[0:30, ch * 128:(ch + 1) * 128], in_=Us[:, ch, b, 4:34], identity=ident[:, :]),
                             reads=[("Us", b), "Usn", "ident"], writes=[pk])
                    S.op("act", lambda h, ps=ps, b=b: h.activation(out=cvo[0:30, b, :], in_=ps[0:30, 0:256], func=AF.Copy), reads=[pk], writes=[("cvo", b)])
                    S.dma("sp", o_scv[l, b], cvo[0:30, b, :], reads=[("cvo", b)])
                if "yc" in DBG:
                    for ch in range(2):
                        S.dma("sp", DBG["yc"][ch], yT[:, 6 + ch, :], reads=[("yT", "c", b) for b in range(4)])
                S.flush()

        if stage >= 8:
            with contextlib.ExitStack() as ph:
                wrw = sb("wrw", [128, KC, 1024], BF16, ph)
                gcol = sb("gcol", [128, KC], F32, ph)
                epsc = sb("epsc", [128, 1], F32, ph)
                mu = sb("mu", [128, KC], F32, ph)
                WA = sb("WA", [128, 256], BF16, ph)
                GU = sb("GU", [128, 256], BF16, ph)
                cw0 = sb("cw0", [128, 2], F32, ph)
                ca0 = sb("ca0", [128, 2], F32, ph)
                ckk = sb("ckk", [128, 2], F32, ph)
                cka = sb("cka", [128, 2], F32, ph)
                crk = sb("crk", [128, 2], F32, ph)
                clg = sb("clg", [128, 2], F32, ph)
                clb = sb("clb", [128, 2], F32, ph)
                cnh = sb("cnh", [128, 1], F32, ph)
                cge = sb("cge", [128, 1], F32, ph)
                onesbd = sb("onesbd", [128, 128], F32, ph)
                identb = sb("identb", [128, 128], BF16, ph)
                MG = sb("MG", [128, 128], F32, ph)
                ML = sb("ML", [64, 64], F32, ph)
                NB, NCH, NI = 256, 4, 16
                sq = sb("sq", [128, KC, NB], BF16, ph)
                rbc = sb("rbc", [128, NB], F32, ph)
                xn = sb("xn", [128, KC, NB], BF16, ph)
                scr = dict(sq=sq, rbc=rbc, eps=epsc)
                ZB = sb("ZB", [128, 8, NB + 2], F32, ph)
                zlast = sb("zlast", [128, 8, 1], F32, ph)
                DT = sb("DT", [128, 8, NB], F32, ph)
                LA = sb("LA", [128, NB], BF16, ph)
                SG = sb("SG", [128, NB], BF16, ph)
                FN = ("ew", "a", "kk", "t1", "km", "cs", "wi", "wv", "we", "be")
                F = {k: (ZB[:, i, 0:NB] if i < 8 else sb("f_" + k, [128, NB], F32, ph)) for i, k in enumerate(FN)}
                FK = {k: (("ZB", i) if i < 8 else k) for i, k in enumerate(FN)}
                AR = sb("AR", [128, 2, NCH, 2, 64], BF16, ph)
                BK = sb("BK", [128, 2, NCH, 2, 64], BF16, ph)
                VB = sb("VB", [128, 2, 64 + NB], BF16, ph)
                Gt = sb("Gt", [128, 2, NB], F32, ph)
                BON = sb("BON", [128, 2, NB], F32, ph)
                WCb = sb("WCb", [128, 2, NCH], F32, ph)
                KB = sb("KB", [128, NCH, 2, 128], BF16, ph)
                ATM = sb("ATM", [128, NCH, 2, 128], BF16, ph)
                XV = sb("XV", [128, NCH, 4, 64], BF16, ph)
                X = sb("X", [128, NCH, 4, 64], BF16, ph)
                GTs = sb("GTs", [128, NI, 128], BF16, ph)
                Nb = [sb(f"Nb{i}", [128, NI, 64], BF16, ph) for i in range(2)]
                Lb = [sb(f"Lb{i}", [128, NI, 64], BF16, ph) for i in range(2)]
                Pb = [sb(f"Pb{i}", [128, NI, 64], BF16, ph) for i in range(2)]
                Qb = [sb(f"Qb{i}", [128, NI, 64], BF16, ph) for i in range(2)]
                XAKs = sb("XAKs", [128, NI, 64], BF16, ph)
                Wms = sb("Wms", [128, NCH, 2, 2, 64], BF16, ph)
                HBh = sb("HBh", [128, NCH + 1, 2, 2, 64], BF16, ph)
                Hf = sb("Hf", [128, 2, 64], F32, ph)
                Hf2 = sb("Hf2", [128, 2, 64], F32, ph)
                HB = sb("HB", [128, NCH + 1, 2, 64], BF16, ph)
                Ysb = sb("Ysb", [128, 2, NB], F32, ph)
                MB = sb("MB", [128, NB], F32, ph)
                SQ2 = sb("SQ2", [128, NB], F32, ph)
                sto = sb("sto", [64, 4, 64], F32, ph)
                tmpG = [sb(f"tmpG{i}", [128, 128], F32, ph) for i in range(2)]
                Pf = sb("Pf", [64, NI, 64], F32, ph)
                XTs = Pf
                Qf = sb("Qf", [64, NI, 64], F32, ph)
                Xf = sb("Xf", [64, 4, 64], F32, ph)

                load_w(wrw, W["w_in"][l], 0, 1024, "wrw")
                wkeys = [("wrw", c) for c in range(KC)]
                load_cols(gcol, W["g_mix"][l], KC, "gcol")
                load_cols(mu, W["rwkv_mu"][l], KC, "mu")
                S.dma("pool", WA[0:64, :], W["rwkv_w_up"][l], writes=["WA0"])
                S.dma("pool", WA[64:128, :], W["rwkv_a_up"][l], writes=["WA1"])
                S.dma("pool", GU[:, :], W["rwkv_g_up"][l], writes=["GU"])
                for (t_, nm) in ((cw0, "rwkv_w0"), (ca0, "rwkv_a0"), (ckk, "rwkv_k_k"), (cka, "rwkv_k_a"), (crk, "rwkv_r_k"), (clg, "rwkv_ln_g"), (clb, "rwkv_ln_b")):
                    load_cols(t_, W[nm][l], 2, "c_" + nm)
                ckeys = ["c_rwkv_w0", "c_rwkv_a0", "c_rwkv_k_k", "c_rwkv_k_a", "c_rwkv_r_k", "c_rwkv_ln_g", "c_rwkv_ln_b", "cnh", "cge"]
                S.op("dve", lambda h: h.tensor_scalar(out=cw0[:, :], in0=cw0[:, :], scalar1=-1.0, scalar2=None, op0=ALU.mult), reads=["c_rwkv_w0"], writes=["c_rwkv_w0"])
                S.op("pool", lambda h: h.memset(epsc[:], RMS_EPS), writes=["eps"])
                S.op("pool", lambda h: h.memset(cnh[:], -0.5), writes=["cnh"])
                S.op("pool", lambda h: h.memset(cge[:], 64e-5), writes=["cge"])
                S.op("pool", lambda h: h.memset(onesbd[:], 0.0), writes=["onesbd"])
                S.op("pool", lambda h: h.memset(onesbd[0:64, 0:64], 1.0), reads=["onesbd"], writes=["onesbd"])
                S.op("pool", lambda h: h.memset(onesbd[64:128, 64:128], 1.0), reads=["onesbd"], writes=["onesbd"])
                S.op("pool", lambda h: h.tensor_copy(out=identb[:], in_=ident[:]), reads=["ident"], writes=["identb"])
                for r0_ in (0, 64):
                    S.op("pool", lambda h, r0_=r0_: h.affine_select(out=MG[r0_:r0_ + 64, 0:64], in_=onesf[r0_:r0_ + 64, 0:64], pattern=[[1, 64]], compare_op=ALU.is_ge, fill=0.0, base=-1, channel_multiplier=-1),
                         reads=["onesf"], writes=[("MG", r0_, 0)])
                    S.op("pool", lambda h, r0_=r0_: h.affine_select(out=MG[r0_:r0_ + 64, 64:128], in_=onesf[r0_:r0_ + 64, 0:64], pattern=[[1, 64]], compare_op=ALU.is_ge, fill=0.0, base=0, channel_multiplier=-1),
                         reads=["onesf"], writes=[("MG", r0_, 1)])
                mgk = [("MG", 0, 0), ("MG", 0, 1), ("MG", 64, 0), ("MG", 64, 1)]
                S.op("pool", lambda h: h.affine_select(out=ML[:, :], in_=onesf[0:64, 0:64], pattern=[[-1, 64]], compare_op=ALU.is_ge, fill=0.0, base=-1, channel_multiplier=1),
                     reads=["onesf"], writes=["ML"])
                S.op("pool", lambda h: h.memset(zlast[:], 0.0), writes=["zlast"])
                S.op("pool", lambda h: h.memset(VB[:, :, 0:64], 0.0), writes=["VBpad"])
                S.op("pool", lambda h: h.memset(Hf[:], 0.0), writes=["Hf"])
                S.op("pool", lambda h: h.memset(HB[:, 0, :, :], 0.0), writes=[("HB", 0)])
                for zi, zt in enumerate(([ATM, XV, XAKs, Wms, HBh] + Nb + Lb + Pb + Qb) if (RWSUB >= 4 or ZI) else []):
                    nd_ = len(zt.shape)
                    pat_ = {3: "p a b -> p (a b)", 4: "p a b c -> p (a b c)", 5: "p a b c d -> p (a b c d)"}[nd_]
                    S.op("pool", lambda h, zt=zt, pat_=pat_: h.memset(zt[:].rearrange(pat_), 0.0), writes=[("zinit", zi)])

                S.flush()

                def cp(eng, out, in_, reads, writes):
                    if eng == "act":
                        S.op("act", lambda h: h.activation(out=out, in_=in_, func=AF.Copy), reads=reads, writes=writes)
                    else:
                        S.op(eng, lambda h: h.tensor_copy(out=out, in_=in_), reads=reads, writes=writes)

                def block_pre(MGm, mgkm, MLm, mlkey):
                    ark = [("AR", oc, i) for oc in range(2) for i in range(2)]
                    bkk = [("BK", oc, i) for oc in range(2) for i in range(2)]
                    for oc in range(2):
                        for c4 in range(NCH // 4):
                            ps, pk = psum()
                            for q in range(4):
                                ch = c4 * 4 + q
                                S.op("pe", lambda h, ps=ps, q=q, ch=ch, oc=oc: h.matmul(ps[:, q * 128:(q + 1) * 128], lhsT=BK[:, oc, ch, :, :].rearrange("p a t -> p (a t)"), rhs=identb[:, :], start=True, stop=True),
                                     reads=bkk + ["identb"], writes=[pk])
                            cp("act", KB[:, c4 * 4:c4 * 4 + 4, oc, :], ps[:, :].rearrange("p (q k) -> p q k", q=4), [pk], [("KB", oc, c4)])
                            ps, pk = psum()
                            for q in range(4):
                                ch = c4 * 4 + q
                                S.op("pe", lambda h, ps=ps, q=q, ch=ch, oc=oc: h.matmul(ps[:, q * 128:(q + 1) * 128], lhsT=AR[:, oc, ch, :, :].rearrange("p a t -> p (a t)"), rhs=identb[:, :], start=True, stop=True),
                                     reads=ark + ["identb"], writes=[pk])
                            cp("dve", ATM[0:64, c4 * 4:c4 * 4 + 4, oc, :], ps[0:64, :].rearrange("p (q k) -> p q k", q=4), [pk] + ([("zinit", 0)] if RWSUB >= 4 else []), [("ATM", oc, c4)])
                            ps, pk = psum()
                            for q in range(4):
                                ch = c4 * 4 + q
                                S.op("pe", lambda h, ps=ps, q=q, ch=ch, oc=oc: h.matmul(ps[:, q * 128:(q + 1) * 128], lhsT=VB[:, oc, ch * 64:ch * 64 + 128], rhs=identb[:, :], start=True, stop=True),
                                     reads=[("VB", oc), "VBpad", "identb"], writes=[pk])
                            cp("act", X[:, :, :, :].rearrange("p c h v -> p c (h v)")[64:128, c4 * 4:c4 * 4 + 4, oc * 128:(oc + 1) * 128], ps[64:128, :].rearrange("p (q k) -> p q k", q=4), [pk], [("Xv", oc, c4)])
                            if RWSUB >= 4 or XVE:
                              cp("dve", XV[:, :, :, :].rearrange("p c h v -> p c (h v)")[64:128, c4 * 4:c4 * 4 + 4, oc * 128:(oc + 1) * 128], ps[64:128, :].rearrange("p (q k) -> p q k", q=4), [pk, ("zinit", 1)], [("XVv", oc, c4)])
                    kbk = [("KB", oc, c4) for oc in range(2) for c4 in range(NCH // 4)]
                    atk = [("ATM", oc, c4) for oc in range(2) for c4 in range(NCH // 4)]
                    xvk = [("Xv", oc, c4) for oc in range(2) for c4 in range(NCH // 4)]
                    xvvk = [("XVv", oc, c4) for oc in range(2) for c4 in range(NCH // 4)]
                    for inst in range(NI):
                        ps, pk = psum()
                        ch, h_ = inst // 4, inst % 4
                        oc, pb = h_ // 2, (h_ % 2) * 64
                        S.op("pe", lambda h, ps=ps, ch=ch, oc=oc, pb=pb: h.matmul(ps[:, 0:128], lhsT=BK[pb:pb + 64, oc, ch, :, :].rearrange("p a t -> p (a t)"),
                                                                             rhs=AR[pb:pb + 64, oc, ch, :, :].rearrange("p a t -> p (a t)"), start=True, stop=True),
                             reads=ark + bkk, writes=[pk])
                        tg = tmpG[inst % 2]
                        S.op("dve", lambda h, ps=ps, tg=tg: h.tensor_tensor(out=tg[:, :], in0=ps[:, 0:128], in1=MGm[:, :], op=ALU.mult),
                             reads=[pk] + mgkm, writes=[("tmpG", inst % 2)])
                        S.op("act", lambda h, tg=tg, inst=inst: h.activation(out=GTs[:, inst, :], in_=tg[:, :], func=AF.Copy),
                             reads=[("tmpG", inst % 2)], writes=[("GTs", inst // 4, inst % 4)])
                    gtk = [("GTs", g4, q_) for g4 in range(NI // 4) for q_ in range(4)]
                    for inst in range(NI):
                        ps, pk = psum()
                        ch, h_ = inst // 4, inst % 4
                        oc, pb = h_ // 2, (h_ % 2) * 64
                        S.op("pe", lambda h, ps=ps, ch=ch, oc=oc, pb=pb: h.matmul(ps[0:64, 0:64], lhsT=AR[pb:pb + 64, oc, ch, 0, :], rhs=BK[pb:pb + 64, oc, ch, 0, :], start=True, stop=True),
                             reads=ark + bkk, writes=[pk])
                        tg = tmpG[inst % 2]
                        S.op("dve", lambda h, ps=ps, tg=tg: h.tensor_tensor(out=tg[0:64, 0:64], in0=ps[0:64, 0:64], in1=MLm[:, :], op=ALU.mult),
                             reads=[pk, mlkey], writes=[("tmpG", inst % 2)])
                        S.op("act", lambda h, tg=tg, inst=inst: h.activation(out=Lb[0][0:64, inst, :], in_=tg[0:64, 0:64], func=AF.Copy),
                             reads=[("tmpG", inst % 2)], writes=[("L", 0, inst // 8, inst % 8)])
                    for g8 in range(NI // 8):
                        S.op("dve", lambda h, g8=g8: h.tensor_copy(out=Nb[0][0:64, g8 * 8:g8 * 8 + 8, :], in_=GTs[0:64, g8 * 8:g8 * 8 + 8, 0:64]), reads=[("GTs", 2 * g8 + a_, q_) for a_ in range(2) for q_ in range(4)], writes=[("N", 0, g8)])
                        S.op("dve", lambda h, g8=g8: h.tensor_tensor(out=Pb[0][0:64, g8 * 8:g8 * 8 + 8, :], in0=Nb[0][0:64, g8 * 8:g8 * 8 + 8, :], in1=identb[0:64, 0:64].unsqueeze(1).broadcast_to([64, 8, 64]), op=ALU.add),
                             reads=[("N", 0, g8), "identb"], writes=[("P", 0, g8)])
                        S.op("dve", lambda h, g8=g8: h.tensor_copy(out=Pf[:, g8 * 8:g8 * 8 + 8, :], in_=Pb[0][0:64, g8 * 8:g8 * 8 + 8, :]), reads=[("P", 0, g8)], writes=[("Pf", g8)])
                        S.op("dve", lambda h, g8=g8: h.tensor_tensor(out=Qb[0][0:64, g8 * 8:g8 * 8 + 8, :], in0=Lb[0][0:64, g8 * 8:g8 * 8 + 8, :], in1=identb[0:64, 0:64].unsqueeze(1).broadcast_to([64, 8, 64]), op=ALU.add),
                             reads=[("L", 0, g8, q_) for q_ in range(8)] + ["identb"], writes=[("Q", 0, g8), ("L", 0, g8)])
                        S.op("dve", lambda h, g8=g8: h.tensor_copy(out=Qf[:, g8 * 8:g8 * 8 + 8, :], in_=Qb[0][0:64, g8 * 8:g8 * 8 + 8, :]), reads=[("Q", 0, g8)], writes=[("Qf", g8)])
                    for j in range(1, 6):
                        a_, b_ = (j - 1) % 2, j % 2
                        for g8 in range(NI // 8):
                            sl8 = slice(g8 * 8, g8 * 8 + 8)
                            psn, pkn = psum()
                            for q in range(8):
                                i_ = g8 * 8 + q
                                S.op("pe", lambda h, psn=psn, q=q, i_=i_, a_=a_: h.matmul(psn[0:64, q * 64:(q + 1) * 64], lhsT=Lb[a_][:, i_, :], rhs=Nb[a_][:, i_, :], start=True, stop=True),
                                     reads=[("L", a_, g8), ("N", a_, g8)], writes=[pkn])
                            cp("act", Nb[b_][0:64, sl8, :], psn[0:64, :].rearrange("p (q k) -> p q k", q=8), [pkn], [("N", b_, g8)])
                            if j < 5:
                                psl, pkl = psum()
                                for q in range(8):
                                    i_ = g8 * 8 + q
                                    S.op("pe", lambda h, psl=psl, q=q, i_=i_, a_=a_: h.matmul(psl[0:64, q * 64:(q + 1) * 64], lhsT=Nb[a_][:, i_, :], rhs=Lb[a_][:, i_, :], start=True, stop=True),
                                         reads=[("L", a_, g8), ("N", a_, g8)], writes=[pkl])
                                cp("act", Lb[b_][0:64, sl8, :], psl[0:64, :].rearrange("p (q k) -> p q k", q=8), [pkl], [("L", b_, g8)])
                            psp, pkp = psum()
                            for q in range(8):
                                i_ = g8 * 8 + q
                                S.op("pe", lambda h, psp=psp, q=q, i_=i_, a_=a_, b_=b_: h.matmul(psp[0:64, q * 64:(q + 1) * 64], lhsT=Qb[a_][:, i_, :], rhs=Nb[b_][:, i_, :], start=True, stop=True),
                                     reads=[("Q", a_, g8), ("N", b_, g8)], writes=[pkp])
                            S.op("dve", lambda h, psp=psp, sl8=sl8: h.tensor_tensor(out=Pf[:, sl8, :], in0=psp[0:64, :].rearrange("p (q k) -> p q k", q=8), in1=Pf[:, sl8, :], op=ALU.add),
                                 reads=[pkp, ("Pf", g8)], writes=[("Pf", g8)])
                            S.op("act", lambda h, sl8=sl8, b_=b_: h.activation(out=Pb[b_][0:64, sl8, :], in_=Pf[:, sl8, :], func=AF.Copy), reads=[("Pf", g8), ("P", a_, g8)], writes=[("P", b_, g8)])
                            if j < 5:
                                psq, pkq = psum()
                                for q in range(8):
                                    i_ = g8 * 8 + q
                                    S.op("pe", lambda h, psq=psq, q=q, i_=i_, a_=a_, b_=b_: h.matmul(psq[0:64, q * 64:(q + 1) * 64], lhsT=Nb[b_][:, i_, :], rhs=Qb[a_][:, i_, :], start=True, stop=True),
                                         reads=[("Q", a_, g8), ("N", b_, g8)], writes=[pkq])
                                S.op("dve", lambda h, psq=psq, sl8=sl8: h.tensor_tensor(out=Qf[:, sl8, :], in0=psq[0:64, :].rearrange("p (q k) -> p q k", q=8), in1=Qf[:, sl8, :], op=ALU.add),
                                     reads=[pkq, ("Qf", g8)], writes=[("Qf", g8)])
                                S.op("act", lambda h, sl8=sl8, b_=b_: h.activation(out=Qb[b_][0:64, sl8, :], in_=Qf[:, sl8, :], func=AF.Copy), reads=[("Qf", g8), ("Q", a_, g8)], writes=[("Q", b_, g8)])
                    PF = Pb[1]
                    pfk = lambda g8: ("P", 1, g8)
                    for g8 in range(NI // 8):
                        sl8 = slice(g8 * 8, g8 * 8 + 8)
                        ps, pk = psum()
                        for q in range(8):
                            i_ = g8 * 8 + q
                            ch, h_ = i_ // 4, i_ % 4
                            S.op("pe", lambda h, ps=ps, q=q, i_=i_, ch=ch, h_=h_: h.matmul(ps[0:64, q * 64:(q + 1) * 64], lhsT=GTs[:, i_, 0:64], rhs=XV[:, ch, h_, :], start=True, stop=True),
                                 reads=gtk + xvvk, writes=[pk])
                        cp("act", XAKs[0:64, sl8, :], ps[0:64, :].rearrange("p (q k) -> p q k", q=8), [pk], [("XAK", g8)])
                        ps, pk = psum()
                        for q in range(8):
                            i_ = g8 * 8 + q
                            S.op("pe", lambda h, ps=ps, q=q, i_=i_: h.matmul(ps[0:64, q * 64:(q + 1) * 64], lhsT=PF[:, i_, :], rhs=XAKs[:, i_, :], start=True, stop=True),
                                 reads=[pfk(g8), ("XAK", g8)], writes=[pk])
                        cp("dve", XTs[:, sl8, :], ps[0:64, :].rearrange("p (q k) -> p q k", q=8), [pk, ("Pf", g8)], [("Pf", g8)])
                        ps, pk = psum()
                        for q in range(8):
                            i_ = g8 * 8 + q
                            ch, h_ = i_ // 4, i_ % 4
                            oc, pb = h_ // 2, (h_ % 2) * 64
                            col = ((ch % 2) * 2 + oc) * 64
                            S.op("pe", lambda h, ps=ps, i_=i_, ch=ch, oc=oc, pb=pb, col=col, h_=h_: h.matmul(ps[pb:pb + 64, col:col + 64], lhsT=ATM[:, ch, oc, (h_ % 2) * 64:(h_ % 2) * 64 + 64], rhs=PF[:, i_, :], start=True, stop=True),
                                 reads=atk + [pfk(g8)], writes=[pk])
                        for c2 in range(2):
                            for hh in range(2):
                                cp("act" if hh == 0 else "dve", Wms[hh * 64:(hh + 1) * 64, g8 * 2 + c2, hh, :, :], ps[hh * 64:(hh + 1) * 64, c2 * 128:(c2 + 1) * 128].rearrange("p (o t) -> p o t", o=2),
                                   [pk, ("zinit", 3)], [("Wm", g8, c2, hh)])
                    wmk = [("Wm", g8, c2, hh) for g8 in range(NI // 8) for c2 in range(2) for hh in range(2)]
                    xtk = [("Pf", g8) for g8 in range(NI // 8)]
                    return dict(ark=ark, bkk=bkk, kbk=kbk, atk=atk, xvk=xvk, xvvk=xvvk, gtk=gtk, wmk=wmk, xtk=xtk, PF=PF)

                for blk in range(T // NB):
                    n = NB
                    blk0 = blk * NB
                    if RWSUB <= 0:
                        break
                    rmsnorm_fm(xT, blk0, n, gcol, xn, "xn", scr)
                    xnk = [("xn", c) for c in range(KC)]
                    if RWX >= 2:
                        S.op("dve", lambda h: h.tensor_copy(out=ZB[:, :, 0:1], in_=zlast[:, :, :]), reads=["zlast"], writes=["ZB0"])
                    for oc in range(8):
                        ps, pk = psum()
                        for c in range(KC):
                            S.op("pe", lambda h, ps=ps, c=c, oc=oc: h.matmul(ps[:, 0:n], lhsT=wrw[:, c, oc * 128:(oc + 1) * 128], rhs=xn[:, c, 0:n], start=(c == 0), stop=(c == KC - 1)),
                                 reads=xnk + wkeys, writes=[pk])
                        cp("act" if oc % 2 == 0 else "dve", ZB[:, oc, 1:NB + 1], ps[:, 0:n], [pk], [("ZB", oc)])
                    zbk = [("ZB", oc) for oc in range(8)] + ["ZB0"]
                    if RWX >= 2:
                        S.op("dve", lambda h: h.tensor_copy(out=zlast[:, :, :], in_=ZB[:, :, NB:NB + 1]), reads=zbk, writes=["zlast"])
                    if RWX >= 3:
                        S.op("dve", lambda h: h.tensor_tensor(out=DT[:, :, :], in0=ZB[:, :, 0:NB], in1=ZB[:, :, 1:NB + 1], op=ALU.subtract), reads=zbk, writes=["DT"])
                    for oc in range(8 if RWX >= 4 else 0):
                        S.op("dve", lambda h, oc=oc: h.tensor_scalar(out=DT[:, oc, :], in0=DT[:, oc, :], scalar1=mu[:, oc:oc + 1], scalar2=None, op0=ALU.mult),
                             reads=["DT", "mu"], writes=["DT"])
                        S.op("dve", lambda h, oc=oc: h.tensor_tensor(out=DT[:, oc, :], in0=DT[:, oc, :], in1=ZB[:, oc, 1:NB + 1], op=ALU.add),
                             reads=["DT"] + zbk, writes=["DT"])
                    if RWSUB < 2:
                        continue
                    S.op("act", lambda h: h.activation(out=LA[0:64, :], in_=DT[0:64, 6, :], func=AF.Tanh), reads=["DT"], writes=["LA0"])
                    S.op("act", lambda h: h.activation(out=LA[64:128, :], in_=DT[64:128, 6, :], func=AF.Copy), reads=["DT"], writes=["LA1"])
                    S.op("act", lambda h: h.activation(out=SG[:, :], in_=DT[:, 7, :], func=AF.Sigmoid), reads=["DT"], writes=["SG"])
                    for oc in range(2):
                        rr, kq, vv = DT[:, oc, :], DT[:, 2 + oc, :], DT[:, 4 + oc, :]
                        psw, pkw = psum()
                        S.op("pe", lambda h, psw=psw, oc=oc: h.matmul(psw[:, 0:n], lhsT=WA[0:64, oc * 128:(oc + 1) * 128], rhs=LA[0:64, :], start=True, stop=True), reads=["WA0", "LA0"], writes=[pkw])
                        psa, pka = psum()
                        S.op("pe", lambda h, psa=psa, oc=oc: h.matmul(psa[:, 0:n], lhsT=WA[64:128, oc * 128:(oc + 1) * 128], rhs=LA[64:128, :], start=True, stop=True), reads=["WA1", "LA1"], writes=[pka])
                        psg, pkg = psum()
                        S.op("pe", lambda h, psg=psg, oc=oc: h.matmul(psg[:, 0:n], lhsT=GU[:, oc * 128:(oc + 1) * 128], rhs=SG[:, :], start=True, stop=True), reads=["GU", "SG"], writes=[pkg])
                        ew, aa, kk, t1, km, cs, wi, wv, we, be = (F[k] for k in FN)
                        S.op("act", lambda h, psw=psw, oc=oc: h.activation(out=ew[:, :], in_=psw[:, 0:n], func=AF.Exp, scale=-1.0, bias=cw0[:, oc:oc + 1]), reads=[pkw] + ckeys, writes=[FK["ew"]])
                        S.op("dve", lambda h: h.tensor_scalar(out=ew[:, :], in0=ew[:, :], scalar1=1.0, scalar2=None, op0=ALU.add), reads=[FK["ew"]], writes=[FK["ew"]])
                        S.op("act", lambda h: h.activation(out=ew[:, :], in_=ew[:, :], func=AF.Ln), reads=[FK["ew"]], writes=[FK["ew"]])
                        S.op("act", lambda h: h.activation(out=ew[:, :], in_=ew[:, :], func=AF.Exp, scale=-1.0, bias=cnh[:, 0:1]), reads=[FK["ew"]] + ckeys, writes=[FK["ew"]])
                        S.op("act", lambda h, psa=psa, oc=oc: h.activation(out=aa[:, :], in_=psa[:, 0:n], func=AF.Sigmoid, bias=ca0[:, oc:oc + 1]), reads=[pka] + ckeys, writes=[FK["a"]])
                        cp("act", Gt[:, oc, :], psg[:, 0:n], [pkg], [("Gt", oc)])
                        S.op("dve", lambda h, kq=kq, oc=oc: h.tensor_scalar(out=kk[:, :], in0=kq, scalar1=ckk[:, oc:oc + 1], scalar2=None, op0=ALU.mult), reads=["DT"] + ckeys, writes=[FK["kk"]])
                        S.op("act", lambda h: h.activation(out=t1[:, :], in_=kk[:, :], func=AF.Square), reads=[FK["kk"]], writes=[FK["t1"]])
                        ps, pk = psum()
                        S.op("pe", lambda h, ps=ps: h.matmul(ps[:, 0:n], lhsT=onesbd[:, :], rhs=t1[:, :], start=True, stop=True), reads=["onesbd", FK["t1"]], writes=[pk])
                        S.op("act", lambda h, ps=ps: h.activation(out=t1[:, :], in_=ps[:, 0:n], func=AF.Sqrt), reads=[pk], writes=[FK["t1"]])
                        S.op("dve", lambda h: h.tensor_scalar(out=t1[:, :], in0=t1[:, :], scalar1=1e-12, scalar2=None, op0=ALU.max), reads=[FK["t1"]], writes=[FK["t1"]])
                        S.op("dve", lambda h: h.reciprocal(out=t1[:, :], in_=t1[:, :]), reads=[FK["t1"]], writes=[FK["t1"]])
                        S.op("dve", lambda h: h.tensor_tensor(out=kk[:, :], in0=kk[:, :], in1=t1[:, :], op=ALU.mult), reads=[FK["kk"], FK["t1"]], writes=[FK["kk"]])
                        S.op("dve", lambda h, oc=oc: h.tensor_scalar(out=t1[:, :], in0=aa[:, :], scalar1=cka[:, oc:oc + 1], scalar2=cka[:, oc:oc + 1], op0=ALU.mult, op1=ALU.subtract), reads=[FK["a"], FK["t1"]] + ckeys, writes=[FK["t1"]])
                        S.op("dve", lambda h: h.tensor_scalar(out=t1[:, :], in0=t1[:, :], scalar1=1.0, scalar2=None, op0=ALU.add), reads=[FK["t1"]], writes=[FK["t1"]])
                        S.op("dve", lambda h, kq=kq: h.tensor_tensor(out=km[:, :], in0=t1[:, :], in1=kq, op=ALU.mult), reads=[FK["t1"], "DT"], writes=[FK["km"]])
                        S.op("dve", lambda h: h.tensor_tensor(out=be[:, :], in0=kk[:, :], in1=aa[:, :], op=ALU.mult), reads=[FK["kk"], FK["a"]], writes=[FK["be"]])
                        S.op("dve", lambda h, rr=rr: h.tensor_tensor(out=t1[:, :], in0=rr, in1=km[:, :], op=ALU.mult), reads=["DT", FK["km"], FK["t1"]], writes=[FK["t1"]])
                        S.op("dve", lambda h, oc=oc: h.tensor_scalar(out=t1[:, :], in0=t1[:, :], scalar1=crk[:, oc:oc + 1], scalar2=None, op0=ALU.mult), reads=[FK["t1"]] + ckeys, writes=[FK["t1"]])
                        ps, pk = psum()
                        S.op("pe", lambda h, ps=ps: h.matmul(ps[:, 0:n], lhsT=onesbd[:, :], rhs=t1[:, :], start=True, stop=True), reads=["onesbd", FK["t1"]], writes=[pk])
                        S.op("dve", lambda h, ps=ps, vv=vv, oc=oc: h.tensor_tensor(out=BON[:, oc, :], in0=ps[:, 0:n], in1=vv, op=ALU.mult), reads=[pk, "DT"], writes=[("BON", oc)])
                        for ch in range(NCH):
                            S.op("dve", lambda h, ch=ch: h.tensor_tensor_scan(out=cs[:, ch * 64:(ch + 1) * 64], data0=onesf[:, 0:64], data1=ew[:, ch * 64:(ch + 1) * 64], initial=0.0, op0=ALU.mult, op1=ALU.add),
                                 reads=[FK["ew"], "onesf"], writes=[FK["cs"]])
                        S.op("act", lambda h: h.activation(out=wi[:, :], in_=cs[:, :], func=AF.Exp, scale=-1.0), reads=[FK["cs"]], writes=[FK["wi"]])
                        S.op("act", lambda h: h.activation(out=wv[:, :], in_=cs[:, :], func=AF.Exp), reads=[FK["cs"]], writes=[FK["wv"]])
                        S.op("dve", lambda h: h.tensor_tensor(out=we[:, :], in0=cs[:, :], in1=ew[:, :], op=ALU.subtract), reads=[FK["cs"], FK["ew"]], writes=[FK["we"]])
                        S.op("act", lambda h: h.activation(out=we[:, :], in_=we[:, :], func=AF.Exp, scale=-1.0), reads=[FK["we"]], writes=[FK["we"]])
                        v3 = lambda a_: a_.rearrange("p (c t) -> p c t", c=NCH)
                        S.op("dve", lambda h: h.tensor_scalar(out=t1[:, :], in0=kk[:, :], scalar1=-1.0, scalar2=None, op0=ALU.mult), reads=[FK["kk"], FK["t1"]], writes=[FK["t1"]])
                        S.op("dve", lambda h, oc=oc: h.tensor_tensor(out=AR[:, oc, :, 0, :], in0=v3(t1[:, :]), in1=v3(we[:, :]), op=ALU.mult), reads=[FK["t1"], FK["we"]], writes=[("AR", oc, 0)])
                        S.op("dve", lambda h, oc=oc, rr=rr: h.tensor_tensor(out=AR[:, oc, :, 1, :], in0=v3(rr), in1=v3(wi[:, :]), op=ALU.mult), reads=["DT", FK["wi"]], writes=[("AR", oc, 1)])
                        S.op("dve", lambda h, oc=oc: h.tensor_tensor(out=BK[:, oc, :, 0, :], in0=v3(be[:, :]), in1=v3(wv[:, :]), op=ALU.mult), reads=[FK["be"], FK["wv"]], writes=[("BK", oc, 0)])
                        S.op("dve", lambda h, oc=oc: h.tensor_tensor(out=BK[:, oc, :, 1, :], in0=v3(km[:, :]), in1=v3(wv[:, :]), op=ALU.mult), reads=[FK["km"], FK["wv"]], writes=[("BK", oc, 1)])
                        S.op("dve", lambda h, oc=oc: h.tensor_copy(out=WCb[:, oc, :], in_=v3(wi[:, :])[:, :, 63]), reads=[FK["wi"]], writes=[("WCb", oc)])
                        S.op("act", lambda h, oc=oc, vv=vv: h.activation(out=VB[:, oc, 64:64 + NB], in_=vv, func=AF.Copy), reads=["DT"], writes=[("VB", oc)])
                    if RWSUB < 3:
                        continue
                    pre_ = block_pre(MG, mgk, ML, "ML")
                    ark, bkk, kbk, atk, xvk, xvvk, gtk, wmk, xtk, PF = (pre_[k_] for k_ in ("ark", "bkk", "kbk", "atk", "xvk", "xvvk", "gtk", "wmk", "xtk", "PF"))
                    if RWSUB < 7:
                        continue
                    for ch in range(NCH):
                        psu, pku = psum()
                        for h_ in range(4):
                            oc, pb = h_ // 2, (h_ % 2) * 64
                            S.op("pe", lambda h, psu=psu, ch=ch, h_=h_, oc=oc, pb=pb: h.matmul(psu[0:64, h_ * 64:(h_ + 1) * 64], lhsT=Wms[:, ch, h_ % 2, oc, :], rhs=HB[:, ch, oc, :], start=True, stop=True),
                                 reads=wmk + [("HB", ch)], writes=[pku])
                        S.op("dve", lambda h, psu=psu, ch=ch: h.tensor_tensor(out=Xf[:, :, :], in0=psu[0:64, 0:256].rearrange("p (a v) -> p a v", a=4), in1=XTs[:, ch * 4:ch * 4 + 4, :], op=ALU.add),
                             reads=[pku] + xtk, writes=["Xf"])
                        S.op("act", lambda h, ch=ch: h.activation(out=X[0:64, ch, :, :], in_=Xf[:, :, :], func=AF.Copy), reads=["Xf"], writes=[("Xu", ch)])
                        psh, pkh = psum()
                        for h_ in range(4):
                            oc, pb = h_ // 2, (h_ % 2) * 64
                            S.op("pe", lambda h, psh=psh, ch=ch, h_=h_, oc=oc, pb=pb: h.matmul(psh[pb:pb + 64, oc * 64:(oc + 1) * 64], lhsT=KB[:, ch, oc, (h_ % 2) * 64:(h_ % 2) * 64 + 64], rhs=X[:, ch, h_, :], start=True, stop=True),
                                 reads=kbk + xvk + [("Xu", ch)], writes=[pkh])
                        S.op("dve", lambda h, psh=psh: h.tensor_tensor(out=Hf2[:, :, :], in0=psh[:, 0:128].rearrange("p (o v) -> p o v", o=2), in1=Hf[:, :, :], op=ALU.add), reads=[pkh, "Hf"], writes=["Hf2"])
                        S.op("dve", lambda h, ch=ch: h.tensor_tensor(out=Hf[:, :, :], in0=Hf2[:, :, :], in1=WCb[:, :, ch:ch + 1].broadcast_to([128, 2, 64]), op=ALU.mult),
                             reads=["Hf2", ("WCb", 0), ("WCb", 1)], writes=["Hf"])
                        S.op("act", lambda h, ch=ch: h.activation(out=HB[:, ch + 1, :, :], in_=Hf[:, :, :], func=AF.Copy), reads=["Hf"], writes=[("HB", ch + 1)])
                        for hh in range(2):
                            S.op("act", lambda h, ch=ch, hh=hh: h.activation(out=HBh[hh * 64:(hh + 1) * 64, ch + 1, hh, :, :], in_=Hf[hh * 64:(hh + 1) * 64, :, :], func=AF.Copy),
                                 reads=["Hf", ("zinit", 4)], writes=[("HBh", ch + 1, hh)])
                    if RWSUB < 8:
                        continue
                    hbk = [("HB", c_) for c_ in range(NCH + 1)]
                    hbhk = [("HBh", c_, hh_) for c_ in range(NCH + 1) for hh_ in range(2)]
                    xuk = [("Xu", c_) for c_ in range(NCH)]
                    for oc in range(2):
                        ps, pk = psum()
                        for ch in range(NCH):
                            for hh in range(2):
                                h_, pb = oc * 2 + hh, hh * 64
                                S.op("pe", lambda h, ps=ps, ch=ch, oc=oc, pb=pb, hh=hh: h.matmul(ps[pb:pb + 64, ch * 64:(ch + 1) * 64], lhsT=HBh[:, ch, hh, oc, :], rhs=AR[:, oc, ch, 1, :], start=True, stop=False),
                                     reads=hbhk + ark, writes=[pk])
                                S.op("pe", lambda h, ps=ps, ch=ch, h_=h_, pb=pb: h.matmul(ps[pb:pb + 64, ch * 64:(ch + 1) * 64], lhsT=X[:, ch, h_, :], rhs=GTs[:, ch * 4 + h_, 64:128], start=False, stop=True),
                                     reads=xuk + xvk + gtk, writes=[pk])
                        cp("act", Ysb[:, oc, :], ps[:, 0:NB], [pk], [("Ysb", oc)])
                        yv = Ysb[:, oc, :]
                        ps, pk = psum()
                        S.op("pe", lambda h, ps=ps, yv=yv: h.matmul(ps[:, 0:n], lhsT=onesbd[:, :], rhs=yv, start=True, stop=True), reads=[("Ysb", oc), "onesbd"], writes=[pk])
                        S.op("act", lambda h, ps=ps: h.activation(out=MB[:, :], in_=ps[:, 0:n], func=AF.Copy, scale=1.0 / 64), reads=[pk], writes=["MB"])
                        S.op("dve", lambda h, yv=yv: h.tensor_tensor(out=yv, in0=yv, in1=MB[:, :], op=ALU.subtract), reads=["MB", ("Ysb", oc)], writes=[("Ysb", oc)])
                        S.op("act", lambda h, yv=yv: h.activation(out=SQ2[:, :], in_=yv, func=AF.Square), reads=[("Ysb", oc)], writes=["SQ2"])
                        ps, pk = psum()
                        S.op("pe", lambda h, ps=ps: h.matmul(ps[:, 0:n], lhsT=onesbd[:, :], rhs=SQ2[:, :], start=True, stop=True), reads=["SQ2", "onesbd"], writes=[pk])
                        S.op("act", lambda h, ps=ps: h.activation(out=MB[:, :], in_=ps[:, 0:n], func=AF.Sqrt, scale=1.0 / 64, bias=cge[:, 0:1]), reads=[pk] + ckeys, writes=["MB"])
                        S.op("dve", lambda h: h.reciprocal(out=MB[:, :], in_=MB[:, :]), reads=["MB"], writes=["MB"])
                        S.op("dve", lambda h, yv=yv, oc=oc: h.scalar_tensor_tensor(out=yv, in0=MB[:, :], scalar=clg[:, oc:oc + 1], in1=yv, op0=ALU.mult, op1=ALU.mult), reads=["MB", ("Ysb", oc)] + ckeys, writes=[("Ysb", oc)])
                        S.op("dve", lambda h, yv=yv, oc=oc: h.scalar_tensor_tensor(out=yv, in0=BON[:, oc, :], scalar=clb[:, oc:oc + 1], in1=yv, op0=ALU.add, op1=ALU.add), reads=[("Ysb", oc), ("BON", oc)] + ckeys, writes=[("Ysb", oc)])
                        S.op("dve", lambda h, yv=yv, oc=oc, blk0=blk0: h.tensor_tensor(out=yT[:, oc, blk0:blk0 + NB], in0=yv, in1=Gt[:, oc, :], op=ALU.mult), reads=[("Ysb", oc), ("Gt", oc)], writes=[("yT", FK["a"], oc, blk)])
                    S.op("act", lambda h: h.activation(out=HB[:, 0, :, :], in_=Hf[:, :, :], func=AF.Copy), reads=["Hf"] + hbk, writes=[("HB", 0)])
                    for hh in range(2):
                        S.op("act", lambda h, hh=hh: h.activation(out=HBh[hh * 64:(hh + 1) * 64, 0, hh, :, :], in_=Hf[hh * 64:(hh + 1) * 64, :, :], func=AF.Copy),
                             reads=["Hf"] + hbhk, writes=[("HBh", 0, hh)])

                for oc in range(2 if RWSUB >= 0 else 0):
                    ps, pk = psum()
                    S.op("pe", lambda h, ps=ps, oc=oc: h.transpose(out=ps[0:64, 0:128], in_=Hf[:, oc, :], identity=ident[:, :]), reads=["Hf", "ident"], writes=[pk])
                    cp("act", sto[:, 2 * oc:2 * oc + 2, :], ps[0:64, 0:128].rearrange("p (a k) -> p a k", a=2), [pk], [("sto", oc)])
                    S.dma("sp", o_prw[l, 2 * oc:2 * oc + 2].rearrange("a v k -> v a k"), sto[:, 2 * oc:2 * oc + 2, :], reads=[("sto", oc)])
                if RWSUB >= 0:
                    S.dma("sp", o_psh[l].rearrange("(c p) -> p c", p=128), zlast[:, :, 0], reads=["zlast"], allow_slow_non_contiguous=True)

                if stage >= 13 and SRW >= 1:
                    S.flush()
                    n = NS
                    ZBs = ZB[:, :, 0:24].rearrange("p c (b t) -> p c b t", b=4)
                    DTs = ZB[:, :, 32:48]
                    FS = [ZB[:, k_, 64:80] for k_ in range(8)] + [ZB[:, 0, 96:112], ZB[:, 1, 96:112]]
                    Gts, BONs, WCs = ZB[:, 2:4, 96:112], ZB[:, 4:6, 96:112], ZB[:, 6:8, 96:100]
                    rmask, rtmp = ZB[:, 0, 128:132], ZB[:, 1, 128:132]
                    cmask = DT[:, 0, :].rearrange("p (b t) -> p b t", b=4)
                    ctmp = DT[:, 6, :].rearrange("p (b t) -> p b t", b=4)
                    BDh, MLs, MGs = DT[:, 1, 0:64], DT[0:64, 1, 64:128], DT[:, 1, 128:256]
                    Hfs = DT[:, 2:4, :].rearrange("p r (b o v) -> p (r b) o v", b=2, o=2)
                    Hfs2 = DT[:, 4:6, :].rearrange("p r (b o v) -> p (r b) o v", b=2, o=2)
                    hst = [DT[0:64, 7, 0:128].rearrange("p (a k) -> p a k", a=2), DT[0:64, 7, 128:256].rearrange("p (a k) -> p a k", a=2)]
                    Yss, MBs, SQs = Ysb[:, :, 0:64], MB[:, 0:64], SQ2[:, 0:64]
                    LAs, SGs = LA[:, 0:NS], SG[:, 0:NS]
                    HBs, HBhs = HB[:, 0:4, :, :], HBh[:, 0:4, :, :, :]
                    Wmb = BON[:, :, :].bitcast(BF16).rearrange("p r (b j t) -> p (r b) j t", b=2, j=4)
                    KBb = Gt[:, :, :].bitcast(BF16).rearrange("p r (b o k) -> p (r b) o k", b=2, o=2)
                    ARbv = [tmpG[b_ // 2][:, (b_ % 2) * 64:(b_ % 2) * 64 + 64].bitcast(BF16).rearrange("p (o t) -> p o t", o=2) for b_ in range(4)]
                    S.op("pool", lambda h: h.tensor_copy(out=rtmp, in_=onesf[:, 0:4]), reads=["onesf"], writes=["rtmp"])
                    for h0 in (0, 64):
                        S.op("pool", lambda h, h0=h0: h.affine_select(out=rmask[h0:h0 + 64, :], in_=rtmp[h0:h0 + 64, :], pattern=[[-4, 4]], compare_op=ALU.is_ge, fill=0.0, base=0, channel_multiplier=1), reads=["rtmp"], writes=[("rm1", h0)])
                    for h0 in (0, 64):
                        S.op("pool", lambda h, h0=h0: h.affine_select(out=rtmp[h0:h0 + 64, :], in_=rmask[h0:h0 + 64, :], pattern=[[4, 4]], compare_op=ALU.is_ge, fill=0.0, base=3, channel_multiplier=-1), reads=[("rm1", 0), ("rm1", 64)], writes=[("rm2", h0)])
                    S.op("pool", lambda h: h.tensor_copy(out=rmask, in_=rtmp), reads=[("rm2", 0), ("rm2", 64)], writes=["rmask"])
                    S.op("pool", lambda h: h.memset(DT[:, 0, :], 1.0), writes=["cm0"])
                    S.op("pool", lambda h: h.affine_select(out=ctmp, in_=cmask, pattern=[[-4, 4], [1, 64]], compare_op=ALU.is_ge, fill=0.0, base=0, channel_multiplier=0), reads=["cm0"], writes=["cm1"])
                    S.op("pool", lambda h: h.affine_select(out=cmask, in_=ctmp, pattern=[[4, 4], [-1, 64]], compare_op=ALU.is_ge, fill=0.0, base=3, channel_multiplier=0), reads=["cm1", "cm0"], writes=["cmask"])
                    S.op("dve", lambda h: h.tensor_scalar(out=BDh, in0=cmask[:, 0, :], scalar1=rmask[:, 0:1], scalar2=None, op0=ALU.mult), reads=["cmask", "rmask"], writes=["BDh"])
                    for b in range(1, 4):
                        S.op("dve", lambda h, b=b: h.scalar_tensor_tensor(out=BDh, in0=cmask[:, b, :], scalar=rmask[:, b:b + 1], in1=BDh, op0=ALU.mult, op1=ALU.add), reads=["cmask", "rmask", "BDh"], writes=["BDh"])
                    for hf in range(2):
                        S.op("dve", lambda h, hf=hf: h.tensor_tensor(out=MGs[:, hf * 64:(hf + 1) * 64], in0=MG[:, hf * 64:(hf + 1) * 64], in1=BDh, op=ALU.mult), reads=["BDh"], writes=[("MGs", hf)])
                    S.op("dve", lambda h: h.tensor_tensor(out=MLs, in0=ML[:, :], in1=DT[0:64, 1, 0:64], op=ALU.mult), reads=["BDh"], writes=["MLs"])
                    for _once in (0,):
                        if SRW < 2:
                            break
                        for b in range(4):
                            for oc in range(2):
                                hs_ = hst[(b * 2 + oc) % 2]
                                hk_ = ("hst", (b * 2 + oc) % 2)
                                S.dma("sp", hs_, srw[l, b, 2 * oc:2 * oc + 2].rearrange("a v k -> v a k"), writes=[hk_])
                                ps, pk = psum()
                                S.op("pe", lambda h, ps=ps, hs_=hs_: h.transpose(out=ps[:, 0:64], in_=hs_.rearrange("p a k -> p (a k)"), identity=ident[0:64, 0:64]), reads=[hk_, "ident"], writes=[pk])
                                S.op("act", lambda h, ps=ps, b=b, oc=oc: h.activation(out=Hfs[:, b, oc, :], in_=ps[:, 0:64], func=AF.Copy), reads=[pk], writes=[("Hfs", b, oc)])
                            hfk = [("Hfs", b, 0), ("Hfs", b, 1)]
                            S.op("act", lambda h, b=b: h.activation(out=HBs[:, b, :, :], in_=Hfs[:, b, :, :], func=AF.Copy), reads=hfk, writes=[("HBs", b)])
                            for hh in range(2):
                                S.op("act", lambda h, b=b, hh=hh: h.activation(out=HBhs[hh * 64:(hh + 1) * 64, b, hh, :, :], in_=Hfs[hh * 64:(hh + 1) * 64, b, :, :], func=AF.Copy), reads=hfk, writes=[("HBhs", b, hh)])
                        if SRW < 3:
                            break
                        rmsnorm_fm(xTs, 0, n, gcol, xn, "xns", scr)
                        xnk = [("xns", c) for c in range(KC)]
                        for b in range(4):
                            S.dma("sp", ZBs[:, :, b, 0], ssh[l, b].rearrange("(c p) -> p c", p=128), writes=[("zs0", b)], allow_slow_non_contiguous=True)
                        for oc in range(8):
                            ps, pk = psum()
                            for c in range(KC):
                                S.op("pe", lambda h, ps=ps, c=c, oc=oc: h.matmul(ps[:, 0:n], lhsT=wrw[:, c, oc * 128:(oc + 1) * 128], rhs=xn[:, c, 0:n], start=(c == 0), stop=(c == KC - 1)), reads=xnk, writes=[pk])
                            S.op("act", lambda h, ps=ps, oc=oc: h.activation(out=ZBs[:, oc, :, 1:5], in_=ps[:, 0:n].rearrange("p (b t) -> p b t", b=4), func=AF.Copy), reads=[pk], writes=[("zsn", oc)])
                        zk_ = [("zs0", b) for b in range(4)] + [("zsn", oc) for oc in range(8)]
                        for b in range(4):
                            S.dma("sp", o_ssh[l, b].rearrange("(c p) -> p c", p=128), ZBs[:, :, b, 4], reads=zk_, allow_slow_non_contiguous=True)
                        for oc in range(8):
                            dv = DTs[:, oc, :].rearrange("p (b t) -> p b t", b=4)
                            S.op("dve", lambda h, oc=oc, dv=dv: h.tensor_tensor(out=dv, in0=ZBs[:, oc, :, 0:4], in1=ZBs[:, oc, :, 1:5], op=ALU.subtract), reads=zk_, writes=[("DTs", oc)])
                            S.op("dve", lambda h, oc=oc, dv=dv: h.tensor_scalar(out=dv, in0=dv, scalar1=mu[:, oc:oc + 1], scalar2=None, op0=ALU.mult), reads=[("DTs", oc)], writes=[("DTs", oc)])
                            S.op("dve", lambda h, oc=oc, dv=dv: h.tensor_tensor(out=dv, in0=dv, in1=ZBs[:, oc, :, 1:5], op=ALU.add), reads=[("DTs", oc)] + zk_, writes=[("DTs", oc)])
                        if SRW < 4:
                            break
                        dk = lambda i_: ("DTs", i_)
                        S.op("act", lambda h: h.activation(out=LAs[0:64, :], in_=DTs[0:64, 6, :], func=AF.Tanh), reads=[dk(6)], writes=["LAs0"])
                        S.op("act", lambda h: h.activation(out=LAs[64:128, :], in_=DTs[64:128, 6, :], func=AF.Copy), reads=[dk(6)], writes=["LAs1"])
                        S.op("act", lambda h: h.activation(out=SGs, in_=DTs[:, 7, :], func=AF.Sigmoid), reads=[dk(7)], writes=["SGs"])
                        for zt_, zn_ in ((AR, "p a b c d -> p (a b c d)"), (BK, "p a b c d -> p (a b c d)"), (VB, "p a b -> p (a b)")):
                            S.op("pool", lambda h, zt_=zt_, zn_=zn_: h.memset(zt_[:].rearrange(zn_), 0.0), writes=[("z0", id(zt_))])
                        z0k = [("z0", id(AR)), ("z0", id(BK)), ("z0", id(VB))]
                        ew, aa, kk, t1, km, cs, wi, wv, we, be = FS
                        fsk = lambda i_: ("FS", i_)
                        for oc in range(2):
                            rr, kq, vv = DTs[:, oc, :], DTs[:, 2 + oc, :], DTs[:, 4 + oc, :]
                            psw, pkw = psum()
                            S.op("pe", lambda h, psw=psw, oc=oc: h.matmul(psw[:, 0:n], lhsT=WA[0:64, oc * 128:(oc + 1) * 128], rhs=LAs[0:64, :], start=True, stop=True), reads=["LAs0"], writes=[pkw])
                            psa, pka = psum()
                            S.op("pe", lambda h, psa=psa, oc=oc: h.matmul(psa[:, 0:n], lhsT=WA[64:128, oc * 128:(oc + 1) * 128], rhs=LAs[64:128, :], start=True, stop=True), reads=["LAs1"], writes=[pka])
                            psg, pkg = psum()
                            S.op("pe", lambda h, psg=psg, oc=oc: h.matmul(psg[:, 0:n], lhsT=GU[:, oc * 128:(oc + 1) * 128], rhs=SGs, start=True, stop=True), reads=["SGs"], writes=[pkg])
                            S.op("act", lambda h, psw=psw, oc=oc: h.activation(out=ew, in_=psw[:, 0:n], func=AF.Exp, scale=-1.0, bias=cw0[:, oc:oc + 1]), reads=[pkw], writes=[fsk(0)])
                            S.op("dve", lambda h: h.tensor_scalar(out=ew, in0=ew, scalar1=1.0, scalar2=None, op0=ALU.add), reads=[fsk(0)], writes=[fsk(0)])
                            S.op("act", lambda h: h.activation(out=ew, in_=ew, func=AF.Ln), reads=[fsk(0)], writes=[fsk(0)])
                            S.op("act", lambda h: h.activation(out=ew, in_=ew, func=AF.Exp, scale=-1.0, bias=cnh[:, 0:1]), reads=[fsk(0)], writes=[fsk(0)])
                            S.op("act", lambda h, psa=psa, oc=oc: h.activation(out=aa, in_=psa[:, 0:n], func=AF.Sigmoid, bias=ca0[:, oc:oc + 1]), reads=[pka], writes=[fsk(1)])
                            S.op("act", lambda h, psg=psg, oc=oc: h.activation(out=Gts[:, oc, :], in_=psg[:, 0:n], func=AF.Copy), reads=[pkg], writes=[("Gts", oc)])
                            S.op("dve", lambda h, kq=kq, oc=oc: h.tensor_scalar(out=kk, in0=kq, scalar1=ckk[:, oc:oc + 1], scalar2=None, op0=ALU.mult), reads=[dk(2 + oc)], writes=[fsk(2)])
                            S.op("act", lambda h: h.activation(out=t1, in_=kk, func=AF.Square), reads=[fsk(2)], writes=[fsk(3)])
                            ps, pk = psum()
                            S.op("pe", lambda h, ps=ps: h.matmul(ps[:, 0:n], lhsT=onesbd[:, :], rhs=t1, start=True, stop=True), reads=[fsk(3)], writes=[pk])
                            S.op("act", lambda h, ps=ps: h.activation(out=t1, in_=ps[:, 0:n], func=AF.Sqrt), reads=[pk], writes=[fsk(3)])
                            S.op("dve", lambda h: h.tensor_scalar(out=t1, in0=t1, scalar1=1e-12, scalar2=None, op0=ALU.max), reads=[fsk(3)], writes=[fsk(3)])
                            S.op("dve", lambda h: h.reciprocal(out=t1, in_=t1), reads=[fsk(3)], writes=[fsk(3)])
                            S.op("dve", lambda h: h.tensor_tensor(out=kk, in0=kk, in1=t1, op=ALU.mult), reads=[fsk(2), fsk(3)], writes=[fsk(2)])
                            S.op("dve", lambda h, oc=oc: h.tensor_scalar(out=t1, in0=aa, scalar1=cka[:, oc:oc + 1], scalar2=cka[:, oc:oc + 1], op0=ALU.mult, op1=ALU.subtract), reads=[fsk(1), fsk(3)], writes=[fsk(3)])
                            S.op("dve", lambda h: h.tensor_scalar(out=t1, in0=t1, scalar1=1.0, scalar2=None, op0=ALU.add), reads=[fsk(3)], writes=[fsk(3)])
                            S.op("dve", lambda h, kq=kq: h.tensor_tensor(out=km, in0=t1, in1=kq, op=ALU.mult), reads=[fsk(3), dk(2 + oc)], writes=[fsk(4)])
                            S.op("dve", lambda h: h.tensor_tensor(out=be, in0=kk, in1=aa, op=ALU.mult), reads=[fsk(2), fsk(1)], writes=[fsk(9)])
                            S.op("dve", lambda h, rr=rr: h.tensor_tensor(out=t1, in0=rr, in1=km, op=ALU.mult), reads=[dk(oc), fsk(4), fsk(3)], writes=[fsk(3)])
                            S.op("dve", lambda h, oc=oc: h.tensor_scalar(out=t1, in0=t1, scalar1=crk[:, oc:oc + 1], scalar2=None, op0=ALU.mult), reads=[fsk(3)], writes=[fsk(3)])
                            ps, pk = psum()
                            S.op("pe", lambda h, ps=ps: h.matmul(ps[:, 0:n], lhsT=onesbd[:, :], rhs=t1, start=True, stop=True), reads=[fsk(3)], writes=[pk])
                            S.op("dve", lambda h, ps=ps, vv=vv, oc=oc: h.tensor_tensor(out=BONs[:, oc, :], in0=ps[:, 0:n], in1=vv, op=ALU.mult), reads=[pk, dk(4 + oc)], writes=[("BONs", oc)])
                            for b in range(4):
                                S.op("dve", lambda h, b=b: h.tensor_tensor_scan(out=cs[:, 4 * b:4 * b + 4], data0=onesf[:, 0:4], data1=ew[:, 4 * b:4 * b + 4], initial=0.0, op0=ALU.mult, op1=ALU.add), reads=[fsk(0)], writes=[fsk(5)])
                            S.op("act", lambda h: h.activation(out=wi, in_=cs, func=AF.Exp, scale=-1.0), reads=[fsk(5)], writes=[fsk(6)])
                            S.op("act", lambda h: h.activation(out=wv, in_=cs, func=AF.Exp), reads=[fsk(5)], writes=[fsk(7)])
                            S.op("dve", lambda h: h.tensor_tensor(out=we, in0=cs, in1=ew, op=ALU.subtract), reads=[fsk(5), fsk(0)], writes=[fsk(8)])
                            S.op("act", lambda h: h.activation(out=we, in_=we, func=AF.Exp, scale=-1.0), reads=[fsk(8)], writes=[fsk(8)])
                            S.op("dve", lambda h: h.tensor_scalar(out=t1, in0=kk, scalar1=-1.0, scalar2=None, op0=ALU.mult), reads=[fsk(2), fsk(3)], writes=[fsk(3)])
                            S.op("dve", lambda h, oc=oc: h.tensor_tensor(out=AR[:, oc, 0, 0, 0:n], in0=t1, in1=we, op=ALU.mult), reads=[fsk(3), fsk(8)] + z0k, writes=[("AR", oc, 0)])
                            S.op("dve", lambda h, oc=oc, rr=rr: h.tensor_tensor(out=AR[:, oc, 0, 1, 0:n], in0=rr, in1=wi, op=ALU.mult), reads=[dk(oc), fsk(6)] + z0k, writes=[("AR", oc, 1)])
                            S.op("dve", lambda h, oc=oc: h.tensor_tensor(out=BK[:, oc, 0, 0, 0:n], in0=be, in1=wv, op=ALU.mult), reads=[fsk(9), fsk(7)] + z0k, writes=[("BK", oc, 0)])
                            S.op("dve", lambda h, oc=oc: h.tensor_tensor(out=BK[:, oc, 0, 1, 0:n], in0=km, in1=wv, op=ALU.mult), reads=[fsk(4), fsk(7)] + z0k, writes=[("BK", oc, 1)])
                            S.op("dve", lambda h, oc=oc: h.tensor_copy(out=WCs[:, oc, :], in_=wi.rearrange("p (b t) -> p b t", b=4)[:, :, 3]), reads=[fsk(6)], writes=[("WCs", oc)])
                            S.op("act", lambda h, oc=oc, vv=vv: h.activation(out=VB[:, oc, 64:64 + n], in_=vv, func=AF.Copy), reads=[dk(4 + oc)] + z0k, writes=[("VB", oc)])
                        if SRW < 5:
                            break
                        pre_ = block_pre(MGs, [("MGs", 0), ("MGs", 1)], MLs, "MLs")
                        ark, bkk, kbk, atk, xvk, xvvk, gtk, wmk, xtk, PF = (pre_[k_] for k_ in ("ark", "bkk", "kbk", "atk", "xvk", "xvvk", "gtk", "wmk", "xtk", "PF"))
                        for b in range(4):
                            S.op("dve", lambda h, b=b: h.tensor_tensor(out=Wmb[:, b, :, :], in0=Wms[:, 0, :, :, :].rearrange("p a o t -> p (a o) t"), in1=cmask[:, b, :].unsqueeze(1).broadcast_to([128, 4, 64]), op=ALU.mult),
                                 reads=wmk + ["cmask"], writes=[("Wmb", b)])
                            S.op("dve", lambda h, b=b: h.tensor_tensor(out=ARbv[b], in0=AR[:, :, 0, 1, :], in1=cmask[:, b, :].unsqueeze(1).broadcast_to([128, 2, 64]), op=ALU.mult),
                                 reads=ark + ["cmask", ("tmpG", b // 2)], writes=[("ARb", b), ("tmpG", b // 2)])
                            S.op("dve", lambda h, b=b: h.tensor_scalar(out=KBb[:, b, :, :], in0=KB[:, 0, :, :], scalar1=rmask[:, b:b + 1], scalar2=None, op0=ALU.mult),
                                 reads=kbk + ["rmask"], writes=[("KBb", b)])
                        if SRW < 6:
                            break
                        psu, pku = psum()
                        for h_ in range(4):
                            oc, hh = h_ // 2, h_ % 2
                            for b in range(4):
                                S.op("pe", lambda h, psu=psu, h_=h_, oc=oc, hh=hh, b=b: h.matmul(psu[0:64, h_ * 64:(h_ + 1) * 64], lhsT=Wmb[:, b, hh * 2 + oc, :], rhs=HBs[:, b, oc, :], start=(b == 0), stop=(b == 3)),
                                     reads=[("Wmb", b), ("HBs", b)], writes=[pku])
                        S.op("dve", lambda h, psu=psu: h.tensor_tensor(out=Xf[:, :, :], in0=psu[0:64, 0:256].rearrange("p (a v) -> p a v", a=4), in1=XTs[:, 0:4, :], op=ALU.add), reads=[pku] + xtk, writes=["Xf"])
                        S.op("act", lambda h: h.activation(out=X[0:64, 0, :, :], in_=Xf[:, :, :], func=AF.Copy), reads=["Xf"], writes=[("Xu", 0)])
                        for b in range(4):
                            psh, pkh = psum()
                            for h_ in range(4):
                                oc, hh = h_ // 2, h_ % 2
                                pb = hh * 64
                                S.op("pe", lambda h, psh=psh, b=b, h_=h_, oc=oc, hh=hh, pb=pb: h.matmul(psh[pb:pb + 64, oc * 64:(oc + 1) * 64], lhsT=KBb[:, b, oc, hh * 64:(hh + 1) * 64], rhs=X[:, 0, h_, :], start=True, stop=True),
                                     reads=[("KBb", b), ("Xu", 0)] + xvk, writes=[pkh])
                            S.op("dve", lambda h, psh=psh, b=b: h.tensor_tensor(out=Hfs2[:, b, :, :], in0=psh[:, 0:128].rearrange("p (o v) -> p o v", o=2), in1=Hfs[:, b, :, :], op=ALU.add),
                                 reads=[pkh, ("Hfs", b, 0), ("Hfs", b, 1), ("HBs", b), ("HBhs", b, 0), ("HBhs", b, 1)], writes=[("Hfs2", b)])
                            S.op("dve", lambda h, b=b: h.tensor_tensor(out=Hfs2[:, b, :, :], in0=Hfs2[:, b, :, :], in1=WCs[:, :, b:b + 1].broadcast_to([128, 2, 64]), op=ALU.mult),
                                 reads=[("Hfs2", b), ("WCs", 0), ("WCs", 1)], writes=[("Hfs2", b)])
                            for oc in range(2):
                                ps, pk = psum()
                                S.op("pe", lambda h, ps=ps, b=b, oc=oc: h.transpose(out=ps[0:64, 0:128], in_=Hfs2[:, b, oc, :], identity=ident[:, :]), reads=[("Hfs2", b), "ident"], writes=[pk])
                                sk_ = ("sto", (b * 2 + oc) % 2)
                                S.op("act", lambda h, ps=ps, b=b, oc=oc: h.activation(out=sto[:, ((b * 2 + oc) % 2) * 2:((b * 2 + oc) % 2) * 2 + 2, :], in_=ps[0:64, 0:128].rearrange("p (a k) -> p a k", a=2), func=AF.Copy), reads=[pk], writes=[sk_])
                                S.dma("sp", o_srw[l, b, 2 * oc:2 * oc + 2].rearrange("a v k -> v a k"), sto[:, ((b * 2 + oc) % 2) * 2:((b * 2 + oc) % 2) * 2 + 2, :], reads=[sk_])
                        if SRW < 7:
                            break
                        for oc in range(2):
                            ps, pk = psum()
                            for hh in range(2):
                                h_, pb = oc * 2 + hh, hh * 64
                                for b in range(4):
                                    S.op("pe", lambda h, ps=ps, b=b, oc=oc, hh=hh, pb=pb: h.matmul(ps[pb:pb + 64, 0:64], lhsT=HBhs[:, b, hh, oc, :], rhs=ARbv[b][:, oc, :], start=(b == 0), stop=False),
                                         reads=[("HBhs", b, hh), ("ARb", b)], writes=[pk])
                                S.op("pe", lambda h, ps=ps, h_=h_, pb=pb: h.matmul(ps[pb:pb + 64, 0:64], lhsT=X[:, 0, h_, :], rhs=GTs[:, h_, 64:128], start=False, stop=True), reads=[("Xu", 0)] + xvk + gtk, writes=[pk])
                            yv = Yss[:, oc, 0:n]
                            S.op("act", lambda h, ps=ps, oc=oc: h.activation(out=Yss[:, oc, :], in_=ps[:, 0:64], func=AF.Copy), reads=[pk], writes=[("Yss", oc)])
                            ps, pk = psum()
                            S.op("pe", lambda h, ps=ps, yv=yv: h.matmul(ps[:, 0:n], lhsT=onesbd[:, :], rhs=yv, start=True, stop=True), reads=[("Yss", oc)], writes=[pk])
                            S.op("act", lambda h, ps=ps: h.activation(out=MBs[:, 0:n], in_=ps[:, 0:n], func=AF.Copy, scale=1.0 / 64), reads=[pk], writes=["MBs"])
                            S.op("dve", lambda h, yv=yv: h.tensor_tensor(out=yv, in0=yv, in1=MBs[:, 0:n], op=ALU.subtract), reads=["MBs", ("Yss", oc)], writes=[("Yss", oc)])
                            S.op("act", lambda h, yv=yv: h.activation(out=SQs[:, 0:n], in_=yv, func=AF.Square), reads=[("Yss", oc)], writes=["SQs"])
                            ps, pk = psum()
                            S.op("pe", lambda h, ps=ps: h.matmul(ps[:, 0:n], lhsT=onesbd[:, :], rhs=SQs[:, 0:n], start=True, stop=True), reads=["SQs"], writes=[pk])
                            S.op("act", lambda h, ps=ps: h.activation(out=MBs[:, 0:n], in_=ps[:, 0:n], func=AF.Sqrt, scale=1.0 / 64, bias=cge[:, 0:1]), reads=[pk], writes=["MBs"])
                            S.op("dve", lambda h: h.reciprocal(out=MBs[:, 0:n], in_=MBs[:, 0:n]), reads=["MBs"], writes=["MBs"])
                            S.op("dve", lambda h, yv=yv, oc=oc: h.scalar_tensor_tensor(out=yv, in0=MBs[:, 0:n], scalar=clg[:, oc:oc + 1], in1=yv, op0=ALU.mult, op1=ALU.mult), reads=["MBs", ("Yss", oc)], writes=[("Yss", oc)])
                            S.op("dve", lambda h, yv=yv, oc=oc: h.scalar_tensor_tensor(out=yv, in0=BONs[:, oc, :], scalar=clb[:, oc:oc + 1], in1=yv, op0=ALU.add, op1=ALU.add), reads=[("Yss", oc), ("BONs", oc)], writes=[("Yss", oc)])
                            S.op("dve", lambda h, yv=yv, oc=oc: h.tensor_tensor(out=yTs[:, oc, :], in0=yv, in1=Gts[:, oc, :], op=ALU.mult), reads=[("Yss", oc), ("Gts", oc)], writes=[("yTs", "a", oc)])
                        if "yas" in DBG and l == 0:
                            for oc in range(2):
                                S.dma("sp", DBG["yas"][oc], yTs[:, oc, :], reads=[("yTs", "a", oc)])
                if "ya" in DBG and os.environ.get("NOYA") is None:
                    for oc in range(2):
                        S.dma("sp", DBG["ya"][oc], yT[:, oc, :], reads=[("yT", FK["a"], oc, b_) for b_ in range(T // NB)])
                S.flush()

        if stage >= 10:
            with contextlib.ExitStack() as ph:
                wout = sb("wout", [128, KC, D], BF16, ph)
                load_w(wout, W["w_out"][l], 0, D, "wout")
                wk_ = [("wout", c) for c in range(KC)]
                for blk in range(4):
                    for oc in range(KC):
                        ps, pk = psum()
                        for c in range(KC):
                            S.op("pe", lambda h, ps=ps, c=c, oc=oc, blk=blk: h.matmul(ps[:, :], lhsT=wout[:, c, oc * 128:(oc + 1) * 128], rhs=yT[:, c, blk * 512:(blk + 1) * 512], start=(c == 0), stop=(c == KC - 1)),
                                 reads=wk_, writes=[pk])
                        S.op("dve", lambda h, ps=ps, oc=oc, blk=blk: h.tensor_tensor(out=xT[:, oc, blk * 512:(blk + 1) * 512], in0=ps[:, :], in1=xT[:, oc, blk * 512:(blk + 1) * 512], op=ALU.add),
                             reads=[pk], writes=[("x", oc, blk)])
                for oc in range(KC):
                    ps, pk = psum()
                    for c in range(KC):
                        S.op("pe", lambda h, ps=ps, c=c, oc=oc: h.matmul(ps[:, 0:NS], lhsT=wout[:, c, oc * 128:(oc + 1) * 128], rhs=yTs[:, c, :], start=(c == 0), stop=(c == KC - 1)), reads=wk_, writes=[pk])
                    S.op("dve", lambda h, ps=ps, oc=oc: h.tensor_tensor(out=xTs[:, oc, :], in0=ps[:, 0:NS], in1=xTs[:, oc, :], op=ALU.add), reads=[pk], writes=[("xs", oc)])
                if "x1" in DBG and l == 0:
                    for c in range(KC):
                        S.dma("sp", DBG["x1"][c], xT[:, c, :], reads=[("x", c, b_) for b_ in range(4)])
                        S.dma("sp", DBG["x1s"][c], xTs[:, c, :], reads=[("xs", c)])
                S.flush()
        lay.close()

        xa = contextlib.ExitStack()
        mkT = sb("mkT", [128, KC, 256], BF16, xa)
        mvb = sb("mvb", [128, 2, D], BF16, xa)
        if stage >= 9:
            with contextlib.ExitStack() as ph:
                wxk = sb("wxk", [128, KC, D], BF16, ph)
                wxv = sb("wxv", [128, KC, D], BF16, ph)
                gcol = sb("gcol", [128, KC], F32, ph)
                epsc = sb("epsc", [128, 1], F32, ph)
                gxk = sb("gxk", [128, 256], F32, ph)
                sq = sb("sq", [128, KC, 256], BF16, ph)
                rbc = sb("rbc", [128, 256], F32, ph)
                mn = sb("mn", [128, KC, 256], BF16, ph)
                tsq = sb("tsq", [128, 512], F32, ph)
                tkm = [sb(f"tkm{i}", [128, 512], F32, ph) for i in range(2)]
                tvm = [sb(f"tvm{i}", [128, 512], F32, ph) for i in range(2)]
                ssm = sb("ssm", [128, 8], F32, ph)
                scr = dict(sq=sq, rbc=rbc, eps=epsc)
                memT = sb("memT", [128, KC, 256], F32, ph)
                mtm = [sb(f"mtm{i}", [128, D], F32, ph) for i in range(2)]
                for mt in range(2):
                    S.dma("sp", mtm[mt][:, :], mem[mt * 128:(mt + 1) * 128, :], writes=[("mtm", mt)])
                    for half in range(2):
                        ps, pk = psum()
                        for q in range(4):
                            c = half * 4 + q
                            S.op("pe", lambda h, ps=ps, q=q, c=c, mt=mt: h.transpose(out=ps[:, q * 128:(q + 1) * 128], in_=mtm[mt][:, c * 128:(c + 1) * 128], identity=ident[:, :]),
                                 reads=[("mtm", mt), "ident"], writes=[pk])
                        S.op("act", lambda h, ps=ps, half=half, mt=mt: h.activation(out=memT[:, half * 4:half * 4 + 4, mt * 128:(mt + 1) * 128], in_=ps[:, :].rearrange("p (q t) -> p q t", q=4), func=AF.Copy),
                             reads=[pk], writes=[("memT", mt, half)])
                load_w(wxk, W["w_xk"][l], 0, D, "wxk")
                load_w(wxv, W["w_xv"][l], 0, D, "wxv")
                load_cols(gcol, W["g_mem"][l], KC, "gcol")
                S.op("pool", lambda h: h.memset(epsc[:], RMS_EPS), writes=["eps"])
                S.dma("sp", gxk[:, :], W["xk_norm"][l].partition_broadcast(128), writes=["gxk"])
                rmsnorm_fm(memT, 0, 256, gcol, mn, "mn", scr, src_keys=[("memT", mt_, hf_) for mt_ in range(2) for hf_ in range(2)])
                mnk = [("mn", c) for c in range(KC)]
                it = 0
                for mt in range(2):
                    for hf in range(2):
                        psk, pkk = psum()
                        for c in range(KC):
                            S.op("pe", lambda h, psk=psk, c=c, mt=mt, hf=hf: h.matmul(psk[:, :], lhsT=mn[:, c, mt * 128:(mt + 1) * 128], rhs=wxk[:, c, hf * 512:(hf + 1) * 512], start=(c == 0), stop=(c == KC - 1)),
                                 reads=mnk + [("wxk", c_) for c_ in range(KC)], writes=[pkk])
                        psv, pkv = psum()
                        for c in range(KC):
                            S.op("pe", lambda h, psv=psv, c=c, mt=mt, hf=hf: h.matmul(psv[:, :], lhsT=mn[:, c, mt * 128:(mt + 1) * 128], rhs=wxv[:, c, hf * 512:(hf + 1) * 512], start=(c == 0), stop=(c == KC - 1)),
                                 reads=mnk + [("wxv", c_) for c_ in range(KC)], writes=[pkv])
                        tk_, tv_ = tkm[it % 2], tvm[it % 2]
                        kk_, vk_, sk_ = ("tkm", it % 2), ("tvm", it % 2), ("ssm", it % 4)
                        ssv = ssm[:, (it % 4) * 2:(it % 4) * 2 + 2]
                        it += 1
                        S.op("act", lambda h, psk=psk: h.activation(out=tsq[:, :], in_=psk[:, :], func=AF.Square), reads=[pkk], writes=["tsq"])
                        S.op("dve", lambda h, ssv=ssv: h.tensor_reduce(out=ssv, in_=tsq[:, :].rearrange("p (h d) -> p h d", h=2), axis=AX.X, op=ALU.add), reads=["tsq"], writes=[sk_])
                        S.op("act", lambda h, ssv=ssv: h.activation(out=ssv, in_=ssv, func=AF.Sqrt, scale=1.0 / 256, bias=epsc[:, 0:1]), reads=[sk_, "eps"], writes=[sk_])
                        S.op("dve", lambda h, ssv=ssv: h.reciprocal(out=ssv, in_=ssv), reads=[sk_], writes=[sk_])
                        S.op("dve", lambda h, psk=psk, tk_=tk_, ssv=ssv: h.tensor_tensor(out=tk_[:, :].rearrange("p (h d) -> p h d", h=2), in0=psk[:, :].rearrange("p (h d) -> p h d", h=2),
                                                                                   in1=ssv.unsqueeze(2).broadcast_to([128, 2, 256]), op=ALU.mult), reads=[pkk, sk_], writes=[kk_])
                        S.op("dve", lambda h, tk_=tk_: h.tensor_tensor(out=tk_[:, :].rearrange("p (h d) -> p h d", h=2), in0=tk_[:, :].rearrange("p (h d) -> p h d", h=2),
                                                                     in1=gxk[:, :].unsqueeze(1).broadcast_to([128, 2, 256]), op=ALU.mult), reads=[kk_, "gxk"], writes=[kk_])
                        S.dma("sp", o_pmk[l, mt * 128:(mt + 1) * 128, hf * 512:(hf + 1) * 512], tk_[:, :], reads=[kk_])
                        ps, pk = psum()
                        for q in range(4):
                            S.op("pe", lambda h, ps=ps, q=q, tk_=tk_: h.transpose(out=ps[:, q * 128:(q + 1) * 128], in_=tk_[:, q * 128:(q + 1) * 128], identity=ident[:, :]), reads=[kk_, "ident"], writes=[pk])
                        S.op("act", lambda h, ps=ps, hf=hf, mt=mt: h.activation(out=mkT[:, hf * 4:hf * 4 + 4, mt * 128:(mt + 1) * 128], in_=ps[:, :].rearrange("p (q t) -> p q t", q=4), func=AF.Copy),
                             reads=[pk], writes=[("mkT", hf, mt)])
                        S.op("act", lambda h, psv=psv, tv_=tv_: h.activation(out=tv_[:, :], in_=psv[:, :], func=AF.Copy), reads=[pkv], writes=[vk_])
                        S.dma("sp", o_pmv[l, mt * 128:(mt + 1) * 128, hf * 512:(hf + 1) * 512], tv_[:, :], reads=[vk_])
                        S.op("pool", lambda h, tv_=tv_, hf=hf, mt=mt: h.tensor_copy(out=mvb[:, mt, hf * 512:(hf + 1) * 512], in_=tv_[:, :]), reads=[vk_], writes=[("mvb", hf, mt)])
                S.flush()

        if stage >= 11:
            with contextlib.ExitStack() as ph:
                TBX = 256
                wq = sb("wq", [128, KC, D], BF16, ph)
                wo = sb("wo", [128, KC, D], BF16, ph)
                gcol = sb("gcol", [128, KC], F32, ph)
                epsc = sb("epsc", [128, 1], F32, ph)
                xqs = sb("xqs", [128, 2], F32, ph)
                sq = sb("sq", [128, KC, TBX], BF16, ph)
                rbc = sb("rbc", [128, TBX], F32, ph)
                xn = sb("xn", [128, KC, TBX], BF16, ph)
                qraw = sb("qraw", [128, KC, TBX], F32, ph)
                qsq = sb("qsq", [128, KC, TBX], BF16, ph)
                qT = sb("qT", [128, KC, TBX], BF16, ph)
                rq = sb("rq", [128, TBX], F32, ph)
                PTx = [sb(f"PTx{i}", [128, TBX], BF16, ph) for i in range(4)]
                oT = sb("oT", [128, KC, TBX], BF16, ph)
                rden = sb("rden", [128, TBX], F32, ph)
                otmp = [sb(f"otmp{i}", [128, TBX], F32, ph) for i in range(2)]
                scr = dict(sq=sq, rbc=rbc, eps=epsc)
                load_w(wq, W["w_xq"][l], 0, D, "wq")
                load_w(wo, W["w_xo"][l], 0, D, "wo")
                wqk = [("wq", c) for c in range(KC)]
                wok = [("wo", c) for c in range(KC)]
                load_cols(gcol, W["g_x"][l], KC, "gcol")
                load_cols(xqs, W["xq_norm"][l], 2, "xqs")
                S.op("dve", lambda h: h.tensor_scalar(out=xqs[:, :], in0=xqs[:, :], scalar1=1.0 / 16, scalar2=None, op0=ALU.mult), reads=["xqs"], writes=["xqs"])
                S.op("pool", lambda h: h.memset(epsc[:], RMS_EPS), writes=["eps"])
                pti = 0
                oti = 0
                for blk in range(T // TBX):
                    b0 = blk * TBX
                    xk_ = [("x", c, blk) for c in range(KC)]
                    rmsnorm_fm(xT, b0, TBX, gcol, xn, "xn", scr, src_keys=xk_)
                    xnk = [("xn", c) for c in range(KC)]
                    for i in range(KC):
                        ps, pk = psum()
                        for c in range(KC):
                            S.op("pe", lambda h, ps=ps, c=c, i=i: h.matmul(ps[:, 0:TBX], lhsT=wq[:, c, i * 128:(i + 1) * 128], rhs=xn[:, c, :], start=(c == 0), stop=(c == KC - 1)), reads=xnk + wqk, writes=[pk])
                        S.op("act", lambda h, ps=ps, i=i: h.activation(out=qraw[:, i, :], in_=ps[:, 0:TBX], func=AF.Copy), reads=[pk], writes=[("qraw", i)])
                        S.op("act", lambda h, ps=ps, i=i: h.activation(out=qsq[:, i, :], in_=ps[:, 0:TBX], func=AF.Square), reads=[pk], writes=[("qsq", i)])
                    for hd in range(4):
                        ps, pk = psum()
                        for dc in range(2):
                            S.op("pe", lambda h, ps=ps, hd=hd, dc=dc: h.matmul(ps[:, 0:TBX], lhsT=onesb[:, :], rhs=qsq[:, hd * 2 + dc, :], start=(dc == 0), stop=(dc == 1)), reads=[("qsq", hd * 2 + dc), "onesb"], writes=[pk])
                        S.op("act", lambda h, ps=ps: h.activation(out=rq[:, :], in_=ps[:, 0:TBX], func=AF.Sqrt, scale=1.0 / 256, bias=epsc[:, 0:1]), reads=[pk, "eps"], writes=["rq"])
                        S.op("dve", lambda h: h.reciprocal(out=rq[:, :], in_=rq[:, :]), reads=["rq"], writes=["rq"])
                        for dc in range(2):
                            i = hd * 2 + dc
                            S.op("dve", lambda h, i=i, dc=dc: h.scalar_tensor_tensor(out=qT[:, i, :], in0=qraw[:, i, :], scalar=xqs[:, dc:dc + 1], in1=rq[:, :], op0=ALU.mult, op1=ALU.mult),
                                 reads=[("qraw", i), "xqs", "rq"], writes=[("qT", i)])
                    for hd in range(4):
                        pts = []
                        for mt in range(2):
                            ps, pk = psum()
                            for dc in range(2):
                                S.op("pe", lambda h, ps=ps, hd=hd, dc=dc, mt=mt: h.matmul(ps[:, 0:TBX], lhsT=mkT[:, hd * 2 + dc, mt * 128:(mt + 1) * 128], rhs=qT[:, hd * 2 + dc, :], start=(dc == 0), stop=(dc == 1)),
                                     reads=[("qT", hd * 2 + dc)], writes=[pk])
                            pt, ptk = PTx[pti % 4], ("PTx", pti % 4)
                            pti += 1
                            S.op("act", lambda h, ps=ps, pt=pt: h.activation(out=pt[:, :], in_=ps[:, 0:TBX], func=AF.Exp), reads=[pk], writes=[ptk])
                            pts.append((pt, ptk))
                        ps, pk = psum()
                        for mt in range(2):
                            S.op("pe", lambda h, ps=ps, mt=mt, pt=pts[mt][0]: h.matmul(ps[:, 0:TBX], lhsT=onesb[:, :], rhs=pt[:, :], start=(mt == 0), stop=(mt == 1)), reads=[pts[mt][1], "onesb"], writes=[pk])
                        S.op("dve", lambda h, ps=ps: h.reciprocal(out=rden[:, :], in_=ps[:, 0:TBX]), reads=[pk], writes=["rden"])
                        for dc in range(2):
                            i = hd * 2 + dc
                            ps, pk = psum()
                            for mt in range(2):
                                S.op("pe", lambda h, ps=ps, mt=mt, i=i, pt=pts[mt][0]: h.matmul(ps[:, 0:TBX], lhsT=mvb[:, mt, i * 128:(i + 1) * 128], rhs=pt[:, :], start=(mt == 0), stop=(mt == 1)), reads=[pts[mt][1]], writes=[pk])
                            ot, otk = otmp[oti % 2], ("otmp", oti % 2)
                            oti += 1
                            S.op("dve", lambda h, ps=ps, ot=ot: h.tensor_tensor(out=ot[:, :], in0=ps[:, 0:TBX], in1=rden[:, :], op=ALU.mult), reads=[pk, "rden"], writes=[otk])
                            S.op("act", lambda h, ot=ot, i=i: h.activation(out=oT[:, i, :], in_=ot[:, :], func=AF.Copy), reads=[otk], writes=[("oT", i)])
                    otks = [("oT", i) for i in range(KC)]
                    for oc in range(KC):
                        ps, pk = psum()
                        for c in range(KC):
                            S.op("pe", lambda h, ps=ps, c=c, oc=oc: h.matmul(ps[:, 0:TBX], lhsT=wo[:, c, oc * 128:(oc + 1) * 128], rhs=oT[:, c, :], start=(c == 0), stop=(c == KC - 1)), reads=otks + wok, writes=[pk])
                        S.op("dve", lambda h, ps=ps, oc=oc, b0=b0: h.tensor_tensor(out=xT[:, oc, b0:b0 + TBX], in0=ps[:, 0:TBX], in1=xT[:, oc, b0:b0 + TBX], op=ALU.add), reads=[pk], writes=[("x", oc, blk)])

                if stage >= 15:
                    cks = [sb(f"cks{i}", [128, D], F32, ph) for i in range(2)]
                    cvs = [sb(f"cvs{i}", [128, D], F32, ph) for i in range(2)]
                    mkTs = sb("mkTs", [128, KC, 256], BF16, ph)
                    mvs = sb("mvs", [128, 2, D], BF16, ph)
                    rmsnorm_fm(xTs, 0, NS, gcol, xn, "xns", scr)
                    xnk = [("xns", c) for c in range(KC)]
                    for i in range(KC):
                        ps, pk = psum()
                        for c in range(KC):
                            S.op("pe", lambda h, ps=ps, c=c, i=i: h.matmul(ps[:, 0:NS], lhsT=wq[:, c, i * 128:(i + 1) * 128], rhs=xn[:, c, 0:NS], start=(c == 0), stop=(c == KC - 1)), reads=xnk + wqk, writes=[pk])
                        S.op("act", lambda h, ps=ps, i=i: h.activation(out=qraw[:, i, 0:NS], in_=ps[:, 0:NS], func=AF.Copy), reads=[pk], writes=[("qraws", i)])
                        S.op("act", lambda h, ps=ps, i=i: h.activation(out=qsq[:, i, 0:NS], in_=ps[:, 0:NS], func=AF.Square), reads=[pk], writes=[("qsqs", i)])
                    for hd in range(4):
                        ps, pk = psum()
                        for dc in range(2):
                            S.op("pe", lambda h, ps=ps, hd=hd, dc=dc: h.matmul(ps[:, 0:NS], lhsT=onesb[:, :], rhs=qsq[:, hd * 2 + dc, 0:NS], start=(dc == 0), stop=(dc == 1)), reads=[("qsqs", hd * 2 + dc), "onesb"], writes=[pk])
                        S.op("act", lambda h, ps=ps: h.activation(out=rq[:, 0:NS], in_=ps[:, 0:NS], func=AF.Sqrt, scale=1.0 / 256, bias=epsc[:, 0:1]), reads=[pk, "eps"], writes=["rqs"])
                        S.op("dve", lambda h: h.reciprocal(out=rq[:, 0:NS], in_=rq[:, 0:NS]), reads=["rqs"], writes=["rqs"])
                        for dc in range(2):
                            i = hd * 2 + dc
                            S.op("dve", lambda h, i=i, dc=dc: h.scalar_tensor_tensor(out=qT[:, i, 0:NS], in0=qraw[:, i, 0:NS], scalar=xqs[:, dc:dc + 1], in1=rq[:, 0:NS], op0=ALU.mult, op1=ALU.mult),
                                 reads=[("qraws", i), "xqs", "rqs"], writes=[("qTs", i)])
                    for b in range(4):
                        for mt in range(2):
                            S.dma("sp", cks[mt][:, :], cmk[l, b, mt * 128:(mt + 1) * 128, :], writes=[("cks", mt)])
                            S.dma("sp", cvs[mt][:, :], cmv[l, b, mt * 128:(mt + 1) * 128, :], writes=[("cvs", mt)])
                            for half in range(2):
                                ps, pk = psum()
                                for q in range(4):
                                    c = half * 4 + q
                                    S.op("pe", lambda h, ps=ps, q=q, c=c, mt=mt: h.transpose(out=ps[:, q * 128:(q + 1) * 128], in_=cks[mt][:, c * 128:(c + 1) * 128], identity=ident[:, :]), reads=[("cks", mt), "ident"], writes=[pk])
                                S.op("act", lambda h, ps=ps, half=half, mt=mt: h.activation(out=mkTs[:, half * 4:half * 4 + 4, mt * 128:(mt + 1) * 128], in_=ps[:, :].rearrange("p (q t) -> p q t", q=4), func=AF.Copy),
                                     reads=[pk], writes=[("mkTs", mt, half)])
                            S.op("pool", lambda h, mt=mt: h.tensor_copy(out=mvs[:, mt, :], in_=cvs[mt][:, :]), reads=[("cvs", mt)], writes=[("mvs", mt)])
                        mkk = [("mkTs", mt, half) for mt in range(2) for half in range(2)]
                        for hd in range(4):
                            pts = []
                            for mt in range(2):
                                ps, pk = psum()
                                for dc in range(2):
                                    S.op("pe", lambda h, ps=ps, hd=hd, dc=dc, mt=mt, b=b: h.matmul(ps[:, 0:4], lhsT=mkTs[:, hd * 2 + dc, mt * 128:(mt + 1) * 128], rhs=qT[:, hd * 2 + dc, 4 * b:4 * b + 4], start=(dc == 0), stop=(dc == 1)),
                                         reads=[("qTs", hd * 2 + dc)] + mkk, writes=[pk])
                                pt, ptk = PTx[pti % 4], ("PTx", pti % 4)
                                pti += 1
                                S.op("act", lambda h, ps=ps, pt=pt: h.activation(out=pt[:, 0:4], in_=ps[:, 0:4], func=AF.Exp), reads=[pk], writes=[ptk])
                                pts.append((pt, ptk))
                            ps, pk = psum()
                            for mt in range(2):
                                S.op("pe", lambda h, ps=ps, mt=mt, pt=pts[mt][0]: h.matmul(ps[:, 0:4], lhsT=onesb[:, :], rhs=pt[:, 0:4], start=(mt == 0), stop=(mt == 1)), reads=[pts[mt][1], "onesb"], writes=[pk])
                            S.op("dve", lambda h, ps=ps: h.reciprocal(out=rden[:, 0:4], in_=ps[:, 0:4]), reads=[pk], writes=["rdens"])
                            for dc in range(2):
                                i = hd * 2 + dc
                                ps, pk = psum()
                                for mt in range(2):
                                    S.op("pe", lambda h, ps=ps, mt=mt, i=i, pt=pts[mt][0]: h.matmul(ps[:, 0:4], lhsT=mvs[:, mt, i * 128:(i + 1) * 128], rhs=pt[:, 0:4], start=(mt == 0), stop=(mt == 1)), reads=[pts[mt][1], ("mvs", mt)], writes=[pk])
                                ot, otk = otmp[oti % 2], ("otmp", oti % 2)
                                oti += 1
                                S.op("dve", lambda h, ps=ps, ot=ot: h.tensor_tensor(out=ot[:, 0:4], in0=ps[:, 0:4], in1=rden[:, 0:4], op=ALU.mult), reads=[pk, "rdens"], writes=[otk])
                                S.op("act", lambda h, ot=ot, i=i, b=b: h.activation(out=oT[:, i, 4 * b:4 * b + 4], in_=ot[:, 0:4], func=AF.Copy), reads=[otk], writes=[("oTs", i, b)])
                    otks = [("oTs", i, b) for i in range(KC) for b in range(4)]
                    for oc in range(KC):
                        ps, pk = psum()
                        for c in range(KC):
                            S.op("pe", lambda h, ps=ps, c=c, oc=oc: h.matmul(ps[:, 0:NS], lhsT=wo[:, c, oc * 128:(oc + 1) * 128], rhs=oT[:, c, 0:NS], start=(c == 0), stop=(c == KC - 1)), reads=otks + wok, writes=[pk])
                        S.op("dve", lambda h, ps=ps, oc=oc: h.tensor_tensor(out=xTs[:, oc, :], in0=ps[:, 0:NS], in1=xTs[:, oc, :], op=ALU.add), reads=[pk], writes=[("xs", oc)])
                    if "x2s" in DBG and l == 0:
                        for c in range(KC):
                            S.dma("sp", DBG["x2s"][c], xTs[:, c, :], reads=[("xs", c)])
                if "x2" in DBG and l == 0:
                    for c in range(KC):
                        S.dma("sp", DBG["x2"][c], xT[:, c, :], reads=[("x", c, b_) for b_ in range(T // TBX)])
                S.flush()
        xa.close()

        if stage >= 12:
            with contextlib.ExitStack() as ph:
                TBF, NJ = 512, 22
                wdn = sb("wdn", [128, NJ, D], BF16, ph)
                gcol = sb("gcol", [128, KC], F32, ph)
                epsc = sb("epsc", [128, 1], F32, ph)
                cw = sb("cw", [128, 2 * NJ, 3], F32, ph)
                cb = sb("cb", [128, 2 * NJ], F32, ph)
                act_ = sb("act", [128, NJ, TBF], BF16, ph)
                wsl = [sb(f"wsl{i}", [128, KC, 256], BF16, ph) for i in range(3)]
                sq = sb("sq", [128, KC, TBF], BF16, ph)
                rbc = sb("rbc", [128, TBF], F32, ph)
                xn = sb("xn", [128, KC, TBF], BF16, ph)
                hb = [sb(f"hb{i}", [128, 2, TBF + 2], F32, ph) for i in range(2)]
                halo = sb("halo", [128, 2 * NJ, 2], F32, ph)
                ta = sb("ta", [128, 2, TBF], F32, ph)
                tb = sb("tb", [128, 2, TBF], F32, ph)
                sg = sb("sg", [128, TBF], F32, ph)
                stg = [sb(f"stg{i}", [2, 512], F32, ph) for i in range(2)]
                scr = dict(sq=sq, rbc=rbc, eps=epsc)
                load_w(wdn, W["w_down"][l], 0, D, "wdn")
                wdk = [("wdn", c) for c in range(NJ)]
                load_cols(gcol, W["g_ffn"][l], KC, "gcol")
                for tap in range(3):
                    S.dma("sp", cw[:, :, tap], W["ffn_conv_w"][l][tap].rearrange("(j p) -> p j", p=128), writes=[("cw", tap)], allow_slow_non_contiguous=True)
                cwk = [("cw", tap) for tap in range(3)]
                load_cols(cb, W["ffn_conv_b"][l], 2 * NJ, "cb")
                S.op("pool", lambda h: h.memset(epsc[:], RMS_EPS), writes=["eps"])
                S.op("pool", lambda h: h.memset(halo[:].rearrange("p a b -> p (a b)"), 0.0), writes=["halo0"])
                S.flush()
                wi_ = 0
                for blk in range(T // TBF):
                    b0 = blk * TBF
                    xk_ = [("x", c, blk) for c in range(KC)]
                    rmsnorm_fm(xT, b0, TBF, gcol, xn, "xn", scr, src_keys=xk_)
                    xnk = [("xn", c) for c in range(KC)]
                    for j in range(NJ):
                        w_ = wsl[wi_ % 3]
                        wk2 = [("wsl", wi_ % 3, 0), ("wsl", wi_ % 3, 1)]
                        wi_ += 1
                        for g in range(2):
                            S.dma("pool", w_[:, :, g * 128:(g + 1) * 128], W["w_up"][l][:, g * D_FF + j * 128:g * D_FF + (j + 1) * 128].rearrange("(c p) n -> p c n", p=128), writes=[wk2[g]])
                        h_ = hb[j % 2]
                        for g in range(2):
                            jj = j + NJ * g
                            hk = ("hb", j % 2, g)
                            ps, pk = psum()
                            for c in range(KC):
                                S.op("pe", lambda h, ps=ps, c=c, g=g, w_=w_: h.matmul(ps[:, :], lhsT=w_[:, c, g * 128:(g + 1) * 128], rhs=xn[:, c, :], start=(c == 0), stop=(c == KC - 1)), reads=xnk + [wk2[g]], writes=[pk])
                            S.op("dve", lambda h, h_=h_, g=g, jj=jj: h.tensor_copy(out=h_[:, g, 0:2], in_=halo[:, jj, :]), reads=[("halo", jj)], writes=[hk])
                            S.op("act", lambda h, ps=ps, h_=h_, g=g: h.activation(out=h_[:, g, 2:TBF + 2], in_=ps[:, :], func=AF.Copy), reads=[pk, hk], writes=[hk])
                            S.op("dve", lambda h, h_=h_, g=g, jj=jj: h.tensor_copy(out=halo[:, jj, :], in_=h_[:, g, TBF:TBF + 2]), reads=[hk], writes=[("halo", jj)])
                            S.op("dve", lambda h, h_=h_, g=g, jj=jj: h.tensor_scalar(out=ta[:, g, :], in0=h_[:, g, 0:TBF], scalar1=cw[:, jj, 0:1], scalar2=cb[:, jj:jj + 1], op0=ALU.mult, op1=ALU.add),
                                 reads=[hk, "cb"] + cwk, writes=[("ta", g)])
                            S.op("dve", lambda h, h_=h_, g=g, jj=jj: h.scalar_tensor_tensor(out=tb[:, g, :], in0=h_[:, g, 1:TBF + 1], scalar=cw[:, jj, 1:2], in1=ta[:, g, :], op0=ALU.mult, op1=ALU.add),
                                 reads=[hk, ("ta", g)] + cwk, writes=[("tb", g)])
                            S.op("dve", lambda h, h_=h_, g=g, jj=jj: h.scalar_tensor_tensor(out=ta[:, g, :], in0=h_[:, g, 2:TBF + 2], scalar=cw[:, jj, 2:3], in1=tb[:, g, :], op0=ALU.mult, op1=ALU.add),
                                 reads=[hk, ("tb", g)] + cwk, writes=[("ta", g)])
                        S.op("act", lambda h: h.activation(out=sg[:, :], in_=ta[:, 0, :], func=AF.Silu), reads=[("ta", 0)], writes=["sg"])
                        S.op("dve", lambda h, j=j: h.tensor_tensor(out=act_[:, j, :], in0=sg[:, :], in1=ta[:, 1, :], op=ALU.mult), reads=["sg", ("ta", 1)], writes=[("act", j)])
                    actk = [("act", j) for j in range(NJ)]
                    for oc in range(KC):
                        ps, pk = psum()
                        for j in range(NJ):
                            S.op("pe", lambda h, ps=ps, j=j, oc=oc: h.matmul(ps[:, :], lhsT=wdn[:, j, oc * 128:(oc + 1) * 128], rhs=act_[:, j, :], start=(j == 0), stop=(j == NJ - 1)), reads=actk + wdk, writes=[pk])
                        S.op("dve", lambda h, ps=ps, oc=oc, b0=b0: h.tensor_tensor(out=xT[:, oc, b0:b0 + TBF], in0=ps[:, :], in1=xT[:, oc, b0:b0 + TBF], op=ALU.add), reads=[pk], writes=[("x", oc, blk)])

                if stage >= 15:
                    sfs = [sb(f"sfs{i}", [8, 512], F32, ph) for i in range(2)]
                    halos = sb("halos", [128, 2 * NJ, 4, 2], F32, ph)
                    haloo = sb("haloo", [128, 2 * NJ, 4, 2], F32, ph)
                    hbS = [sb(f"hbS{i}", [128, 2, 4, 6], F32, ph) for i in range(2)]
                    taS = sb("taS", [128, 2, NS], F32, ph)
                    tbS = sb("tbS", [128, 2, NS], F32, ph)
                    for g11 in range(11):
                        sf_ = sfs[g11 % 2]
                        S.dma("sp", sf_[:, :], sff[l][:, g11 * 512:(g11 + 1) * 512], writes=[("sfs", g11 % 2)])
                        ps, pk = psum()
                        for q in range(4):
                            S.op("pe", lambda h, ps=ps, q=q, sf_=sf_: h.transpose(out=ps[:, q * 8:(q + 1) * 8], in_=sf_[0:8, q * 128:(q + 1) * 128], identity=ident[0:8, 0:8]), reads=[("sfs", g11 % 2), "ident"], writes=[pk])
                        S.op("act", lambda h, ps=ps, g11=g11: h.activation(out=halos[:, g11 * 4:(g11 + 1) * 4, :, :].rearrange("p a b t -> p a (b t)"), in_=ps[:, 0:32].rearrange("p (a k) -> p a k", a=4), func=AF.Copy), reads=[pk], writes=[("halos", g11)])
                    rmsnorm_fm(xTs, 0, NS, gcol, xn, "xns", scr)
                    xnk = [("xns", c) for c in range(KC)]
                    for j in range(NJ):
                        w_ = wsl[wi_ % 3]
                        wk2 = [("wsl", wi_ % 3, 0), ("wsl", wi_ % 3, 1)]
                        wi_ += 1
                        for g in range(2):
                            S.dma("pool", w_[:, :, g * 128:(g + 1) * 128], W["w_up"][l][:, g * D_FF + j * 128:g * D_FF + (j + 1) * 128].rearrange("(c p) n -> p c n", p=128), writes=[wk2[g]])
                        hS = hbS[j % 2]
                        for g in range(2):
                            jj = j + NJ * g
                            hk = ("hbS", j % 2, g)
                            ps, pk = psum()
                            for c in range(KC):
                                S.op("pe", lambda h, ps=ps, c=c, g=g, w_=w_: h.matmul(ps[:, 0:NS], lhsT=w_[:, c, g * 128:(g + 1) * 128], rhs=xn[:, c, 0:NS], start=(c == 0), stop=(c == KC - 1)), reads=xnk + [wk2[g]], writes=[pk])
                            S.op("dve", lambda h, hS=hS, g=g, jj=jj: h.tensor_copy(out=hS[:, g, :, 0:2], in_=halos[:, jj, :, :]), reads=[("halos", jj // 4)], writes=[hk])
                            S.op("act", lambda h, ps=ps, hS=hS, g=g: h.activation(out=hS[:, g, :, 2:6], in_=ps[:, 0:NS].rearrange("p (b t) -> p b t", b=4), func=AF.Copy), reads=[pk, hk], writes=[hk])
                            S.op("dve", lambda h, hS=hS, g=g, jj=jj: h.tensor_copy(out=haloo[:, jj, :, :], in_=hS[:, g, :, 4:6]), reads=[hk], writes=[("haloo", jj)])
                            tav = taS[:, g, :].rearrange("p (b t) -> p b t", b=4)
                            tbv = tbS[:, g, :].rearrange("p (b t) -> p b t", b=4)
                            S.op("dve", lambda h, hS=hS, g=g, jj=jj, tav=tav: h.tensor_scalar(out=tav, in0=hS[:, g, :, 0:4], scalar1=cw[:, jj, 0:1], scalar2=cb[:, jj:jj + 1], op0=ALU.mult, op1=ALU.add), reads=[hk, "cb"] + cwk, writes=[("taS", g)])
                            S.op("dve", lambda h, hS=hS, g=g, jj=jj, tav=tav, tbv=tbv: h.scalar_tensor_tensor(out=tbv, in0=hS[:, g, :, 1:5], scalar=cw[:, jj, 1:2], in1=tav, op0=ALU.mult, op1=ALU.add), reads=[hk, ("taS", g)] + cwk, writes=[("tbS", g)])
                            S.op("dve", lambda h, hS=hS, g=g, jj=jj, tav=tav, tbv=tbv: h.scalar_tensor_tensor(out=tav, in0=hS[:, g, :, 2:6], scalar=cw[:, jj, 2:3], in1=tbv, op0=ALU.mult, op1=ALU.add), reads=[hk, ("tbS", g)] + cwk, writes=[("taS", g)])
                        S.op("act", lambda h: h.activation(out=sg[:, 0:NS], in_=taS[:, 0, :], func=AF.Silu), reads=[("taS", 0)], writes=["sgs"])
                        S.op("dve", lambda h, j=j: h.tensor_tensor(out=act_[:, j, 0:NS], in0=sg[:, 0:NS], in1=taS[:, 1, :], op=ALU.mult), reads=["sgs", ("taS", 1)], writes=[("acts", j)])
                    actk = [("acts", j) for j in range(NJ)]
                    for oc in range(KC):
                        ps, pk = psum()
                        for j in range(NJ):
                            S.op("pe", lambda h, ps=ps, j=j, oc=oc: h.matmul(ps[:, 0:NS], lhsT=wdn[:, j, oc * 128:(oc + 1) * 128], rhs=act_[:, j, 0:NS], start=(j == 0), stop=(j == NJ - 1)), reads=actk + wdk, writes=[pk])
                        S.op("dve", lambda h, ps=ps, oc=oc: h.tensor_tensor(out=xTs[:, oc, :], in0=ps[:, 0:NS], in1=xTs[:, oc, :], op=ALU.add), reads=[pk], writes=[("xs", oc)])
                    for g11 in range(11):
                        ps, pk = psum()
                        for q in range(4):
                            jj = g11 * 4 + q
                            S.op("pe", lambda h, ps=ps, q=q, jj=jj: h.transpose(out=ps[0:8, q * 128:(q + 1) * 128], in_=haloo[:, jj, :, :].rearrange("p b t -> p (b t)"), identity=ident[:, :]), reads=[("haloo", jj), "ident"], writes=[pk])
                        sf_ = sfs[g11 % 2]
                        S.op("act", lambda h, ps=ps, sf_=sf_: h.activation(out=sf_[:, :], in_=ps[0:8, :], func=AF.Copy), reads=[pk], writes=[("sfo", g11 % 2)])
                        S.dma("sp", o_sff[l][:, g11 * 512:(g11 + 1) * 512], sf_[:, :], reads=[("sfo", g11 % 2)])
                    if "x3s" in DBG and l == 0:
                        for c in range(KC):
                            S.dma("sp", DBG["x3s"][c], xTs[:, c, :], reads=[("xs", c)])
                for g11 in range(11):
                    ps, pk = psum()
                    for q in range(4):
                        jj = g11 * 4 + q
                        S.op("pe", lambda h, ps=ps, q=q, jj=jj: h.transpose(out=ps[0:2, q * 128:(q + 1) * 128], in_=halo[:, jj, :], identity=ident[:, :]), reads=[("halo", jj), "ident"], writes=[pk])
                    sg_ = stg[g11 % 2]
                    S.op("act", lambda h, ps=ps, sg_=sg_: h.activation(out=sg_[:, :], in_=ps[0:2, :], func=AF.Copy), reads=[pk], writes=[("stg", g11 % 2)])
                    S.dma("sp", o_pff[l][:, g11 * 512:(g11 + 1) * 512], sg_[:, :], reads=[("stg", g11 % 2)])
                if "x3" in DBG and l == 0:
                    for c in range(KC):
                        S.dma("sp", DBG["x3"][c], xT[:, c, :], reads=[("x", c, b_) for b_ in range(T // TBF)])
                S.flush()

    if stage >= 12:
        with contextlib.ExitStack() as ph:
            ysb = [sb(f"ysb{i}", [128, D], F32, ph) for i in range(2)]
            for tt in range(NT):
                y_ = ysb[tt % 2]
                for half in range(2):
                    ps, pk = psum()
                    for q in range(4):
                        S.op("pe", lambda h, ps=ps, q=q, half=half, tt=tt: h.transpose(out=ps[:, q * 128:(q + 1) * 128], in_=xT[:, half * 4 + q, tt * 128:(tt + 1) * 128], identity=ident[:, :]), reads=["ident"], writes=[pk])
                    if half == 0:
                        S.op("act", lambda h, ps=ps, y_=y_: h.activation(out=y_[:, 0:512], in_=ps[:, :], func=AF.Copy), reads=[pk], writes=[("ysb", tt % 2, 0)])
                    else:
                        S.op("dve", lambda h, ps=ps, y_=y_: h.tensor_copy(out=y_[:, 512:1024], in_=ps[:, :]), reads=[pk], writes=[("ysb", tt % 2, 1)])
                S.dma("sp", o_yp[tt * 128:(tt + 1) * 128, :], y_[:, :], reads=[("ysb", tt % 2, 0), ("ysb", tt % 2, 1)])
            yss = sb("yss", [NS, D], F32, ph)
            for half in range(2):
                ps, pk = psum()
                for q in range(4):
                    S.op("pe", lambda h, ps=ps, q=q, half=half: h.transpose(out=ps[0:NS, q * 128:(q + 1) * 128], in_=xTs[:, half * 4 + q, :], identity=ident[:, :]), reads=["ident"], writes=[pk])
                S.op("act", lambda h, ps=ps, half=half: h.activation(out=yss[:, half * 512:(half + 1) * 512], in_=ps[0:NS, :], func=AF.Copy), reads=[pk], writes=[("yss", half)])
            S.dma("sp", o_ys[:, :], yss[:, :], reads=[("yss", 0), ("yss", 1)])
            S.flush()

    S.flush()
    st.close()
    return nc


def _core_inputs(inp, b):
    sl = slice(4 * b, 4 * b + 4)
    m = {
        "xp": np.ascontiguousarray(inp["x_prompt"][b]),
        "xs": np.ascontiguousarray(inp["x_sample"][sl].reshape(NS, D)),
        "mem": np.ascontiguousarray(inp["mem_prompt"][b]),
    }
    m["scv"] = np.ascontiguousarray(inp["state_conv"][:, sl])
    m["srw"] = np.ascontiguousarray(inp["state_rwkv"][:, sl])
    for l_ in range(DEPTH):
        m[f"fk{l_}"] = inp["cache_fox_k"][l_].reshape(-1, 512)
        m[f"fv{l_}"] = inp["cache_fox_v"][l_].reshape(-1, 512)
        m[f"flf{l_}"] = inp["cache_fox_logf"][l_].reshape(-1, 1024)
    m["ptab"] = np.ascontiguousarray(inp["page_table"][sl]).astype(np.int32)
    m["cmk"] = np.ascontiguousarray(inp["cache_mem_k"][:, sl]).reshape(DEPTH, 4, 256, D)
    m["cmv"] = np.ascontiguousarray(inp["cache_mem_v"][:, sl]).reshape(DEPTH, 4, 256, D)
    m["sff"] = np.ascontiguousarray(inp["state_ffn"][:, sl]).reshape(DEPTH, 8, 2 * D_FF)
    m["ssh"] = np.ascontiguousarray(inp["state_rwkv_shift"][:, sl, 0])
    for k in ("g_mix", "w_in", "fox_q_norm", "fox_k_norm", "fox_b_f", "conv_w", "conv_b", "conv_ln_g", "conv_ln_b",
              "rwkv_mu", "rwkv_w0", "rwkv_w_up", "rwkv_a0", "rwkv_a_up", "rwkv_g_up", "rwkv_k_k", "rwkv_k_a", "rwkv_ln_g", "rwkv_ln_b",
              "g_mem", "w_xk", "w_xv", "xk_norm", "w_out", "g_x", "w_xq", "xq_norm", "w_xo", "g_ffn", "w_up", "ffn_conv_w", "ffn_conv_b", "w_down"):
        m[k] = np.ascontiguousarray(inp[k])
    m["rwkv_r_k"] = np.ascontiguousarray(inp["rwkv_r_k"].reshape(DEPTH, 256))
    return m


def run(inputs, cores=tuple(range(NCORES)), stage=99):
    nc = build(stage)
    in_maps = [_core_inputs(inputs, b) for b in cores]
    res = run_bass_kernel_spmd(nc, in_maps, core_ids=list(range(len(cores))))
    return res.results


def kernel(**inputs):
    inputs = {k: np.asarray(v) for k, v in inputs.items()}
    r = run(inputs)
    B, DB = 8, 32
    f = np.float32
    out = dict(
        y_prompt=np.zeros((B, T, D), f), y_sample=np.zeros((DB, 4, D), f),
        p_fox_k=np.zeros((DEPTH, B, T, 8, 64), f), p_fox_v=np.zeros((DEPTH, B, T, 8, 64), f), p_fox_logf=np.zeros((DEPTH, B, T, 8), f),
        p_rwkv=np.zeros((DEPTH, B, 4, 64, 64), f), p_rwkv_shift=np.zeros((DEPTH, B, 1, 1024), f),
        p_conv=np.zeros((DEPTH, B, 30, 256), f), p_ffn=np.zeros((DEPTH, B, 2, 5632), f),
        p_mem_k=np.zeros((DEPTH, B, 256, 4, 256), f), p_mem_v=np.zeros((DEPTH, B, 256, 4, 256), f),
        s_fox_k=np.zeros((DEPTH, DB, 4, 8, 64), f), s_fox_v=np.zeros((DEPTH, DB, 4, 8, 64), f), s_fox_logf=np.zeros((DEPTH, DB, 4, 8), f),
        s_rwkv=np.zeros((DEPTH, DB, 4, 64, 64), f), s_rwkv_shift=np.zeros((DEPTH, DB, 1, 1024), f),
        s_conv=np.zeros((DEPTH, DB, 30, 256), f), s_ffn=np.zeros((DEPTH, DB, 2, 5632), f),
    )
    for b in range(NCORES):
        rb = r[b]
        sl = slice(4 * b, 4 * b + 4)
        out["p_fox_k"][:, b] = rb["p_fox_k"].reshape(DEPTH, T, 8, 64)
        out["p_fox_v"][:, b] = rb["p_fox_v"].reshape(DEPTH, T, 8, 64)
        out["p_fox_logf"][:, b] = rb["p_fox_logf"]
        out["s_fox_k"][:, sl] = rb["s_fox_k"].reshape(DEPTH, 4, 4, 8, 64)
        out["s_fox_v"][:, sl] = rb["s_fox_v"].reshape(DEPTH, 4, 4, 8, 64)
        out["s_fox_logf"][:, sl] = rb["s_fox_logf"].reshape(DEPTH, 4, 4, 8)
        out["p_conv"][:, b] = rb["p_conv"]
        out["s_conv"][:, sl] = rb["s_conv"]
        out["s_ffn"][:, sl] = rb["s_ffn"].reshape(DEPTH, 4, 2, 2 * D_FF)
        out["y_sample"][sl] = rb["y_sample"].reshape(4, 4, D)
        out["p_rwkv"][:, b] = rb["p_rwkv"]
        out["p_ffn"][:, b] = rb["p_ffn"]
        out["y_prompt"][b] = rb["y_prompt"]
        out["p_mem_k"][:, b] = rb["p_mem_k"].reshape(DEPTH, 256, 4, 256)
        out["p_mem_v"][:, b] = rb["p_mem_v"].reshape(DEPTH, 256, 4, 256)
        out["p_rwkv_shift"][:, b, 0] = rb["p_rwkv_shift"]
        out["s_rwkv"][:, sl] = rb["s_rwkv"]
        out["s_rwkv_shift"][:, sl, 0] = rb["s_rwkv_shift"]
    order = ["y_prompt", "y_sample", "p_fox_k", "p_fox_v", "p_fox_logf", "p_rwkv", "p_rwkv_shift", "p_conv", "p_ffn",
             "p_mem_k", "p_mem_v", "s_fox_k", "s_fox_v", "s_fox_logf", "s_rwkv", "s_rwkv_shift", "s_conv", "s_ffn"]
    return tuple(out[k] for k in order)
```

```python
import contextlib
import os
RWSUB = int(os.environ.get('RWSUB', '99'))
RWX = int(os.environ.get('RWX', '99'))
ZI = int(os.environ.get('ZI', '0'))
SRW = int(os.environ.get('SRW', '7'))
XVE = int(os.environ.get('XVE', '0'))
import numpy as np
import concourse.bass as bass
import concourse.mybir as mybir
from concourse.bass_utils import run_bass_kernel_spmd

F32 = mybir.dt.float32
BF16 = mybir.dt.bfloat16
I32 = mybir.dt.int32
AF = mybir.ActivationFunctionType
ALU = mybir.AluOpType
AX = mybir.AxisListType

NCORES = 8
D = 1024
KC = 8
T = 2048
NT = 16
NS = 16
DEPTH = 2
H_B = 8
IN_COLS = 3080
FOX0 = 1024
CONV0 = 1024 + 1544
D_FF = 2816
RMS_EPS = 1e-6
SAME_ENGINE_SYNC = os.environ.get("SES", "1") == "1"


class Sched:
    def __init__(self, nc, st):
        self.nc = nc
        self.st = st
        self.E = {}
        for n, h in (("pe", nc.tensor), ("act", nc.scalar), ("dve", nc.vector), ("pool", nc.gpsimd), ("sp", nc.sync)):
            self.E[n] = dict(h=h, sem=st.enter_context(nc.semaphore("q_" + n)), cnt=0, known={}, prog=[], gen=0)
        self.dpool = {"sp": [dict(sem=st.enter_context(nc.semaphore(f"dqh{i}")), val=0, i=("h", i)) for i in range(28)],
                      "pool": [dict(sem=st.enter_context(nc.semaphore(f"dqs{i}")), val=0, i=("s", i)) for i in range(12)]}
        self.dsems = self.dpool["sp"] + self.dpool["pool"]
        self.di = {"sp": 0, "pool": 0}
        self.lw = {}
        self.rd = {}

    def _deps(self, reads, writes):
        d = []
        for k in reads:
            if k in self.lw:
                d.append(self.lw[k])
        for k in writes:
            if k in self.lw:
                d.append(self.lw[k])
            d += self.rd.get(k, [])
        return d

    def _wait(self, en, deps):
        e = self.E[en]
        for (sem, val, key) in deps:
            if key == ("e", en) and not SAME_ENGINE_SYNC:
                continue
            if key == ("e", "pe") and en == "pe":
                continue
            if e["known"].get(key, 0) >= val:
                continue
            e["known"][key] = val
            e["prog"].append(lambda h, sem=sem, val=val: h.wait_ge(sem, val))

    def op(self, en, fn, reads=(), writes=()):
        e = self.E[en]
        writes = list(writes) + [k for k in reads if isinstance(k, tuple) and k and k[0] == "ps" and k not in writes]
        self._wait(en, self._deps(reads, writes))
        e["cnt"] += 1
        c = e["cnt"]
        sem = e["sem"]
        e["prog"].append(lambda h, fn=fn, sem=sem: fn(h).then_inc(sem, 1))
        tok = (sem, c, ("e", en))
        for k in writes:
            self.lw[k] = tok
            self.rd[k] = []
        for k in reads:
            if k not in writes:
                self.rd.setdefault(k, []).append(tok)

    def dma(self, qn, out, in_, reads=(), writes=(), **kw):
        e = self.E[qn]
        pool_ = self.dpool[qn]
        d = pool_[self.di[qn]]
        self.di[qn] = (self.di[qn] + 1) % len(pool_)
        deps = self._deps(reads, writes)
        if d["val"] > 0:
            deps.append((d["sem"], d["val"], ("d", d["i"])))
        self._wait(qn, deps)
        d["val"] += 16
        sem, val = d["sem"], d["val"]
        e["prog"].append(lambda h, out=out, in_=in_, sem=sem, kw=kw: h.dma_start(out=out, in_=in_, **kw).then_inc(sem, 16))
        tok = (sem, val, ("d", d["i"]))
        for k in writes:
            self.lw[k] = tok
            self.rd[k] = []
        for k in reads:
            if k not in writes:
                self.rd.setdefault(k, []).append(tok)

    def idma(self, out, in_, idx_ap, reads=(), writes=()):
        qn = "pool"
        e = self.E[qn]
        pool_ = self.dpool[qn]
        d = pool_[self.di[qn]]
        self.di[qn] = (self.di[qn] + 1) % len(pool_)
        deps = self._deps(reads, writes)
        if d["val"] > 0:
            deps.append((d["sem"], d["val"], ("d", d["i"])))
        self._wait(qn, deps)
        d["val"] += 16
        sem, val = d["sem"], d["val"]
        e["prog"].append(lambda h, out=out, in_=in_, idx_ap=idx_ap, sem=sem: h.indirect_dma_start(
            out=out, out_offset=None, in_=in_, in_offset=bass.IndirectOffsetOnAxis(ap=idx_ap, axis=0)).then_inc(sem, 16))
        tok = (sem, val, ("d", d["i"]))
        for k in writes:
            self.lw[k] = tok
            self.rd[k] = []
        for k in reads:
            if k not in writes:
                self.rd.setdefault(k, []).append(tok)

    def flush(self):
        for en, e in self.E.items():
            deps = [(o["sem"], o["cnt"], ("e", on)) for on, o in self.E.items() if on != en and o["cnt"] > 0]
            deps += [(d["sem"], d["val"], ("d", d["i"])) for d in self.dsems if d["val"] > 0]
            self._wait(en, deps)
        with self.nc.Block() as blk:
            for en, dec in (("pe", blk.tensor), ("act", blk.scalar), ("dve", blk.vector), ("pool", blk.gpsimd), ("sp", blk.sync)):
                prog = self.E[en]["prog"]
                if prog:
                    def body(h, prog=prog):
                        for f in prog:
                            f(h)
                    dec(body)
                self.E[en]["prog"] = []
        self.lw.clear()
        self.rd.clear()
        for en, e in self.E.items():
            if e["cnt"] > 12000:
                e["gen"] += 1
                e["sem"] = self.st.enter_context(self.nc.semaphore(f"q_{en}_{e['gen']}"))
                e["cnt"] = 0
                for o in self.E.values():
                    o["known"].pop(("e", en), None)
        for d in self.dsems:
            if d["val"] > 12000:
                d["sem"] = self.st.enter_context(self.nc.semaphore(f"dq{d['i'][0]}{d['i'][1]}_{d['val']}"))
                d["val"] = 0
                for o in self.E.values():
                    o["known"].pop(("d", d["i"]), None)


def build(stage=99):
    nc = bass.Bass("TRN2", target_bir_lowering=False)
    st = contextlib.ExitStack()

    def din(name, shape, dt=F32):
        return nc.dram_tensor(name, list(shape), dt, kind="ExternalInput").ap()

    def dout(name, shape, dt=F32):
        return nc.dram_tensor(name, list(shape), dt, kind="ExternalOutput").ap()

    xp = din("xp", [T, D])
    xs = din("xs", [NS, D])
    mem = din("mem", [256, D])
    W = {}
    for name, shape in (("g_mix", [DEPTH, D]), ("w_in", [DEPTH, D, IN_COLS]), ("fox_q_norm", [DEPTH, 64]),
                        ("fox_k_norm", [DEPTH, 64]), ("fox_b_f", [DEPTH, 8]), ("conv_w", [DEPTH, 31, 256]), ("conv_b", [DEPTH, 256]),
                        ("conv_ln_g", [DEPTH, 256]), ("conv_ln_b", [DEPTH, 256]),
                        ("rwkv_mu", [DEPTH, 1024]), ("rwkv_w0", [DEPTH, 256]), ("rwkv_w_up", [DEPTH, 64, 256]), ("rwkv_a0", [DEPTH, 256]),
                        ("rwkv_a_up", [DEPTH, 64, 256]), ("rwkv_g_up", [DEPTH, 128, 256]), ("rwkv_k_k", [DEPTH, 256]), ("rwkv_k_a", [DEPTH, 256]),
                        ("rwkv_r_k", [DEPTH, 256]), ("rwkv_ln_g", [DEPTH, 256]), ("rwkv_ln_b", [DEPTH, 256]),
                        ("g_mem", [DEPTH, D]), ("w_xk", [DEPTH, D, D]), ("w_xv", [DEPTH, D, D]), ("xk_norm", [DEPTH, 256]),
                        ("w_out", [DEPTH, D, D]), ("g_x", [DEPTH, D]), ("w_xq", [DEPTH, D, D]), ("xq_norm", [DEPTH, 256]), ("w_xo", [DEPTH, D, D]),
                        ("g_ffn", [DEPTH, D]), ("w_up", [DEPTH, D, 2 * D_FF]), ("ffn_conv_w", [DEPTH, 3, 2 * D_FF]), ("ffn_conv_b", [DEPTH, 2 * D_FF]),
                        ("w_down", [DEPTH, D_FF, D])):
        W[name] = din(name, shape)
    o_pfk = dout("p_fox_k", [DEPTH, T, 512])
    o_pfv = dout("p_fox_v", [DEPTH, T, 512])
    o_pfl = dout("p_fox_logf", [DEPTH, T, 8])
    o_sfk = dout("s_fox_k", [DEPTH, NS, 512])
    o_sfv = dout("s_fox_v", [DEPTH, NS, 512])
    o_sfl = dout("s_fox_logf", [DEPTH, NS, 8])
    scv = din("scv", [DEPTH, 4, 30, 256])
    o_pcv = dout("p_conv", [DEPTH, 30, 256])
    o_scv = dout("s_conv", [DEPTH, 4, 30, 256])
    srw = din("srw", [DEPTH, 4, 4, 64, 64])
    ssh = din("ssh", [DEPTH, 4, 1024])
    o_prw = dout("p_rwkv", [DEPTH, 4, 64, 64])
    o_psh = dout("p_rwkv_shift", [DEPTH, 1024])
    o_srw = dout("s_rwkv", [DEPTH, 4, 4, 64, 64])
    o_ssh = dout("s_rwkv_shift", [DEPTH, 4, 1024])
    o_pmk = dout("p_mem_k", [DEPTH, 256, D])
    o_pff = dout("p_ffn", [DEPTH, 2, 2 * D_FF])
    NPHYS = 2560
    fk = [din(f"fk{l_}", [NPHYS * 128, 512]) for l_ in range(DEPTH)]
    fv = [din(f"fv{l_}", [NPHYS * 128, 512]) for l_ in range(DEPTH)]
    flf = [din(f"flf{l_}", [NPHYS, 1024]) for l_ in range(DEPTH)]
    ptab = din("ptab", [4, 64], I32)
    cmk = din("cmk", [DEPTH, 4, 256, D])
    cmv = din("cmv", [DEPTH, 4, 256, D])
    sff = din("sff", [DEPTH, 8, 2 * D_FF])
    o_sff = dout("s_ffn", [DEPTH, 8, 2 * D_FF])
    o_ys = dout("y_sample", [NS, D])
    o_yp = dout("y_prompt", [T, D])
    o_pmv = dout("p_mem_v", [DEPTH, 256, D])

    DBG = {}
    if stage < 99:
        DBG["yb"] = dout("dbg_yb", [T, 512])
        DBG["yc"] = dout("dbg_yc", [2, 128, T], BF16)
        DBG["ya"] = dout("dbg_ya", [2, 128, T], BF16)
        DBG["x1"] = dout("dbg_x1", [KC, 128, T])
        DBG["yas"] = dout("dbg_yas", [2, 128, NS], BF16)
        DBG["ybs"] = dout("dbg_ybs", [NS, 512])
        for k_ in ("x1s", "x2s", "x3s"):
            DBG[k_] = dout("dbg_" + k_, [KC, 128, NS])
        DBG["x2"] = dout("dbg_x2", [KC, 128, T])
        DBG["x3"] = dout("dbg_x3", [KC, 128, T])
    S = Sched(nc, st)
    uid = [0]

    def sb(name, shape, dt=F32, stack=None):
        uid[0] += 1
        return (stack or st).enter_context(nc.sbuf_tensor(f"{name}_{uid[0]}", list(shape), dt))

    PS = [st.enter_context(nc.psum_tensor(f"ps{i}", [128, 512], F32)) for i in range(8)]
    psi = [0]

    def psum():
        i = psi[0]
        psi[0] = (i + 1) % 6
        return PS[i], ("ps", i)

    ident = sb("ident", [128, 128])
    onesf = sb("onesf", [128, 128])
    onesb = sb("onesb", [128, 128], BF16)
    xT = sb("xT", [128, KC, T])
    xTs = sb("xTs", [128, KC, NS])

    S.op("pool", lambda h: h.memset(onesf[:], 1.0), writes=["onesf"])
    S.op("pool", lambda h: h.memset(onesb[:], 1.0), writes=["onesb"])
    S.op("pool", lambda h: h.affine_select(out=ident[:], in_=onesf[:], pattern=[[-1, 128]], compare_op=ALU.is_equal,
                                           fill=0.0, base=0, channel_multiplier=1), reads=["onesf"], writes=["ident"])

    triU = sb("triU", [128, 128])
    maskb = sb("maskb", [128, 128], BF16)
    S.op("pool", lambda h: h.affine_select(out=triU[:], in_=onesf[:], pattern=[[1, 128]], compare_op=ALU.is_ge,
                                           fill=0.0, base=0, channel_multiplier=-1), reads=["onesf"], writes=["triU"])
    S.op("pool", lambda h: h.tensor_copy(out=maskb[:], in_=triU[:]), reads=["triU"], writes=["maskb"])

    with contextlib.ExitStack() as ph:
        xtm = [sb(f"xtm{i}", [128, D], stack=ph) for i in range(2)]

        def load_T(src_rows, n, dstT, col0, bi):
            t_ = xtm[bi]
            S.dma("sp", t_[0:n, :], src_rows, writes=[("xtm", bi)])
            for half in range(2):
                ps, pk = psum()
                for q in range(4):
                    c = half * 4 + q
                    S.op("pe", lambda h, ps=ps, q=q, c=c, t_=t_: h.transpose(out=ps[:, q * 128:q * 128 + n], in_=t_[0:n, c * 128:(c + 1) * 128],
                                                                          identity=ident[0:n, 0:n]),
                         reads=[("xtm", bi), "ident"], writes=[pk])
                eng = "act" if half == 0 else "dve"
                src = ps[:, :].rearrange("p (q t) -> p q t", q=4)[:, :, 0:n]
                dst = dstT[:, half * 4:half * 4 + 4, col0:col0 + n]
                if eng == "act":
                    S.op("act", lambda h, src=src, dst=dst: h.activation(out=dst, in_=src, func=AF.Copy), reads=[pk], writes=[("xT", id(dstT), col0)])
                else:
                    S.op("dve", lambda h, src=src, dst=dst: h.tensor_copy(out=dst, in_=src), reads=[pk], writes=[("xT", id(dstT), col0, 1)])

        for tt in range(NT):
            load_T(xp[tt * 128:(tt + 1) * 128, :], 128, xT, tt * 128, tt % 2)
        load_T(xs[:, :], NS, xTs, 0, 0)
        S.flush()

    def load_cols(tile_, vec, nch, stack_key):
        S.dma("sp", tile_[:, 0:nch], vec.rearrange("(c p) -> p c", p=128), writes=[stack_key], allow_slow_non_contiguous=True)

    def load_w(wt, wd, col0, ncols, key):
        nch = wd.shape[0] // 128
        for c in range(nch):
            S.dma("pool", wt[:, c, 0:ncols], wd[c * 128:(c + 1) * 128, col0:col0 + ncols], writes=[(key, c)])

    def rmsnorm_fm(xsrc, col0, n, gcol, xn, xn_key, scr, src_keys=()):
        sq, rbc = scr["sq"], scr["rbc"]
        S.op("act", lambda h: h.activation(out=sq[:, :, 0:n], in_=xsrc[:, :, col0:col0 + n], func=AF.Square), reads=list(src_keys), writes=["sq"])
        ps, pk = psum()
        for c in range(KC):
            S.op("pe", lambda h, c=c: h.matmul(ps[:, 0:n], lhsT=onesb[:, :], rhs=sq[:, c, 0:n], start=(c == 0), stop=(c == KC - 1)),
                 reads=["sq", "onesb"], writes=[pk])
        S.op("act", lambda h: h.activation(out=rbc[:, 0:n], in_=ps[:, 0:n], func=AF.Sqrt, scale=1.0 / D, bias=scr["eps"][:, 0:1]),
             reads=[pk, "eps"], writes=["rbc"])
        S.op("dve", lambda h: h.reciprocal(out=rbc[:, 0:n], in_=rbc[:, 0:n]), reads=["rbc"], writes=["rbc"])
        for c in range(KC):
            S.op("dve", lambda h, c=c: h.scalar_tensor_tensor(out=xn[:, c, 0:n], in0=xsrc[:, c, col0:col0 + n], scalar=gcol[:, c:c + 1],
                                                             in1=rbc[:, 0:n], op0=ALU.mult, op1=ALU.mult),
                 reads=["rbc", "gcol"] + list(src_keys), writes=[(xn_key, c)])

    for l in range(0 if stage < 1 else (DEPTH if stage >= 50 else 1)):
        lay = contextlib.ExitStack()
        yT = sb("yT", [128, KC, T], BF16, lay)
        yTs = sb("yTs", [128, KC, NS], BF16, lay)
        mx = contextlib.ExitStack()
        vS = sb("vS", [NS, 512], F32, mx)
        kS = sb("kS", [NS, 512], F32, mx)
        qS = sb("qS", [NS, 512], F32, mx)
        LFk = sb("LFk", [128, NT + 1, 8], F32, mx)
        QT = sb("QT", [128, 4, T], BF16, mx)
        KT = sb("KT", [128, 4, T], BF16, mx)
        Vp = sb("Vp", [128, NT, 8, 65], BF16, mx)
        S.op("pool", lambda h: h.memset(Vp[:, :, :, 64:65], 1.0), writes=["Vp1"])
        with contextlib.ExitStack() as ph:
            wfox = sb("wfox", [128, KC, 1544], BF16, ph)
            gcol = sb("gcol", [128, KC], F32, ph)
            epsc = sb("epsc", [128, 1], F32, ph)
            gq = sb("gq", [128, 64], F32, ph)
            gk = sb("gk", [128, 64], F32, ph)
            bfb = sb("bfb", [128, 8], F32, ph)
            sq = sb("sq", [128, KC, 512], BF16, ph)
            rbc = sb("rbc", [128, 512], F32, ph)
            xn = sb("xn", [128, KC, 512], BF16, ph)
            tq = sb("tq", [128, 512], F32, ph)
            tk = sb("tk", [128, 512], F32, ph)
            tv = sb("tv", [128, 512], F32, ph)
            tsq = sb("tsq", [128, 512], F32, ph)
            ss = sb("ss", [128, 16], F32, ph)
            LF = LFk
            scr = dict(sq=sq, rbc=rbc, eps=epsc)

            load_w(wfox, W["w_in"][l], FOX0, 1544, "wfox")
            load_cols(gcol, W["g_mix"][l], KC, "gcol")
            S.op("pool", lambda h: h.memset(epsc[:], RMS_EPS), writes=["eps"])
            S.dma("sp", gq[:, :], W["fox_q_norm"][l].partition_broadcast(128), writes=["gq"])
            S.dma("sp", gk[:, :], W["fox_k_norm"][l].partition_broadcast(128), writes=["gk"])
            S.dma("sp", bfb[:, :], W["fox_b_f"][l].partition_broadcast(128), writes=["bfb"])
            S.op("dve", lambda h: h.tensor_scalar(out=gq[:, :], in0=gq[:, :], scalar1=0.125, scalar2=None, op0=ALU.mult), reads=["gq"], writes=["gq"])
            wkeys = [("wfox", c) for c in range(KC)]

            def headnorm(ps, pk, n, gt, gkey, dst, dkey, si):
                S.op("act", lambda h: h.activation(out=tsq[0:n, :], in_=ps[0:n, :], func=AF.Square), reads=[pk], writes=["tsq"])
                S.op("dve", lambda h: h.tensor_reduce(out=ss[0:n, si * 8:si * 8 + 8], in_=tsq[0:n, :].rearrange("p (h d) -> p h d", h=8), axis=AX.X, op=ALU.add),
                     reads=["tsq"], writes=[("ss", si)])
                S.op("act", lambda h: h.activation(out=ss[0:n, si * 8:si * 8 + 8], in_=ss[0:n, si * 8:si * 8 + 8], func=AF.Sqrt, scale=1.0 / 64, bias=epsc[0:n, 0:1]),
                     reads=[("ss", si), "eps"], writes=[("ss", si)])
                S.op("dve", lambda h: h.reciprocal(out=ss[0:n, si * 8:si * 8 + 8], in_=ss[0:n, si * 8:si * 8 + 8]), reads=[("ss", si)], writes=[("ss", si)])
                S.op("dve", lambda h: h.tensor_tensor(out=dst[0:n, :].rearrange("p (h d) -> p h d", h=8), in0=ps[0:n, :].rearrange("p (h d) -> p h d", h=8),
                                                      in1=ss[0:n, si * 8:si * 8 + 8].unsqueeze(2).broadcast_to([n, 8, 64]), op=ALU.mult),
                     reads=[pk, ("ss", si)], writes=[dkey])
                S.op("dve", lambda h: h.tensor_tensor(out=dst[0:n, :].rearrange("p (h d) -> p h d", h=8), in0=dst[0:n, :].rearrange("p (h d) -> p h d", h=8),
                                                      in1=gt[0:n, :].unsqueeze(1).broadcast_to([n, 8, 64]), op=ALU.mult),
                     reads=[dkey, gkey], writes=[dkey])

            def projA(xsrc, blk0, nblk, o_k, o_v, o_l, row0, lf_tile0):
                rmsnorm_fm(xsrc, blk0, nblk, gcol, xn, "xn", scr)
                xnk = [("xn", c) for c in range(KC)]
                for t0 in range(0, nblk, 128):
                    n = min(128, nblk - t0)
                    ti = lf_tile0 + t0 // 128
                    r0 = row0 + t0
                    pss = []
                    for gi, (c0, nc_) in enumerate(((0, 512), (512, 512), (1024, 512), (1536, 8))):
                        ps, pk = psum()
                        for c in range(KC):
                            S.op("pe", lambda h, ps=ps, c=c, c0=c0, nc_=nc_, t0=t0, n=n: h.matmul(ps[0:n, 0:nc_], lhsT=xn[:, c, t0:t0 + n], rhs=wfox[:, c, c0:c0 + nc_],
                                                                                                 start=(c == 0), stop=(c == KC - 1)),
                                 reads=xnk + wkeys, writes=[pk])
                        pss.append((ps, pk))
                    if stage >= 4:
                        headnorm(pss[0][0], pss[0][1], n, gq, "gq", tq, "tq", 0)
                        headnorm(pss[1][0], pss[1][1], n, gk, "gk", tk, "tk", 1)
                        S.dma("sp", o_k[r0:r0 + n, :], tk[0:n, :], reads=["tk"])
                    S.op("act", lambda h, ps=pss[2][0], n=n: h.activation(out=tv[0:n, :], in_=ps[0:n, :], func=AF.Copy), reads=[pss[2][1]], writes=["tv"])
                    S.dma("sp", o_v[r0:r0 + n, :], tv[0:n, :], reads=["tv"])
                    if n == 128:
                        for (src_t, skey, dstT, dkey) in ((tq, "tq", QT, "QT"), (tk, "tk", KT, "KT")):
                            ps, pk = psum()
                            for q in range(4):
                                S.op("pe", lambda h, ps=ps, q=q, src_t=src_t: h.transpose(out=ps[:, q * 128:(q + 1) * 128], in_=src_t[:, q * 128:(q + 1) * 128], identity=ident[:, :]),
                                     reads=[skey, "ident"], writes=[pk])
                            S.op("act", lambda h, ps=ps, dstT=dstT, r0=r0: h.activation(out=dstT[:, :, r0:r0 + 128], in_=ps[:, :].rearrange("p (q t) -> p q t", q=4), func=AF.Copy),
                                 reads=[pk], writes=[(dkey, ti)])
                        S.op("pool", lambda h, ti=ti: h.tensor_copy(out=Vp[:, ti, :, 0:64], in_=tv[:, :].rearrange("p (h d) -> p h d", h=8)), reads=["tv"], writes=[("Vp", ti)])
                    else:
                        S.op("pool", lambda h: h.tensor_copy(out=qS[:, :], in_=tq[0:NS, :]), reads=["tq"], writes=["qS"])
                        S.op("pool", lambda h: h.tensor_copy(out=kS[:, :], in_=tk[0:NS, :]), reads=["tk"], writes=["kS"])
                        S.op("pool", lambda h: h.tensor_copy(out=vS[:, :], in_=tv[0:NS, :]), reads=["tv"], writes=["vS"])
                    if stage < 5:
                        continue
                    lfv = LF[0:n, ti, :]
                    S.op("dve", lambda h, ps=pss[3][0], n=n, lfv=lfv: h.tensor_tensor(out=lfv, in0=ps[0:n, 0:8], in1=bfb[0:n, :], op=ALU.add),
                         reads=[pss[3][1], "bfb"], writes=[("LF", ti)])
                    S.op("act", lambda h, lfv=lfv: h.activation(out=lfv, in_=lfv, func=AF.Exp, scale=-1.0), reads=[("LF", ti)], writes=[("LF", ti)])
                    S.op("dve", lambda h, lfv=lfv: h.tensor_scalar(out=lfv, in0=lfv, scalar1=1.0, scalar2=None, op0=ALU.add), reads=[("LF", ti)], writes=[("LF", ti)])
                    S.op("act", lambda h, lfv=lfv: h.activation(out=lfv, in_=lfv, func=AF.Ln), reads=[("LF", ti)], writes=[("LF", ti)])
                    S.op("dve", lambda h, lfv=lfv: h.tensor_scalar(out=lfv, in0=lfv, scalar1=-1.0, scalar2=None, op0=ALU.mult), reads=[("LF", ti)], writes=[("LF", ti)])
                    S.dma("sp", o_l[r0:r0 + n, :], lfv, reads=[("LF", ti)])

            if stage == 2:
                rmsnorm_fm(xT, 0, 512, gcol, xn, "xn", scr)
            if stage >= 3:
                for b in range(4):
                    projA(xT, b * 512, 512, o_pfk[l], o_pfv[l], o_pfl[l], b * 512, b * 4)
                projA(xTs, 0, NS, o_sfk[l], o_sfv[l], o_sfl[l], 0, NT)
            S.flush()

        if stage >= 6:
            with contextlib.ExitStack() as ph:
                Cw = sb("Cw", [128, NT, 8], F32, ph)
                offs = sb("offs", [128, NT, 8], F32, ph)
                tot = sb("tot", [128, NT, 8], F32, ph)
                negC = sb("negC", [128, NT, 8], F32, ph)
                BI = sb("BI", [128, NT, NT, 8], F32, ph)
                PT = [sb(f"PT{i}", [128, 128], BF16, ph) for i in range(4)]
                ybt = [sb(f"ybt{i}", [128, 512], F32, ph) for i in range(2)]
                rden = sb("rden", [128, 8], F32, ph)
                lfflat = LFk[:, 0:NT, :].rearrange("p t h -> p (t h)")
                ps, pk = psum()
                S.op("pe", lambda h, ps=ps: h.matmul(ps[:, 0:128], lhsT=triU[:, :], rhs=lfflat, start=True, stop=True), reads=["triU"], writes=[pk])
                S.op("act", lambda h, ps=ps: h.activation(out=Cw[:, :, :].rearrange("p t h -> p (t h)"), in_=ps[:, 0:128], func=AF.Copy), reads=[pk], writes=["Cw"])
                ps, pk = psum()
                S.op("pe", lambda h, ps=ps: h.matmul(ps[:, 0:128], lhsT=onesf[:, :], rhs=lfflat, start=True, stop=True), reads=["onesf"], writes=[pk])
                S.op("act", lambda h, ps=ps: h.activation(out=tot[:, :, :].rearrange("p t h -> p (t h)"), in_=ps[:, 0:128], func=AF.Copy), reads=[pk], writes=["tot"])
                S.op("dve", lambda h: h.memset(offs[:, 0, :], 0.0), writes=["offs"])
                for i in range(1, NT):
                    S.op("dve", lambda h, i=i: h.tensor_tensor(out=offs[:, i, :], in0=offs[:, i - 1, :], in1=tot[:, i - 1, :], op=ALU.add), reads=["offs", "tot"], writes=["offs"])
                S.op("dve", lambda h: h.tensor_tensor(out=negC[:, :, :], in0=Cw[:, :, :], in1=offs[:, :, :], op=ALU.add), reads=["Cw", "offs"], writes=["negC"])
                S.op("dve", lambda h: h.tensor_scalar(out=negC[:, :, :], in0=negC[:, :, :], scalar1=-1.0, scalar2=None, op0=ALU.mult), reads=["negC"], writes=["negC"])
                for j in range(NT):
                    S.op("dve", lambda h, j=j: h.tensor_tensor(out=BI[:, j, 0:j + 1, :], in0=negC[:, 0:j + 1, :],
                                                               in1=offs[:, j:j + 1, :].broadcast_to([128, j + 1, 8]), op=ALU.add),
                         reads=["negC", "offs"], writes=[("BI", j)])
                pti = 0
                for j in range(NT):
                    yb_t = ybt[j % 2]
                    ykey = ("ybt", j % 2)
                    for h_ in range(8):
                        pair, pb = h_ // 2, (h_ % 2) * 64
                        acc = PS[6 + (h_ // 4)]
                        akey = ("ps", 6 + (h_ // 4))
                        ac0 = (h_ % 4) * 65
                        LA = 3
                        sbank = {}

                        def emit_s(i, pair=pair, pb=pb, j=j):
                            ps, pk = psum()
                            S.op("pe", lambda h, ps=ps, pair=pair, pb=pb, i=i, j=j: h.matmul(ps[:, 0:128], lhsT=KT[pb:pb + 64, pair, i * 128:(i + 1) * 128],
                                                                                        rhs=QT[pb:pb + 64, pair, j * 128:(j + 1) * 128], start=True, stop=True),
                                 reads=[("KT", i), ("QT", j)], writes=[pk])
                            sbank[i] = (ps, pk)

                        for i in range(min(LA, j + 1)):
                            emit_s(i)
                        for i in range(j + 1):
                            ps, pk = sbank.pop(i)
                            pt = PT[pti % 4]
                            ptk = ("PT", pti % 4)
                            pti += 1
                            S.op("act", lambda h, ps=ps, pt=pt, i=i, j=j, h_=h_: h.activation(out=pt[:, :], in_=ps[:, 0:128], func=AF.Exp, bias=BI[:, j, i, h_:h_ + 1]),
                                 reads=[pk, ("BI", j)], writes=[ptk])
                            if i == j:
                                S.op("pool", lambda h, pt=pt: h.tensor_tensor(out=pt[:, :], in0=pt[:, :], in1=maskb[:, :], op=ALU.mult), reads=[ptk, "maskb"], writes=[ptk])
                            if i + LA <= j:
                                emit_s(i + LA)
                            S.op("pe", lambda h, pt=pt, acc=acc, ac0=ac0, i=i, j=j, h_=h_: h.matmul(acc[:, ac0:ac0 + 65], lhsT=pt[:, :], rhs=Vp[:, i, h_, :], start=(i == 0), stop=(i == j)),
                                 reads=[ptk, ("Vp", i), "Vp1"], writes=[akey])
                        S.op("dve", lambda h, acc=acc, ac0=ac0, h_=h_: h.reciprocal(out=rden[:, h_:h_ + 1], in_=acc[:, ac0 + 64:ac0 + 65]), reads=[akey], writes=[("rden", h_)])
                        S.op("dve", lambda h, acc=acc, ac0=ac0, h_=h_, yb_t=yb_t: h.tensor_scalar(out=yb_t[:, h_ * 64:(h_ + 1) * 64], in0=acc[:, ac0:ac0 + 64], scalar1=rden[:, h_:h_ + 1],
                                                                                               scalar2=None, op0=ALU.mult),
                             reads=[akey, ("rden", h_)], writes=[ykey])
                    if "yb" in DBG:
                        S.dma("sp", DBG["yb"][j * 128:(j + 1) * 128, :], yb_t[:, :], reads=[ykey])
                    ps, pk = psum()
                    for q in range(4):
                        S.op("pe", lambda h, ps=ps, q=q, yb_t=yb_t: h.transpose(out=ps[:, q * 128:(q + 1) * 128], in_=yb_t[:, q * 128:(q + 1) * 128], identity=ident[:, :]),
                             reads=[ykey, "ident"], writes=[pk])
                    S.op("act", lambda h, ps=ps, j=j: h.activation(out=yT[:, 2:6, j * 128:(j + 1) * 128], in_=ps[:, :].rearrange("p (q t) -> p q t", q=4), func=AF.Copy),
                         reads=[pk], writes=[("yT", "b", j)])
                S.flush()
        if stage >= 14:
            with contextlib.ExitStack() as ph:
                ptb_i = sb("ptb_i", [128, 64], I32, ph)
                ptf = sb("ptf", [128, 64], F32, ph)
                idx = sb("idx", [128, 64], I32, ph)
                pio_i = sb("pio_i", [128, 1], I32, ph)
                pio = sb("pio", [128, 1], F32, ph)
                idxp = sb("idxp", [64, 1], I32, ph)
                Kt = [sb(f"Kt{i}", [128, 512], F32, ph) for i in range(4)]
                Vt = [sb(f"Vt{i}", [128, 512], F32, ph) for i in range(4)]
                Vb = [sb(f"Vb{i}", [128, 512], BF16, ph) for i in range(2)]
                KTp = [sb(f"KTp{i}", [128, 4, 128], BF16, ph) for i in range(2)]
                sbt = [sb(f"sbt{i}", [128, 8, 4], F32, ph) for i in range(2)]
                PTt = [sb(f"PTt{i}", [128, 32], BF16, ph) for i in range(2)]
                lft = sb("lft", [64, 1024], F32, ph)
                Pfx = sb("Pfx", [64, 1024], F32, ph)
                tot = sb("tot", [64, 8], F32, ph)
                later = sb("later", [64, 8], F32, ph)
                MLt = sb("MLt", [64, 64], F32, ph)
                EXT = sb("EXT", [128, 8, 64], F32, ph)
                qpad = sb("qpad", [128, 4, 2, 16], BF16, ph)
                KTn = sb("KTn", [128, 4, 16], BF16, ph)
                Vn = sb("Vn", [16, 512], BF16, ph)
                E4 = sb("E4", [4, 16], F32, ph)
                E4t = sb("E4t", [4, 16], F32, ph)
                BT = sb("BT", [16, 16], F32, ph)
                negcT = sb("negcT", [16, 8], F32, ph)
                maskn = sb("maskn", [16, 4, 4], F32, ph)
                mtmp = sb("mtmp", [16, 4, 4], F32, ph)
                sbn = sb("sbn", [16, 8, 4], F32, ph)
                PTn = sb("PTn", [16, 32], BF16, ph)
                onorm = sb("onorm", [32, 512], F32, ph)
                rdn = sb("rdn", [32, 1], F32, ph)
                ybS = sb("ybS", [NS, 512], F32, ph)
                S.op("pool", lambda h: h.iota(pio_i[:, 0:1], [[0, 1]], base=0, channel_multiplier=1), writes=["pio_i"])
                S.op("dve", lambda h: h.tensor_copy(out=pio[:, :], in_=pio_i[:, :]), reads=["pio_i"], writes=["pio"])
                S.op("pool", lambda h: h.affine_select(out=MLt[:, :], in_=onesf[0:64, 0:64], pattern=[[-1, 64]], compare_op=ALU.is_ge, fill=0.0, base=-1, channel_multiplier=1), reads=["onesf"], writes=["MLt"])
                S.op("pool", lambda h: h.affine_select(out=E4t[:, :], in_=onesf[0:4, 0:16], pattern=[[1, 16]], compare_op=ALU.is_ge, fill=0.0, base=0, channel_multiplier=-4), reads=["onesf"], writes=["E4t"])
                S.op("pool", lambda h: h.affine_select(out=E4[:, :], in_=E4t[:, :], pattern=[[-1, 16]], compare_op=ALU.is_ge, fill=0.0, base=3, channel_multiplier=4), reads=["E4t"], writes=["E4"])
                ps, pk = psum()
                S.op("pe", lambda h, ps=ps: h.matmul(ps[0:16, 0:16], lhsT=E4[:, :], rhs=E4[:, :], start=True, stop=True), reads=["E4"], writes=[pk])
                S.op("dve", lambda h, ps=ps: h.tensor_tensor(out=BT[:, :], in0=ps[0:16, 0:16], in1=triU[0:16, 0:16], op=ALU.mult), reads=[pk, "triU"], writes=["BT"])
                ps, pk = psum()
                S.op("pe", lambda h, ps=ps: h.matmul(ps[0:16, 0:8], lhsT=BT[:, :], rhs=LFk[0:NS, NT, :], start=True, stop=True), reads=["BT"], writes=[pk])
                S.op("dve", lambda h, ps=ps: h.tensor_scalar(out=negcT[:, :], in0=ps[0:16, 0:8], scalar1=-1.0, scalar2=None, op0=ALU.mult), reads=[pk], writes=["negcT"])
                S.op("pool", lambda h: h.memset(mtmp[:].rearrange("p a b -> p (a b)"), 1.0), writes=["mt0"])
                S.op("pool", lambda h: h.affine_select(out=maskn[:, :, :], in_=mtmp[:, :, :], pattern=[[4, 4], [1, 4]], compare_op=ALU.is_ge, fill=0.0, base=0, channel_multiplier=-1), reads=["mt0"], writes=["mk1"])
                S.op("pool", lambda h: h.affine_select(out=mtmp[:, :, :], in_=maskn[:, :, :], pattern=[[-4, 4], [0, 4]], compare_op=ALU.is_ge, fill=0.0, base=0, channel_multiplier=1), reads=["mk1"], writes=["maskn"])
                MSK = mtmp
                S.op("pool", lambda h: h.memset(qpad[:].rearrange("p a b c -> p (a b c)"), 0.0), writes=["qp0"])
                ps, pk = psum()
                for pr in range(4):
                    S.op("pe", lambda h, ps=ps, pr=pr: h.transpose(out=ps[:, pr * 16:(pr + 1) * 16], in_=qS[0:NS, pr * 128:(pr + 1) * 128], identity=ident[0:NS, 0:NS]), reads=["ident"], writes=[pk])
                for hh in range(2):
                    S.op("act", lambda h, ps=ps, hh=hh: h.activation(out=qpad[hh * 64:(hh + 1) * 64, :, hh, :], in_=ps[hh * 64:(hh + 1) * 64, 0:64].rearrange("p (a t) -> p a t", a=4), func=AF.Copy), reads=[pk, "qp0"], writes=[("qpad", hh)])
                qpk = [("qpad", 0), ("qpad", 1)]
                ps, pk = psum()
                for pr in range(4):
                    S.op("pe", lambda h, ps=ps, pr=pr: h.transpose(out=ps[:, pr * 16:(pr + 1) * 16], in_=kS[0:NS, pr * 128:(pr + 1) * 128], identity=ident[0:NS, 0:NS]), reads=["ident"], writes=[pk])
                S.op("act", lambda h, ps=ps: h.activation(out=KTn[:, :, :], in_=ps[:, 0:64].rearrange("p (a t) -> p a t", a=4), func=AF.Copy), reads=[pk], writes=["KTn"])
                S.op("act", lambda h: h.activation(out=Vn[:, :], in_=vS[0:NS, :], func=AF.Copy), writes=["Vn"])
                O_, okey = PS[6], ("ps", 6)
                D_, dkey = PS[7], ("ps", 7)
                ki = 0
                for b in range(4):
                    S.dma("sp", ptb_i[:, :], ptab[b].partition_broadcast(128), writes=["ptb_i"])
                    S.dma("sp", idxp[:, :], ptab[b].rearrange("(j o) -> j o", o=1), writes=["idxp"])
                    S.op("dve", lambda h: h.tensor_copy(out=ptf[:, :], in_=ptb_i[:, :]), reads=["ptb_i"], writes=["ptf"])
                    S.op("dve", lambda h: h.tensor_scalar(out=ptf[:, :], in0=ptf[:, :], scalar1=128.0, scalar2=None, op0=ALU.mult), reads=["ptf"], writes=["ptf"])
                    S.op("dve", lambda h: h.tensor_scalar(out=ptf[:, :], in0=ptf[:, :], scalar1=pio[:, 0:1], scalar2=None, op0=ALU.add), reads=["ptf", "pio"], writes=["ptf"])
                    S.op("dve", lambda h: h.tensor_copy(out=idx[:, :], in_=ptf[:, :]), reads=["ptf"], writes=["idx"])
                    S.idma(lft[:, :], flf[l], idxp[:, 0:1], reads=["idxp"], writes=["lft"])
                    l3 = lft[:, :].rearrange("p (t h) -> p t h", h=8)
                    p3 = Pfx[:, :].rearrange("p (t h) -> p t h", h=8)
                    for h_ in range(8):
                        S.op("dve", lambda h, h_=h_: h.tensor_tensor_scan(out=p3[:, :, h_], data0=onesf[0:64, 0:128], data1=l3[:, :, h_], initial=0.0, op0=ALU.mult, op1=ALU.add), reads=["lft", "onesf"], writes=[("Pfx", h_)])
                    pfk_ = [("Pfx", h_) for h_ in range(8)]
                    S.op("dve", lambda h: h.tensor_copy(out=tot[:, :], in_=p3[:, 127, :]), reads=pfk_, writes=["tot"])
                    S.op("dve", lambda h: h.tensor_tensor(out=p3, in0=tot[:, :].unsqueeze(1).broadcast_to([64, 128, 8]), in1=p3, op=ALU.subtract), reads=pfk_ + ["tot"], writes=pfk_)
                    ps, pk = psum()
                    S.op("pe", lambda h, ps=ps: h.matmul(ps[0:64, 0:8], lhsT=MLt[:, :], rhs=tot[:, :], start=True, stop=True), reads=["MLt", "tot"], writes=[pk])
                    S.op("act", lambda h, ps=ps: h.activation(out=later[:, :], in_=ps[0:64, 0:8], func=AF.Copy), reads=[pk], writes=["later"])
                    S.op("dve", lambda h: h.tensor_tensor(out=p3, in0=p3, in1=later[:, :].unsqueeze(1).broadcast_to([64, 128, 8]), op=ALU.add), reads=pfk_ + ["later"], writes=pfk_)
                    for h4 in range(2):
                        ps, pk = psum()
                        for q in range(4):
                            S.op("pe", lambda h, ps=ps, q=q, h4=h4: h.transpose(out=ps[:, q * 64:(q + 1) * 64], in_=p3[:, :, h4 * 4 + q], identity=ident[0:64, 0:64]), reads=pfk_ + ["ident"], writes=[pk])
                        S.op("act", lambda h, ps=ps, h4=h4: h.activation(out=EXT[:, h4 * 4:h4 * 4 + 4, :], in_=ps[:, 0:256].rearrange("p (a j) -> p a j", a=4), func=AF.Copy), reads=[pk], writes=[("EXT", h4)])
                    extk = [("EXT", 0), ("EXT", 1)]
                    for j in range(64):
                        kt_, vt_ = Kt[ki % 4], Vt[ki % 4]
                        kk_, vk_ = ("Kt", ki % 4), ("Vt", ki % 4)
                        vb_, vbk = Vb[ki % 2], ("Vb", ki % 2)
                        ktp, ktk = KTp[ki % 2], ("KTp", ki % 2)
                        sb_, sbk = sbt[ki % 2], ("sbt", ki % 2)
                        pt_, ptk = PTt[ki % 2], ("PTt", ki % 2)
                        ki += 1
                        S.idma(kt_[:, :], fk[l], idx[:, j:j + 1], reads=["idx"], writes=[kk_])
                        S.idma(vt_[:, :], fv[l], idx[:, j:j + 1], reads=["idx"], writes=[vk_])
                        ps, pk = psum()
                        for q in range(4):
                            S.op("pe", lambda h, ps=ps, q=q, kt_=kt_: h.transpose(out=ps[:, q * 128:(q + 1) * 128], in_=kt_[:, q * 128:(q + 1) * 128], identity=ident[:, :]), reads=[kk_, "ident"], writes=[pk])
                        S.op("act", lambda h, ps=ps, ktp=ktp: h.activation(out=ktp[:, :, :], in_=ps[:, :].rearrange("p (a t) -> p a t", a=4), func=AF.Copy), reads=[pk], writes=[ktk])
                        S.op("dve", lambda h, vb_=vb_, vt_=vt_: h.tensor_copy(out=vb_[:, :], in_=vt_[:, :]), reads=[vk_], writes=[vbk])
                        ps, pk = psum()
                        for h_ in range(8):
                            S.op("pe", lambda h, ps=ps, h_=h_, ktp=ktp, b=b: h.matmul(ps[:, h_ * 4:(h_ + 1) * 4], lhsT=ktp[:, h_ // 2, :], rhs=qpad[:, h_ // 2, h_ % 2, 4 * b:4 * b + 4], start=True, stop=True), reads=[ktk] + qpk, writes=[pk])
                        S.op("dve", lambda h, ps=ps, sb_=sb_, j=j: h.tensor_tensor(out=sb_[:, :, :], in0=ps[:, 0:32].rearrange("p (a q) -> p a q", a=8), in1=EXT[:, :, j].unsqueeze(2).broadcast_to([128, 8, 4]), op=ALU.add), reads=[pk] + extk, writes=[sbk])
                        S.op("act", lambda h, sb_=sb_, pt_=pt_: h.activation(out=pt_[:, :], in_=sb_[:, :, :].rearrange("p a q -> p (a q)"), func=AF.Exp), reads=[sbk], writes=[ptk])
                        S.op("pe", lambda h, pt_=pt_, vb_=vb_, j=j: h.matmul(O_[0:32, :], lhsT=pt_[:, :], rhs=vb_[:, :], start=(j == 0), stop=False), reads=[ptk, vbk], writes=[okey])
                        S.op("pe", lambda h, pt_=pt_, j=j: h.matmul(D_[0:32, 0:1], lhsT=pt_[:, :], rhs=onesb[:, 0:1], start=(j == 0), stop=False), reads=[ptk, "onesb"], writes=[dkey])
                    ps, pk = psum()
                    for h_ in range(8):
                        S.op("pe", lambda h, ps=ps, h_=h_, b=b: h.matmul(ps[0:NS, h_ * 4:(h_ + 1) * 4], lhsT=KTn[:, h_ // 2, :], rhs=qpad[:, h_ // 2, h_ % 2, 4 * b:4 * b + 4], start=True, stop=True), reads=["KTn"] + qpk, writes=[pk])
                    S.op("dve", lambda h, ps=ps: h.tensor_tensor(out=sbn[:, :, :], in0=ps[0:NS, 0:32].rearrange("p (a q) -> p a q", a=8), in1=negcT[:, :].unsqueeze(2).broadcast_to([NS, 8, 4]), op=ALU.add), reads=[pk, "negcT"], writes=["sbn"])
                    S.op("act", lambda h: h.activation(out=sbn[:, :, :].rearrange("p a q -> p (a q)"), in_=sbn[:, :, :].rearrange("p a q -> p (a q)"), func=AF.Exp), reads=["sbn"], writes=["sbn"])
                    S.op("dve", lambda h, b=b: h.tensor_tensor(out=PTn[:, :].rearrange("p (a q) -> p a q", a=8), in0=sbn[:, :, :], in1=MSK[:, b, :].unsqueeze(1).broadcast_to([NS, 8, 4]), op=ALU.mult), reads=["sbn", "maskn"], writes=["PTn"])
                    S.op("pe", lambda h: h.matmul(O_[0:32, :], lhsT=PTn[:, :], rhs=Vn[:, :], start=False, stop=True), reads=["PTn", "Vn"], writes=[okey])
                    S.op("pe", lambda h: h.matmul(D_[0:32, 0:1], lhsT=PTn[:, :], rhs=onesb[0:NS, 0:1], start=False, stop=True), reads=["PTn", "onesb"], writes=[dkey])
                    S.op("dve", lambda h: h.reciprocal(out=rdn[:, :], in_=D_[0:32, 0:1]), reads=[dkey], writes=["rdn"])
                    S.op("dve", lambda h: h.tensor_scalar(out=onorm[:, :], in0=O_[0:32, :], scalar1=rdn[:, 0:1], scalar2=None, op0=ALU.mult), reads=[okey, "rdn"], writes=["onorm"])
                    for h_ in range(8):
                        S.dma("sp", ybS[4 * b:4 * b + 4, h_ * 64:(h_ + 1) * 64], onorm[h_ * 4:(h_ + 1) * 4, h_ * 64:(h_ + 1) * 64], reads=["onorm"], writes=[("ybS", b, h_)])
                ybk = [("ybS", b, h_) for b in range(4) for h_ in range(8)]
                ps, pk = psum()
                for q in range(4):
                    S.op("pe", lambda h, ps=ps, q=q: h.transpose(out=ps[:, q * 16:(q + 1) * 16], in_=ybS[0:NS, q * 128:(q + 1) * 128], identity=ident[0:NS, 0:NS]), reads=ybk + ["ident"], writes=[pk])
                S.op("act", lambda h, ps=ps: h.activation(out=yTs[:, 2:6, :], in_=ps[:, 0:64].rearrange("p (a t) -> p a t", a=4), func=AF.Copy), reads=[pk], writes=[("yTs", "b")])
                if "ybs" in DBG and l == 0:
                    S.dma("sp", DBG["ybs"][:, :], ybS[:, :], reads=ybk)
                S.flush()

        mx.close()

        if stage >= 7:
            with contextlib.ExitStack() as ph:
                wcv = sb("wcv", [128, KC, 512], BF16, ph)
                gcol = sb("gcol", [128, KC], F32, ph)
                epsc = sb("epsc", [128, 1], F32, ph)
                epsl = sb("epsl", [128, 1], F32, ph)
                cwT = sb("cwT", [128, 2, 31], F32, ph)
                cbc = sb("cbc", [128, 2], F32, ph)
                lng = sb("lng", [128, 2], F32, ph)
                lnb = sb("lnb", [128, 2], F32, ph)
                sq = sb("sq", [128, KC, 512], BF16, ph)
                rbc = sb("rbc", [128, 512], F32, ph)
                xn = sb("xn", [128, KC, 512], BF16, ph)
                sig = sb("sig", [128, 512], F32, ph)
                U = sb("U", [128, 2, 30 + T], F32, ph)
                Y = sb("Y", [128, 2, T], F32, ph)
                Us = sb("Us", [128, 2, 4, 34], F32, ph)
                Ys = sb("Ys", [128, 2, 4, 4], F32, ph)
                mbc = sb("mbc", [128, 512], F32, ph)
                sq2 = sb("sq2", [128, 2, 512], F32, ph)
                cvt = sb("cvt", [32, 4, 256], F32, ph)
                pco = sb("pco", [32, 4, 256], F32, ph)
                cvo = sb("cvo", [32, 4, 256], F32, ph)
                ycd = sb("ycd", [128, 256], F32, ph)
                scr = dict(sq=sq, rbc=rbc, eps=epsc)
                load_w(wcv, W["w_in"][l], CONV0, 512, "wcv")
                wkeys = [("wcv", c) for c in range(KC)]
                load_cols(gcol, W["g_mix"][l], KC, "gcol")
                S.op("pool", lambda h: h.memset(epsc[:], RMS_EPS), writes=["eps"])
                S.op("pool", lambda h: h.memset(epsl[:], 1e-5), writes=["epsl"])
                for c_ in range(2):
                    S.dma("sp", cwT[:, c_, :], W["conv_w"][l][:, c_ * 128:(c_ + 1) * 128].rearrange("j p -> p j"), writes=[("cwT", c_)], allow_slow_non_contiguous=True)
                load_cols(cbc, W["conv_b"][l], 2, "cbc")
                load_cols(lng, W["conv_ln_g"][l], 2, "lng")
                load_cols(lnb, W["conv_ln_b"][l], 2, "lnb")
                S.op("pool", lambda h: h.memset(U[:, :, 0:30], 0.0), writes=["Upad"])
                S.dma("sp", cvt[0:30, :, :], scv[l].rearrange("b t c -> t b c"), writes=["cvt"])
                for b in range(4):
                    ps, pk = psum()
                    for ch in range(2):
                        S.op("pe", lambda h, ps=ps, ch=ch, b=b: h.transpose(out=ps[:, ch * 32:ch * 32 + 30], in_=cvt[0:30, b, ch * 128:(ch + 1) * 128], identity=ident[0:30, 0:30]),
                             reads=["cvt", "ident"], writes=[pk])
                    S.op("act", lambda h, ps=ps, b=b: h.activation(out=Us[:, :, b, 0:30], in_=ps[:, 0:64].rearrange("p (c t) -> p c t", c=2)[:, :, 0:30], func=AF.Copy),
                         reads=[pk], writes=[("Us", b)])

                def glu_block(xsrc, blk0, n, dst_fn, dkey, vw=lambda a: a):
                    rmsnorm_fm(xsrc, blk0, n, gcol, xn, "xn", scr)
                    xnk = [("xn", c) for c in range(KC)]
                    pss = []
                    for oc in range(4):
                        ps, pk = psum()
                        for c in range(KC):
                            S.op("pe", lambda h, ps=ps, c=c, oc=oc: h.matmul(ps[:, 0:n], lhsT=wcv[:, c, oc * 128:(oc + 1) * 128], rhs=xn[:, c, 0:n], start=(c == 0), stop=(c == KC - 1)),
                                 reads=xnk + wkeys, writes=[pk])
                        pss.append((ps, pk))
                    for ch in range(2):
                        S.op("act", lambda h, ps=pss[2 + ch][0]: h.activation(out=sig[:, 0:n], in_=ps[:, 0:n], func=AF.Sigmoid), reads=[pss[2 + ch][1]], writes=["sig"])
                        S.op("dve", lambda h, ps=pss[ch][0], ch=ch: h.tensor_tensor(out=dst_fn(ch), in0=vw(ps[:, 0:n]), in1=vw(sig[:, 0:n]), op=ALU.mult),
                             reads=[pss[ch][1], "sig"], writes=[dkey])

                for b in range(4):
                    glu_block(xT, b * 512, 512, lambda ch, b=b: U[:, ch, 30 + b * 512:30 + (b + 1) * 512], ("U", b))
                glu_block(xTs, 0, NS, lambda ch: Us[:, ch, :, 30:34], "Usn", vw=lambda a: a.rearrange("p (b t) -> p b t", b=4))

                ukeys = [("U", b) for b in range(4)] + ["Upad"]
                for ch in range(2):
                    for j in range(31):
                        if j == 0:
                            S.op("dve", lambda h, ch=ch: h.tensor_scalar(out=Y[:, ch, :], in0=U[:, ch, 0:T], scalar1=cwT[:, ch, 0:1], scalar2=cbc[:, ch:ch + 1], op0=ALU.mult, op1=ALU.add),
                                 reads=ukeys + [("cwT", 0), ("cwT", 1), "cbc"], writes=[("Y", ch)])
                            S.op("dve", lambda h, ch=ch: h.tensor_scalar(out=Ys[:, ch, :, :], in0=Us[:, ch, :, 0:4], scalar1=cwT[:, ch, 0:1], scalar2=cbc[:, ch:ch + 1], op0=ALU.mult, op1=ALU.add),
                                 reads=[("Us", b) for b in range(4)] + ["Usn", ("cwT", 0), ("cwT", 1), "cbc"], writes=[("Ys", ch)])
                        else:
                            S.op("dve", lambda h, ch=ch, j=j: h.scalar_tensor_tensor(out=Y[:, ch, :], in0=U[:, ch, j:j + T], scalar=cwT[:, ch, j:j + 1], in1=Y[:, ch, :], op0=ALU.mult, op1=ALU.add),
                                 reads=[("Y", ch)], writes=[("Y", ch)])
                            S.op("dve", lambda h, ch=ch, j=j: h.scalar_tensor_tensor(out=Ys[:, ch, :, :], in0=Us[:, ch, :, j:j + 4], scalar=cwT[:, ch, j:j + 1], in1=Ys[:, ch, :, :], op0=ALU.mult, op1=ALU.add),
                                 reads=[("Ys", ch)], writes=[("Ys", ch)])

                def ln_silu(yv, n, outv, ykeys, okey):
                    ps, pk = psum()
                    for ch in range(2):
                        S.op("pe", lambda h, ps=ps, ch=ch: h.matmul(ps[:, 0:n], lhsT=onesf[:, :], rhs=yv(ch), start=(ch == 0), stop=(ch == 1)), reads=ykeys + ["onesf"], writes=[pk])
                    S.op("act", lambda h, ps=ps: h.activation(out=mbc[:, 0:n], in_=ps[:, 0:n], func=AF.Copy, scale=1.0 / 256), reads=[pk], writes=["mbc"])
                    for ch in range(2):
                        S.op("dve", lambda h, ch=ch: h.tensor_tensor(out=yv(ch), in0=yv(ch), in1=mbc[:, 0:n], op=ALU.subtract), reads=["mbc"] + ykeys, writes=ykeys)
                        S.op("act", lambda h, ch=ch: h.activation(out=sq2[:, ch, 0:n], in_=yv(ch), func=AF.Square), reads=ykeys, writes=[("sq2", ch)])
                    ps, pk = psum()
                    for ch in range(2):
                        S.op("pe", lambda h, ps=ps, ch=ch: h.matmul(ps[:, 0:n], lhsT=onesf[:, :], rhs=sq2[:, ch, 0:n], start=(ch == 0), stop=(ch == 1)), reads=[("sq2", 0), ("sq2", 1), "onesf"], writes=[pk])
                    S.op("act", lambda h, ps=ps: h.activation(out=mbc[:, 0:n], in_=ps[:, 0:n], func=AF.Sqrt, scale=1.0 / 256, bias=epsl[:, 0:1]), reads=[pk, "epsl"], writes=["mbc"])
                    S.op("dve", lambda h: h.reciprocal(out=mbc[:, 0:n], in_=mbc[:, 0:n]), reads=["mbc"], writes=["mbc"])
                    for ch in range(2):
                        S.op("dve", lambda h, ch=ch: h.scalar_tensor_tensor(out=yv(ch), in0=yv(ch), scalar=lng[:, ch:ch + 1], in1=mbc[:, 0:n], op0=ALU.mult, op1=ALU.mult),
                             reads=["mbc", "lng"] + ykeys, writes=ykeys)
                        S.op("act", lambda h, ch=ch: h.activation(out=outv(ch), in_=yv(ch), func=AF.Silu, bias=lnb[:, ch:ch + 1]), reads=ykeys + ["lnb"], writes=[okey])

                for b in range(4):
                    ln_silu(lambda ch, b=b: Y[:, ch, b * 512:(b + 1) * 512], 512, lambda ch, b=b: yT[:, 6 + ch, b * 512:(b + 1) * 512], [("Y", 0), ("Y", 1)], ("yT", "c", b))
                ln_silu(lambda ch: Ys[:, ch, :, :].rearrange("p b t -> p (b t)"), NS, lambda ch: yTs[:, 6 + ch, :], [("Ys", 0), ("Ys", 1)], ("yTs", "c"))

                ps, pk = psum()
                for ch in range(2):
                    S.op("pe", lambda h, ps=ps, ch=ch: h.transpose(out=ps[0:30, ch * 128:(ch + 1) * 128], in_=U[:, ch, T:T + 30], identity=ident[:, :]), reads=ukeys + ["ident"], writes=[pk])
                S.op("act", lambda h, ps=ps: h.activation(out=pco[0:30, 0, :], in_=ps[0:30, 0:256], func=AF.Copy), reads=[pk], writes=["pco"])
                S.dma("sp", o_pcv[l], pco[0:30, 0, :], reads=["pco"])
                for b in range(4):
                    ps, pk = psum()
                    for ch in range(2):
                        S.op("pe", lambda h, ps=ps, ch=ch, b=b: h.transpose(out=ps[0:30, ch * 128:(ch + 1) * 128], in_=Us[:, ch, b, 4:34], identity=ident[:, :]),
                             reads=[("Us", b), "Usn", "ident"], writes=[pk])
                    S.op("act", lambda h, ps=ps, b=b: h.activation(out=cvo[0:30, b, :], in_=ps[0:30, 0:256], func=AF.Copy), reads=[pk], writes=[("cvo", b)])
                    S.dma("sp", o_scv[l, b], cvo[0:30, b, :], reads=[("cvo", b)])
                if "yc" in DBG:
                    for ch in range(2):
                        S.dma("sp", DBG["yc"][ch], yT[:, 6 + ch, :], reads=[("yT", "c", b) for b in range(4)])
                S.flush()

        if stage >= 8:
            with contextlib.ExitStack() as ph:
                wrw = sb("wrw", [128, KC, 1024], BF16, ph)
                gcol = sb("gcol", [128, KC], F32, ph)
                epsc = sb("epsc", [128, 1], F32, ph)
                mu = sb("mu", [128, KC], F32, ph)
                WA = sb("WA", [128, 256], BF16, ph)
                GU = sb("GU", [128, 256], BF16, ph)
                cw0 = sb("cw0", [128, 2], F32, ph)
                ca0 = sb("ca0", [128, 2], F32, ph)
                ckk = sb("ckk", [128, 2], F32, ph)
                cka = sb("cka", [128, 2], F32, ph)
                crk = sb("crk", [128, 2], F32, ph)
                clg = sb("clg", [128, 2], F32, ph)
                clb = sb("clb", [128, 2], F32, ph)
                cnh = sb("cnh", [128, 1], F32, ph)
                cge = sb("cge", [128, 1], F32, ph)
                onesbd = sb("onesbd", [128, 128], F32, ph)
                identb = sb("identb", [128, 128], BF16, ph)
                MG = sb("MG", [128, 128], F32, ph)
                ML = sb("ML", [64, 64], F32, ph)
                NB, NCH, NI = 256, 4, 16
                sq = sb("sq", [128, KC, NB], BF16, ph)
                rbc = sb("rbc", [128, NB], F32, ph)
                xn = sb("xn", [128, KC, NB], BF16, ph)
                scr = dict(sq=sq, rbc=rbc, eps=epsc)
                ZB = sb("ZB", [128, 8, NB + 2], F32, ph)
                zlast = sb("zlast", [128, 8, 1], F32, ph)
                DT = sb("DT", [128, 8, NB], F32, ph)
                LA = sb("LA", [128, NB], BF16, ph)
                SG = sb("SG", [128, NB], BF16, ph)
                FN = ("ew", "a", "kk", "t1", "km", "cs", "wi", "wv", "we", "be")
                F = {k: (ZB[:, i, 0:NB] if i < 8 else sb("f_" + k, [128, NB], F32, ph)) for i, k in enumerate(FN)}
                FK = {k: (("ZB", i) if i < 8 else k) for i, k in enumerate(FN)}
                AR = sb("AR", [128, 2, NCH, 2, 64], BF16, ph)
                BK = sb("BK", [128, 2, NCH, 2, 64], BF16, ph)
                VB = sb("VB", [128, 2, 64 + NB], BF16, ph)
                Gt = sb("Gt", [128, 2, NB], F32, ph)
                BON = sb("BON", [128, 2, NB], F32, ph)
                WCb = sb("WCb", [128, 2, NCH], F32, ph)
                KB = sb("KB", [128, NCH, 2, 128], BF16, ph)
                ATM = sb("ATM", [128, NCH, 2, 128], BF16, ph)
                XV = sb("XV", [128, NCH, 4, 64], BF16, ph)
                X = sb("X", [128, NCH, 4, 64], BF16, ph)
                GTs = sb("GTs", [128, NI, 128], BF16, ph)
                Nb = [sb(f"Nb{i}", [128, NI, 64], BF16, ph) for i in range(2)]
                Lb = [sb(f"Lb{i}", [128, NI, 64], BF16, ph) for i in range(2)]
                Pb = [sb(f"Pb{i}", [128, NI, 64], BF16, ph) for i in range(2)]
                Qb = [sb(f"Qb{i}", [128, NI, 64], BF16, ph) for i in range(2)]
                XAKs = sb("XAKs", [128, NI, 64], BF16, ph)
                Wms = sb("Wms", [128, NCH, 2, 2, 64], BF16, ph)
                HBh = sb("HBh", [128, NCH + 1, 2, 2, 64], BF16, ph)
                Hf = sb("Hf", [128, 2, 64], F32, ph)
                Hf2 = sb("Hf2", [128, 2, 64], F32, ph)
                HB = sb("HB", [128, NCH + 1, 2, 64], BF16, ph)
                Ysb = sb("Ysb", [128, 2, NB], F32, ph)
                MB = sb("MB", [128, NB], F32, ph)
                SQ2 = sb("SQ2", [128, NB], F32, ph)
                sto = sb("sto", [64, 4, 64], F32, ph)
                tmpG = [sb(f"tmpG{i}", [128, 128], F32, ph) for i in range(2)]
                Pf = sb("Pf", [64, NI, 64], F32, ph)
                XTs = Pf
                Qf = sb("Qf", [64, NI, 64], F32, ph)
                Xf = sb("Xf", [64, 4, 64], F32, ph)

                load_w(wrw, W["w_in"][l], 0, 1024, "wrw")
                wkeys = [("wrw", c) for c in range(KC)]
                load_cols(gcol, W["g_mix"][l], KC, "gcol")
                load_cols(mu, W["rwkv_mu"][l], KC, "mu")
                S.dma("pool", WA[0:64, :], W["rwkv_w_up"][l], writes=["WA0"])
                S.dma("pool", WA[64:128, :], W["rwkv_a_up"][l], writes=["WA1"])
                S.dma("pool", GU[:, :], W["rwkv_g_up"][l], writes=["GU"])
                for (t_, nm) in ((cw0, "rwkv_w0"), (ca0, "rwkv_a0"), (ckk, "rwkv_k_k"), (cka, "rwkv_k_a"), (crk, "rwkv_r_k"), (clg, "rwkv_ln_g"), (clb, "rwkv_ln_b")):
                    load_cols(t_, W[nm][l], 2, "c_" + nm)
                ckeys = ["c_rwkv_w0", "c_rwkv_a0", "c_rwkv_k_k", "c_rwkv_k_a", "c_rwkv_r_k", "c_rwkv_ln_g", "c_rwkv_ln_b", "cnh", "cge"]
                S.op("dve", lambda h: h.tensor_scalar(out=cw0[:, :], in0=cw0[:, :], scalar1=-1.0, scalar2=None, op0=ALU.mult), reads=["c_rwkv_w0"], writes=["c_rwkv_w0"])
                S.op("pool", lambda h: h.memset(epsc[:], RMS_EPS), writes=["eps"])
                S.op("pool", lambda h: h.memset(cnh[:], -0.5), writes=["cnh"])
                S.op("pool", lambda h: h.memset(cge[:], 64e-5), writes=["cge"])
                S.op("pool", lambda h: h.memset(onesbd[:], 0.0), writes=["onesbd"])
                S.op("pool", lambda h: h.memset(onesbd[0:64, 0:64], 1.0), reads=["onesbd"], writes=["onesbd"])
                S.op("pool", lambda h: h.memset(onesbd[64:128, 64:128], 1.0), reads=["onesbd"], writes=["onesbd"])
                S.op("pool", lambda h: h.tensor_copy(out=identb[:], in_=ident[:]), reads=["ident"], writes=["identb"])
                for r0_ in (0, 64):
                    S.op("pool", lambda h, r0_=r0_: h.affine_select(out=MG[r0_:r0_ + 64, 0:64], in_=onesf[r0_:r0_ + 64, 0:64], pattern=[[1, 64]], compare_op=ALU.is_ge, fill=0.0, base=-1, channel_multiplier=-1),
                         reads=["onesf"], writes=[("MG", r0_, 0)])
                    S.op("pool", lambda h, r0_=r0_: h.affine_select(out=MG[r0_:r0_ + 64, 64:128], in_=onesf[r0_:r0_ + 64, 0:64], pattern=[[1, 64]], compare_op=ALU.is_ge, fill=0.0, base=0, channel_multiplier=-1),
                         reads=["onesf"], writes=[("MG", r0_, 1)])
                mgk = [("MG", 0, 0), ("MG", 0, 1), ("MG", 64, 0), ("MG", 64, 1)]
                S.op("pool", lambda h: h.affine_select(out=ML[:, :], in_=onesf[0:64, 0:64], pattern=[[-1, 64]], compare_op=ALU.is_ge, fill=0.0, base=-1, channel_multiplier=1),
                     reads=["onesf"], writes=["ML"])
                S.op("pool", lambda h: h.memset(zlast[:], 0.0), writes=["zlast"])
                S.op("pool", lambda h: h.memset(VB[:, :, 0:64], 0.0), writes=["VBpad"])
                S.op("pool", lambda h: h.memset(Hf[:], 0.0), writes=["Hf"])
                S.op("pool", lambda h: h.memset(HB[:, 0, :, :], 0.0), writes=[("HB", 0)])
                for zi, zt in enumerate(([ATM, XV, XAKs, Wms, HBh] + Nb + Lb + Pb + Qb) if (RWSUB >= 4 or ZI) else []):
                    nd_ = len(zt.shape)
                    pat_ = {3: "p a b -> p (a b)", 4: "p a b c -> p (a b c)", 5: "p a b c d -> p (a b c d)"}[nd_]
                    S.op("pool", lambda h, zt=zt, pat_=pat_: h.memset(zt[:].rearrange(pat_), 0.0), writes=[("zinit", zi)])

                S.flush()

                def cp(eng, out, in_, reads, writes):
                    if eng == "act":
                        S.op("act", lambda h: h.activation(out=out, in_=in_, func=AF.Copy), reads=reads, writes=writes)
                    else:
                        S.op(eng, lambda h: h.tensor_copy(out=out, in_=in_), reads=reads, writes=writes)

                def block_pre(MGm, mgkm, MLm, mlkey):
                    ark = [("AR", oc, i) for oc in range(2) for i in range(2)]
                    bkk = [("BK", oc, i) for oc in range(2) for i in range(2)]
                    for oc in range(2):
                        for c4 in range(NCH // 4):
                            ps, pk = psum()
                            for q in range(4):
                                ch = c4 * 4 + q
                                S.op("pe", lambda h, ps=ps, q=q, ch=ch, oc=oc: h.matmul(ps[:, q * 128:(q + 1) * 128], lhsT=BK[:, oc, ch, :, :].rearrange("p a t -> p (a t)"), rhs=identb[:, :], start=True, stop=True),
                                     reads=bkk + ["identb"], writes=[pk])
                            cp("act", KB[:, c4 * 4:c4 * 4 + 4, oc, :], ps[:, :].rearrange("p (q k) -> p q k", q=4), [pk], [("KB", oc, c4)])
                            ps, pk = psum()
                            for q in range(4):
                                ch = c4 * 4 + q
                                S.op("pe", lambda h, ps=ps, q=q, ch=ch, oc=oc: h.matmul(ps[:, q * 128:(q + 1) * 128], lhsT=AR[:, oc, ch, :, :].rearrange("p a t -> p (a t)"), rhs=identb[:, :], start=True, stop=True),
                                     reads=ark + ["identb"], writes=[pk])
                            cp("dve", ATM[0:64, c4 * 4:c4 * 4 + 4, oc, :], ps[0:64, :].rearrange("p (q k) -> p q k", q=4), [pk] + ([("zinit", 0)] if RWSUB >= 4 else []), [("ATM", oc, c4)])
                            ps, pk = psum()
                            for q in range(4):
                                ch = c4 * 4 + q
                                S.op("pe", lambda h, ps=ps, q=q, ch=ch, oc=oc: h.matmul(ps[:, q * 128:(q + 1) * 128], lhsT=VB[:, oc, ch * 64:ch * 64 + 128], rhs=identb[:, :], start=True, stop=True),
                                     reads=[("VB", oc), "VBpad", "identb"], writes=[pk])
                            cp("act", X[:, :, :, :].rearrange("p c h v -> p c (h v)")[64:128, c4 * 4:c4 * 4 + 4, oc * 128:(oc + 1) * 128], ps[64:128, :].rearrange("p (q k) -> p q k", q=4), [pk], [("Xv", oc, c4)])
                            if RWSUB >= 4 or XVE:
                              cp("dve", XV[:, :, :, :].rearrange("p c h v -> p c (h v)")[64:128, c4 * 4:c4 * 4 + 4, oc * 128:(oc + 1) * 128], ps[64:128, :].rearrange("p (q k) -> p q k", q=4), [pk, ("zinit", 1)], [("XVv", oc, c4)])
                    kbk = [("KB", oc, c4) for oc in range(2) for c4 in range(NCH // 4)]
                    atk = [("ATM", oc, c4) for oc in range(2) for c4 in range(NCH // 4)]
                    xvk = [("Xv", oc, c4) for oc in range(2) for c4 in range(NCH // 4)]
                    xvvk = [("XVv", oc, c4) for oc in range(2) for c4 in range(NCH // 4)]
                    for inst in range(NI):
                        ps, pk = psum()
                        ch, h_ = inst // 4, inst % 4
                        oc, pb = h_ // 2, (h_ % 2) * 64
                        S.op("pe", lambda h, ps=ps, ch=ch, oc=oc, pb=pb: h.matmul(ps[:, 0:128], lhsT=BK[pb:pb + 64, oc, ch, :, :].rearrange("p a t -> p (a t)"),
                                                                             rhs=AR[pb:pb + 64, oc, ch, :, :].rearrange("p a t -> p (a t)"), start=True, stop=True),
                             reads=ark + bkk, writes=[pk])
                        tg = tmpG[inst % 2]
                        S.op("dve", lambda h, ps=ps, tg=tg: h.tensor_tensor(out=tg[:, :], in0=ps[:, 0:128], in1=MGm[:, :], op=ALU.mult),
                             reads=[pk] + mgkm, writes=[("tmpG", inst % 2)])
                        S.op("act", lambda h, tg=tg, inst=inst: h.activation(out=GTs[:, inst, :], in_=tg[:, :], func=AF.Copy),
                             reads=[("tmpG", inst % 2)], writes=[("GTs", inst // 4, inst % 4)])
                    gtk = [("GTs", g4, q_) for g4 in range(NI // 4) for q_ in range(4)]
                    for inst in range(NI):
                        ps, pk = psum()
                        ch, h_ = inst // 4, inst % 4
                        oc, pb = h_ // 2, (h_ % 2) * 64
                        S.op("pe", lambda h, ps=ps, ch=ch, oc=oc, pb=pb: h.matmul(ps[0:64, 0:64], lhsT=AR[pb:pb + 64, oc, ch, 0, :], rhs=BK[pb:pb + 64, oc, ch, 0, :], start=True, stop=True),
                             reads=ark + bkk, writes=[pk])
                        tg = tmpG[inst % 2]
                        S.op("dve", lambda h, ps=ps, tg=tg: h.tensor_tensor(out=tg[0:64, 0:64], in0=ps[0:64, 0:64], in1=MLm[:, :], op=ALU.mult),
                             reads=[pk, mlkey], writes=[("tmpG", inst % 2)])
                        S.op("act", lambda h, tg=tg, inst=inst: h.activation(out=Lb[0][0:64, inst, :], in_=tg[0:64, 0:64], func=AF.Copy),
                             reads=[("tmpG", inst % 2)], writes=[("L", 0, inst // 8, inst % 8)])
                    for g8 in range(NI // 8):
                        S.op("dve", lambda h, g8=g8: h.tensor_copy(out=Nb[0][0:64, g8 * 8:g8 * 8 + 8, :], in_=GTs[0:64, g8 * 8:g8 * 8 + 8, 0:64]), reads=[("GTs", 2 * g8 + a_, q_) for a_ in range(2) for q_ in range(4)], writes=[("N", 0, g8)])
                        S.op("dve", lambda h, g8=g8: h.tensor_tensor(out=Pb[0][0:64, g8 * 8:g8 * 8 + 8, :], in0=Nb[0][0:64, g8 * 8:g8 * 8 + 8, :], in1=identb[0:64, 0:64].unsqueeze(1).broadcast_to([64, 8, 64]), op=ALU.add),
                             reads=[("N", 0, g8), "identb"], writes=[("P", 0, g8)])
                        S.op("dve", lambda h, g8=g8: h.tensor_copy(out=Pf[:, g8 * 8:g8 * 8 + 8, :], in_=Pb[0][0:64, g8 * 8:g8 * 8 + 8, :]), reads=[("P", 0, g8)], writes=[("Pf", g8)])
                        S.op("dve", lambda h, g8=g8: h.tensor_tensor(out=Qb[0][0:64, g8 * 8:g8 * 8 + 8, :], in0=Lb[0][0:64, g8 * 8:g8 * 8 + 8, :], in1=identb[0:64, 0:64].unsqueeze(1).broadcast_to([64, 8, 64]), op=ALU.add),
                             reads=[("L", 0, g8, q_) for q_ in range(8)] + ["identb"], writes=[("Q", 0, g8), ("L", 0, g8)])
                        S.op("dve", lambda h, g8=g8: h.tensor_copy(out=Qf[:, g8 * 8:g8 * 8 + 8, :], in_=Qb[0][0:64, g8 * 8:g8 * 8 + 8, :]), reads=[("Q", 0, g8)], writes=[("Qf", g8)])
                    for j in range(1, 6):
                        a_, b_ = (j - 1) % 2, j % 2
                        for g8 in range(NI // 8):
                            sl8 = slice(g8 * 8, g8 * 8 + 8)
                            psn, pkn = psum()
                            for q in range(8):
                                i_ = g8 * 8 + q
                                S.op("pe", lambda h, psn=psn, q=q, i_=i_, a_=a_: h.matmul(psn[0:64, q * 64:(q + 1) * 64], lhsT=Lb[a_][:, i_, :], rhs=Nb[a_][:, i_, :], start=True, stop=True),
                                     reads=[("L", a_, g8), ("N", a_, g8)], writes=[pkn])
                            cp("act", Nb[b_][0:64, sl8, :], psn[0:64, :].rearrange("p (q k) -> p q k", q=8), [pkn], [("N", b_, g8)])
                            if j < 5:
                                psl, pkl = psum()
                                for q in range(8):
                                    i_ = g8 * 8 + q
                                    S.op("pe", lambda h, psl=psl, q=q, i_=i_, a_=a_: h.matmul(psl[0:64, q * 64:(q + 1) * 64], lhsT=Nb[a_][:, i_, :], rhs=Lb[a_][:, i_, :], start=True, stop=True),
                                         reads=[("L", a_, g8), ("N", a_, g8)], writes=[pkl])
                                cp("act", Lb[b_][0:64, sl8, :], psl[0:64, :].rearrange("p (q k) -> p q k", q=8), [pkl], [("L", b_, g8)])
                            psp, pkp = psum()
                            for q in range(8):
                                i_ = g8 * 8 + q
                                S.op("pe", lambda h, psp=psp, q=q, i_=i_, a_=a_, b_=b_: h.matmul(psp[0:64, q * 64:(q + 1) * 64], lhsT=Qb[a_][:, i_, :], rhs=Nb[b_][:, i_, :], start=True, stop=True),
                                     reads=[("Q", a_, g8), ("N", b_, g8)], writes=[pkp])
                            S.op("dve", lambda h, psp=psp, sl8=sl8: h.tensor_tensor(out=Pf[:, sl8, :], in0=psp[0:64, :].rearrange("p (q k) -> p q k", q=8), in1=Pf[:, sl8, :], op=ALU.add),
                                 reads=[pkp, ("Pf", g8)], writes=[("Pf", g8)])
                            S.op("act", lambda h, sl8=sl8, b_=b_: h.activation(out=Pb[b_][0:64, sl8, :], in_=Pf[:, sl8, :], func=AF.Copy), reads=[("Pf", g8), ("P", a_, g8)], writes=[("P", b_, g8)])
                            if j < 5:
                                psq, pkq = psum()
                                for q in range(8):
                                    i_ = g8 * 8 + q
                                    S.op("pe", lambda h, psq=psq, q=q, i_=i_, a_=a_, b_=b_: h.matmul(psq[0:64, q * 64:(q + 1) * 64], lhsT=Nb[b_][:, i_, :], rhs=Qb[a_][:, i_, :], start=True, stop=True),
                                         reads=[("Q", a_, g8), ("N", b_, g8)], writes=[pkq])
                                S.op("dve", lambda h, psq=psq, sl8=sl8: h.tensor_tensor(out=Qf[:, sl8, :], in0=psq[0:64, :].rearrange("p (q k) -> p q k", q=8), in1=Qf[:, sl8, :], op=ALU.add),
                                     reads=[pkq, ("Qf", g8)], writes=[("Qf", g8)])
                                S.op("act", lambda h, sl8=sl8, b_=b_: h.activation(out=Qb[b_][0:64, sl8, :], in_=Qf[:, sl8, :], func=AF.Copy), reads=[("Qf", g8), ("Q", a_, g8)], writes=[("Q", b_, g8)])
                    PF = Pb[1]
                    pfk = lambda g8: ("P", 1, g8)
                    for g8 in range(NI // 8):
                        sl8 = slice(g8 * 8, g8 * 8 + 8)
                        ps, pk = psum()
                        for q in range(8):
                            i_ = g8 * 8 + q
                            ch, h_ = i_ // 4, i_ % 4
                            S.op("pe", lambda h, ps=ps, q=q, i_=i_, ch=ch, h_=h_: h.matmul(ps[0:64, q * 64:(q + 1) * 64], lhsT=GTs[:, i_, 0:64], rhs=XV[:, ch, h_, :], start=True, stop=True),
                                 reads=gtk + xvvk, writes=[pk])
                        cp("act", XAKs[0:64, sl8, :], ps[0:64, :].rearrange("p (q k) -> p q k", q=8), [pk], [("XAK", g8)])
                        ps, pk = psum()
                        for q in range(8):
                            i_ = g8 * 8 + q
                            S.op("pe", lambda h, ps=ps, q=q, i_=i_: h.matmul(ps[0:64, q * 64:(q + 1) * 64], lhsT=PF[:, i_, :], rhs=XAKs[:, i_, :], start=True, stop=True),
                                 reads=[pfk(g8), ("XAK", g8)], writes=[pk])
                        cp("dve", XTs[:, sl8, :], ps[0:64, :].rearrange("p (q k) -> p q k", q=8), [pk, ("Pf", g8)], [("Pf", g8)])
                        ps, pk = psum()
                        for q in range(8):
                            i_ = g8 * 8 + q
                            ch, h_ = i_ // 4, i_ % 4
                            oc, pb = h_ // 2, (h_ % 2) * 64
                            col = ((ch % 2) * 2 + oc) * 64
                            S.op("pe", lambda h, ps=ps, i_=i_, ch=ch, oc=oc, pb=pb, col=col, h_=h_: h.matmul(ps[pb:pb + 64, col:col + 64], lhsT=ATM[:, ch, oc, (h_ % 2) * 64:(h_ % 2) * 64 + 64], rhs=PF[:, i_, :], start=True, stop=True),
                                 reads=atk + [pfk(g8)], writes=[pk])
                        for c2 in range(2):
                            for hh in range(2):
                                cp("act" if hh == 0 else "dve", Wms[hh * 64:(hh + 1) * 64, g8 * 2 + c2, hh, :, :], ps[hh * 64:(hh + 1) * 64, c2 * 128:(c2 + 1) * 128].rearrange("p (o t) -> p o t", o=2),
                                   [pk, ("zinit", 3)], [("Wm", g8, c2, hh)])
                    wmk = [("Wm", g8, c2, hh) for g8 in range(NI // 8) for c2 in range(2) for hh in range(2)]
                    xtk = [("Pf", g8) for g8 in range(NI // 8)]
                    return dict(ark=ark, bkk=bkk, kbk=kbk, atk=atk, xvk=xvk, xvvk=xvvk, gtk=gtk, wmk=wmk, xtk=xtk, PF=PF)

                for blk in range(T // NB):
                    n = NB
                    blk0 = blk * NB
                    if RWSUB <= 0:
                        break
                    rmsnorm_fm(xT, blk0, n, gcol, xn, "xn", scr)
                    xnk = [("xn", c) for c in range(KC)]
                    if RWX >= 2:
                        S.op("dve", lambda h: h.tensor_copy(out=ZB[:, :, 0:1], in_=zlast[:, :, :]), reads=["zlast"], writes=["ZB0"])
                    for oc in range(8):
                        ps, pk = psum()
                        for c in range(KC):
                            S.op("pe", lambda h, ps=ps, c=c, oc=oc: h.matmul(ps[:, 0:n], lhsT=wrw[:, c, oc * 128:(oc + 1) * 128], rhs=xn[:, c, 0:n], start=(c == 0), stop=(c == KC - 1)),
                                 reads=xnk + wkeys, writes=[pk])
                        cp("act" if oc % 2 == 0 else "dve", ZB[:, oc, 1:NB + 1], ps[:, 0:n], [pk], [("ZB", oc)])
                    zbk = [("ZB", oc) for oc in range(8)] + ["ZB0"]
                    if RWX >= 2:
                        S.op("dve", lambda h: h.tensor_copy(out=zlast[:, :, :], in_=ZB[:, :, NB:NB + 1]), reads=zbk, writes=["zlast"])
                    if RWX >= 3:
                        S.op("dve", lambda h: h.tensor_tensor(out=DT[:, :, :], in0=ZB[:, :, 0:NB], in1=ZB[:, :, 1:NB + 1], op=ALU.subtract), reads=zbk, writes=["DT"])
                    for oc in range(8 if RWX >= 4 else 0):
                        S.op("dve", lambda h, oc=oc: h.tensor_scalar(out=DT[:, oc, :], in0=DT[:, oc, :], scalar1=mu[:, oc:oc + 1], scalar2=None, op0=ALU.mult),
                             reads=["DT", "mu"], writes=["DT"])
                        S.op("dve", lambda h, oc=oc: h.tensor_tensor(out=DT[:, oc, :], in0=DT[:, oc, :], in1=ZB[:, oc, 1:NB + 1], op=ALU.add),
                             reads=["DT"] + zbk, writes=["DT"])
                    if RWSUB < 2:
                        continue
                    S.op("act", lambda h: h.activation(out=LA[0:64, :], in_=DT[0:64, 6, :], func=AF.Tanh), reads=["DT"], writes=["LA0"])
                    S.op("act", lambda h: h.activation(out=LA[64:128, :], in_=DT[64:128, 6, :], func=AF.Copy), reads=["DT"], writes=["LA1"])
                    S.op("act", lambda h: h.activation(out=SG[:, :], in_=DT[:, 7, :], func=AF.Sigmoid), reads=["DT"], writes=["SG"])
                    for oc in range(2):
                        rr, kq, vv = DT[:, oc, :], DT[:, 2 + oc, :], DT[:, 4 + oc, :]
                        psw, pkw = psum()
                        S.op("pe", lambda h, psw=psw, oc=oc: h.matmul(psw[:, 0:n], lhsT=WA[0:64, oc * 128:(oc + 1) * 128], rhs=LA[0:64, :], start=True, stop=True), reads=["WA0", "LA0"], writes=[pkw])
                        psa, pka = psum()
                        S.op("pe", lambda h, psa=psa, oc=oc: h.matmul(psa[:, 0:n], lhsT=WA[64:128, oc * 128:(oc + 1) * 128], rhs=LA[64:128, :], start=True, stop=True), reads=["WA1", "LA1"], writes=[pka])
                        psg, pkg = psum()
                        S.op("pe", lambda h, psg=psg, oc=oc: h.matmul(psg[:, 0:n], lhsT=GU[:, oc * 128:(oc + 1) * 128], rhs=SG[:, :], start=True, stop=True), reads=["GU", "SG"], writes=[pkg])
                        ew, aa, kk, t1, km, cs, wi, wv, we, be = (F[k] for k in FN)
                        S.op("act", lambda h, psw=psw, oc=oc: h.activation(out=ew[:, :], in_=psw[:, 0:n], func=AF.Exp, scale=-1.0, bias=cw0[:, oc:oc + 1]), reads=[pkw] + ckeys, writes=[FK["ew"]])
                        S.op("dve", lambda h: h.tensor_scalar(out=ew[:, :], in0=ew[:, :], scalar1=1.0, scalar2=None, op0=ALU.add), reads=[FK["ew"]], writes=[FK["ew"]])
                        S.op("act", lambda h: h.activation(out=ew[:, :], in_=ew[:, :], func=AF.Ln), reads=[FK["ew"]], writes=[FK["ew"]])
                        S.op("act", lambda h: h.activation(out=ew[:, :], in_=ew[:, :], func=AF.Exp, scale=-1.0, bias=cnh[:, 0:1]), reads=[FK["ew"]] + ckeys, writes=[FK["ew"]])
                        S.op("act", lambda h, psa=psa, oc=oc: h.activation(out=aa[:, :], in_=psa[:, 0:n], func=AF.Sigmoid, bias=ca0[:, oc:oc + 1]), reads=[pka] + ckeys, writes=[FK["a"]])
                        cp("act", Gt[:, oc, :], psg[:, 0:n], [pkg], [("Gt", oc)])
                        S.op("dve", lambda h, kq=kq, oc=oc: h.tensor_scalar(out=kk[:, :], in0=kq, scalar1=ckk[:, oc:oc + 1], scalar2=None, op0=ALU.mult), reads=["DT"] + ckeys, writes=[FK["kk"]])
                        S.op("act", lambda h: h.activation(out=t1[:, :], in_=kk[:, :], func=AF.Square), reads=[FK["kk"]], writes=[FK["t1"]])
                        ps, pk = psum()
                        S.op("pe", lambda h, ps=ps: h.matmul(ps[:, 0:n], lhsT=onesbd[:, :], rhs=t1[:, :], start=True, stop=True), reads=["onesbd", FK["t1"]], writes=[pk])
                        S.op("act", lambda h, ps=ps: h.activation(out=t1[:, :], in_=ps[:, 0:n], func=AF.Sqrt), reads=[pk], writes=[FK["t1"]])
                        S.op("dve", lambda h: h.tensor_scalar(out=t1[:, :], in0=t1[:, :], scalar1=1e-12, scalar2=None, op0=ALU.max), reads=[FK["t1"]], writes=[FK["t1"]])
                        S.op("dve", lambda h: h.reciprocal(out=t1[:, :], in_=t1[:, :]), reads=[FK["t1"]], writes=[FK["t1"]])
                        S.op("dve", lambda h: h.tensor_tensor(out=kk[:, :], in0=kk[:, :], in1=t1[:, :], op=ALU.mult), reads=[FK["kk"], FK["t1"]], writes=[FK["kk"]])
                        S.op("dve", lambda h, oc=oc: h.tensor_scalar(out=t1[:, :], in0=aa[:, :], scalar1=cka[:, oc:oc + 1], scalar2=cka[:, oc:oc + 1], op0=ALU.mult, op1=ALU.subtract), reads=[FK["a"], FK["t1"]] + ckeys, writes=[FK["t1"]])
                        S.op("dve", lambda h: h.tensor_scalar(out=t1[:, :], in0=t1[:, :], scalar1=1.0, scalar2=None, op0=ALU.add), reads=[FK["t1"]], writes=[FK["t1"]])
                        S.op("dve", lambda h, kq=kq: h.tensor_tensor(out=km[:, :], in0=t1[:, :], in1=kq, op=ALU.mult), reads=[FK["t1"], "DT"], writes=[FK["km"]])
                        S.op("dve", lambda h: h.tensor_tensor(out=be[:, :], in0=kk[:, :], in1=aa[:, :], op=ALU.mult), reads=[FK["kk"], FK["a"]], writes=[FK["be"]])
                        S.op("dve", lambda h, rr=rr: h.tensor_tensor(out=t1[:, :], in0=rr, in1=km[:, :], op=ALU.mult), reads=["DT", FK["km"], FK["t1"]], writes=[FK["t1"]])
                        S.op("dve", lambda h, oc=oc: h.tensor_scalar(out=t1[:, :], in0=t1[:, :], scalar1=crk[:, oc:oc + 1], scalar2=None, op0=ALU.mult), reads=[FK["t1"]] + ckeys, writes=[FK["t1"]])
                        ps, pk = psum()
                        S.op("pe", lambda h, ps=ps: h.matmul(ps[:, 0:n], lhsT=onesbd[:, :], rhs=t1[:, :], start=True, stop=True), reads=["onesbd", FK["t1"]], writes=[pk])
                        S.op("dve", lambda h, ps=ps, vv=vv, oc=oc: h.tensor_tensor(out=BON[:, oc, :], in0=ps[:, 0:n], in1=vv, op=ALU.mult), reads=[pk, "DT"], writes=[("BON", oc)])
                        for ch in range(NCH):
                            S.op("dve", lambda h, ch=ch: h.tensor_tensor_scan(out=cs[:, ch * 64:(ch + 1) * 64], data0=onesf[:, 0:64], data1=ew[:, ch * 64:(ch + 1) * 64], initial=0.0, op0=ALU.mult, op1=ALU.add),
                                 reads=[FK["ew"], "onesf"], writes=[FK["cs"]])
                        S.op("act", lambda h: h.activation(out=wi[:, :], in_=cs[:, :], func=AF.Exp, scale=-1.0), reads=[FK["cs"]], writes=[FK["wi"]])
                        S.op("act", lambda h: h.activation(out=wv[:, :], in_=cs[:, :], func=AF.Exp), reads=[FK["cs"]], writes=[FK["wv"]])
                        S.op("dve", lambda h: h.tensor_tensor(out=we[:, :], in0=cs[:, :], in1=ew[:, :], op=ALU.subtract), reads=[FK["cs"], FK["ew"]], writes=[FK["we"]])
                        S.op("act", lambda h: h.activation(out=we[:, :], in_=we[:, :], func=AF.Exp, scale=-1.0), reads=[FK["we"]], writes=[FK["we"]])
                        v3 = lambda a_: a_.rearrange("p (c t) -> p c t", c=NCH)
                        S.op("dve", lambda h: h.tensor_scalar(out=t1[:, :], in0=kk[:, :], scalar1=-1.0, scalar2=None, op0=ALU.mult), reads=[FK["kk"], FK["t1"]], writes=[FK["t1"]])
                        S.op("dve", lambda h, oc=oc: h.tensor_tensor(out=AR[:, oc, :, 0, :], in0=v3(t1[:, :]), in1=v3(we[:, :]), op=ALU.mult), reads=[FK["t1"], FK["we"]], writes=[("AR", oc, 0)])
                        S.op("dve", lambda h, oc=oc, rr=rr: h.tensor_tensor(out=AR[:, oc, :, 1, :], in0=v3(rr), in1=v3(wi[:, :]), op=ALU.mult), reads=["DT", FK["wi"]], writes=[("AR", oc, 1)])
                        S.op("dve", lambda h, oc=oc: h.tensor_tensor(out=BK[:, oc, :, 0, :], in0=v3(be[:, :]), in1=v3(wv[:, :]), op=ALU.mult), reads=[FK["be"], FK["wv"]], writes=[("BK", oc, 0)])
                        S.op("dve", lambda h, oc=oc: h.tensor_tensor(out=BK[:, oc, :, 1, :], in0=v3(km[:, :]), in1=v3(wv[:, :]), op=ALU.mult), reads=[FK["km"], FK["wv"]], writes=[("BK", oc, 1)])
                        S.op("dve", lambda h, oc=oc: h.tensor_copy(out=WCb[:, oc, :], in_=v3(wi[:, :])[:, :, 63]), reads=[FK["wi"]], writes=[("WCb", oc)])
                        S.op("act", lambda h, oc=oc, vv=vv: h.activation(out=VB[:, oc, 64:64 + NB], in_=vv, func=AF.Copy), reads=["DT"], writes=[("VB", oc)])
                    if RWSUB < 3:
                        continue
                    pre_ = block_pre(MG, mgk, ML, "ML")
                    ark, bkk, kbk, atk, xvk, xvvk, gtk, wmk, xtk, PF = (pre_[k_] for k_ in ("ark", "bkk", "kbk", "atk", "xvk", "xvvk", "gtk", "wmk", "xtk", "PF"))
                    if RWSUB < 7:
                        continue
                    for ch in range(NCH):
                        psu, pku = psum()
                        for h_ in range(4):
                            oc, pb = h_ // 2, (h_ % 2) * 64
                            S.op("pe", lambda h, psu=psu, ch=ch, h_=h_, oc=oc, pb=pb: h.matmul(psu[0:64, h_ * 64:(h_ + 1) * 64], lhsT=Wms[:, ch, h_ % 2, oc, :], rhs=HB[:, ch, oc, :], start=True, stop=True),
                                 reads=wmk + [("HB", ch)], writes=[pku])
                        S.op("dve", lambda h, psu=psu, ch=ch: h.tensor_tensor(out=Xf[:, :, :], in0=psu[0:64, 0:256].rearrange("p (a v) -> p a v", a=4), in1=XTs[:, ch * 4:ch * 4 + 4, :], op=ALU.add),
                             reads=[pku] + xtk, writes=["Xf"])
                        S.op("act", lambda h, ch=ch: h.activation(out=X[0:64, ch, :, :], in_=Xf[:, :, :], func=AF.Copy), reads=["Xf"], writes=[("Xu", ch)])
                        psh, pkh = psum()
                        for h_ in range(4):
                            oc, pb = h_ // 2, (h_ % 2) * 64
                            S.op("pe", lambda h, psh=psh, ch=ch, h_=h_, oc=oc, pb=pb: h.matmul(psh[pb:pb + 64, oc * 64:(oc + 1) * 64], lhsT=KB[:, ch, oc, (h_ % 2) * 64:(h_ % 2) * 64 + 64], rhs=X[:, ch, h_, :], start=True, stop=True),
                                 reads=kbk + xvk + [("Xu", ch)], writes=[pkh])
                        S.op("dve", lambda h, psh=psh: h.tensor_tensor(out=Hf2[:, :, :], in0=psh[:, 0:128].rearrange("p (o v) -> p o v", o=2), in1=Hf[:, :, :], op=ALU.add), reads=[pkh, "Hf"], writes=["Hf2"])
                        S.op("dve", lambda h, ch=ch: h.tensor_tensor(out=Hf[:, :, :], in0=Hf2[:, :, :], in1=WCb[:, :, ch:ch + 1].broadcast_to([128, 2, 64]), op=ALU.mult),
                             reads=["Hf2", ("WCb", 0), ("WCb", 1)], writes=["Hf"])
                        S.op("act", lambda h, ch=ch: h.activation(out=HB[:, ch + 1, :, :], in_=Hf[:, :, :], func=AF.Copy), reads=["Hf"], writes=[("HB", ch + 1)])
                        for hh in range(2):
                            S.op("act", lambda h, ch=ch, hh=hh: h.activation(out=HBh[hh * 64:(hh + 1) * 64, ch + 1, hh, :, :], in_=Hf[hh * 64:(hh + 1) * 64, :, :], func=AF.Copy),
                                 reads=["Hf", ("zinit", 4)], writes=[("HBh", ch + 1, hh)])
                    if RWSUB < 8:
                        continue
                    hbk = [("HB", c_) for c_ in range(NCH + 1)]
                    hbhk = [("HBh", c_, hh_) for c_ in range(NCH + 1) for hh_ in range(2)]
                    xuk = [("Xu", c_) for c_ in range(NCH)]
                    for oc in range(2):
                        ps, pk = psum()
                        for ch in range(NCH):
                            for hh in range(2):
                                h_, pb = oc * 2 + hh, hh * 64
                                S.op("pe", lambda h, ps=ps, ch=ch, oc=oc, pb=pb, hh=hh: h.matmul(ps[pb:pb + 64, ch * 64:(ch + 1) * 64], lhsT=HBh[:, ch, hh, oc, :], rhs=AR[:, oc, ch, 1, :], start=True, stop=False),
                                     reads=hbhk + ark, writes=[pk])
                                S.op("pe", lambda h, ps=ps, ch=ch, h_=h_, pb=pb: h.matmul(ps[pb:pb + 64, ch * 64:(ch + 1) * 64], lhsT=X[:, ch, h_, :], rhs=GTs[:, ch * 4 + h_, 64:128], start=False, stop=True),
                                     reads=xuk + xvk + gtk, writes=[pk])
                        cp("act", Ysb[:, oc, :], ps[:, 0:NB], [pk], [("Ysb", oc)])
                        yv = Ysb[:, oc, :]
                        ps, pk = psum()
                        S.op("pe", lambda h, ps=ps, yv=yv: h.matmul(ps[:, 0:n], lhsT=onesbd[:, :], rhs=yv, start=True, stop=True), reads=[("Ysb", oc), "onesbd"], writes=[pk])
                        S.op("act", lambda h, ps=ps: h.activation(out=MB[:, :], in_=ps[:, 0:n], func=AF.Copy, scale=1.0 / 64), reads=[pk], writes=["MB"])
                        S.op("dve", lambda h, yv=yv: h.tensor_tensor(out=yv, in0=yv, in1=MB[:, :], op=ALU.subtract), reads=["MB", ("Ysb", oc)], writes=[("Ysb", oc)])
                        S.op("act", lambda h, yv=yv: h.activation(out=SQ2[:, :], in_=yv, func=AF.Square), reads=[("Ysb", oc)], writes=["SQ2"])
                        ps, pk = psum()
                        S.op("pe", lambda h, ps=ps: h.matmul(ps[:, 0:n], lhsT=onesbd[:, :], rhs=SQ2[:, :], start=True, stop=True), reads=["SQ2", "onesbd"], writes=[pk])
                        S.op("act", lambda h, ps=ps: h.activation(out=MB[:, :], in_=ps[:, 0:n], func=AF.Sqrt, scale=1.0 / 64, bias=cge[:, 0:1]), reads=[pk] + ckeys, writes=["MB"])
                        S.op("dve", lambda h: h.reciprocal(out=MB[:, :], in_=MB[:, :]), reads=["MB"], writes=["MB"])
                        S.op("dve", lambda h, yv=yv, oc=oc: h.scalar_tensor_tensor(out=yv, in0=MB[:, :], scalar=clg[:, oc:oc + 1], in1=yv, op0=ALU.mult, op1=ALU.mult), reads=["MB", ("Ysb", oc)] + ckeys, writes=[("Ysb", oc)])
                        S.op("dve", lambda h, yv=yv, oc=oc: h.scalar_tensor_tensor(out=yv, in0=BON[:, oc, :], scalar=clb[:, oc:oc + 1], in1=yv, op0=ALU.add, op1=ALU.add), reads=[("Ysb", oc), ("BON", oc)] + ckeys, writes=[("Ysb", oc)])
                        S.op("dve", lambda h, yv=yv, oc=oc, blk0=blk0: h.tensor_tensor(out=yT[:, oc, blk0:blk0 + NB], in0=yv, in1=Gt[:, oc, :], op=ALU.mult), reads=[("Ysb", oc), ("Gt", oc)], writes=[("yT", FK["a"], oc, blk)])
                    S.op("act", lambda h: h.activation(out=HB[:, 0, :, :], in_=Hf[:, :, :], func=AF.Copy), reads=["Hf"] + hbk, writes=[("HB", 0)])
                    for hh in range(2):
                        S.op("act", lambda h, hh=hh: h.activation(out=HBh[hh * 64:(hh + 1) * 64, 0, hh, :, :], in_=Hf[hh * 64:(hh + 1) * 64, :, :], func=AF.Copy),
                             reads=["Hf"] + hbhk, writes=[("HBh", 0, hh)])

                for oc in range(2 if RWSUB >= 0 else 0):
                    ps, pk = psum()
                    S.op("pe", lambda h, ps=ps, oc=oc: h.transpose(out=ps[0:64, 0:128], in_=Hf[:, oc, :], identity=ident[:, :]), reads=["Hf", "ident"], writes=[pk])
                    cp("act", sto[:, 2 * oc:2 * oc + 2, :], ps[0:64, 0:128].rearrange("p (a k) -> p a k", a=2), [pk], [("sto", oc)])
                    S.dma("sp", o_prw[l, 2 * oc:2 * oc + 2].rearrange("a v k -> v a k"), sto[:, 2 * oc:2 * oc + 2, :], reads=[("sto", oc)])
                if RWSUB >= 0:
                    S.dma("sp", o_psh[l].rearrange("(c p) -> p c", p=128), zlast[:, :, 0], reads=["zlast"], allow_slow_non_contiguous=True)

                if stage >= 13 and SRW >= 1:
                    S.flush()
                    n = NS
                    ZBs = ZB[:, :, 0:24].rearrange("p c (b t) -> p c b t", b=4)
                    DTs = ZB[:, :, 32:48]
                    FS = [ZB[:, k_, 64:80] for k_ in range(8)] + [ZB[:, 0, 96:112], ZB[:, 1, 96:112]]
                    Gts, BONs, WCs = ZB[:, 2:4, 96:112], ZB[:, 4:6, 96:112], ZB[:, 6:8, 96:100]
                    rmask, rtmp = ZB[:, 0, 128:132], ZB[:, 1, 128:132]
                    cmask = DT[:, 0, :].rearrange("p (b t) -> p b t", b=4)
                    ctmp = DT[:, 6, :].rearrange("p (b t) -> p b t", b=4)
                    BDh, MLs, MGs = DT[:, 1, 0:64], DT[0:64, 1, 64:128], DT[:, 1, 128:256]
                    Hfs = DT[:, 2:4, :].rearrange("p r (b o v) -> p (r b) o v", b=2, o=2)
                    Hfs2 = DT[:, 4:6, :].rearrange("p r (b o v) -> p (r b) o v", b=2, o=2)
                    hst = [DT[0:64, 7, 0:128].rearrange("p (a k) -> p a k", a=2), DT[0:64, 7, 128:256].rearrange("p (a k) -> p a k", a=2)]
                    Yss, MBs, SQs = Ysb[:, :, 0:64], MB[:, 0:64], SQ2[:, 0:64]
                    LAs, SGs = LA[:, 0:NS], SG[:, 0:NS]
                    HBs, HBhs = HB[:, 0:4, :, :], HBh[:, 0:4, :, :, :]
                    Wmb = BON[:, :, :].bitcast(BF16).rearrange("p r (b j t) -> p (r b) j t", b=2, j=4)
                    KBb = Gt[:, :, :].bitcast(BF16).rearrange("p r (b o k) -> p (r b) o k", b=2, o=2)
                    ARbv = [tmpG[b_ // 2][:, (b_ % 2) * 64:(b_ % 2) * 64 + 64].bitcast(BF16).rearrange("p (o t) -> p o t", o=2) for b_ in range(4)]
                    S.op("pool", lambda h: h.tensor_copy(out=rtmp, in_=onesf[:, 0:4]), reads=["onesf"], writes=["rtmp"])
                    for h0 in (0, 64):
                        S.op("pool", lambda h, h0=h0: h.affine_select(out=rmask[h0:h0 + 64, :], in_=rtmp[h0:h0 + 64, :], pattern=[[-4, 4]], compare_op=ALU.is_ge, fill=0.0, base=0, channel_multiplier=1), reads=["rtmp"], writes=[("rm1", h0)])
                    for h0 in (0, 64):
                        S.op("pool", lambda h, h0=h0: h.affine_select(out=rtmp[h0:h0 + 64, :], in_=rmask[h0:h0 + 64, :], pattern=[[4, 4]], compare_op=ALU.is_ge, fill=0.0, base=3, channel_multiplier=-1), reads=[("rm1", 0), ("rm1", 64)], writes=[("rm2", h0)])
                    S.op("pool", lambda h: h.tensor_copy(out=rmask, in_=rtmp), reads=[("rm2", 0), ("rm2", 64)], writes=["rmask"])
                    S.op("pool", lambda h: h.memset(DT[:, 0, :], 1.0), writes=["cm0"])
                    S.op("pool", lambda h: h.affine_select(out=ctmp, in_=cmask, pattern=[[-4, 4], [1, 64]], compare_op=ALU.is_ge, fill=0.0, base=0, channel_multiplier=0), reads=["cm0"], writes=["cm1"])
                    S.op("pool", lambda h: h.affine_select(out=cmask, in_=ctmp, pattern=[[4, 4], [-1, 64]], compare_op=ALU.is_ge, fill=0.0, base=3, channel_multiplier=0), reads=["cm1", "cm0"], writes=["cmask"])
                    S.op("dve", lambda h: h.tensor_scalar(out=BDh, in0=cmask[:, 0, :], scalar1=rmask[:, 0:1], scalar2=None, op0=ALU.mult), reads=["cmask", "rmask"], writes=["BDh"])
                    for b in range(1, 4):
                        S.op("dve", lambda h, b=b: h.scalar_tensor_tensor(out=BDh, in0=cmask[:, b, :], scalar=rmask[:, b:b + 1], in1=BDh, op0=ALU.mult, op1=ALU.add), reads=["cmask", "rmask", "BDh"], writes=["BDh"])
                    for hf in range(2):
                        S.op("dve", lambda h, hf=hf: h.tensor_tensor(out=MGs[:, hf * 64:(hf + 1) * 64], in0=MG[:, hf * 64:(hf + 1) * 64], in1=BDh, op=ALU.mult), reads=["BDh"], writes=[("MGs", hf)])
                    S.op("dve", lambda h: h.tensor_tensor(out=MLs, in0=ML[:, :], in1=DT[0:64, 1, 0:64], op=ALU.mult), reads=["BDh"], writes=["MLs"])
                    for _once in (0,):
                        if SRW < 2:
                            break
                        for b in range(4):
                            for oc in range(2):
                                hs_ = hst[(b * 2 + oc) % 2]
                                hk_ = ("hst", (b * 2 + oc) % 2)
                                S.dma("sp", hs_, srw[l, b, 2 * oc:2 * oc + 2].rearrange("a v k -> v a k"), writes=[hk_])
                                ps, pk = psum()
                                S.op("pe", lambda h, ps=ps, hs_=hs_: h.transpose(out=ps[:, 0:64], in_=hs_.rearrange("p a k -> p (a k)"), identity=ident[0:64, 0:64]), reads=[hk_, "ident"], writes=[pk])
                                S.op("act", lambda h, ps=ps, b=b, oc=oc: h.activation(out=Hfs[:, b, oc, :], in_=ps[:, 0:64], func=AF.Copy), reads=[pk], writes=[("Hfs", b, oc)])
                            hfk = [("Hfs", b, 0), ("Hfs", b, 1)]
                            S.op("act", lambda h, b=b: h.activation(out=HBs[:, b, :, :], in_=Hfs[:, b, :, :], func=AF.Copy), reads=hfk, writes=[("HBs", b)])
                            for hh in range(2):
                                S.op("act", lambda h, b=b, hh=hh: h.activation(out=HBhs[hh * 64:(hh + 1) * 64, b, hh, :, :], in_=Hfs[hh * 64:(hh + 1) * 64, b, :, :], func=AF.Copy), reads=hfk, writes=[("HBhs", b, hh)])
                        if SRW < 3:
                            break
                        rmsnorm_fm(xTs, 0, n, gcol, xn, "xns", scr)
                        xnk = [("xns", c) for c in range(KC)]
                        for b in range(4):
                            S.dma("sp", ZBs[:, :, b, 0], ssh[l, b].rearrange("(c p) -> p c", p=128), writes=[("zs0", b)], allow_slow_non_contiguous=True)
                        for oc in range(8):
                            ps, pk = psum()
                            for c in range(KC):
                                S.op("pe", lambda h, ps=ps, c=c, oc=oc: h.matmul(ps[:, 0:n], lhsT=wrw[:, c, oc * 128:(oc + 1) * 128], rhs=xn[:, c, 0:n], start=(c == 0), stop=(c == KC - 1)), reads=xnk, writes=[pk])
                            S.op("act", lambda h, ps=ps, oc=oc: h.activation(out=ZBs[:, oc, :, 1:5], in_=ps[:, 0:n].rearrange("p (b t) -> p b t", b=4), func=AF.Copy), reads=[pk], writes=[("zsn", oc)])
                        zk_ = [("zs0", b) for b in range(4)] + [("zsn", oc) for oc in range(8)]
                        for b in range(4):
                            S.dma("sp", o_ssh[l, b].rearrange("(c p) -> p c", p=128), ZBs[:, :, b, 4], reads=zk_, allow_slow_non_contiguous=True)
                        for oc in range(8):
                            dv = DTs[:, oc, :].rearrange("p (b t) -> p b t", b=4)
                            S.op("dve", lambda h, oc=oc, dv=dv: h.tensor_tensor(out=dv, in0=ZBs[:, oc, :, 0:4], in1=ZBs[:, oc, :, 1:5], op=ALU.subtract), reads=zk_, writes=[("DTs", oc)])
                            S.op("dve", lambda h, oc=oc, dv=dv: h.tensor_scalar(out=dv, in0=dv, scalar1=mu[:, oc:oc + 1], scalar2=None, op0=ALU.mult), reads=[("DTs", oc)], writes=[("DTs", oc)])
                            S.op("dve", lambda h, oc=oc, dv=dv: h.tensor_tensor(out=dv, in0=dv, in1=ZBs[:, oc, :, 1:5], op=ALU.add), reads=[("DTs", oc)] + zk_, writes=[("DTs", oc)])
                        if SRW < 4:
                            break
                        dk = lambda i_: ("DTs", i_)
                        S.op("act", lambda h: h.activation(out=LAs[0:64, :], in_=DTs[0:64, 6, :], func=AF.Tanh), reads=[dk(6)], writes=["LAs0"])
                        S.op("act", lambda h: h.activation(out=LAs[64:128, :], in_=DTs[64:128, 6, :], func=AF.Copy), reads=[dk(6)], writes=["LAs1"])
                        S.op("act", lambda h: h.activation(out=SGs, in_=DTs[:, 7, :], func=AF.Sigmoid), reads=[dk(7)], writes=["SGs"])
                        for zt_, zn_ in ((AR, "p a b c d -> p (a b c d)"), (BK, "p a b c d -> p (a b c d)"), (VB, "p a b -> p (a b)")):
                            S.op("pool", lambda h, zt_=zt_, zn_=zn_: h.memset(zt_[:].rearrange(zn_), 0.0), writes=[("z0", id(zt_))])
                        z0k = [("z0", id(AR)), ("z0", id(BK)), ("z0", id(VB))]
                        ew, aa, kk, t1, km, cs, wi, wv, we, be = FS
                        fsk = lambda i_: ("FS", i_)
                        for oc in range(2):
                            rr, kq, vv = DTs[:, oc, :], DTs[:, 2 + oc, :], DTs[:, 4 + oc, :]
                            psw, pkw = psum()
                            S.op("pe", lambda h, psw=psw, oc=oc: h.matmul(psw[:, 0:n], lhsT=WA[0:64, oc * 128:(oc + 1) * 128], rhs=LAs[0:64, :], start=True, stop=True), reads=["LAs0"], writes=[pkw])
                            psa, pka = psum()
                            S.op("pe", lambda h, psa=psa, oc=oc: h.matmul(psa[:, 0:n], lhsT=WA[64:128, oc * 128:(oc + 1) * 128], rhs=LAs[64:128, :], start=True, stop=True), reads=["LAs1"], writes=[pka])
                            psg, pkg = psum()
                            S.op("pe", lambda h, psg=psg, oc=oc: h.matmul(psg[:, 0:n], lhsT=GU[:, oc * 128:(oc + 1) * 128], rhs=SGs, start=True, stop=True), reads=["SGs"], writes=[pkg])
                            S.op("act", lambda h, psw=psw, oc=oc: h.activation(out=ew, in_=psw[:, 0:n], func=AF.Exp, scale=-1.0, bias=cw0[:, oc:oc + 1]), reads=[pkw], writes=[fsk(0)])
                            S.op("dve", lambda h: h.tensor_scalar(out=ew, in0=ew, scalar1=1.0, scalar2=None, op0=ALU.add), reads=[fsk(0)], writes=[fsk(0)])
                            S.op("act", lambda h: h.activation(out=ew, in_=ew, func=AF.Ln), reads=[fsk(0)], writes=[fsk(0)])
                            S.op("act", lambda h: h.activation(out=ew, in_=ew, func=AF.Exp, scale=-1.0, bias=cnh[:, 0:1]), reads=[fsk(0)], writes=[fsk(0)])
                            S.op("act", lambda h, psa=psa, oc=oc: h.activation(out=aa, in_=psa[:, 0:n], func=AF.Sigmoid, bias=ca0[:, oc:oc + 1]), reads=[pka], writes=[fsk(1)])
                            S.op("act", lambda h, psg=psg, oc=oc: h.activation(out=Gts[:, oc, :], in_=psg[:, 0:n], func=AF.Copy), reads=[pkg], writes=[("Gts", oc)])
                            S.op("dve", lambda h, kq=kq, oc=oc: h.tensor_scalar(out=kk, in0=kq, scalar1=ckk[:, oc:oc + 1], scalar2=None, op0=ALU.mult), reads=[dk(2 + oc)], writes=[fsk(2)])
                            S.op("act", lambda h: h.activation(out=t1, in_=kk, func=AF.Square), reads=[fsk(2)], writes=[fsk(3)])
                            ps, pk = psum()
                            S.op("pe", lambda h, ps=ps: h.matmul(ps[:, 0:n], lhsT=onesbd[:, :], rhs=t1, start=True, stop=True), reads=[fsk(3)], writes=[pk])
                            S.op("act", lambda h, ps=ps: h.activation(out=t1, in_=ps[:, 0:n], func=AF.Sqrt), reads=[pk], writes=[fsk(3)])
                            S.op("dve", lambda h: h.tensor_scalar(out=t1, in0=t1, scalar1=1e-12, scalar2=None, op0=ALU.max), reads=[fsk(3)], writes=[fsk(3)])
                            S.op("dve", lambda h: h.reciprocal(out=t1, in_=t1), reads=[fsk(3)], writes=[fsk(3)])
                            S.op("dve", lambda h: h.tensor_tensor(out=kk, in0=kk, in1=t1, op=ALU.mult), reads=[fsk(2), fsk(3)], writes=[fsk(2)])
                            S.op("dve", lambda h, oc=oc: h.tensor_scalar(out=t1, in0=aa, scalar1=cka[:, oc:oc + 1], scalar2=cka[:, oc:oc + 1], op0=ALU.mult, op1=ALU.subtract), reads=[fsk(1), fsk(3)], writes=[fsk(3)])
                            S.op("dve", lambda h: h.tensor_scalar(out=t1, in0=t1, scalar1=1.0, scalar2=None, op0=ALU.add), reads=[fsk(3)], writes=[fsk(3)])
                            S.op("dve", lambda h, kq=kq: h.tensor_tensor(out=km, in0=t1, in1=kq, op=ALU.mult), reads=[fsk(3), dk(2 + oc)], writes=[fsk(4)])
                            S.op("dve", lambda h: h.tensor_tensor(out=be, in0=kk, in1=aa, op=ALU.mult), reads=[fsk(2), fsk(1)], writes=[fsk(9)])
                            S.op("dve", lambda h, rr=rr: h.tensor_tensor(out=t1, in0=rr, in1=km, op=ALU.mult), reads=[dk(oc), fsk(4), fsk(3)], writes=[fsk(3)])
                            S.op("dve", lambda h, oc=oc: h.tensor_scalar(out=t1, in0=t1, scalar1=crk[:, oc:oc + 1], scalar2=None, op0=ALU.mult), reads=[fsk(3)], writes=[fsk(3)])
                            ps, pk = psum()
                            S.op("pe", lambda h, ps=ps: h.matmul(ps[:, 0:n], lhsT=onesbd[:, :], rhs=t1, start=True, stop=True), reads=[fsk(3)], writes=[pk])
                            S.op("dve", lambda h, ps=ps, vv=vv, oc=oc: h.tensor_tensor(out=BONs[:, oc, :], in0=ps[:, 0:n], in1=vv, op=ALU.mult), reads=[pk, dk(4 + oc)], writes=[("BONs", oc)])
                            for b in range(4):
                                S.op("dve", lambda h, b=b: h.tensor_tensor_scan(out=cs[:, 4 * b:4 * b + 4], data0=onesf[:, 0:4], data1=ew[:, 4 * b:4 * b + 4], initial=0.0, op0=ALU.mult, op1=ALU.add), reads=[fsk(0)], writes=[fsk(5)])
                            S.op("act", lambda h: h.activation(out=wi, in_=cs, func=AF.Exp, scale=-1.0), reads=[fsk(5)], writes=[fsk(6)])
                            S.op("act", lambda h: h.activation(out=wv, in_=cs, func=AF.Exp), reads=[fsk(5)], writes=[fsk(7)])
                            S.op("dve", lambda h: h.tensor_tensor(out=we, in0=cs, in1=ew, op=ALU.subtract), reads=[fsk(5), fsk(0)], writes=[fsk(8)])
                            S.op("act", lambda h: h.activation(out=we, in_=we, func=AF.Exp, scale=-1.0), reads=[fsk(8)], writes=[fsk(8)])
                            S.op("dve", lambda h: h.tensor_scalar(out=t1, in0=kk, scalar1=-1.0, scalar2=None, op0=ALU.mult), reads=[fsk(2), fsk(3)], writes=[fsk(3)])
                            S.op("dve", lambda h, oc=oc: h.tensor_tensor(out=AR[:, oc, 0, 0, 0:n], in0=t1, in1=we, op=ALU.mult), reads=[fsk(3), fsk(8)] + z0k, writes=[("AR", oc, 0)])
                            S.op("dve", lambda h, oc=oc, rr=rr: h.tensor_tensor(out=AR[:, oc, 0, 1, 0:n], in0=rr, in1=wi, op=ALU.mult), reads=[dk(oc), fsk(6)] + z0k, writes=[("AR", oc, 1)])
                            S.op("dve", lambda h, oc=oc: h.tensor_tensor(out=BK[:, oc, 0, 0, 0:n], in0=be, in1=wv, op=ALU.mult), reads=[fsk(9), fsk(7)] + z0k, writes=[("BK", oc, 0)])
                            S.op("dve", lambda h, oc=oc: h.tensor_tensor(out=BK[:, oc, 0, 1, 0:n], in0=km, in1=wv, op=ALU.mult), reads=[fsk(4), fsk(7)] + z0k, writes=[("BK", oc, 1)])
                            S.op("dve", lambda h, oc=oc: h.tensor_copy(out=WCs[:, oc, :], in_=wi.rearrange("p (b t) -> p b t", b=4)[:, :, 3]), reads=[fsk(6)], writes=[("WCs", oc)])
                            S.op("act", lambda h, oc=oc, vv=vv: h.activation(out=VB[:, oc, 64:64 + n], in_=vv, func=AF.Copy), reads=[dk(4 + oc)] + z0k, writes=[("VB", oc)])
                        if SRW < 5:
                            break
                        pre_ = block_pre(MGs, [("MGs", 0), ("MGs", 1)], MLs, "MLs")
                        ark, bkk, kbk, atk, xvk, xvvk, gtk, wmk, xtk, PF = (pre_[k_] for k_ in ("ark", "bkk", "kbk", "atk", "xvk", "xvvk", "gtk", "wmk", "xtk", "PF"))
                        for b in range(4):
                            S.op("dve", lambda h, b=b: h.tensor_tensor(out=Wmb[:, b, :, :], in0=Wms[:, 0, :, :, :].rearrange("p a o t -> p (a o) t"), in1=cmask[:, b, :].unsqueeze(1).broadcast_to([128, 4, 64]), op=ALU.mult),
                                 reads=wmk + ["cmask"], writes=[("Wmb", b)])
                            S.op("dve", lambda h, b=b: h.tensor_tensor(out=ARbv[b], in0=AR[:, :, 0, 1, :], in1=cmask[:, b, :].unsqueeze(1).broadcast_to([128, 2, 64]), op=ALU.mult),
                                 reads=ark + ["cmask", ("tmpG", b // 2)], writes=[("ARb", b), ("tmpG", b // 2)])
                            S.op("dve", lambda h, b=b: h.tensor_scalar(out=KBb[:, b, :, :], in0=KB[:, 0, :, :], scalar1=rmask[:, b:b + 1], scalar2=None, op0=ALU.mult),
                                 reads=kbk + ["rmask"], writes=[("KBb", b)])
                        if SRW < 6:
                            break
                        psu, pku = psum()
                        for h_ in range(4):
                            oc, hh = h_ // 2, h_ % 2
                            for b in range(4):
                                S.op("pe", lambda h, psu=psu, h_=h_, oc=oc, hh=hh, b=b: h.matmul(psu[0:64, h_ * 64:(h_ + 1) * 64], lhsT=Wmb[:, b, hh * 2 + oc, :], rhs=HBs[:, b, oc, :], start=(b == 0), stop=(b == 3)),
                                     reads=[("Wmb", b), ("HBs", b)], writes=[pku])
                        S.op("dve", lambda h, psu=psu: h.tensor_tensor(out=Xf[:, :, :], in0=psu[0:64, 0:256].rearrange("p (a v) -> p a v", a=4), in1=XTs[:, 0:4, :], op=ALU.add), reads=[pku] + xtk, writes=["Xf"])
                        S.op("act", lambda h: h.activation(out=X[0:64, 0, :, :], in_=Xf[:, :, :], func=AF.Copy), reads=["Xf"], writes=[("Xu", 0)])
                        for b in range(4):
                            psh, pkh = psum()
                            for h_ in range(4):
                                oc, hh = h_ // 2, h_ % 2
                                pb = hh * 64
                                S.op("pe", lambda h, psh=psh, b=b, h_=h_, oc=oc, hh=hh, pb=pb: h.matmul(psh[pb:pb + 64, oc * 64:(oc + 1) * 64], lhsT=KBb[:, b, oc, hh * 64:(hh + 1) * 64], rhs=X[:, 0, h_, :], start=True, stop=True),
                                     reads=[("KBb", b), ("Xu", 0)] + xvk, writes=[pkh])
                            S.op("dve", lambda h, psh=psh, b=b: h.tensor_tensor(out=Hfs2[:, b, :, :], in0=psh[:, 0:128].rearrange("p (o v) -> p o v", o=2), in1=Hfs[:, b, :, :], op=ALU.add),
                                 reads=[pkh, ("Hfs", b, 0), ("Hfs", b, 1), ("HBs", b), ("HBhs", b, 0), ("HBhs", b, 1)], writes=[("Hfs2", b)])
                            S.op("dve", lambda h, b=b: h.tensor_tensor(out=Hfs2[:, b, :, :], in0=Hfs2[:, b, :, :], in1=WCs[:, :, b:b + 1].broadcast_to([128, 2, 64]), op=ALU.mult),
                                 reads=[("Hfs2", b), ("WCs", 0), ("WCs", 1)], writes=[("Hfs2", b)])
                            for oc in range(2):
                                ps, pk = psum()
                                S.op("pe", lambda h, ps=ps, b=b, oc=oc: h.transpose(out=ps[0:64, 0:128], in_=Hfs2[:, b, oc, :], identity=ident[:, :]), reads=[("Hfs2", b), "ident"], writes=[pk])
                                sk_ = ("sto", (b * 2 + oc) % 2)
                                S.op("act", lambda h, ps=ps, b=b, oc=oc: h.activation(out=sto[:, ((b * 2 + oc) % 2) * 2:((b * 2 + oc) % 2) * 2 + 2, :], in_=ps[0:64, 0:128].rearrange("p (a k) -> p a k", a=2), func=AF.Copy), reads=[pk], writes=[sk_])
                                S.dma("sp", o_srw[l, b, 2 * oc:2 * oc + 2].rearrange("a v k -> v a k"), sto[:, ((b * 2 + oc) % 2) * 2:((b * 2 + oc) % 2) * 2 + 2, :], reads=[sk_])
                        if SRW < 7:
                            break
                        for oc in range(2):
                            ps, pk = psum()
                            for hh in range(2):
                                h_, pb = oc * 2 + hh, hh * 64
                                for b in range(4):
                                    S.op("pe", lambda h, ps=ps, b=b, oc=oc, hh=hh, pb=pb: h.matmul(ps[pb:pb + 64, 0:64], lhsT=HBhs[:, b, hh, oc, :], rhs=ARbv[b][:, oc, :], start=(b == 0), stop=False),
                                         reads=[("HBhs", b, hh), ("ARb", b)], writes=[pk])
                                S.op("pe", lambda h, ps=ps, h_=h_, pb=pb: h.matmul(ps[pb:pb + 64, 0:64], lhsT=X[:, 0, h_, :], rhs=GTs[:, h_, 64:128], start=False, stop=True), reads=[("Xu", 0)] + xvk + gtk, writes=[pk])
                            yv = Yss[:, oc, 0:n]
                            S.op("act", lambda h, ps=ps, oc=oc: h.activation(out=Yss[:, oc, :], in_=ps[:, 0:64], func=AF.Copy), reads=[pk], writes=[("Yss", oc)])
                            ps, pk = psum()
                            S.op("pe", lambda h, ps=ps, yv=yv: h.matmul(ps[:, 0:n], lhsT=onesbd[:, :], rhs=yv, start=True, stop=True), reads=[("Yss", oc)], writes=[pk])
                            S.op("act", lambda h, ps=ps: h.activation(out=MBs[:, 0:n], in_=ps[:, 0:n], func=AF.Copy, scale=1.0 / 64), reads=[pk], writes=["MBs"])
                            S.op("dve", lambda h, yv=yv: h.tensor_tensor(out=yv, in0=yv, in1=MBs[:, 0:n], op=ALU.subtract), reads=["MBs", ("Yss", oc)], writes=[("Yss", oc)])
                            S.op("act", lambda h, yv=yv: h.activation(out=SQs[:, 0:n], in_=yv, func=AF.Square), reads=[("Yss", oc)], writes=["SQs"])
                            ps, pk = psum()
                            S.op("pe", lambda h, ps=ps: h.matmul(ps[:, 0:n], lhsT=onesbd[:, :], rhs=SQs[:, 0:n], start=True, stop=True), reads=["SQs"], writes=[pk])
                            S.op("act", lambda h, ps=ps: h.activation(out=MBs[:, 0:n], in_=ps[:, 0:n], func=AF.Sqrt, scale=1.0 / 64, bias=cge[:, 0:1]), reads=[pk], writes=["MBs"])
                            S.op("dve", lambda h: h.reciprocal(out=MBs[:, 0:n], in_=MBs[:, 0:n]), reads=["MBs"], writes=["MBs"])
                            S.op("dve", lambda h, yv=yv, oc=oc: h.scalar_tensor_tensor(out=yv, in0=MBs[:, 0:n], scalar=clg[:, oc:oc + 1], in1=yv, op0=ALU.mult, op1=ALU.mult), reads=["MBs", ("Yss", oc)], writes=[("Yss", oc)])
                            S.op("dve", lambda h, yv=yv, oc=oc: h.scalar_tensor_tensor(out=yv, in0=BONs[:, oc, :], scalar=clb[:, oc:oc + 1], in1=yv, op0=ALU.add, op1=ALU.add), reads=[("Yss", oc), ("BONs", oc)], writes=[("Yss", oc)])
                            S.op("dve", lambda h, yv=yv, oc=oc: h.tensor_tensor(out=yTs[:, oc, :], in0=yv, in1=Gts[:, oc, :], op=ALU.mult), reads=[("Yss", oc), ("Gts", oc)], writes=[("yTs", "a", oc)])
                        if "yas" in DBG and l == 0:
                            for oc in range(2):
                                S.dma("sp", DBG["yas"][oc], yTs[:, oc, :], reads=[("yTs", "a", oc)])
                if "ya" in DBG and os.environ.get("NOYA") is None:
                    for oc in range(2):
                        S.dma("sp", DBG["ya"][oc], yT[:, oc, :], reads=[("yT", FK["a"], oc, b_) for b_ in range(T // NB)])
                S.flush()

        if stage >= 10:
            with contextlib.ExitStack() as ph:
                wout = sb("wout", [128, KC, D], BF16, ph)
                load_w(wout, W["w_out"][l], 0, D, "wout")
                wk_ = [("wout", c) for c in range(KC)]
                for blk in range(4):
                    for oc in range(KC):
                        ps, pk = psum()
                        for c in range(KC):
                            S.op("pe", lambda h, ps=ps, c=c, oc=oc, blk=blk: h.matmul(ps[:, :], lhsT=wout[:, c, oc * 128:(oc + 1) * 128], rhs=yT[:, c, blk * 512:(blk + 1) * 512], start=(c == 0), stop=(c == KC - 1)),
                                 reads=wk_, writes=[pk])
                        S.op("dve", lambda h, ps=ps, oc=oc, blk=blk: h.tensor_tensor(out=xT[:, oc, blk * 512:(blk + 1) * 512], in0=ps[:, :], in1=xT[:, oc, blk * 512:(blk + 1) * 512], op=ALU.add),
                             reads=[pk], writes=[("x", oc, blk)])
                for oc in range(KC):
                    ps, pk = psum()
                    for c in range(KC):
                        S.op("pe", lambda h, ps=ps, c=c, oc=oc: h.matmul(ps[:, 0:NS], lhsT=wout[:, c, oc * 128:(oc + 1) * 128], rhs=yTs[:, c, :], start=(c == 0), stop=(c == KC - 1)), reads=wk_, writes=[pk])
                    S.op("dve", lambda h, ps=ps, oc=oc: h.tensor_tensor(out=xTs[:, oc, :], in0=ps[:, 0:NS], in1=xTs[:, oc, :], op=ALU.add), reads=[pk], writes=[("xs", oc)])
                if "x1" in DBG and l == 0:
                    for c in range(KC):
                        S.dma("sp", DBG["x1"][c], xT[:, c, :], reads=[("x", c, b_) for b_ in range(4)])
                        S.dma("sp", DBG["x1s"][c], xTs[:, c, :], reads=[("xs", c)])
                S.flush()
        lay.close()

        xa = contextlib.ExitStack()
        mkT = sb("mkT", [128, KC, 256], BF16, xa)
        mvb = sb("mvb", [128, 2, D], BF16, xa)
        if stage >= 9:
            with contextlib.ExitStack() as ph:
                wxk = sb("wxk", [128, KC, D], BF16, ph)
                wxv = sb("wxv", [128, KC, D], BF16, ph)
                gcol = sb("gcol", [128, KC], F32, ph)
                epsc = sb("epsc", [128, 1], F32, ph)
                gxk = sb("gxk", [128, 256], F32, ph)
                sq = sb("sq", [128, KC, 256], BF16, ph)
                rbc = sb("rbc", [128, 256], F32, ph)
                mn = sb("mn", [128, KC, 256], BF16, ph)
                tsq = sb("tsq", [128, 512], F32, ph)
                tkm = [sb(f"tkm{i}", [128, 512], F32, ph) for i in range(2)]
                tvm = [sb(f"tvm{i}", [128, 512], F32, ph) for i in range(2)]
                ssm = sb("ssm", [128, 8], F32, ph)
                scr = dict(sq=sq, rbc=rbc, eps=epsc)
                memT = sb("memT", [128, KC, 256], F32, ph)
                mtm = [sb(f"mtm{i}", [128, D], F32, ph) for i in range(2)]
                for mt in range(2):
                    S.dma("sp", mtm[mt][:, :], mem[mt * 128:(mt + 1) * 128, :], writes=[("mtm", mt)])
                    for half in range(2):
                        ps, pk = psum()
                        for q in range(4):
                            c = half * 4 + q
                            S.op("pe", lambda h, ps=ps, q=q, c=c, mt=mt: h.transpose(out=ps[:, q * 128:(q + 1) * 128], in_=mtm[mt][:, c * 128:(c + 1) * 128], identity=ident[:, :]),
                                 reads=[("mtm", mt), "ident"], writes=[pk])
                        S.op("act", lambda h, ps=ps, half=half, mt=mt: h.activation(out=memT[:, half * 4:half * 4 + 4, mt * 128:(mt + 1) * 128], in_=ps[:, :].rearrange("p (q t) -> p q t", q=4), func=AF.Copy),
                             reads=[pk], writes=[("memT", mt, half)])
                load_w(wxk, W["w_xk"][l], 0, D, "wxk")
                load_w(wxv, W["w_xv"][l], 0, D, "wxv")
                load_cols(gcol, W["g_mem"][l], KC, "gcol")
                S.op("pool", lambda h: h.memset(epsc[:], RMS_EPS), writes=["eps"])
                S.dma("sp", gxk[:, :], W["xk_norm"][l].partition_broadcast(128), writes=["gxk"])
                rmsnorm_fm(memT, 0, 256, gcol, mn, "mn", scr, src_keys=[("memT", mt_, hf_) for mt_ in range(2) for hf_ in range(2)])
                mnk = [("mn", c) for c in range(KC)]
                it = 0
                for mt in range(2):
                    for hf in range(2):
                        psk, pkk = psum()
                        for c in range(KC):
                            S.op("pe", lambda h, psk=psk, c=c, mt=mt, hf=hf: h.matmul(psk[:, :], lhsT=mn[:, c, mt * 128:(mt + 1) * 128], rhs=wxk[:, c, hf * 512:(hf + 1) * 512], start=(c == 0), stop=(c == KC - 1)),
                                 reads=mnk + [("wxk", c_) for c_ in range(KC)], writes=[pkk])
                        psv, pkv = psum()
                        for c in range(KC):
                            S.op("pe", lambda h, psv=psv, c=c, mt=mt, hf=hf: h.matmul(psv[:, :], lhsT=mn[:, c, mt * 128:(mt + 1) * 128], rhs=wxv[:, c, hf * 512:(hf + 1) * 512], start=(c == 0), stop=(c == KC - 1)),
                                 reads=mnk + [("wxv", c_) for c_ in range(KC)], writes=[pkv])
                        tk_, tv_ = tkm[it % 2], tvm[it % 2]
                        kk_, vk_, sk_ = ("tkm", it % 2), ("tvm", it % 2), ("ssm", it % 4)
                        ssv = ssm[:, (it % 4) * 2:(it % 4) * 2 + 2]
                        it += 1
                        S.op("act", lambda h, psk=psk: h.activation(out=tsq[:, :], in_=psk[:, :], func=AF.Square), reads=[pkk], writes=["tsq"])
                        S.op("dve", lambda h, ssv=ssv: h.tensor_reduce(out=ssv, in_=tsq[:, :].rearrange("p (h d) -> p h d", h=2), axis=AX.X, op=ALU.add), reads=["tsq"], writes=[sk_])
                        S.op("act", lambda h, ssv=ssv: h.activation(out=ssv, in_=ssv, func=AF.Sqrt, scale=1.0 / 256, bias=epsc[:, 0:1]), reads=[sk_, "eps"], writes=[sk_])
                        S.op("dve", lambda h, ssv=ssv: h.reciprocal(out=ssv, in_=ssv), reads=[sk_], writes=[sk_])
                        S.op("dve", lambda h, psk=psk, tk_=tk_, ssv=ssv: h.tensor_tensor(out=tk_[:, :].rearrange("p (h d) -> p h d", h=2), in0=psk[:, :].rearrange("p (h d) -> p h d", h=2),
                                                                                   in1=ssv.unsqueeze(2).broadcast_to([128, 2, 256]), op=ALU.mult), reads=[pkk, sk_], writes=[kk_])
                        S.op("dve", lambda h, tk_=tk_: h.tensor_tensor(out=tk_[:, :].rearrange("p (h d) -> p h d", h=2), in0=tk_[:, :].rearrange("p (h d) -> p h d", h=2),
                                                                     in1=gxk[:, :].unsqueeze(1).broadcast_to([128, 2, 256]), op=ALU.mult), reads=[kk_, "gxk"], writes=[kk_])
                        S.dma("sp", o_pmk[l, mt * 128:(mt + 1) * 128, hf * 512:(hf + 1) * 512], tk_[:, :], reads=[kk_])
                        ps, pk = psum()
                        for q in range(4):
                            S.op("pe", lambda h, ps=ps, q=q, tk_=tk_: h.transpose(out=ps[:, q * 128:(q + 1) * 128], in_=tk_[:, q * 128:(q + 1) * 128], identity=ident[:, :]), reads=[kk_, "ident"], writes=[pk])
                        S.op("act", lambda h, ps=ps, hf=hf, mt=mt: h.activation(out=mkT[:, hf * 4:hf * 4 + 4, mt * 128:(mt + 1) * 128], in_=ps[:, :].rearrange("p (q t) -> p q t", q=4), func=AF.Copy),
                             reads=[pk], writes=[("mkT", hf, mt)])
                        S.op("act", lambda h, psv=psv, tv_=tv_: h.activation(out=tv_[:, :], in_=psv[:, :], func=AF.Copy), reads=[pkv], writes=[vk_])
                        S.dma("sp", o_pmv[l, mt * 128:(mt + 1) * 128, hf * 512:(hf + 1) * 512], tv_[:, :], reads=[vk_])
                        S.op("pool", lambda h, tv_=tv_, hf=hf, mt=mt: h.tensor_copy(out=mvb[:, mt, hf * 512:(hf + 1) * 512], in_=tv_[:, :]), reads=[vk_], writes=[("mvb", hf, mt)])
                S.flush()

        if stage >= 11:
            with contextlib.ExitStack() as ph:
                TBX = 256
                wq = sb("wq", [128, KC, D], BF16, ph)
                wo = sb("wo", [128, KC, D], BF16, ph)
                gcol = sb("gcol", [128, KC], F32, ph)
                epsc = sb("epsc", [128, 1], F32, ph)
                xqs = sb("xqs", [128, 2], F32, ph)
                sq = sb("sq", [128, KC, TBX], BF16, ph)
                rbc = sb("rbc", [128, TBX], F32, ph)
                xn = sb("xn", [128, KC, TBX], BF16, ph)
                qraw = sb("qraw", [128, KC, TBX], F32, ph)
                qsq = sb("qsq", [128, KC, TBX], BF16, ph)
                qT = sb("qT", [128, KC, TBX], BF16, ph)
                rq = sb("rq", [128, TBX], F32, ph)
                PTx = [sb(f"PTx{i}", [128, TBX], BF16, ph) for i in range(4)]
                oT = sb("oT", [128, KC, TBX], BF16, ph)
                rden = sb("rden", [128, TBX], F32, ph)
                otmp = [sb(f"otmp{i}", [128, TBX], F32, ph) for i in range(2)]
                scr = dict(sq=sq, rbc=rbc, eps=epsc)
                load_w(wq, W["w_xq"][l], 0, D, "wq")
                load_w(wo, W["w_xo"][l], 0, D, "wo")
                wqk = [("wq", c) for c in range(KC)]
                wok = [("wo", c) for c in range(KC)]
                load_cols(gcol, W["g_x"][l], KC, "gcol")
                load_cols(xqs, W["xq_norm"][l], 2, "xqs")
                S.op("dve", lambda h: h.tensor_scalar(out=xqs[:, :], in0=xqs[:, :], scalar1=1.0 / 16, scalar2=None, op0=ALU.mult), reads=["xqs"], writes=["xqs"])
                S.op("pool", lambda h: h.memset(epsc[:], RMS_EPS), writes=["eps"])
                pti = 0
                oti = 0
                for blk in range(T // TBX):
                    b0 = blk * TBX
                    xk_ = [("x", c, blk) for c in range(KC)]
                    rmsnorm_fm(xT, b0, TBX, gcol, xn, "xn", scr, src_keys=xk_)
                    xnk = [("xn", c) for c in range(KC)]
                    for i in range(KC):
                        ps, pk = psum()
                        for c in range(KC):
                            S.op("pe", lambda h, ps=ps, c=c, i=i: h.matmul(ps[:, 0:TBX], lhsT=wq[:, c, i * 128:(i + 1) * 128], rhs=xn[:, c, :], start=(c == 0), stop=(c == KC - 1)), reads=xnk + wqk, writes=[pk])
                        S.op("act", lambda h, ps=ps, i=i: h.activation(out=qraw[:, i, :], in_=ps[:, 0:TBX], func=AF.Copy), reads=[pk], writes=[("qraw", i)])
                        S.op("act", lambda h, ps=ps, i=i: h.activation(out=qsq[:, i, :], in_=ps[:, 0:TBX], func=AF.Square), reads=[pk], writes=[("qsq", i)])
                    for hd in range(4):
                        ps, pk = psum()
                        for dc in range(2):
                            S.op("pe", lambda h, ps=ps, hd=hd, dc=dc: h.matmul(ps[:, 0:TBX], lhsT=onesb[:, :], rhs=qsq[:, hd * 2 + dc, :], start=(dc == 0), stop=(dc == 1)), reads=[("qsq", hd * 2 + dc), "onesb"], writes=[pk])
                        S.op("act", lambda h, ps=ps: h.activation(out=rq[:, :], in_=ps[:, 0:TBX], func=AF.Sqrt, scale=1.0 / 256, bias=epsc[:, 0:1]), reads=[pk, "eps"], writes=["rq"])
                        S.op("dve", lambda h: h.reciprocal(out=rq[:, :], in_=rq[:, :]), reads=["rq"], writes=["rq"])
                        for dc in range(2):
                            i = hd * 2 + dc
                            S.op("dve", lambda h, i=i, dc=dc: h.scalar_tensor_tensor(out=qT[:, i, :], in0=qraw[:, i, :], scalar=xqs[:, dc:dc + 1], in1=rq[:, :], op0=ALU.mult, op1=ALU.mult),
                                 reads=[("qraw", i), "xqs", "rq"], writes=[("qT", i)])
                    for hd in range(4):
                        pts = []
                        for mt in range(2):
                            ps, pk = psum()
                            for dc in range(2):
                                S.op("pe", lambda h, ps=ps, hd=hd, dc=dc, mt=mt: h.matmul(ps[:, 0:TBX], lhsT=mkT[:, hd * 2 + dc, mt * 128:(mt + 1) * 128], rhs=qT[:, hd * 2 + dc, :], start=(dc == 0), stop=(dc == 1)),
                                     reads=[("qT", hd * 2 + dc)], writes=[pk])
                            pt, ptk = PTx[pti % 4], ("PTx", pti % 4)
                            pti += 1
                            S.op("act", lambda h, ps=ps, pt=pt: h.activation(out=pt[:, :], in_=ps[:, 0:TBX], func=AF.Exp), reads=[pk], writes=[ptk])
                            pts.append((pt, ptk))
                        ps, pk = psum()
                        for mt in range(2):
                            S.op("pe", lambda h, ps=ps, mt=mt, pt=pts[mt][0]: h.matmul(ps[:, 0:TBX], lhsT=onesb[:, :], rhs=pt[:, :], start=(mt == 0), stop=(mt == 1)), reads=[pts[mt][1], "onesb"], writes=[pk])
                        S.op("dve", lambda h, ps=ps: h.reciprocal(out=rden[:, :], in_=ps[:, 0:TBX]), reads=[pk], writes=["rden"])
                        for dc in range(2):
                            i = hd * 2 + dc
                            ps, pk = psum()
                            for mt in range(2):
                                S.op("pe", lambda h, ps=ps, mt=mt, i=i, pt=pts[mt][0]: h.matmul(ps[:, 0:TBX], lhsT=mvb[:, mt, i * 128:(i + 1) * 128], rhs=pt[:, :], start=(mt == 0), stop=(mt == 1)), reads=[pts[mt][1]], writes=[pk])
                            ot, otk = otmp[oti % 2], ("otmp", oti % 2)
                            oti += 1
                            S.op("dve", lambda h, ps=ps, ot=ot: h.tensor_tensor(out=ot[:, :], in0=ps[:, 0:TBX], in1=rden[:, :], op=ALU.mult), reads=[pk, "rden"], writes=[otk])
                            S.op("act", lambda h, ot=ot, i=i: h.activation(out=oT[:, i, :], in_=ot[:, :], func=AF.Copy), reads=[otk], writes=[("oT", i)])
                    otks = [("oT", i) for i in range(KC)]
                    for oc in range(KC):
                        ps, pk = psum()
                        for c in range(KC):
                            S.op("pe", lambda h, ps=ps, c=c, oc=oc: h.matmul(ps[:, 0:TBX], lhsT=wo[:, c, oc * 128:(oc + 1) * 128], rhs=oT[:, c, :], start=(c == 0), stop=(c == KC - 1)), reads=otks + wok, writes=[pk])
                        S.op("dve", lambda h, ps=ps, oc=oc, b0=b0: h.tensor_tensor(out=xT[:, oc, b0:b0 + TBX], in0=ps[:, 0:TBX], in1=xT[:, oc, b0:b0 + TBX], op=ALU.add), reads=[pk], writes=[("x", oc, blk)])

                if stage >= 15:
                    cks = [sb(f"cks{i}", [128, D], F32, ph) for i in range(2)]
                    cvs = [sb(f"cvs{i}", [128, D], F32, ph) for i in range(2)]
                    mkTs = sb("mkTs", [128, KC, 256], BF16, ph)
                    mvs = sb("mvs", [128, 2, D], BF16, ph)
                    rmsnorm_fm(xTs, 0, NS, gcol, xn, "xns", scr)
                    xnk = [("xns", c) for c in range(KC)]
                    for i in range(KC):
                        ps, pk = psum()
                        for c in range(KC):
                            S.op("pe", lambda h, ps=ps, c=c, i=i: h.matmul(ps[:, 0:NS], lhsT=wq[:, c, i * 128:(i + 1) * 128], rhs=xn[:, c, 0:NS], start=(c == 0), stop=(c == KC - 1)), reads=xnk + wqk, writes=[pk])
                        S.op("act", lambda h, ps=ps, i=i: h.activation(out=qraw[:, i, 0:NS], in_=ps[:, 0:NS], func=AF.Copy), reads=[pk], writes=[("qraws", i)])
                        S.op("act", lambda h, ps=ps, i=i: h.activation(out=qsq[:, i, 0:NS], in_=ps[:, 0:NS], func=AF.Square), reads=[pk], writes=[("qsqs", i)])
                    for hd in range(4):
                        ps, pk = psum()
                        for dc in range(2):
                            S.op("pe", lambda h, ps=ps, hd=hd, dc=dc: h.matmul(ps[:, 0:NS], lhsT=onesb[:, :], rhs=qsq[:, hd * 2 + dc, 0:NS], start=(dc == 0), stop=(dc == 1)), reads=[("qsqs", hd * 2 + dc), "onesb"], writes=[pk])
                        S.op("act", lambda h, ps=ps: h.activation(out=rq[:, 0:NS], in_=ps[:, 0:NS], func=AF.Sqrt, scale=1.0 / 256, bias=epsc[:, 0:1]), reads=[pk, "eps"], writes=["rqs"])
                        S.op("dve", lambda h: h.reciprocal(out=rq[:, 0:NS], in_=rq[:, 0:NS]), reads=["rqs"], writes=["rqs"])
                        for dc in range(2):
                            i = hd * 2 + dc
                            S.op("dve", lambda h, i=i, dc=dc: h.scalar_tensor_tensor(out=qT[:, i, 0:NS], in0=qraw[:, i, 0:NS], scalar=xqs[:, dc:dc + 1], in1=rq[:, 0:NS], op0=ALU.mult, op1=ALU.mult),
                                 reads=[("qraws", i), "xqs", "rqs"], writes=[("qTs", i)])
                    for b in range(4):
                        for mt in range(2):
                            S.dma("sp", cks[mt][:, :], cmk[l, b, mt * 128:(mt + 1) * 128, :], writes=[("cks", mt)])
                            S.dma("sp", cvs[mt][:, :], cmv[l, b, mt * 128:(mt + 1) * 128, :], writes=[("cvs", mt)])
                            for half in range(2):
                                ps, pk = psum()
                                for q in range(4):
                                    c = half * 4 + q
                                    S.op("pe", lambda h, ps=ps, q=q, c=c, mt=mt: h.transpose(out=ps[:, q * 128:(q + 1) * 128], in_=cks[mt][:, c * 128:(c + 1) * 128], identity=ident[:, :]), reads=[("cks", mt), "ident"], writes=[pk])
                                S.op("act", lambda h, ps=ps, half=half, mt=mt: h.activation(out=mkTs[:, half * 4:half * 4 + 4, mt * 128:(mt + 1) * 128], in_=ps[:, :].rearrange("p (q t) -> p q t", q=4), func=AF.Copy),
                                     reads=[pk], writes=[("mkTs", mt, half)])
                            S.op("pool", lambda h, mt=mt: h.tensor_copy(out=mvs[:, mt, :], in_=cvs[mt][:, :]), reads=[("cvs", mt)], writes=[("mvs", mt)])
                        mkk = [("mkTs", mt, half) for mt in range(2) for half in range(2)]
                        for hd in range(4):
                            pts = []
                            for mt in range(2):
                                ps, pk = psum()
                                for dc in range(2):
                                    S.op("pe", lambda h, ps=ps, hd=hd, dc=dc, mt=mt, b=b: h.matmul(ps[:, 0:4], lhsT=mkTs[:, hd * 2 + dc, mt * 128:(mt + 1) * 128], rhs=qT[:, hd * 2 + dc, 4 * b:4 * b + 4], start=(dc == 0), stop=(dc == 1)),
                                         reads=[("qTs", hd * 2 + dc)] + mkk, writes=[pk])
                                pt, ptk = PTx[pti % 4], ("PTx", pti % 4)
                                pti += 1
                                S.op("act", lambda h, ps=ps, pt=pt: h.activation(out=pt[:, 0:4], in_=ps[:, 0:4], func=AF.Exp), reads=[pk], writes=[ptk])
                                pts.append((pt, ptk))
                            ps, pk = psum()
                            for mt in range(2):
                                S.op("pe", lambda h, ps=ps, mt=mt, pt=pts[mt][0]: h.matmul(ps[:, 0:4], lhsT=onesb[:, :], rhs=pt[:, 0:4], start=(mt == 0), stop=(mt == 1)), reads=[pts[mt][1], "onesb"], writes=[pk])
                            S.op("dve", lambda h, ps=ps: h.reciprocal(out=rden[:, 0:4], in_=ps[:, 0:4]), reads=[pk], writes=["rdens"])
                            for dc in range(2):
                                i = hd * 2 + dc
                                ps, pk = psum()
                                for mt in range(2):
                                    S.op("pe", lambda h, ps=ps, mt=mt, i=i, pt=pts[mt][0]: h.matmul(ps[:, 0:4], lhsT=mvs[:, mt, i * 128:(i + 1) * 128], rhs=pt[:, 0:4], start=(mt == 0), stop=(mt == 1)), reads=[pts[mt][1], ("mvs", mt)], writes=[pk])
                                ot, otk = otmp[oti % 2], ("otmp", oti % 2)
                                oti += 1
                                S.op("dve", lambda h, ps=ps, ot=ot: h.tensor_tensor(out=ot[:, 0:4], in0=ps[:, 0:4], in1=rden[:, 0:4], op=ALU.mult), reads=[pk, "rdens"], writes=[otk])
                                S.op("act", lambda h, ot=ot, i=i, b=b: h.activation(out=oT[:, i, 4 * b:4 * b + 4], in_=ot[:, 0:4], func=AF.Copy), reads=[otk], writes=[("oTs", i, b)])
                    otks = [("oTs", i, b) for i in range(KC) for b in range(4)]
                    for oc in range(KC):
                        ps, pk = psum()
                        for c in range(KC):
                            S.op("pe", lambda h, ps=ps, c=c, oc=oc: h.matmul(ps[:, 0:NS], lhsT=wo[:, c, oc * 128:(oc + 1) * 128], rhs=oT[:, c, 0:NS], start=(c == 0), stop=(c == KC - 1)), reads=otks + wok, writes=[pk])
                        S.op("dve", lambda h, ps=ps, oc=oc: h.tensor_tensor(out=xTs[:, oc, :], in0=ps[:, 0:NS], in1=xTs[:, oc, :], op=ALU.add), reads=[pk], writes=[("xs", oc)])
                    if "x2s" in DBG and l == 0:
                        for c in range(KC):
                            S.dma("sp", DBG["x2s"][c], xTs[:, c, :], reads=[("xs", c)])
                if "x2" in DBG and l == 0:
                    for c in range(KC):
                        S.dma("sp", DBG["x2"][c], xT[:, c, :], reads=[("x", c, b_) for b_ in range(T // TBX)])
                S.flush()
        xa.close()

        if stage >= 12:
            with contextlib.ExitStack() as ph:
                TBF, NJ = 512, 22
                wdn = sb("wdn", [128, NJ, D], BF16, ph)
                gcol = sb("gcol", [128, KC], F32, ph)
                epsc = sb("epsc", [128, 1], F32, ph)
                cw = sb("cw", [128, 2 * NJ, 3], F32, ph)
                cb = sb("cb", [128, 2 * NJ], F32, ph)
                act_ = sb("act", [128, NJ, TBF], BF16, ph)
                wsl = [sb(f"wsl{i}", [128, KC, 256], BF16, ph) for i in range(3)]
                sq = sb("sq", [128, KC, TBF], BF16, ph)
                rbc = sb("rbc", [128, TBF], F32, ph)
                xn = sb("xn", [128, KC, TBF], BF16, ph)
                hb = [sb(f"hb{i}", [128, 2, TBF + 2], F32, ph) for i in range(2)]
                halo = sb("halo", [128, 2 * NJ, 2], F32, ph)
                ta = sb("ta", [128, 2, TBF], F32, ph)
                tb = sb("tb", [128, 2, TBF], F32, ph)
                sg = sb("sg", [128, TBF], F32, ph)
                stg = [sb(f"stg{i}", [2, 512], F32, ph) for i in range(2)]
                scr = dict(sq=sq, rbc=rbc, eps=epsc)
                load_w(wdn, W["w_down"][l], 0, D, "wdn")
                wdk = [("wdn", c) for c in range(NJ)]
                load_cols(gcol, W["g_ffn"][l], KC, "gcol")
                for tap in range(3):
                    S.dma("sp", cw[:, :, tap], W["ffn_conv_w"][l][tap].rearrange("(j p) -> p j", p=128), writes=[("cw", tap)], allow_slow_non_contiguous=True)
                cwk = [("cw", tap) for tap in range(3)]
                load_cols(cb, W["ffn_conv_b"][l], 2 * NJ, "cb")
                S.op("pool", lambda h: h.memset(epsc[:], RMS_EPS), writes=["eps"])
                S.op("pool", lambda h: h.memset(halo[:].rearrange("p a b -> p (a b)"), 0.0), writes=["halo0"])
                S.flush()
                wi_ = 0
                for blk in range(T // TBF):
                    b0 = blk * TBF
                    xk_ = [("x", c, blk) for c in range(KC)]
                    rmsnorm_fm(xT, b0, TBF, gcol, xn, "xn", scr, src_keys=xk_)
                    xnk = [("xn", c) for c in range(KC)]
                    for j in range(NJ):
                        w_ = wsl[wi_ % 3]
                        wk2 = [("wsl", wi_ % 3, 0), ("wsl", wi_ % 3, 1)]
                        wi_ += 1
                        for g in range(2):
                            S.dma("pool", w_[:, :, g * 128:(g + 1) * 128], W["w_up"][l][:, g * D_FF + j * 128:g * D_FF + (j + 1) * 128].rearrange("(c p) n -> p c n", p=128), writes=[wk2[g]])
                        h_ = hb[j % 2]
                        for g in range(2):
                            jj = j + NJ * g
                            hk = ("hb", j % 2, g)
                            ps, pk = psum()
                            for c in range(KC):
                                S.op("pe", lambda h, ps=ps, c=c, g=g, w_=w_: h.matmul(ps[:, :], lhsT=w_[:, c, g * 128:(g + 1) * 128], rhs=xn[:, c, :], start=(c == 0), stop=(c == KC - 1)), reads=xnk + [wk2[g]], writes=[pk])
                            S.op("dve", lambda h, h_=h_, g=g, jj=jj: h.tensor_copy(out=h_[:, g, 0:2], in_=halo[:, jj, :]), reads=[("halo", jj)], writes=[hk])
                            S.op("act", lambda h, ps=ps, h_=h_, g=g: h.activation(out=h_[:, g, 2:TBF + 2], in_=ps[:, :], func=AF.Copy), reads=[pk, hk], writes=[hk])
                            S.op("dve", lambda h, h_=h_, g=g, jj=jj: h.tensor_copy(out=halo[:, jj, :], in_=h_[:, g, TBF:TBF + 2]), reads=[hk], writes=[("halo", jj)])
                            S.op("dve", lambda h, h_=h_, g=g, jj=jj: h.tensor_scalar(out=ta[:, g, :], in0=h_[:, g, 0:TBF], scalar1=cw[:, jj, 0:1], scalar2=cb[:, jj:jj + 1], op0=ALU.mult, op1=ALU.add),
                                 reads=[hk, "cb"] + cwk, writes=[("ta", g)])
                            S.op("dve", lambda h, h_=h_, g=g, jj=jj: h.scalar_tensor_tensor(out=tb[:, g, :], in0=h_[:, g, 1:TBF + 1], scalar=cw[:, jj, 1:2], in1=ta[:, g, :], op0=ALU.mult, op1=ALU.add),
                                 reads=[hk, ("ta", g)] + cwk, writes=[("tb", g)])
                            S.op("dve", lambda h, h_=h_, g=g, jj=jj: h.scalar_tensor_tensor(out=ta[:, g, :], in0=h_[:, g, 2:TBF + 2], scalar=cw[:, jj, 2:3], in1=tb[:, g, :], op0=ALU.mult, op1=ALU.add),
                                 reads=[hk, ("tb", g)] + cwk, writes=[("ta", g)])
                        S.op("act", lambda h: h.activation(out=sg[:, :], in_=ta[:, 0, :], func=AF.Silu), reads=[("ta", 0)], writes=["sg"])
                        S.op("dve", lambda h, j=j: h.tensor_tensor(out=act_[:, j, :], in0=sg[:, :], in1=ta[:, 1, :], op=ALU.mult), reads=["sg", ("ta", 1)], writes=[("act", j)])
                    actk = [("act", j) for j in range(NJ)]
                    for oc in range(KC):
                        ps, pk = psum()
                        for j in range(NJ):
                            S.op("pe", lambda h, ps=ps, j=j, oc=oc: h.matmul(ps[:, :], lhsT=wdn[:, j, oc * 128:(oc + 1) * 128], rhs=act_[:, j, :], start=(j == 0), stop=(j == NJ - 1)), reads=actk + wdk, writes=[pk])
                        S.op("dve", lambda h, ps=ps, oc=oc, b0=b0: h.tensor_tensor(out=xT[:, oc, b0:b0 + TBF], in0=ps[:, :], in1=xT[:, oc, b0:b0 + TBF], op=ALU.add), reads=[pk], writes=[("x", oc, blk)])

                if stage >= 15:
                    sfs = [sb(f"sfs{i}", [8, 512], F32, ph) for i in range(2)]
                    halos = sb("halos", [128, 2 * NJ, 4, 2], F32, ph)
                    haloo = sb("haloo", [128, 2 * NJ, 4, 2], F32, ph)
                    hbS = [sb(f"hbS{i}", [128, 2, 4, 6], F32, ph) for i in range(2)]
                    taS = sb("taS", [128, 2, NS], F32, ph)
                    tbS = sb("tbS", [128, 2, NS], F32, ph)
                    for g11 in range(11):
                        sf_ = sfs[g11 % 2]
                        S.dma("sp", sf_[:, :], sff[l][:, g11 * 512:(g11 + 1) * 512], writes=[("sfs", g11 % 2)])
                        ps, pk = psum()
                        for q in range(4):
                            S.op("pe", lambda h, ps=ps, q=q, sf_=sf_: h.transpose(out=ps[:, q * 8:(q + 1) * 8], in_=sf_[0:8, q * 128:(q + 1) * 128], identity=ident[0:8, 0:8]), reads=[("sfs", g11 % 2), "ident"], writes=[pk])
                        S.op("act", lambda h, ps=ps, g11=g11: h.activation(out=halos[:, g11 * 4:(g11 + 1) * 4, :, :].rearrange("p a b t -> p a (b t)"), in_=ps[:, 0:32].rearrange("p (a k) -> p a k", a=4), func=AF.Copy), reads=[pk], writes=[("halos", g11)])
                    rmsnorm_fm(xTs, 0, NS, gcol, xn, "xns", scr)
                    xnk = [("xns", c) for c in range(KC)]
                    for j in range(NJ):
                        w_ = wsl[wi_ % 3]
                        wk2 = [("wsl", wi_ % 3, 0), ("wsl", wi_ % 3, 1)]
                        wi_ += 1
                        for g in range(2):
                            S.dma("pool", w_[:, :, g * 128:(g + 1) * 128], W["w_up"][l][:, g * D_FF + j * 128:g * D_FF + (j + 1) * 128].rearrange("(c p) n -> p c n", p=128), writes=[wk2[g]])
                        hS = hbS[j % 2]
                        for g in range(2):
                            jj = j + NJ * g
                            hk = ("hbS", j % 2, g)
                            ps, pk = psum()
                            for c in range(KC):
                                S.op("pe", lambda h, ps=ps, c=c, g=g, w_=w_: h.matmul(ps[:, 0:NS], lhsT=w_[:, c, g * 128:(g + 1) * 128], rhs=xn[:, c, 0:NS], start=(c == 0), stop=(c == KC - 1)), reads=xnk + [wk2[g]], writes=[pk])
                            S.op("dve", lambda h, hS=hS, g=g, jj=jj: h.tensor_copy(out=hS[:, g, :, 0:2], in_=halos[:, jj, :, :]), reads=[("halos", jj // 4)], writes=[hk])
                            S.op("act", lambda h, ps=ps, hS=hS, g=g: h.activation(out=hS[:, g, :, 2:6], in_=ps[:, 0:NS].rearrange("p (b t) -> p b t", b=4), func=AF.Copy), reads=[pk, hk], writes=[hk])
                            S.op("dve", lambda h, hS=hS, g=g, jj=jj: h.tensor_copy(out=haloo[:, jj, :, :], in_=hS[:, g, :, 4:6]), reads=[hk], writes=[("haloo", jj)])
                            tav = taS[:, g, :].rearrange("p (b t) -> p b t", b=4)
                            tbv = tbS[:, g, :].rearrange("p (b t) -> p b t", b=4)
                            S.op("dve", lambda h, hS=hS, g=g, jj=jj, tav=tav: h.tensor_scalar(out=tav, in0=hS[:, g, :, 0:4], scalar1=cw[:, jj, 0:1], scalar2=cb[:, jj:jj + 1], op0=ALU.mult, op1=ALU.add), reads=[hk, "cb"] + cwk, writes=[("taS", g)])
                            S.op("dve", lambda h, hS=hS, g=g, jj=jj, tav=tav, tbv=tbv: h.scalar_tensor_tensor(out=tbv, in0=hS[:, g, :, 1:5], scalar=cw[:, jj, 1:2], in1=tav, op0=ALU.mult, op1=ALU.add), reads=[hk, ("taS", g)] + cwk, writes=[("tbS", g)])
                            S.op("dve", lambda h, hS=hS, g=g, jj=jj, tav=tav, tbv=tbv: h.scalar_tensor_tensor(out=tav, in0=hS[:, g, :, 2:6], scalar=cw[:, jj, 2:3], in1=tbv, op0=ALU.mult, op1=ALU.add), reads=[hk, ("tbS", g)] + cwk, writes=[("taS", g)])
                        S.op("act", lambda h: h.activation(out=sg[:, 0:NS], in_=taS[:, 0, :], func=AF.Silu), reads=[("taS", 0)], writes=["sgs"])
                        S.op("dve", lambda h, j=j: h.tensor_tensor(out=act_[:, j, 0:NS], in0=sg[:, 0:NS], in1=taS[:, 1, :], op=ALU.mult), reads=["sgs", ("taS", 1)], writes=[("acts", j)])
                    actk = [("acts", j) for j in range(NJ)]
                    for oc in range(KC):
                        ps, pk = psum()
                        for j in range(NJ):
                            S.op("pe", lambda h, ps=ps, j=j, oc=oc: h.matmul(ps[:, 0:NS], lhsT=wdn[:, j, oc * 128:(oc + 1) * 128], rhs=act_[:, j, 0:NS], start=(j == 0), stop=(j == NJ - 1)), reads=actk + wdk, writes=[pk])
                        S.op("dve", lambda h, ps=ps, oc=oc: h.tensor_tensor(out=xTs[:, oc, :], in0=ps[:, 0:NS], in1=xTs[:, oc, :], op=ALU.add), reads=[pk], writes=[("xs", oc)])
                    for g11 in range(11):
                        ps, pk = psum()
                        for q in range(4):
                            jj = g11 * 4 + q
                            S.op("pe", lambda h, ps=ps, q=q, jj=jj: h.transpose(out=ps[0:8, q * 128:(q + 1) * 128], in_=haloo[:, jj, :, :].rearrange("p b t -> p (b t)"), identity=ident[:, :]), reads=[("haloo", jj), "ident"], writes=[pk])
                        sf_ = sfs[g11 % 2]
                        S.op("act", lambda h, ps=ps, sf_=sf_: h.activation(out=sf_[:, :], in_=ps[0:8, :], func=AF.Copy), reads=[pk], writes=[("sfo", g11 % 2)])
                        S.dma("sp", o_sff[l][:, g11 * 512:(g11 + 1) * 512], sf_[:, :], reads=[("sfo", g11 % 2)])
                    if "x3s" in DBG and l == 0:
                        for c in range(KC):
                            S.dma("sp", DBG["x3s"][c], xTs[:, c, :], reads=[("xs", c)])
                for g11 in range(11):
                    ps, pk = psum()
                    for q in range(4):
                        jj = g11 * 4 + q
                        S.op("pe", lambda h, ps=ps, q=q, jj=jj: h.transpose(out=ps[0:2, q * 128:(q + 1) * 128], in_=halo[:, jj, :], identity=ident[:, :]), reads=[("halo", jj), "ident"], writes=[pk])
                    sg_ = stg[g11 % 2]
                    S.op("act", lambda h, ps=ps, sg_=sg_: h.activation(out=sg_[:, :], in_=ps[0:2, :], func=AF.Copy), reads=[pk], writes=[("stg", g11 % 2)])
                    S.dma("sp", o_pff[l][:, g11 * 512:(g11 + 1) * 512], sg_[:, :], reads=[("stg", g11 % 2)])
                if "x3" in DBG and l == 0:
                    for c in range(KC):
                        S.dma("sp", DBG["x3"][c], xT[:, c, :], reads=[("x", c, b_) for b_ in range(T // TBF)])
                S.flush()

    if stage >= 12:
        with contextlib.ExitStack() as ph:
            ysb = [sb(f"ysb{i}", [128, D], F32, ph) for i in range(2)]
            for tt in range(NT):
                y_ = ysb[tt % 2]
                for half in range(2):
                    ps, pk = psum()
                    for q in range(4):
                        S.op("pe", lambda h, ps=ps, q=q, half=half, tt=tt: h.transpose(out=ps[:, q * 128:(q + 1) * 128], in_=xT[:, half * 4 + q, tt * 128:(tt + 1) * 128], identity=ident[:, :]), reads=["ident"], writes=[pk])
                    if half == 0:
                        S.op("act", lambda h, ps=ps, y_=y_: h.activation(out=y_[:, 0:512], in_=ps[:, :], func=AF.Copy), reads=[pk], writes=[("ysb", tt % 2, 0)])
                    else:
                        S.op("dve", lambda h, ps=ps, y_=y_: h.tensor_copy(out=y_[:, 512:1024], in_=ps[:, :]), reads=[pk], writes=[("ysb", tt % 2, 1)])
                S.dma("sp", o_yp[tt * 128:(tt + 1) * 128, :], y_[:, :], reads=[("ysb", tt % 2, 0), ("ysb", tt % 2, 1)])
            yss = sb("yss", [NS, D], F32, ph)
            for half in range(2):
                ps, pk = psum()
                for q in range(4):
                    S.op("pe", lambda h, ps=ps, q=q, half=half: h.transpose(out=ps[0:NS, q * 128:(q + 1) * 128], in_=xTs[:, half * 4 + q, :], identity=ident[:, :]), reads=["ident"], writes=[pk])
                S.op("act", lambda h, ps=ps, half=half: h.activation(out=yss[:, half * 512:(half + 1) * 512], in_=ps[0:NS, :], func=AF.Copy), reads=[pk], writes=[("yss", half)])
            S.dma("sp", o_ys[:, :], yss[:, :], reads=[("yss", 0), ("yss", 1)])
            S.flush()

    S.flush()
    st.close()
    return nc


def _core_inputs(inp, b):
    sl = slice(4 * b, 4 * b + 4)
    m = {
        "xp": np.ascontiguousarray(inp["x_prompt"][b]),
        "xs": np.ascontiguousarray(inp["x_sample"][sl].reshape(NS, D)),
        "mem": np.ascontiguousarray(inp["mem_prompt"][b]),
    }
    m["scv"] = np.ascontiguousarray(inp["state_conv"][:, sl])
    m["srw"] = np.ascontiguousarray(inp["state_rwkv"][:, sl])
    for l_ in range(DEPTH):
        m[f"fk{l_}"] = inp["cache_fox_k"][l_].reshape(-1, 512)
        m[f"fv{l_}"] = inp["cache_fox_v"][l_].reshape(-1, 512)
        m[f"flf{l_}"] = inp["cache_fox_logf"][l_].reshape(-1, 1024)
    m["ptab"] = np.ascontiguousarray(inp["page_table"][sl]).astype(np.int32)
    m["cmk"] = np.ascontiguousarray(inp["cache_mem_k"][:, sl]).reshape(DEPTH, 4, 256, D)
    m["cmv"] = np.ascontiguousarray(inp["cache_mem_v"][:, sl]).reshape(DEPTH, 4, 256, D)
    m["sff"] = np.ascontiguousarray(inp["state_ffn"][:, sl]).reshape(DEPTH, 8, 2 * D_FF)
    m["ssh"] = np.ascontiguousarray(inp["state_rwkv_shift"][:, sl, 0])
    for k in ("g_mix", "w_in", "fox_q_norm", "fox_k_norm", "fox_b_f", "conv_w", "conv_b", "conv_ln_g", "conv_ln_b",
              "rwkv_mu", "rwkv_w0", "rwkv_w_up", "rwkv_a0", "rwkv_a_up", "rwkv_g_up", "rwkv_k_k", "rwkv_k_a", "rwkv_ln_g", "rwkv_ln_b",
              "g_mem", "w_xk", "w_xv", "xk_norm", "w_out", "g_x", "w_xq", "xq_norm", "w_xo", "g_ffn", "w_up", "ffn_conv_w", "ffn_conv_b", "w_down"):
        m[k] = np.ascontiguousarray(inp[k])
    m["rwkv_r_k"] = np.ascontiguousarray(inp["rwkv_r_k"].reshape(DEPTH, 256))
    return m


def run(inputs, cores=tuple(range(NCORES)), stage=99):
    nc = build(stage)
    in_maps = [_core_inputs(inputs, b) for b in cores]
    res = run_bass_kernel_spmd(nc, in_maps, core_ids=list(range(len(cores))), trace=bool(os.environ.get("KTRACE")))
    if os.environ.get("KTRACE"):
        print("KTRACE exec_time_ns", res.exec_time_ns)
    return res.results


def kernel(**inputs):
    inputs = {k: np.asarray(v) for k, v in inputs.items()}
    r = run(inputs)
    B, DB = 8, 32
    f = np.float32
    out = dict(
        y_prompt=np.zeros((B, T, D), f), y_sample=np.zeros((DB, 4, D), f),
        p_fox_k=np.zeros((DEPTH, B, T, 8, 64), f), p_fox_v=np.zeros((DEPTH, B, T, 8, 64), f), p_fox_logf=np.zeros((DEPTH, B, T, 8), f),
        p_rwkv=np.zeros((DEPTH, B, 4, 64, 64), f), p_rwkv_shift=np.zeros((DEPTH, B, 1, 1024), f),
        p_conv=np.zeros((DEPTH, B, 30, 256), f), p_ffn=np.zeros((DEPTH, B, 2, 5632), f),
        p_mem_k=np.zeros((DEPTH, B, 256, 4, 256), f), p_mem_v=np.zeros((DEPTH, B, 256, 4, 256), f),
        s_fox_k=np.zeros((DEPTH, DB, 4, 8, 64), f), s_fox_v=np.zeros((DEPTH, DB, 4, 8, 64), f), s_fox_logf=np.zeros((DEPTH, DB, 4, 8), f),
        s_rwkv=np.zeros((DEPTH, DB, 4, 64, 64), f), s_rwkv_shift=np.zeros((DEPTH, DB, 1, 1024), f),
        s_conv=np.zeros((DEPTH, DB, 30, 256), f), s_ffn=np.zeros((DEPTH, DB, 2, 5632), f),
    )
    for b in range(NCORES):
        rb = r[b]
        sl = slice(4 * b, 4 * b + 4)
        out["p_fox_k"][:, b] = rb["p_fox_k"].reshape(DEPTH, T, 8, 64)
        out["p_fox_v"][:, b] = rb["p_fox_v"].reshape(DEPTH, T, 8, 64)
        out["p_fox_logf"][:, b] = rb["p_fox_logf"]
        out["s_fox_k"][:, sl] = rb["s_fox_k"].reshape(DEPTH, 4, 4, 8, 64)
        out["s_fox_v"][:, sl] = rb["s_fox_v"].reshape(DEPTH, 4, 4, 8, 64)
        out["s_fox_logf"][:, sl] = rb["s_fox_logf"].reshape(DEPTH, 4, 4, 8)
        out["p_conv"][:, b] = rb["p_conv"]
        out["s_conv"][:, sl] = rb["s_conv"]
        out["s_ffn"][:, sl] = rb["s_ffn"].reshape(DEPTH, 4, 2, 2 * D_FF)
        out["y_sample"][sl] = rb["y_sample"].reshape(4, 4, D)
        out["p_rwkv"][:, b] = rb["p_rwkv"]
        out["p_ffn"][:, b] = rb["p_ffn"]
        out["y_prompt"][b] = rb["y_prompt"]
        out["p_mem_k"][:, b] = rb["p_mem_k"].reshape(DEPTH, 256, 4, 256)
        out["p_mem_v"][:, b] = rb["p_mem_v"].reshape(DEPTH, 256, 4, 256)
        out["p_rwkv_shift"][:, b, 0] = rb["p_rwkv_shift"]
        out["s_rwkv"][:, sl] = rb["s_rwkv"]
        out["s_rwkv_shift"][:, sl, 0] = rb["s_rwkv_shift"]
    order = ["y_prompt", "y_sample", "p_fox_k", "p_fox_v", "p_fox_logf", "p_rwkv", "p_rwkv_shift", "p_conv", "p_ffn",
             "p_mem_k", "p_mem_v", "s_fox_k", "s_fox_v", "s_fox_logf", "s_rwkv", "s_rwkv_shift", "s_conv", "s_ffn"]
    return tuple(out[k] for k in order)
```

```python
import contextlib
import os
RWSUB = int(os.environ.get('RWSUB', '99'))
RWX = int(os.environ.get('RWX', '99'))
ZI = int(os.environ.get('ZI', '0'))
SRW = int(os.environ.get('SRW', '7'))
XVE = int(os.environ.get('XVE', '0'))
import numpy as np
import concourse.bass as bass
import concourse.mybir as mybir
from concourse.bass_utils import run_bass_kernel_spmd

F32 = mybir.dt.float32
BF16 = mybir.dt.bfloat16
I32 = mybir.dt.int32
AF = mybir.ActivationFunctionType
ALU = mybir.AluOpType
AX = mybir.AxisListType

NCORES = 8
D = 1024
KC = 8
T = 2048
NT = 16
NS = 16
DEPTH = 2
H_B = 8
IN_COLS = 3080
FOX0 = 1024
CONV0 = 1024 + 1544
D_FF = 2816
RMS_EPS = 1e-6
SAME_ENGINE_SYNC = os.environ.get("SES", "1") == "1"


class Sched:
    def __init__(self, nc, st):
        self.nc = nc
        self.st = st
        self.E = {}
        for n, h in (("pe", nc.tensor), ("act", nc.scalar), ("dve", nc.vector), ("pool", nc.gpsimd), ("sp", nc.sync)):
            self.E[n] = dict(h=h, sem=st.enter_context(nc.semaphore("q_" + n)), cnt=0, known={}, prog=[], gen=0)
        self.dpool = {"sp": [dict(sem=st.enter_context(nc.semaphore(f"dqh{i}")), val=0, i=("h", i)) for i in range(28)],
                      "pool": [dict(sem=st.enter_context(nc.semaphore(f"dqs{i}")), val=0, i=("s", i)) for i in range(12)]}
        self.dsems = self.dpool["sp"] + self.dpool["pool"]
        self.di = {"sp": 0, "pool": 0}
        self.lw = {}
        self.rd = {}

    def _deps(self, reads, writes):
        d = []
        for k in reads:
            if k in self.lw:
                d.append(self.lw[k])
        for k in writes:
            if k in self.lw:
                d.append(self.lw[k])
            d += self.rd.get(k, [])
        return d

    def _wait(self, en, deps):
        e = self.E[en]
        for (sem, val, key) in deps:
            if key == ("e", en) and not SAME_ENGINE_SYNC:
                continue
            if key == ("e", "pe") and en == "pe":
                continue
            if e["known"].get(key, 0) >= val:
                continue
            e["known"][key] = val
            e["prog"].append(lambda h, sem=sem, val=val: h.wait_ge(sem, val))

    def op(self, en, fn, reads=(), writes=()):
        e = self.E[en]
        writes = list(writes) + [k for k in reads if isinstance(k, tuple) and k and k[0] == "ps" and k not in writes]
        self._wait(en, self._deps(reads, writes))
        e["cnt"] += 1
        c = e["cnt"]
        sem = e["sem"]
        e["prog"].append(lambda h, fn=fn, sem=sem: fn(h).then_inc(sem, 1))
        tok = (sem, c, ("e", en))
        for k in writes:
            self.lw[k] = tok
            self.rd[k] = []
        for k in reads:
            if k not in writes:
                self.rd.setdefault(k, []).append(tok)

    def dma(self, qn, out, in_, reads=(), writes=(), **kw):
        e = self.E[qn]
        pool_ = self.dpool[qn]
        d = pool_[self.di[qn]]
        self.di[qn] = (self.di[qn] + 1) % len(pool_)
        deps = self._deps(reads, writes)
        if d["val"] > 0:
            deps.append((d["sem"], d["val"], ("d", d["i"])))
        self._wait(qn, deps)
        d["val"] += 16
        sem, val = d["sem"], d["val"]
        e["prog"].append(lambda h, out=out, in_=in_, sem=sem, kw=kw: h.dma_start(out=out, in_=in_, **kw).then_inc(sem, 16))
        tok = (sem, val, ("d", d["i"]))
        for k in writes:
            self.lw[k] = tok
            self.rd[k] = []
        for k in reads:
            if k not in writes:
                self.rd.setdefault(k, []).append(tok)

    def idma(self, out, in_, idx_ap, reads=(), writes=()):
        qn = "pool"
        e = self.E[qn]
        pool_ = self.dpool[qn]
        d = pool_[self.di[qn]]
        self.di[qn] = (self.di[qn] + 1) % len(pool_)
        deps = self._deps(reads, writes)
        if d["val"] > 0:
            deps.append((d["sem"], d["val"], ("d", d["i"])))
        self._wait(qn, deps)
        d["val"] += 16
        sem, val = d["sem"], d["val"]
        e["prog"].append(lambda h, out=out, in_=in_, idx_ap=idx_ap, sem=sem: h.indirect_dma_start(
            out=out, out_offset=None, in_=in_, in_offset=bass.IndirectOffsetOnAxis(ap=idx_ap, axis=0)).then_inc(sem, 16))
        tok = (sem, val, ("d", d["i"]))
        for k in writes:
            self.lw[k] = tok
            self.rd[k] = []
        for k in reads:
            if k not in writes:
                self.rd.setdefault(k, []).append(tok)

    def flush(self):
        for en, e in self.E.items():
            deps = [(o["sem"], o["cnt"], ("e", on)) for on, o in self.E.items() if on != en and o["cnt"] > 0]
            deps += [(d["sem"], d["val"], ("d", d["i"])) for d in self.dsems if d["val"] > 0]
            self._wait(en, deps)
        with self.nc.Block() as blk:
            for en, dec in (("pe", blk.tensor), ("act", blk.scalar), ("dve", blk.vector), ("pool", blk.gpsimd), ("sp", blk.sync)):
                prog = self.E[en]["prog"]
                if prog:
                    def body(h, prog=prog):
                        for f in prog:
                            f(h)
                    dec(body)
                self.E[en]["prog"] = []
        self.lw.clear()
        self.rd.clear()
        for en, e in self.E.items():
            if e["cnt"] > 12000:
                e["gen"] += 1
                e["sem"] = self.st.enter_context(self.nc.semaphore(f"q_{en}_{e['gen']}"))
                e["cnt"] = 0
                for o in self.E.values():
                    o["known"].pop(("e", en), None)
        for d in self.dsems:
            if d["val"] > 12000:
                d["sem"] = self.st.enter_context(self.nc.semaphore(f"dq{d['i'][0]}{d['i'][1]}_{d['val']}"))
                d["val"] = 0
                for o in self.E.values():
                    o["known"].pop(("d", d["i"]), None)


def build(stage=99):
    nc = bass.Bass("TRN2", target_bir_lowering=False)
    st = contextlib.ExitStack()

    def din(name, shape, dt=F32):
        return nc.dram_tensor(name, list(shape), dt, kind="ExternalInput").ap()

    def dout(name, shape, dt=F32):
        return nc.dram_tensor(name, list(shape), dt, kind="ExternalOutput").ap()

    xp = din("xp", [T, D])
    xs = din("xs", [NS, D])
    mem = din("mem", [256, D])
    W = {}
    for name, shape in (("g_mix", [DEPTH, D]), ("w_in", [DEPTH, D, IN_COLS]), ("fox_q_norm", [DEPTH, 64]),
                        ("fox_k_norm", [DEPTH, 64]), ("fox_b_f", [DEPTH, 8]), ("conv_w", [DEPTH, 31, 256]), ("conv_b", [DEPTH, 256]),
                        ("conv_ln_g", [DEPTH, 256]), ("conv_ln_b", [DEPTH, 256]),
                        ("rwkv_mu", [DEPTH, 1024]), ("rwkv_w0", [DEPTH, 256]), ("rwkv_w_up", [DEPTH, 64, 256]), ("rwkv_a0", [DEPTH, 256]),
                        ("rwkv_a_up", [DEPTH, 64, 256]), ("rwkv_g_up", [DEPTH, 128, 256]), ("rwkv_k_k", [DEPTH, 256]), ("rwkv_k_a", [DEPTH, 256]),
                        ("rwkv_r_k", [DEPTH, 256]), ("rwkv_ln_g", [DEPTH, 256]), ("rwkv_ln_b", [DEPTH, 256]),
                        ("g_mem", [DEPTH, D]), ("w_xk", [DEPTH, D, D]), ("w_xv", [DEPTH, D, D]), ("xk_norm", [DEPTH, 256]),
                        ("w_out", [DEPTH, D, D]), ("g_x", [DEPTH, D]), ("w_xq", [DEPTH, D, D]), ("xq_norm", [DEPTH, 256]), ("w_xo", [DEPTH, D, D]),
                        ("g_ffn", [DEPTH, D]), ("w_up", [DEPTH, D, 2 * D_FF]), ("ffn_conv_w", [DEPTH, 3, 2 * D_FF]), ("ffn_conv_b", [DEPTH, 2 * D_FF]),
                        ("w_down", [DEPTH, D_FF, D])):
        W[name] = din(name, shape)
    o_pfk = dout("p_fox_k", [DEPTH, T, 512])
    o_pfv = dout("p_fox_v", [DEPTH, T, 512])
    o_pfl = dout("p_fox_logf", [DEPTH, T, 8])
    o_sfk = dout("s_fox_k", [DEPTH, NS, 512])
    o_sfv = dout("s_fox_v", [DEPTH, NS, 512])
    o_sfl = dout("s_fox_logf", [DEPTH, NS, 8])
    scv = din("scv", [DEPTH, 4, 30, 256])
    o_pcv = dout("p_conv", [DEPTH, 30, 256])
    o_scv = dout("s_conv", [DEPTH, 4, 30, 256])
    srw = din("srw", [DEPTH, 4, 4, 64, 64])
    ssh = din("ssh", [DEPTH, 4, 1024])
    o_prw = dout("p_rwkv", [DEPTH, 4, 64, 64])
    o_psh = dout("p_rwkv_shift", [DEPTH, 1024])
    o_srw = dout("s_rwkv", [DEPTH, 4, 4, 64, 64])
    o_ssh = dout("s_rwkv_shift", [DEPTH, 4, 1024])
    o_pmk = dout("p_mem_k", [DEPTH, 256, D])
    o_pff = dout("p_ffn", [DEPTH, 2, 2 * D_FF])
    NPHYS = 2560
    fk = [din(f"fk{l_}", [NPHYS * 128, 512]) for l_ in range(DEPTH)]
    fv = [din(f"fv{l_}", [NPHYS * 128, 512]) for l_ in range(DEPTH)]
    flf = [din(f"flf{l_}", [NPHYS, 1024]) for l_ in range(DEPTH)]
    ptab = din("ptab", [4, 64], I32)
    cmk = din("cmk", [DEPTH, 4, 256, D])
    cmv = din("cmv", [DEPTH, 4, 256, D])
    sff = din("sff", [DEPTH, 8, 2 * D_FF])
    o_sff = dout("s_ffn", [DEPTH, 8, 2 * D_FF])
    o_ys = dout("y_sample", [NS, D])
    o_yp = dout("y_prompt", [T, D])
    o_pmv = dout("p_mem_v", [DEPTH, 256, D])

    DBG = {}
    if stage < 99:
        DBG["yb"] = dout("dbg_yb", [T, 512])
        DBG["yc"] = dout("dbg_yc", [2, 128, T], BF16)
        DBG["ya"] = dout("dbg_ya", [2, 128, T], BF16)
        DBG["x1"] = dout("dbg_x1", [KC, 128, T])
        DBG["yas"] = dout("dbg_yas", [2, 128, NS], BF16)
        DBG["ybs"] = dout("dbg_ybs", [NS, 512])
        for k_ in ("x1s", "x2s", "x3s"):
            DBG[k_] = dout("dbg_" + k_, [KC, 128, NS])
        DBG["x2"] = dout("dbg_x2", [KC, 128, T])
        DBG["x3"] = dout("dbg_x3", [KC, 128, T])
    S = Sched(nc, st)
    uid = [0]

    def sb(name, shape, dt=F32, stack=None):
        uid[0] += 1
        return (stack or st).enter_context(nc.sbuf_tensor(f"{name}_{uid[0]}", list(shape), dt))

    PS = [st.enter_context(nc.psum_tensor(f"ps{i}", [128, 512], F32)) for i in range(8)]
    psi = [0]

    def psum():
        i = psi[0]
        psi[0] = (i + 1) % 6
        return PS[i], ("ps", i)

    ident = sb("ident", [128, 128])
    onesf = sb("onesf", [128, 128])
    onesb = sb("onesb", [128, 128], BF16)
    xT = sb("xT", [128, KC, T])
    xTs = sb("xTs", [128, KC, NS])

    S.op("pool", lambda h: h.memset(onesf[:], 1.0), writes=["onesf"])
    S.op("pool", lambda h: h.memset(onesb[:], 1.0), writes=["onesb"])
    S.op("pool", lambda h: h.affine_select(out=ident[:], in_=onesf[:], pattern=[[-1, 128]], compare_op=ALU.is_equal,
                                           fill=0.0, base=0, channel_multiplier=1), reads=["onesf"], writes=["ident"])

    triU = sb("triU", [128, 128])
    maskb = sb("maskb", [128, 128], BF16)
    S.op("pool", lambda h: h.affine_select(out=triU[:], in_=onesf[:], pattern=[[1, 128]], compare_op=ALU.is_ge,
                                           fill=0.0, base=0, channel_multiplier=-1), reads=["onesf"], writes=["triU"])
    S.op("pool", lambda h: h.tensor_copy(out=maskb[:], in_=triU[:]), reads=["triU"], writes=["maskb"])

    with contextlib.ExitStack() as ph:
        xtm = [sb(f"xtm{i}", [128, D], stack=ph) for i in range(2)]

        def load_T(src_rows, n, dstT, col0, bi):
            t_ = xtm[bi]
            S.dma("sp", t_[0:n, :], src_rows, writes=[("xtm", bi)])
            for half in range(2):
                ps, pk = psum()
                for q in range(4):
                    c = half * 4 + q
                    S.op("pe", lambda h, ps=ps, q=q, c=c, t_=t_: h.transpose(out=ps[:, q * 128:q * 128 + n], in_=t_[0:n, c * 128:(c + 1) * 128],
                                                                          identity=ident[0:n, 0:n]),
                         reads=[("xtm", bi), "ident"], writes=[pk])
                eng = "act" if half == 0 else "dve"
                src = ps[:, :].rearrange("p (q t) -> p q t", q=4)[:, :, 0:n]
                dst = dstT[:, half * 4:half * 4 + 4, col0:col0 + n]
                if eng == "act":
                    S.op("act", lambda h, src=src, dst=dst: h.activation(out=dst, in_=src, func=AF.Copy), reads=[pk], writes=[("xT", id(dstT), col0)])
                else:
                    S.op("dve", lambda h, src=src, dst=dst: h.tensor_copy(out=dst, in_=src), reads=[pk], writes=[("xT", id(dstT), col0, 1)])

        for tt in range(NT):
            load_T(xp[tt * 128:(tt + 1) * 128, :], 128, xT, tt * 128, tt % 2)
        load_T(xs[:, :], NS, xTs, 0, 0)
        S.flush()

    def load_cols(tile_, vec, nch, stack_key):
        S.dma("sp", tile_[:, 0:nch], vec.rearrange("(c p) -> p c", p=128), writes=[stack_key], allow_slow_non_contiguous=True)

    def load_w(wt, wd, col0, ncols, key):
        nch = wd.shape[0] // 128
        for c in range(nch):
            S.dma("pool", wt[:, c, 0:ncols], wd[c * 128:(c + 1) * 128, col0:col0 + ncols], writes=[(key, c)])

    def rmsnorm_fm(xsrc, col0, n, gcol, xn, xn_key, scr, src_keys=()):
        sq, rbc = scr["sq"], scr["rbc"]
        S.op("act", lambda h: h.activation(out=sq[:, :, 0:n], in_=xsrc[:, :, col0:col0 + n], func=AF.Square), reads=list(src_keys), writes=["sq"])
        ps, pk = psum()
        for c in range(KC):
            S.op("pe", lambda h, c=c: h.matmul(ps[:, 0:n], lhsT=onesb[:, :], rhs=sq[:, c, 0:n], start=(c == 0), stop=(c == KC - 1)),
                 reads=["sq", "onesb"], writes=[pk])
        S.op("act", lambda h: h.activation(out=rbc[:, 0:n], in_=ps[:, 0:n], func=AF.Sqrt, scale=1.0 / D, bias=scr["eps"][:, 0:1]),
             reads=[pk, "eps"], writes=["rbc"])
        S.op("dve", lambda h: h.reciprocal(out=rbc[:, 0:n], in_=rbc[:, 0:n]), reads=["rbc"], writes=["rbc"])
        for c in range(KC):
            S.op("dve", lambda h, c=c: h.scalar_tensor_tensor(out=xn[:, c, 0:n], in0=xsrc[:, c, col0:col0 + n], scalar=gcol[:, c:c + 1],
                                                             in1=rbc[:, 0:n], op0=ALU.mult, op1=ALU.mult),
                 reads=["rbc", "gcol"] + list(src_keys), writes=[(xn_key, c)])

    for l in range(0 if stage < 1 else (DEPTH if stage >= 50 else 1)):
        lay = contextlib.ExitStack()
        yT = sb("yT", [128, KC, T], BF16, lay)
        yTs = sb("yTs", [128, KC, NS], BF16, lay)
        mx = contextlib.ExitStack()
        vS = sb("vS", [NS, 512], F32, mx)
        kS = sb("kS", [NS, 512], F32, mx)
        qS = sb("qS", [NS, 512], F32, mx)
        LFk = sb("LFk", [128, NT + 1, 8], F32, mx)
        QT = sb("QT", [128, 4, T], BF16, mx)
        KT = sb("KT", [128, 4, T], BF16, mx)
        Vp = sb("Vp", [128, NT, 8, 65], BF16, mx)
        S.op("pool", lambda h: h.memset(Vp[:, :, :, 64:65], 1.0), writes=["Vp1"])
        with contextlib.ExitStack() as ph:
            wfox = sb("wfox", [128, KC, 1544], BF16, ph)
            gcol = sb("gcol", [128, KC], F32, ph)
            epsc = sb("epsc", [128, 1], F32, ph)
            gq = sb("gq", [128, 64], F32, ph)
            gk = sb("gk", [128, 64], F32, ph)
            bfb = sb("bfb", [128, 8], F32, ph)
            sq = sb("sq", [128, KC, 512], BF16, ph)
            rbc = sb("rbc", [128, 512], F32, ph)
            xn = sb("xn", [128, KC, 512], BF16, ph)
            tq = sb("tq", [128, 512], F32, ph)
            tk = sb("tk", [128, 512], F32, ph)
            tv = sb("tv", [128, 512], F32, ph)
            tsq = sb("tsq", [128, 512], F32, ph)
            ss = sb("ss", [128, 16], F32, ph)
            LF = LFk
            scr = dict(sq=sq, rbc=rbc, eps=epsc)

            load_w(wfox, W["w_in"][l], FOX0, 1544, "wfox")
            load_cols(gcol, W["g_mix"][l], KC, "gcol")
            S.op("pool", lambda h: h.memset(epsc[:], RMS_EPS), writes=["eps"])
            S.dma("sp", gq[:, :], W["fox_q_norm"][l].partition_broadcast(128), writes=["gq"])
            S.dma("sp", gk[:, :], W["fox_k_norm"][l].partition_broadcast(128), writes=["gk"])
            S.dma("sp", bfb[:, :], W["fox_b_f"][l].partition_broadcast(128), writes=["bfb"])
            S.op("dve", lambda h: h.tensor_scalar(out=gq[:, :], in0=gq[:, :], scalar1=0.125, scalar2=None, op0=ALU.mult), reads=["gq"], writes=["gq"])
            wkeys = [("wfox", c) for c in range(KC)]

            def headnorm(ps, pk, n, gt, gkey, dst, dkey, si):
                S.op("act", lambda h: h.activation(out=tsq[0:n, :], in_=ps[0:n, :], func=AF.Square), reads=[pk], writes=["tsq"])
                S.op("dve", lambda h: h.tensor_reduce(out=ss[0:n, si * 8:si * 8 + 8], in_=tsq[0:n, :].rearrange("p (h d) -> p h d", h=8), axis=AX.X, op=ALU.add),
                     reads=["tsq"], writes=[("ss", si)])
                S.op("act", lambda h: h.activation(out=ss[0:n, si * 8:si * 8 + 8], in_=ss[0:n, si * 8:si * 8 + 8], func=AF.Sqrt, scale=1.0 / 64, bias=epsc[0:n, 0:1]),
                     reads=[("ss", si), "eps"], writes=[("ss", si)])
                S.op("dve", lambda h: h.reciprocal(out=ss[0:n, si * 8:si * 8 + 8], in_=ss[0:n, si * 8:si * 8 + 8]), reads=[("ss", si)], writes=[("ss", si)])
                S.op("dve", lambda h: h.tensor_tensor(out=dst[0:n, :].rearrange("p (h d) -> p h d", h=8), in0=ps[0:n, :].rearrange("p (h d) -> p h d", h=8),
                                                      in1=ss[0:n, si * 8:si * 8 + 8].unsqueeze(2).broadcast_to([n, 8, 64]), op=ALU.mult),
                     reads=[pk, ("ss", si)], writes=[dkey])
                S.op("dve", lambda h: h.tensor_tensor(out=dst[0:n, :].rearrange("p (h d) -> p h d", h=8), in0=dst[0:n, :].rearrange("p (h d) -> p h d", h=8),
                                                      in1=gt[0:n, :].unsqueeze(1).broadcast_to([n, 8, 64]), op=ALU.mult),
                     reads=[dkey, gkey], writes=[dkey])

            def projA(xsrc, blk0, nblk, o_k, o_v, o_l, row0, lf_tile0):
                rmsnorm_fm(xsrc, blk0, nblk, gcol, xn, "xn", scr)
                xnk = [("xn", c) for c in range(KC)]
                for t0 in range(0, nblk, 128):
                    n = min(128, nblk - t0)
                    ti = lf_tile0 + t0 // 128
                    r0 = row0 + t0
                    pss = []
                    for gi, (c0, nc_) in enumerate(((0, 512), (512, 512), (1024, 512), (1536, 8))):
                        ps, pk = psum()
                        for c in range(KC):
                            S.op("pe", lambda h, ps=ps, c=c, c0=c0, nc_=nc_, t0=t0, n=n: h.matmul(ps[0:n, 0:nc_], lhsT=xn[:, c, t0:t0 + n], rhs=wfox[:, c, c0:c0 + nc_],
                                                                                                 start=(c == 0), stop=(c == KC - 1)),
                                 reads=xnk + wkeys, writes=[pk])
                        pss.append((ps, pk))
                    if stage >= 4:
                        headnorm(pss[0][0], pss[0][1], n, gq, "gq", tq, "tq", 0)
                        headnorm(pss[1][0], pss[1][1], n, gk, "gk", tk, "tk", 1)
                        S.dma("sp", o_k[r0:r0 + n, :], tk[0:n, :], reads=["tk"])
                    S.op("act", lambda h, ps=pss[2][0], n=n: h.activation(out=tv[0:n, :], in_=ps[0:n, :], func=AF.Copy), reads=[pss[2][1]], writes=["tv"])
                    S.dma("sp", o_v[r0:r0 + n, :], tv[0:n, :], reads=["tv"])
                    if n == 128:
                        for (src_t, skey, dstT, dkey) in ((tq, "tq", QT, "QT"), (tk, "tk", KT, "KT")):
                            ps, pk = psum()
                            for q in range(4):
                                S.op("pe", lambda h, ps=ps, q=q, src_t=src_t: h.transpose(out=ps[:, q * 128:(q + 1) * 128], in_=src_t[:, q * 128:(q + 1) * 128], identity=ident[:, :]),
                                     reads=[skey, "ident"], writes=[pk])
                            S.op("act", lambda h, ps=ps, dstT=dstT, r0=r0: h.activation(out=dstT[:, :, r0:r0 + 128], in_=ps[:, :].rearrange("p (q t) -> p q t", q=4), func=AF.Copy),
                                 reads=[pk], writes=[(dkey, ti)])
                        S.op("pool", lambda h, ti=ti: h.tensor_copy(out=Vp[:, ti, :, 0:64], in_=tv[:, :].rearrange("p (h d) -> p h d", h=8)), reads=["tv"], writes=[("Vp", ti)])
                    else:
                        S.op("pool", lambda h: h.tensor_copy(out=qS[:, :], in_=tq[0:NS, :]), reads=["tq"], writes=["qS"])
                        S.op("pool", lambda h: h.tensor_copy(out=kS[:, :], in_=tk[0:NS, :]), reads=["tk"], writes=["kS"])
                        S.op("pool", lambda h: h.tensor_copy(out=vS[:, :], in_=tv[0:NS, :]), reads=["tv"], writes=["vS"])
                    if stage < 5:
                        continue
                    lfv = LF[0:n, ti, :]
                    S.op("dve", lambda h, ps=pss[3][0], n=n, lfv=lfv: h.tensor_tensor(out=lfv, in0=ps[0:n, 0:8], in1=bfb[0:n, :], op=ALU.add),
                         reads=[pss[3][1], "bfb"], writes=[("LF", ti)])
                    S.op("act", lambda h, lfv=lfv: h.activation(out=lfv, in_=lfv, func=AF.Exp, scale=-1.0), reads=[("LF", ti)], writes=[("LF", ti)])
                    S.op("dve", lambda h, lfv=lfv: h.tensor_scalar(out=lfv, in0=lfv, scalar1=1.0, scalar2=None, op0=ALU.add), reads=[("LF", ti)], writes=[("LF", ti)])
                    S.op("act", lambda h, lfv=lfv: h.activation(out=lfv, in_=lfv, func=AF.Ln), reads=[("LF", ti)], writes=[("LF", ti)])
                    S.op("dve", lambda h, lfv=lfv: h.tensor_scalar(out=lfv, in0=lfv, scalar1=-1.0, scalar2=None, op0=ALU.mult), reads=[("LF", ti)], writes=[("LF", ti)])
                    S.dma("sp", o_l[r0:r0 + n, :], lfv, reads=[("LF", ti)])

            if stage == 2:
                rmsnorm_fm(xT, 0, 512, gcol, xn, "xn", scr)
            if stage >= 3:
                for b in range(4):
                    projA(xT, b * 512, 512, o_pfk[l], o_pfv[l], o_pfl[l], b * 512, b * 4)
                projA(xTs, 0, NS, o_sfk[l], o_sfv[l], o_sfl[l], 0, NT)
            S.flush()

        if stage >= 6:
            with contextlib.ExitStack() as ph:
                Cw = sb("Cw", [128, NT, 8], F32, ph)
                offs = sb("offs", [128, NT, 8], F32, ph)
                tot = sb("tot", [128, NT, 8], F32, ph)
                negC = sb("negC", [128, NT, 8], F32, ph)
                BI = sb("BI", [128, NT, NT, 8], F32, ph)
                PT = [sb(f"PT{i}", [128, 128], BF16, ph) for i in range(4)]
                ybt = [sb(f"ybt{i}", [128, 512], F32, ph) for i in range(2)]
                rden = sb("rden", [128, 8], F32, ph)
                lfflat = LFk[:, 0:NT, :].rearrange("p t h -> p (t h)")
                ps, pk = psum()
                S.op("pe", lambda h, ps=ps: h.matmul(ps[:, 0:128], lhsT=triU[:, :], rhs=lfflat, start=True, stop=True), reads=["triU"], writes=[pk])
                S.op("act", lambda h, ps=ps: h.activation(out=Cw[:, :, :].rearrange("p t h -> p (t h)"), in_=ps[:, 0:128], func=AF.Copy), reads=[pk], writes=["Cw"])
                ps, pk = psum()
                S.op("pe", lambda h, ps=ps: h.matmul(ps[:, 0:128], lhsT=onesf[:, :], rhs=lfflat, start=True, stop=True), reads=["onesf"], writes=[pk])
                S.op("act", lambda h, ps=ps: h.activation(out=tot[:, :, :].rearrange("p t h -> p (t h)"), in_=ps[:, 0:128], func=AF.Copy), reads=[pk], writes=["tot"])
                S.op("dve", lambda h: h.memset(offs[:, 0, :], 0.0), writes=["offs"])
                for i in range(1, NT):
                    S.op("dve", lambda h, i=i: h.tensor_tensor(out=offs[:, i, :], in0=offs[:, i - 1, :], in1=tot[:, i - 1, :], op=ALU.add), reads=["offs", "tot"], writes=["offs"])
                S.op("dve", lambda h: h.tensor_tensor(out=negC[:, :, :], in0=Cw[:, :, :], in1=offs[:, :, :], op=ALU.add), reads=["Cw", "offs"], writes=["negC"])
                S.op("dve", lambda h: h.tensor_scalar(out=negC[:, :, :], in0=negC[:, :, :], scalar1=-1.0, scalar2=None, op0=ALU.mult), reads=["negC"], writes=["negC"])
                for j in range(NT):
                    S.op("dve", lambda h, j=j: h.tensor_tensor(out=BI[:, j, 0:j + 1, :], in0=negC[:, 0:j + 1, :],
                                                               in1=offs[:, j:j + 1, :].broadcast_to([128, j + 1, 8]), op=ALU.add),
                         reads=["negC", "offs"], writes=[("BI", j)])
                pti = 0
                for j in range(NT):
                    yb_t = ybt[j % 2]
                    ykey = ("ybt", j % 2)
                    for h_ in range(8):
                        pair, pb = h_ // 2, (h_ % 2) * 64
                        acc = PS[6 + (h_ // 4)]
                        akey = ("ps", 6 + (h_ // 4))
                        ac0 = (h_ % 4) * 65
                        LA = 3
                        sbank = {}

                        def emit_s(i, pair=pair, pb=pb, j=j):
                            ps, pk = psum()
                            S.op("pe", lambda h, ps=ps, pair=pair, pb=pb, i=i, j=j: h.matmul(ps[:, 0:128], lhsT=KT[pb:pb + 64, pair, i * 128:(i + 1) * 128],
                                                                                        rhs=QT[pb:pb + 64, pair, j * 128:(j + 1) * 128], start=True, stop=True),
                                 reads=[("KT", i), ("QT", j)], writes=[pk])
                            sbank[i] = (ps, pk)

                        for i in range(min(LA, j + 1)):
                            emit_s(i)
                        for i in range(j + 1):
                            ps, pk = sbank.pop(i)
                            pt = PT[pti % 4]
                            ptk = ("PT", pti % 4)
                            pti += 1
                            S.op("act", lambda h, ps=ps, pt=pt, i=i, j=j, h_=h_: h.activation(out=pt[:, :], in_=ps[:, 0:128], func=AF.Exp, bias=BI[:, j, i, h_:h_ + 1]),
                                 reads=[pk, ("BI", j)], writes=[ptk])
                            if i == j:
                                S.op("pool", lambda h, pt=pt: h.tensor_tensor(out=pt[:, :], in0=pt[:, :], in1=maskb[:, :], op=ALU.mult), reads=[ptk, "maskb"], writes=[ptk])
                            if i + LA <= j:
                                emit_s(i + LA)
                            S.op("pe", lambda h, pt=pt, acc=acc, ac0=ac0, i=i, j=j, h_=h_: h.matmul(acc[:, ac0:ac0 + 65], lhsT=pt[:, :], rhs=Vp[:, i, h_, :], start=(i == 0), stop=(i == j)),
                                 reads=[ptk, ("Vp", i), "Vp1"], writes=[akey])
                        S.op("dve", lambda h, acc=acc, ac0=ac0, h_=h_: h.reciprocal(out=rden[:, h_:h_ + 1], in_=acc[:, ac0 + 64:ac0 + 65]), reads=[akey], writes=[("rden", h_)])
                        S.op("dve", lambda h, acc=acc, ac0=ac0, h_=h_, yb_t=yb_t: h.tensor_scalar(out=yb_t[:, h_ * 64:(h_ + 1) * 64], in0=acc[:, ac0:ac0 + 64], scalar1=rden[:, h_:h_ + 1],
                                                                                               scalar2=None, op0=ALU.mult),
                             reads=[akey, ("rden", h_)], writes=[ykey])
                    if "yb" in DBG:
                        S.dma("sp", DBG["yb"][j * 128:(j + 1) * 128, :], yb_t[:, :], reads=[ykey])
                    ps, pk = psum()
                    for q in range(4):
                        S.op("pe", lambda h, ps=ps, q=q, yb_t=yb_t: h.transpose(out=ps[:, q * 128:(q + 1) * 128], in_=yb_t[:, q * 128:(q + 1) * 128], identity=ident[:, :]),
                             reads=[ykey, "ident"], writes=[pk])
                    S.op("act", lambda h, ps=ps, j=j: h.activation(out=yT[:, 2:6, j * 128:(j + 1) * 128], in_=ps[:, :].rearrange("p (q t) -> p q t", q=4), func=AF.Copy),
                         reads=[pk], writes=[("yT", "b", j)])
                S.flush()
        if stage >= 14:
            with contextlib.ExitStack() as ph:
                ptb_i = sb("ptb_i", [128, 64], I32, ph)
                ptf = sb("ptf", [128, 64], F32, ph)
                idx = sb("idx", [128, 64], I32, ph)
                pio_i = sb("pio_i", [128, 1], I32, ph)
                pio = sb("pio", [128, 1], F32, ph)
                idxp = sb("idxp", [64, 1], I32, ph)
                Kt = [sb(f"Kt{i}", [128, 512], F32, ph) for i in range(4)]
                Vt = [sb(f"Vt{i}", [128, 512], F32, ph) for i in range(4)]
                Vb = [sb(f"Vb{i}", [128, 512], BF16, ph) for i in range(2)]
                KTp = [sb(f"KTp{i}", [128, 4, 128], BF16, ph) for i in range(2)]
                sbt = [sb(f"sbt{i}", [128, 8, 4], F32, ph) for i in range(2)]
                PTt = [sb(f"PTt{i}", [128, 32], BF16, ph) for i in range(2)]
                lft = sb("lft", [64, 1024], F32, ph)
                Pfx = sb("Pfx", [64, 1024], F32, ph)
                tot = sb("tot", [64, 8], F32, ph)
                later = sb("later", [64, 8], F32, ph)
                MLt = sb("MLt", [64, 64], F32, ph)
                EXT = sb("EXT", [128, 8, 64], F32, ph)
                qpad = sb("qpad", [128, 4, 2, 16], BF16, ph)
                KTn = sb("KTn", [128, 4, 16], BF16, ph)
                Vn = sb("Vn", [16, 512], BF16, ph)
                E4 = sb("E4", [4, 16], F32, ph)
                E4t = sb("E4t", [4, 16], F32, ph)
                BT = sb("BT", [16, 16], F32, ph)
                negcT = sb("negcT", [16, 8], F32, ph)
                maskn = sb("maskn", [16, 4, 4], F32, ph)
                mtmp = sb("mtmp", [16, 4, 4], F32, ph)
                sbn = sb("sbn", [16, 8, 4], F32, ph)
                PTn = sb("PTn", [16, 32], BF16, ph)
                onorm = sb("onorm", [32, 512], F32, ph)
                rdn = sb("rdn", [32, 1], F32, ph)
                ybS = sb("ybS", [NS, 512], F32, ph)
                S.op("pool", lambda h: h.iota(pio_i[:, 0:1], [[0, 1]], base=0, channel_multiplier=1), writes=["pio_i"])
                S.op("dve", lambda h: h.tensor_copy(out=pio[:, :], in_=pio_i[:, :]), reads=["pio_i"], writes=["pio"])
                S.op("pool", lambda h: h.affine_select(out=MLt[:, :], in_=onesf[0:64, 0:64], pattern=[[-1, 64]], compare_op=ALU.is_ge, fill=0.0, base=-1, channel_multiplier=1), reads=["onesf"], writes=["MLt"])
                S.op("pool", lambda h: h.affine_select(out=E4t[:, :], in_=onesf[0:4, 0:16], pattern=[[1, 16]], compare_op=ALU.is_ge, fill=0.0, base=0, channel_multiplier=-4), reads=["onesf"], writes=["E4t"])
                S.op("pool", lambda h: h.affine_select(out=E4[:, :], in_=E4t[:, :], pattern=[[-1, 16]], compare_op=ALU.is_ge, fill=0.0, base=3, channel_multiplier=4), reads=["E4t"], writes=["E4"])
                ps, pk = psum()
                S.op("pe", lambda h, ps=ps: h.matmul(ps[0:16, 0:16], lhsT=E4[:, :], rhs=E4[:, :], start=True, stop=True), reads=["E4"], writes=[pk])
                S.op("dve", lambda h, ps=ps: h.tensor_tensor(out=BT[:, :], in0=ps[0:16, 0:16], in1=triU[0:16, 0:16], op=ALU.mult), reads=[pk, "triU"], writes=["BT"])
                ps, pk = psum()
                S.op("pe", lambda h, ps=ps: h.matmul(ps[0:16, 0:8], lhsT=BT[:, :], rhs=LFk[0:NS, NT, :], start=True, stop=True), reads=["BT"], writes=[pk])
                S.op("dve", lambda h, ps=ps: h.tensor_scalar(out=negcT[:, :], in0=ps[0:16, 0:8], scalar1=-1.0, scalar2=None, op0=ALU.mult), reads=[pk], writes=["negcT"])
                S.op("pool", lambda h: h.memset(mtmp[:].rearrange("p a b -> p (a b)"), 1.0), writes=["mt0"])
                S.op("pool", lambda h: h.affine_select(out=maskn[:, :, :], in_=mtmp[:, :, :], pattern=[[4, 4], [1, 4]], compare_op=ALU.is_ge, fill=0.0, base=0, channel_multiplier=-1), reads=["mt0"], writes=["mk1"])
                S.op("pool", lambda h: h.affine_select(out=mtmp[:, :, :], in_=maskn[:, :, :], pattern=[[-4, 4], [0, 4]], compare_op=ALU.is_ge, fill=0.0, base=0, channel_multiplier=1), reads=["mk1"], writes=["maskn"])
                MSK = mtmp
                S.op("pool", lambda h: h.memset(qpad[:].rearrange("p a b c -> p (a b c)"), 0.0), writes=["qp0"])
                ps, pk = psum()
                for pr in range(4):
                    S.op("pe", lambda h, ps=ps, pr=pr: h.transpose(out=ps[:, pr * 16:(pr + 1) * 16], in_=qS[0:NS, pr * 128:(pr + 1) * 128], identity=ident[0:NS, 0:NS]), reads=["ident"], writes=[pk])
                for hh in range(2):
                    S.op("act", lambda h, ps=ps, hh=hh: h.activation(out=qpad[hh * 64:(hh + 1) * 64, :, hh, :], in_=ps[hh * 64:(hh + 1) * 64, 0:64].rearrange("p (a t) -> p a t", a=4), func=AF.Copy), reads=[pk, "qp0"], writes=[("qpad", hh)])
                qpk = [("qpad", 0), ("qpad", 1)]
                ps, pk = psum()
                for pr in range(4):
                    S.op("pe", lambda h, ps=ps, pr=pr: h.transpose(out=ps[:, pr * 16:(pr + 1) * 16], in_=kS[0:NS, pr * 128:(pr + 1) * 128], identity=ident[0:NS, 0:NS]), reads=["ident"], writes=[pk])
                S.op("act", lambda h, ps=ps: h.activation(out=KTn[:, :, :], in_=ps[:, 0:64].rearrange("p (a t) -> p a t", a=4), func=AF.Copy), reads=[pk], writes=["KTn"])
                S.op("act", lambda h: h.activation(out=Vn[:, :], in_=vS[0:NS, :], func=AF.Copy), writes=["Vn"])
                O_, okey = PS[6], ("ps", 6)
                D_, dkey = PS[7], ("ps", 7)
                ki = 0
                for b in range(4):
                    S.dma("sp", ptb_i[:, :], ptab[b].partition_broadcast(128), writes=["ptb_i"])
                    S.dma("sp", idxp[:, :], ptab[b].rearrange("(j o) -> j o", o=1), writes=["idxp"])
                    S.op("dve", lambda h: h.tensor_copy(out=ptf[:, :], in_=ptb_i[:, :]), reads=["ptb_i"], writes=["ptf"])
                    S.op("dve", lambda h: h.tensor_scalar(out=ptf[:, :], in0=ptf[:, :], scalar1=128.0, scalar2=None, op0=ALU.mult), reads=["ptf"], writes=["ptf"])
                    S.op("dve", lambda h: h.tensor_scalar(out=ptf[:, :], in0=ptf[:, :], scalar1=pio[:, 0:1], scalar2=None, op0=ALU.add), reads=["ptf", "pio"], writes=["ptf"])
                    S.op("dve", lambda h: h.tensor_copy(out=idx[:, :], in_=ptf[:, :]), reads=["ptf"], writes=["idx"])
                    S.idma(lft[:, :], flf[l], idxp[:, 0:1], reads=["idxp"], writes=["lft"])
                    l3 = lft[:, :].rearrange("p (t h) -> p t h", h=8)
                    p3 = Pfx[:, :].rearrange("p (t h) -> p t h", h=8)
                    for h_ in range(8):
                        S.op("dve", lambda h, h_=h_: h.tensor_tensor_scan(out=p3[:, :, h_], data0=onesf[0:64, 0:128], data1=l3[:, :, h_], initial=0.0, op0=ALU.mult, op1=ALU.add), reads=["lft", "onesf"], writes=[("Pfx", h_)])
                    pfk_ = [("Pfx", h_) for h_ in range(8)]
                    S.op("dve", lambda h: h.tensor_copy(out=tot[:, :], in_=p3[:, 127, :]), reads=pfk_, writes=["tot"])
                    S.op("dve", lambda h: h.tensor_tensor(out=p3, in0=tot[:, :].unsqueeze(1).broadcast_to([64, 128, 8]), in1=p3, op=ALU.subtract), reads=pfk_ + ["tot"], writes=pfk_)
                    ps, pk = psum()
                    S.op("pe", lambda h, ps=ps: h.matmul(ps[0:64, 0:8], lhsT=MLt[:, :], rhs=tot[:, :], start=True, stop=True), reads=["MLt", "tot"], writes=[pk])
                    S.op("act", lambda h, ps=ps: h.activation(out=later[:, :], in_=ps[0:64, 0:8], func=AF.Copy), reads=[pk], writes=["later"])
                    S.op("dve", lambda h: h.tensor_tensor(out=p3, in0=p3, in1=later[:, :].unsqueeze(1).broadcast_to([64, 128, 8]), op=ALU.add), reads=pfk_ + ["later"], writes=pfk_)
                    for h4 in range(2):
                        ps, pk = psum()
                        for q in range(4):
                            S.op("pe", lambda h, ps=ps, q=q, h4=h4: h.transpose(out=ps[:, q * 64:(q + 1) * 64], in_=p3[:, :, h4 * 4 + q], identity=ident[0:64, 0:64]), reads=pfk_ + ["ident"], writes=[pk])
                        S.op("act", lambda h, ps=ps, h4=h4: h.activation(out=EXT[:, h4 * 4:h4 * 4 + 4, :], in_=ps[:, 0:256].rearrange("p (a j) -> p a j", a=4), func=AF.Copy), reads=[pk], writes=[("EXT", h4)])
                    extk = [("EXT", 0), ("EXT", 1)]
                    def front(j, b=b):
                        ki = b * 64 + j
                        kt_, vt_ = Kt[ki % 4], Vt[ki % 4]
                        kk_, vk_ = ("Kt", ki % 4), ("Vt", ki % 4)
                        vb_, vbk = Vb[ki % 2], ("Vb", ki % 2)
                        ktp, ktk = KTp[ki % 2], ("KTp", ki % 2)
                        sb_, sbk = sbt[ki % 2], ("sbt", ki % 2)
                        pt_, ptk = PTt[ki % 2], ("PTt", ki % 2)
                        S.idma(kt_[:, :], fk[l], idx[:, j:j + 1], reads=["idx"], writes=[kk_])
                        S.idma(vt_[:, :], fv[l], idx[:, j:j + 1], reads=["idx"], writes=[vk_])
                        ps, pk = psum()
                        for q in range(4):
                            S.op("pe", lambda h, ps=ps, q=q, kt_=kt_: h.transpose(out=ps[:, q * 128:(q + 1) * 128], in_=kt_[:, q * 128:(q + 1) * 128], identity=ident[:, :]), reads=[kk_, "ident"], writes=[pk])
                        S.op("act", lambda h, ps=ps, ktp=ktp: h.activation(out=ktp[:, :, :], in_=ps[:, :].rearrange("p (a t) -> p a t", a=4), func=AF.Copy), reads=[pk], writes=[ktk])
                        S.op("dve", lambda h, vb_=vb_, vt_=vt_: h.tensor_copy(out=vb_[:, :], in_=vt_[:, :]), reads=[vk_], writes=[vbk])
                        ps, pk = psum()
                        for h_ in range(8):
                            S.op("pe", lambda h, ps=ps, h_=h_, ktp=ktp, b=b: h.matmul(ps[:, h_ * 4:(h_ + 1) * 4], lhsT=ktp[:, h_ // 2, :], rhs=qpad[:, h_ // 2, h_ % 2, 4 * b:4 * b + 4], start=True, stop=True), reads=[ktk] + qpk, writes=[pk])
                        S.op("dve", lambda h, ps=ps, sb_=sb_, j=j: h.tensor_tensor(out=sb_[:, :, :], in0=ps[:, 0:32].rearrange("p (a q) -> p a q", a=8), in1=EXT[:, :, j].unsqueeze(2).broadcast_to([128, 8, 4]), op=ALU.add), reads=[pk] + extk, writes=[sbk])
                        S.op("act", lambda h, sb_=sb_, pt_=pt_: h.activation(out=pt_[:, :], in_=sb_[:, :, :].rearrange("p a q -> p (a q)"), func=AF.Exp), reads=[sbk], writes=[ptk])
                        return pt_, ptk, vb_, vbk

                    def back(j, st_):
                        pt_, ptk, vb_, vbk = st_
                        S.op("pe", lambda h, pt_=pt_, vb_=vb_, j=j: h.matmul(O_[0:32, :], lhsT=pt_[:, :], rhs=vb_[:, :], start=(j == 0), stop=False), reads=[ptk, vbk], writes=[okey])
                        S.op("pe", lambda h, pt_=pt_, j=j: h.matmul(D_[0:32, 0:1], lhsT=pt_[:, :], rhs=onesb[:, 0:1], start=(j == 0), stop=False), reads=[ptk, "onesb"], writes=[dkey])

                    pend = front(0)
                    for j in range(64):
                        nxt = front(j + 1) if j + 1 < 64 else None
                        back(j, pend)
                        pend = nxt
                    ps, pk = psum()
                    for h_ in range(8):
                        S.op("pe", lambda h, ps=ps, h_=h_, b=b: h.matmul(ps[0:NS, h_ * 4:(h_ + 1) * 4], lhsT=KTn[:, h_ // 2, :], rhs=qpad[:, h_ // 2, h_ % 2, 4 * b:4 * b + 4], start=True, stop=True), reads=["KTn"] + qpk, writes=[pk])
                    S.op("dve", lambda h, ps=ps: h.tensor_tensor(out=sbn[:, :, :], in0=ps[0:NS, 0:32].rearrange("p (a q) -> p a q", a=8), in1=negcT[:, :].unsqueeze(2).broadcast_to([NS, 8, 4]), op=ALU.add), reads=[pk, "negcT"], writes=["sbn"])
                    S.op("act", lambda h: h.activation(out=sbn[:, :, :].rearrange("p a q -> p (a q)"), in_=sbn[:, :, :].rearrange("p a q -> p (a q)"), func=AF.Exp), reads=["sbn"], writes=["sbn"])
                    S.op("dve", lambda h, b=b: h.tensor_tensor(out=PTn[:, :].rearrange("p (a q) -> p a q", a=8), in0=sbn[:, :, :], in1=MSK[:, b, :].unsqueeze(1).broadcast_to([NS, 8, 4]), op=ALU.mult), reads=["sbn", "maskn"], writes=["PTn"])
                    S.op("pe", lambda h: h.matmul(O_[0:32, :], lhsT=PTn[:, :], rhs=Vn[:, :], start=False, stop=True), reads=["PTn", "Vn"], writes=[okey])
                    S.op("pe", lambda h: h.matmul(D_[0:32, 0:1], lhsT=PTn[:, :], rhs=onesb[0:NS, 0:1], start=False, stop=True), reads=["PTn", "onesb"], writes=[dkey])
                    S.op("dve", lambda h: h.reciprocal(out=rdn[:, :], in_=D_[0:32, 0:1]), reads=[dkey], writes=["rdn"])
                    S.op("dve", lambda h: h.tensor_scalar(out=onorm[:, :], in0=O_[0:32, :], scalar1=rdn[:, 0:1], scalar2=None, op0=ALU.mult), reads=[okey, "rdn"], writes=["onorm"])
                    for h_ in range(8):
                        S.dma("sp", ybS[4 * b:4 * b + 4, h_ * 64:(h_ + 1) * 64], onorm[h_ * 4:(h_ + 1) * 4, h_ * 64:(h_ + 1) * 64], reads=["onorm"], writes=[("ybS", b, h_)])
                ybk = [("ybS", b, h_) for b in range(4) for h_ in range(8)]
                ps, pk = psum()
                for q in range(4):
                    S.op("pe", lambda h, ps=ps, q=q: h.transpose(out=ps[:, q * 16:(q + 1) * 16], in_=ybS[0:NS, q * 128:(q + 1) * 128], identity=ident[0:NS, 0:NS]), reads=ybk + ["ident"], writes=[pk])
                S.op("act", lambda h, ps=ps: h.activation(out=yTs[:, 2:6, :], in_=ps[:, 0:64].rearrange("p (a t) -> p a t", a=4), func=AF.Copy), reads=[pk], writes=[("yTs", "b")])
                if "ybs" in DBG and l == 0:
                    S.dma("sp", DBG["ybs"][:, :], ybS[:, :], reads=ybk)
                S.flush()

        mx.close()

        if stage >= 7:
            with contextlib.ExitStack() as ph:
                wcv = sb("wcv", [128, KC, 512], BF16, ph)
                gcol = sb("gcol", [128, KC], F32, ph)
                epsc = sb("epsc", [128, 1], F32, ph)
                epsl = sb("epsl", [128, 1], F32, ph)
                cwT = sb("cwT", [128, 2, 31], F32, ph)
                cbc = sb("cbc", [128, 2], F32, ph)
                lng = sb("lng", [128, 2], F32, ph)
                lnb = sb("lnb", [128, 2], F32, ph)
                sq = sb("sq", [128, KC, 512], BF16, ph)
                rbc = sb("rbc", [128, 512], F32, ph)
                xn = sb("xn", [128, KC, 512], BF16, ph)
                sig = sb("sig", [128, 512], F32, ph)
                U = sb("U", [128, 2, 30 + T], F32, ph)
                Y = sb("Y", [128, 2, T], F32, ph)
                Us = sb("Us", [128, 2, 4, 34], F32, ph)
                Ys = sb("Ys", [128, 2, 4, 4], F32, ph)
                mbc = sb("mbc", [128, 512], F32, ph)
                sq2 = sb("sq2", [128, 2, 512], F32, ph)
                cvt = sb("cvt", [32, 4, 256], F32, ph)
                pco = sb("pco", [32, 4, 256], F32, ph)
                cvo = sb("cvo", [32, 4, 256], F32, ph)
                ycd = sb("ycd", [128, 256], F32, ph)
                scr = dict(sq=sq, rbc=rbc, eps=epsc)
                load_w(wcv, W["w_in"][l], CONV0, 512, "wcv")
                wkeys = [("wcv", c) for c in range(KC)]
                load_cols(gcol, W["g_mix"][l], KC, "gcol")
                S.op("pool", lambda h: h.memset(epsc[:], RMS_EPS), writes=["eps"])
                S.op("pool", lambda h: h.memset(epsl[:], 1e-5), writes=["epsl"])
                for c_ in range(2):
                    S.dma("sp", cwT[:, c_, :], W["conv_w"][l][:, c_ * 128:(c_ + 1) * 128].rearrange("j p -> p j"), writes=[("cwT", c_)], allow_slow_non_contiguous=True)
                load_cols(cbc, W["conv_b"][l], 2, "cbc")
                load_cols(lng, W["conv_ln_g"][l], 2, "lng")
                load_cols(lnb, W["conv_ln_b"][l], 2, "lnb")
                S.op("pool", lambda h: h.memset(U[:, :, 0:30], 0.0), writes=["Upad"])
                S.dma("sp", cvt[0:30, :, :], scv[l].rearrange("b t c -> t b c"), writes=["cvt"])
                for b in range(4):
                    ps, pk = psum()
                    for ch in range(2):
                        S.op("pe", lambda h, ps=ps, ch=ch, b=b: h.transpose(out=ps[:, ch * 32:ch * 32 + 30], in_=cvt[0:30, b, ch * 128:(ch + 1) * 128], identity=ident[0:30, 0:30]),
                             reads=["cvt", "ident"], writes=[pk])
                    S.op("act", lambda h, ps=ps, b=b: h.activation(out=Us[:, :, b, 0:30], in_=ps[:, 0:64].rearrange("p (c t) -> p c t", c=2)[:, :, 0:30], func=AF.Copy),
                         reads=[pk], writes=[("Us", b)])

                def glu_block(xsrc, blk0, n, dst_fn, dkey, vw=lambda a: a):
                    rmsnorm_fm(xsrc, blk0, n, gcol, xn, "xn", scr)
                    xnk = [("xn", c) for c in range(KC)]
                    pss = []
                    for oc in range(4):
                        ps, pk = psum()
                        for c in range(KC):
                            S.op("pe", lambda h, ps=ps, c=c, oc=oc: h.matmul(ps[:, 0:n], lhsT=wcv[:, c, oc * 128:(oc + 1) * 128], rhs=xn[:, c, 0:n], start=(c == 0), stop=(c == KC - 1)),
                                 reads=xnk + wkeys, writes=[pk])
                        pss.append((ps, pk))
                    for ch in range(2):
                        S.op("act", lambda h, ps=pss[2 + ch][0]: h.activation(out=sig[:, 0:n], in_=ps[:, 0:n], func=AF.Sigmoid), reads=[pss[2 + ch][1]], writes=["sig"])
                        S.op("dve", lambda h, ps=pss[ch][0], ch=ch: h.tensor_tensor(out=dst_fn(ch), in0=vw(ps[:, 0:n]), in1=vw(sig[:, 0:n]), op=ALU.mult),
                             reads=[pss[ch][1], "sig"], writes=[dkey])

                for b in range(4):
                    glu_block(xT, b * 512, 512, lambda ch, b=b: U[:, ch, 30 + b * 512:30 + (b + 1) * 512], ("U", b))
                glu_block(xTs, 0, NS, lambda ch: Us[:, ch, :, 30:34], "Usn", vw=lambda a: a.rearrange("p (b t) -> p b t", b=4))

                ukeys = [("U", b) for b in range(4)] + ["Upad"]
                for ch in range(2):
                    for j in range(31):
                        if j == 0:
                            S.op("dve", lambda h, ch=ch: h.tensor_scalar(out=Y[:, ch, :], in0=U[:, ch, 0:T], scalar1=cwT[:, ch, 0:1], scalar2=cbc[:, ch:ch + 1], op0=ALU.mult, op1=ALU.add),
                                 reads=ukeys + [("cwT", 0), ("cwT", 1), "cbc"], writes=[("Y", ch)])
                            S.op("dve", lambda h, ch=ch: h.tensor_scalar(out=Ys[:, ch, :, :], in0=Us[:, ch, :, 0:4], scalar1=cwT[:, ch, 0:1], scalar2=cbc[:, ch:ch + 1], op0=ALU.mult, op1=ALU.add),
                                 reads=[("Us", b) for b in range(4)] + ["Usn", ("cwT", 0), ("cwT", 1), "cbc"], writes=[("Ys", ch)])
                        else:
                            S.op("dve", lambda h, ch=ch, j=j: h.scalar_tensor_tensor(out=Y[:, ch, :], in0=U[:, ch, j:j + T], scalar=cwT[:, ch, j:j + 1], in1=Y[:, ch, :], op0=ALU.mult, op1=ALU.add),
                                 reads=[("Y", ch)], writes=[("Y", ch)])
                            S.op("dve", lambda h, ch=ch, j=j: h.scalar_tensor_tensor(out=Ys[:, ch, :, :], in0=Us[:, ch, :, j:j + 4], scalar=cwT[:, ch, j:j + 1], in1=Ys[:, ch, :, :], op0=ALU.mult, op1=ALU.add),
                                 reads=[("Ys", ch)], writes=[("Ys", ch)])

                def ln_silu(yv, n, outv, ykeys, okey):
                    ps, pk = psum()
                    for ch in range(2):
                        S.op("pe", lambda h, ps=ps, ch=ch: h.matmul(ps[:, 0:n], lhsT=onesf[:, :], rhs=yv(ch), start=(ch == 0), stop=(ch == 1)), reads=ykeys + ["onesf"], writes=[pk])
                    S.op("act", lambda h, ps=ps: h.activation(out=mbc[:, 0:n], in_=ps[:, 0:n], func=AF.Copy, scale=1.0 / 256), reads=[pk], writes=["mbc"])
                    for ch in range(2):
                        S.op("dve", lambda h, ch=ch: h.tensor_tensor(out=yv(ch), in0=yv(ch), in1=mbc[:, 0:n], op=ALU.subtract), reads=["mbc"] + ykeys, writes=ykeys)
                        S.op("act", lambda h, ch=ch: h.activation(out=sq2[:, ch, 0:n], in_=yv(ch), func=AF.Square), reads=ykeys, writes=[("sq2", ch)])
                    ps, pk = psum()
                    for ch in range(2):
                        S.op("pe", lambda h, ps=ps, ch=ch: h.matmul(ps[:, 0:n], lhsT=onesf[:, :], rhs=sq2[:, ch, 0:n], start=(ch == 0), stop=(ch == 1)), reads=[("sq2", 0), ("sq2", 1), "onesf"], writes=[pk])
                    S.op("act", lambda h, ps=ps: h.activation(out=mbc[:, 0:n], in_=ps[:, 0:n], func=AF.Sqrt, scale=1.0 / 256, bias=epsl[:, 0:1]), reads=[pk, "epsl"], writes=["mbc"])
                    S.op("dve", lambda h: h.reciprocal(out=mbc[:, 0:n], in_=mbc[:, 0:n]), reads=["mbc"], writes=["mbc"])
                    for ch in range(2):
                        S.op("dve", lambda h, ch=ch: h.scalar_tensor_tensor(out=yv(ch), in0=yv(ch), scalar=lng[:, ch:ch + 1], in1=mbc[:, 0:n], op0=ALU.mult, op1=ALU.mult),
                             reads=["mbc", "lng"] + ykeys, writes=ykeys)
                        S.op("act", lambda h, ch=ch: h.activation(out=outv(ch), in_=yv(ch), func=AF.Silu, bias=lnb[:, ch:ch + 1]), reads=ykeys + ["lnb"], writes=[okey])

                for b in range(4):
                    ln_silu(lambda ch, b=b: Y[:, ch, b * 512:(b + 1) * 512], 512, lambda ch, b=b: yT[:, 6 + ch, b * 512:(b + 1) * 512], [("Y", 0), ("Y", 1)], ("yT", "c", b))
                ln_silu(lambda ch: Ys[:, ch, :, :].rearrange("p b t -> p (b t)"), NS, lambda ch: yTs[:, 6 + ch, :], [("Ys", 0), ("Ys", 1)], ("yTs", "c"))

                ps, pk = psum()
                for ch in range(2):
                    S.op("pe", lambda h, ps=ps, ch=ch: h.transpose(out=ps[0:30, ch * 128:(ch + 1) * 128], in_=U[:, ch, T:T + 30], identity=ident[:, :]), reads=ukeys + ["ident"], writes=[pk])
                S.op("act", lambda h, ps=ps: h.activation(out=pco[0:30, 0, :], in_=ps[0:30, 0:256], func=AF.Copy), reads=[pk], writes=["pco"])
                S.dma("sp", o_pcv[l], pco[0:30, 0, :], reads=["pco"])
                for b in range(4):
                    ps, pk = psum()
                    for ch in range(2):
                        S.op("pe", lambda h, ps=ps, ch=ch, b=b: h.transpose(out=ps[0:30, ch * 128:(ch + 1) * 128], in_=Us[:, ch, b, 4:34], identity=ident[:, :]),
                             reads=[("Us", b), "Usn", "ident"], writes=[pk])
                    S.op("act", lambda h, ps=ps, b=b: h.activation(out=cvo[0:30, b, :], in_=ps[0:30, 0:256], func=AF.Copy), reads=[pk], writes=[("cvo", b)])
                    S.dma("sp", o_scv[l, b], cvo[0:30, b, :], reads=[("cvo", b)])
                if "yc" in DBG:
                    for ch in range(2):
                        S.dma("sp", DBG["yc"][ch], yT[:, 6 + ch, :], reads=[("yT", "c", b) for b in range(4)])
                S.flush()

        if stage >= 8:
            with contextlib.ExitStack() as ph:
                wrw = sb("wrw", [128, KC, 1024], BF16, ph)
                gcol = sb("gcol", [128, KC], F32, ph)
                epsc = sb("epsc", [128, 1], F32, ph)
                mu = sb("mu", [128, KC], F32, ph)
                WA = sb("WA", [128, 256], BF16, ph)
                GU = sb("GU", [128, 256], BF16, ph)
                cw0 = sb("cw0", [128, 2], F32, ph)
                ca0 = sb("ca0", [128, 2], F32, ph)
                ckk = sb("ckk", [128, 2], F32, ph)
                cka = sb("cka", [128, 2], F32, ph)
                crk = sb("crk", [128, 2], F32, ph)
                clg = sb("clg", [128, 2], F32, ph)
                clb = sb("clb", [128, 2], F32, ph)
                cnh = sb("cnh", [128, 1], F32, ph)
                cge = sb("cge", [128, 1], F32, ph)
                onesbd = sb("onesbd", [128, 128], F32, ph)
                identb = sb("identb", [128, 128], BF16, ph)
                MG = sb("MG", [128, 128], F32, ph)
                ML = sb("ML", [64, 64], F32, ph)
                NB, NCH, NI = 256, 4, 16
                sq = sb("sq", [128, KC, NB], BF16, ph)
                rbc = sb("rbc", [128, NB], F32, ph)
                xn = sb("xn", [128, KC, NB], BF16, ph)
                scr = dict(sq=sq, rbc=rbc, eps=epsc)
                ZB = sb("ZB", [128, 8, NB + 2], F32, ph)
                zlast = sb("zlast", [128, 8, 1], F32, ph)
                DT = sb("DT", [128, 8, NB], F32, ph)
                LA = sb("LA", [128, NB], BF16, ph)
                SG = sb("SG", [128, NB], BF16, ph)
                FN = ("ew", "a", "kk", "t1", "km", "cs", "wi", "wv", "we", "be")
                F = {k: (ZB[:, i, 0:NB] if i < 8 else sb("f_" + k, [128, NB], F32, ph)) for i, k in enumerate(FN)}
                FK = {k: (("ZB", i) if i < 8 else k) for i, k in enumerate(FN)}
                AR = sb("AR", [128, 2, NCH, 2, 64], BF16, ph)
                BK = sb("BK", [128, 2, NCH, 2, 64], BF16, ph)
                VB = sb("VB", [128, 2, 64 + NB], BF16, ph)
                Gt = sb("Gt", [128, 2, NB], F32, ph)
                BON = sb("BON", [128, 2, NB], F32, ph)
                WCb = sb("WCb", [128, 2, NCH], F32, ph)
                KB = sb("KB", [128, NCH, 2, 128], BF16, ph)
                ATM = sb("ATM", [128, NCH, 2, 128], BF16, ph)
                XV = sb("XV", [128, NCH, 4, 64], BF16, ph)
                X = sb("X", [128, NCH, 4, 64], BF16, ph)
                GTs = sb("GTs", [128, NI, 128], BF16, ph)
                Nb = [sb(f"Nb{i}", [128, NI, 64], BF16, ph) for i in range(2)]
                Lb = [sb(f"Lb{i}", [128, NI, 64], BF16, ph) for i in range(2)]
                Pb = [sb(f"Pb{i}", [128, NI, 64], BF16, ph) for i in range(2)]
                Qb = [sb(f"Qb{i}", [128, NI, 64], BF16, ph) for i in range(2)]
                XAKs = sb("XAKs", [128, NI, 64], BF16, ph)
                Wms = sb("Wms", [128, NCH, 2, 2, 64], BF16, ph)
                HBh = sb("HBh", [128, NCH + 1, 2, 2, 64], BF16, ph)
                Hf = sb("Hf", [128, 2, 64], F32, ph)
                Hf2 = sb("Hf2", [128, 2, 64], F32, ph)
                HB = sb("HB", [128, NCH + 1, 2, 64], BF16, ph)
                Ysb = sb("Ysb", [128, 2, NB], F32, ph)
                MB = sb("MB", [128, NB], F32, ph)
                SQ2 = sb("SQ2", [128, NB], F32, ph)
                sto = sb("sto", [64, 4, 64], F32, ph)
                tmpG = [sb(f"tmpG{i}", [128, 128], F32, ph) for i in range(2)]
                Pf = sb("Pf", [64, NI, 64], F32, ph)
                XTs = Pf
                Qf = sb("Qf", [64, NI, 64], F32, ph)
                Xf = sb("Xf", [64, 4, 64], F32, ph)

                load_w(wrw, W["w_in"][l], 0, 1024, "wrw")
                wkeys = [("wrw", c) for c in range(KC)]
                load_cols(gcol, W["g_mix"][l], KC, "gcol")
                load_cols(mu, W["rwkv_mu"][l], KC, "mu")
                S.dma("pool", WA[0:64, :], W["rwkv_w_up"][l], writes=["WA0"])
                S.dma("pool", WA[64:128, :], W["rwkv_a_up"][l], writes=["WA1"])
                S.dma("pool", GU[:, :], W["rwkv_g_up"][l], writes=["GU"])
                for (t_, nm) in ((cw0, "rwkv_w0"), (ca0, "rwkv_a0"), (ckk, "rwkv_k_k"), (cka, "rwkv_k_a"), (crk, "rwkv_r_k"), (clg, "rwkv_ln_g"), (clb, "rwkv_ln_b")):
                    load_cols(t_, W[nm][l], 2, "c_" + nm)
                ckeys = ["c_rwkv_w0", "c_rwkv_a0", "c_rwkv_k_k", "c_rwkv_k_a", "c_rwkv_r_k", "c_rwkv_ln_g", "c_rwkv_ln_b", "cnh", "cge"]
                S.op("dve", lambda h: h.tensor_scalar(out=cw0[:, :], in0=cw0[:, :], scalar1=-1.0, scalar2=None, op0=ALU.mult), reads=["c_rwkv_w0"], writes=["c_rwkv_w0"])
                S.op("pool", lambda h: h.memset(epsc[:], RMS_EPS), writes=["eps"])
                S.op("pool", lambda h: h.memset(cnh[:], -0.5), writes=["cnh"])
                S.op("pool", lambda h: h.memset(cge[:], 64e-5), writes=["cge"])
                S.op("pool", lambda h: h.memset(onesbd[:], 0.0), writes=["onesbd"])
                S.op("pool", lambda h: h.memset(onesbd[0:64, 0:64], 1.0), reads=["onesbd"], writes=["onesbd"])
                S.op("pool", lambda h: h.memset(onesbd[64:128, 64:128], 1.0), reads=["onesbd"], writes=["onesbd"])
                S.op("pool", lambda h: h.tensor_copy(out=identb[:], in_=ident[:]), reads=["ident"], writes=["identb"])
                for r0_ in (0, 64):
                    S.op("pool", lambda h, r0_=r0_: h.affine_select(out=MG[r0_:r0_ + 64, 0:64], in_=onesf[r0_:r0_ + 64, 0:64], pattern=[[1, 64]], compare_op=ALU.is_ge, fill=0.0, base=-1, channel_multiplier=-1),
                         reads=["onesf"], writes=[("MG", r0_, 0)])
                    S.op("pool", lambda h, r0_=r0_: h.affine_select(out=MG[r0_:r0_ + 64, 64:128], in_=onesf[r0_:r0_ + 64, 0:64], pattern=[[1, 64]], compare_op=ALU.is_ge, fill=0.0, base=0, channel_multiplier=-1),
                         reads=["onesf"], writes=[("MG", r0_, 1)])
                mgk = [("MG", 0, 0), ("MG", 0, 1), ("MG", 64, 0), ("MG", 64, 1)]
                S.op("pool", lambda h: h.affine_select(out=ML[:, :], in_=onesf[0:64, 0:64], pattern=[[-1, 64]], compare_op=ALU.is_ge, fill=0.0, base=-1, channel_multiplier=1),
                     reads=["onesf"], writes=["ML"])
                S.op("pool", lambda h: h.memset(zlast[:], 0.0), writes=["zlast"])
                S.op("pool", lambda h: h.memset(VB[:, :, 0:64], 0.0), writes=["VBpad"])
                S.op("pool", lambda h: h.memset(Hf[:], 0.0), writes=["Hf"])
                S.op("pool", lambda h: h.memset(HB[:, 0, :, :], 0.0), writes=[("HB", 0)])
                for zi, zt in enumerate(([ATM, XV, XAKs, Wms, HBh] + Nb + Lb + Pb + Qb) if (RWSUB >= 4 or ZI) else []):
                    nd_ = len(zt.shape)
                    pat_ = {3: "p a b -> p (a b)", 4: "p a b c -> p (a b c)", 5: "p a b c d -> p (a b c d)"}[nd_]
                    S.op("pool", lambda h, zt=zt, pat_=pat_: h.memset(zt[:].rearrange(pat_), 0.0), writes=[("zinit", zi)])

                S.flush()

                def cp(eng, out, in_, reads, writes):
                    if eng == "act":
                        S.op("act", lambda h: h.activation(out=out, in_=in_, func=AF.Copy), reads=reads, writes=writes)
                    else:
                        S.op(eng, lambda h: h.tensor_copy(out=out, in_=in_), reads=reads, writes=writes)

                def block_pre(MGm, mgkm, MLm, mlkey):
                    ark = [("AR", oc, i) for oc in range(2) for i in range(2)]
                    bkk = [("BK", oc, i) for oc in range(2) for i in range(2)]
                    for oc in range(2):
                        for c4 in range(NCH // 4):
                            ps, pk = psum()
                            for q in range(4):
                                ch = c4 * 4 + q
                                S.op("pe", lambda h, ps=ps, q=q, ch=ch, oc=oc: h.matmul(ps[:, q * 128:(q + 1) * 128], lhsT=BK[:, oc, ch, :, :].rearrange("p a t -> p (a t)"), rhs=identb[:, :], start=True, stop=True),
                                     reads=bkk + ["identb"], writes=[pk])
                            cp("act", KB[:, c4 * 4:c4 * 4 + 4, oc, :], ps[:, :].rearrange("p (q k) -> p q k", q=4), [pk], [("KB", oc, c4)])
                            ps, pk = psum()
                            for q in range(4):
                                ch = c4 * 4 + q
                                S.op("pe", lambda h, ps=ps, q=q, ch=ch, oc=oc: h.matmul(ps[:, q * 128:(q + 1) * 128], lhsT=AR[:, oc, ch, :, :].rearrange("p a t -> p (a t)"), rhs=identb[:, :], start=True, stop=True),
                                     reads=ark + ["identb"], writes=[pk])
                            cp("dve", ATM[0:64, c4 * 4:c4 * 4 + 4, oc, :], ps[0:64, :].rearrange("p (q k) -> p q k", q=4), [pk] + ([("zinit", 0)] if RWSUB >= 4 else []), [("ATM", oc, c4)])
                            ps, pk = psum()
                            for q in range(4):
                                ch = c4 * 4 + q
                                S.op("pe", lambda h, ps=ps, q=q, ch=ch, oc=oc: h.matmul(ps[:, q * 128:(q + 1) * 128], lhsT=VB[:, oc, ch * 64:ch * 64 + 128], rhs=identb[:, :], start=True, stop=True),
                                     reads=[("VB", oc), "VBpad", "identb"], writes=[pk])
                            cp("act", X[:, :, :, :].rearrange("p c h v -> p c (h v)")[64:128, c4 * 4:c4 * 4 + 4, oc * 128:(oc + 1) * 128], ps[64:128, :].rearrange("p (q k) -> p q k", q=4), [pk], [("Xv", oc, c4)])
                            if RWSUB >= 4 or XVE:
                              cp("dve", XV[:, :, :, :].rearrange("p c h v -> p c (h v)")[64:128, c4 * 4:c4 * 4 + 4, oc * 128:(oc + 1) * 128], ps[64:128, :].rearrange("p (q k) -> p q k", q=4), [pk, ("zinit", 1)], [("XVv", oc, c4)])
                    kbk = [("KB", oc, c4) for oc in range(2) for c4 in range(NCH // 4)]
                    atk = [("ATM", oc, c4) for oc in range(2) for c4 in range(NCH // 4)]
                    xvk = [("Xv", oc, c4) for oc in range(2) for c4 in range(NCH // 4)]
                    xvvk = [("XVv", oc, c4) for oc in range(2) for c4 in range(NCH // 4)]
                    for inst in range(NI):
                        ps, pk = psum()
                        ch, h_ = inst // 4, inst % 4
                        oc, pb = h_ // 2, (h_ % 2) * 64
                        S.op("pe", lambda h, ps=ps, ch=ch, oc=oc, pb=pb: h.matmul(ps[:, 0:128], lhsT=BK[pb:pb + 64, oc, ch, :, :].rearrange("p a t -> p (a t)"),
                                                                             rhs=AR[pb:pb + 64, oc, ch, :, :].rearrange("p a t -> p (a t)"), start=True, stop=True),
                             reads=ark + bkk, writes=[pk])
                        tg = tmpG[inst % 2]
                        S.op("dve", lambda h, ps=ps, tg=tg: h.tensor_tensor(out=tg[:, :], in0=ps[:, 0:128], in1=MGm[:, :], op=ALU.mult),
                             reads=[pk] + mgkm, writes=[("tmpG", inst % 2)])
                        S.op("act", lambda h, tg=tg, inst=inst: h.activation(out=GTs[:, inst, :], in_=tg[:, :], func=AF.Copy),
                             reads=[("tmpG", inst % 2)], writes=[("GTs", inst // 4, inst % 4)])
                    gtk = [("GTs", g4, q_) for g4 in range(NI // 4) for q_ in range(4)]
                    for inst in range(NI):
                        ps, pk = psum()
                        ch, h_ = inst // 4, inst % 4
                        oc, pb = h_ // 2, (h_ % 2) * 64
                        S.op("pe", lambda h, ps=ps, ch=ch, oc=oc, pb=pb: h.matmul(ps[0:64, 0:64], lhsT=AR[pb:pb + 64, oc, ch, 0, :], rhs=BK[pb:pb + 64, oc, ch, 0, :], start=True, stop=True),
                             reads=ark + bkk, writes=[pk])
                        tg = tmpG[inst % 2]
                        S.op("dve", lambda h, ps=ps, tg=tg: h.tensor_tensor(out=tg[0:64, 0:64], in0=ps[0:64, 0:64], in1=MLm[:, :], op=ALU.mult),
                             reads=[pk, mlkey], writes=[("tmpG", inst % 2)])
                        S.op("act", lambda h, tg=tg, inst=inst: h.activation(out=Lb[0][0:64, inst, :], in_=tg[0:64, 0:64], func=AF.Copy),
                             reads=[("tmpG", inst % 2)], writes=[("L", 0, inst // 8, inst % 8)])
                    for g8 in range(NI // 8):
                        S.op("dve", lambda h, g8=g8: h.tensor_copy(out=Nb[0][0:64, g8 * 8:g8 * 8 + 8, :], in_=GTs[0:64, g8 * 8:g8 * 8 + 8, 0:64]), reads=[("GTs", 2 * g8 + a_, q_) for a_ in range(2) for q_ in range(4)], writes=[("N", 0, g8)])
                        S.op("dve", lambda h, g8=g8: h.tensor_tensor(out=Pb[0][0:64, g8 * 8:g8 * 8 + 8, :], in0=Nb[0][0:64, g8 * 8:g8 * 8 + 8, :], in1=identb[0:64, 0:64].unsqueeze(1).broadcast_to([64, 8, 64]), op=ALU.add),
                             reads=[("N", 0, g8), "identb"], writes=[("P", 0, g8)])
                        S.op("dve", lambda h, g8=g8: h.tensor_copy(out=Pf[:, g8 * 8:g8 * 8 + 8, :], in_=Pb[0][0:64, g8 * 8:g8 * 8 + 8, :]), reads=[("P", 0, g8)], writes=[("Pf", g8)])
                        S.op("dve", lambda h, g8=g8: h.tensor_tensor(out=Qb[0][0:64, g8 * 8:g8 * 8 + 8, :], in0=Lb[0][0:64, g8 * 8:g8 * 8 + 8, :], in1=identb[0:64, 0:64].unsqueeze(1).broadcast_to([64, 8, 64]), op=ALU.add),
                             reads=[("L", 0, g8, q_) for q_ in range(8)] + ["identb"], writes=[("Q", 0, g8), ("L", 0, g8)])
                        S.op("dve", lambda h, g8=g8: h.tensor_copy(out=Qf[:, g8 * 8:g8 * 8 + 8, :], in_=Qb[0][0:64, g8 * 8:g8 * 8 + 8, :]), reads=[("Q", 0, g8)], writes=[("Qf", g8)])
                    for j in range(1, 6):
                        a_, b_ = (j - 1) % 2, j % 2
                        for g8 in range(NI // 8):
                            sl8 = slice(g8 * 8, g8 * 8 + 8)
                            psn, pkn = psum()
                            for q in range(8):
                                i_ = g8 * 8 + q
                                S.op("pe", lambda h, psn=psn, q=q, i_=i_, a_=a_: h.matmul(psn[0:64, q * 64:(q + 1) * 64], lhsT=Lb[a_][:, i_, :], rhs=Nb[a_][:, i_, :], start=True, stop=True),
                                     reads=[("L", a_, g8), ("N", a_, g8)], writes=[pkn])
                            cp("act", Nb[b_][0:64, sl8, :], psn[0:64, :].rearrange("p (q k) -> p q k", q=8), [pkn], [("N", b_, g8)])
                            if j < 5:
                                psl, pkl = psum()
                                for q in range(8):
                                    i_ = g8 * 8 + q
                                    S.op("pe", lambda h, psl=psl, q=q, i_=i_, a_=a_: h.matmul(psl[0:64, q * 64:(q + 1) * 64], lhsT=Nb[a_][:, i_, :], rhs=Lb[a_][:, i_, :], start=True, stop=True),
                                         reads=[("L", a_, g8), ("N", a_, g8)], writes=[pkl])
                                cp("act", Lb[b_][0:64, sl8, :], psl[0:64, :].rearrange("p (q k) -> p q k", q=8), [pkl], [("L", b_, g8)])
                            psp, pkp = psum()
                            for q in range(8):
                                i_ = g8 * 8 + q
                                S.op("pe", lambda h, psp=psp, q=q, i_=i_, a_=a_, b_=b_: h.matmul(psp[0:64, q * 64:(q + 1) * 64], lhsT=Qb[a_][:, i_, :], rhs=Nb[b_][:, i_, :], start=True, stop=True),
                                     reads=[("Q", a_, g8), ("N", b_, g8)], writes=[pkp])
                            S.op("dve", lambda h, psp=psp, sl8=sl8: h.tensor_tensor(out=Pf[:, sl8, :], in0=psp[0:64, :].rearrange("p (q k) -> p q k", q=8), in1=Pf[:, sl8, :], op=ALU.add),
                                 reads=[pkp, ("Pf", g8)], writes=[("Pf", g8)])
                            S.op("act", lambda h, sl8=sl8, b_=b_: h.activation(out=Pb[b_][0:64, sl8, :], in_=Pf[:, sl8, :], func=AF.Copy), reads=[("Pf", g8), ("P", a_, g8)], writes=[("P", b_, g8)])
                            if j < 5:
                                psq, pkq = psum()
                                for q in range(8):
                                    i_ = g8 * 8 + q
                                    S.op("pe", lambda h, psq=psq, q=q, i_=i_, a_=a_, b_=b_: h.matmul(psq[0:64, q * 64:(q + 1) * 64], lhsT=Nb[b_][:, i_, :], rhs=Qb[a_][:, i_, :], start=True, stop=True),
                                         reads=[("Q", a_, g8), ("N", b_, g8)], writes=[pkq])
                                S.op("dve", lambda h, psq=psq, sl8=sl8: h.tensor_tensor(out=Qf[:, sl8, :], in0=psq[0:64, :].rearrange("p (q k) -> p q k", q=8), in1=Qf[:, sl8, :], op=ALU.add),
                                     reads=[pkq, ("Qf", g8)], writes=[("Qf", g8)])
                                S.op("act", lambda h, sl8=sl8, b_=b_: h.activation(out=Qb[b_][0:64, sl8, :], in_=Qf[:, sl8, :], func=AF.Copy), reads=[("Qf", g8), ("Q", a_, g8)], writes=[("Q", b_, g8)])
                    PF = Pb[1]
                    pfk = lambda g8: ("P", 1, g8)
                    for g8 in range(NI // 8):
                        sl8 = slice(g8 * 8, g8 * 8 + 8)
                        ps, pk = psum()
                        for q in range(8):
                            i_ = g8 * 8 + q
                            ch, h_ = i_ // 4, i_ % 4
                            S.op("pe", lambda h, ps=ps, q=q, i_=i_, ch=ch, h_=h_: h.matmul(ps[0:64, q * 64:(q + 1) * 64], lhsT=GTs[:, i_, 0:64], rhs=XV[:, ch, h_, :], start=True, stop=True),
                                 reads=gtk + xvvk, writes=[pk])
                        cp("act", XAKs[0:64, sl8, :], ps[0:64, :].rearrange("p (q k) -> p q k", q=8), [pk], [("XAK", g8)])
                        ps, pk = psum()
                        for q in range(8):
                            i_ = g8 * 8 + q
                            S.op("pe", lambda h, ps=ps, q=q, i_=i_: h.matmul(ps[0:64, q * 64:(q + 1) * 64], lhsT=PF[:, i_, :], rhs=XAKs[:, i_, :], start=True, stop=True),
                                 reads=[pfk(g8), ("XAK", g8)], writes=[pk])
                        cp("dve", XTs[:, sl8, :], ps[0:64, :].rearrange("p (q k) -> p q k", q=8), [pk, ("Pf", g8)], [("Pf", g8)])
                        ps, pk = psum()
                        for q in range(8):
                            i_ = g8 * 8 + q
                            ch, h_ = i_ // 4, i_ % 4
                            oc, pb = h_ // 2, (h_ % 2) * 64
                            col = ((ch % 2) * 2 + oc) * 64
                            S.op("pe", lambda h, ps=ps, i_=i_, ch=ch, oc=oc, pb=pb, col=col, h_=h_: h.matmul(ps[pb:pb + 64, col:col + 64], lhsT=ATM[:, ch, oc, (h_ % 2) * 64:(h_ % 2) * 64 + 64], rhs=PF[:, i_, :], start=True, stop=True),
                                 reads=atk + [pfk(g8)], writes=[pk])
                        for c2 in range(2):
                            for hh in range(2):
                                cp("act" if hh == 0 else "dve", Wms[hh * 64:(hh + 1) * 64, g8 * 2 + c2, hh, :, :], ps[hh * 64:(hh + 1) * 64, c2 * 128:(c2 + 1) * 128].rearrange("p (o t) -> p o t", o=2),
                                   [pk, ("zinit", 3)], [("Wm", g8, c2, hh)])
                    wmk = [("Wm", g8, c2, hh) for g8 in range(NI // 8) for c2 in range(2) for hh in range(2)]
                    xtk = [("Pf", g8) for g8 in range(NI // 8)]
                    return dict(ark=ark, bkk=bkk, kbk=kbk, atk=atk, xvk=xvk, xvvk=xvvk, gtk=gtk, wmk=wmk, xtk=xtk, PF=PF)

                for blk in range(T // NB):
                    n = NB
                    blk0 = blk * NB
                    if RWSUB <= 0:
                        break
                    rmsnorm_fm(xT, blk0, n, gcol, xn, "xn", scr)
                    xnk = [("xn", c) for c in range(KC)]
                    if RWX >= 2:
                        S.op("dve", lambda h: h.tensor_copy(out=ZB[:, :, 0:1], in_=zlast[:, :, :]), reads=["zlast"], writes=["ZB0"])
                    for oc in range(8):
                        ps, pk = psum()
                        for c in range(KC):
                            S.op("pe", lambda h, ps=ps, c=c, oc=oc: h.matmul(ps[:, 0:n], lhsT=wrw[:, c, oc * 128:(oc + 1) * 128], rhs=xn[:, c, 0:n], start=(c == 0), stop=(c == KC - 1)),
                                 reads=xnk + wkeys, writes=[pk])
                        cp("act" if oc % 2 == 0 else "dve", ZB[:, oc, 1:NB + 1], ps[:, 0:n], [pk], [("ZB", oc)])
                    zbk = [("ZB", oc) for oc in range(8)] + ["ZB0"]
                    if RWX >= 2:
                        S.op("dve", lambda h: h.tensor_copy(out=zlast[:, :, :], in_=ZB[:, :, NB:NB + 1]), reads=zbk, writes=["zlast"])
                    if RWX >= 3:
                        S.op("dve", lambda h: h.tensor_tensor(out=DT[:, :, :], in0=ZB[:, :, 0:NB], in1=ZB[:, :, 1:NB + 1], op=ALU.subtract), reads=zbk, writes=["DT"])
                    for oc in range(8 if RWX >= 4 else 0):
                        S.op("dve", lambda h, oc=oc: h.tensor_scalar(out=DT[:, oc, :], in0=DT[:, oc, :], scalar1=mu[:, oc:oc + 1], scalar2=None, op0=ALU.mult),
                             reads=["DT", "mu"], writes=["DT"])
                        S.op("dve", lambda h, oc=oc: h.tensor_tensor(out=DT[:, oc, :], in0=DT[:, oc, :], in1=ZB[:, oc, 1:NB + 1], op=ALU.add),
                             reads=["DT"] + zbk, writes=["DT"])
                    if RWSUB < 2:
                        continue
                    S.op("act", lambda h: h.activation(out=LA[0:64, :], in_=DT[0:64, 6, :], func=AF.Tanh), reads=["DT"], writes=["LA0"])
                    S.op("act", lambda h: h.activation(out=LA[64:128, :], in_=DT[64:128, 6, :], func=AF.Copy), reads=["DT"], writes=["LA1"])
                    S.op("act", lambda h: h.activation(out=SG[:, :], in_=DT[:, 7, :], func=AF.Sigmoid), reads=["DT"], writes=["SG"])
                    for oc in range(2):
                        rr, kq, vv = DT[:, oc, :], DT[:, 2 + oc, :], DT[:, 4 + oc, :]
                        psw, pkw = psum()
                        S.op("pe", lambda h, psw=psw, oc=oc: h.matmul(psw[:, 0:n], lhsT=WA[0:64, oc * 128:(oc + 1) * 128], rhs=LA[0:64, :], start=True, stop=True), reads=["WA0", "LA0"], writes=[pkw])
                        psa, pka = psum()
                        S.op("pe", lambda h, psa=psa, oc=oc: h.matmul(psa[:, 0:n], lhsT=WA[64:128, oc * 128:(oc + 1) * 128], rhs=LA[64:128, :], start=True, stop=True), reads=["WA1", "LA1"], writes=[pka])
                        psg, pkg = psum()
                        S.op("pe", lambda h, psg=psg, oc=oc: h.matmul(psg[:, 0:n], lhsT=GU[:, oc * 128:(oc + 1) * 128], rhs=SG[:, :], start=True, stop=True), reads=["GU", "SG"], writes=[pkg])
                        ew, aa, kk, t1, km, cs, wi, wv, we, be = (F[k] for k in FN)
                        S.op("act", lambda h, psw=psw, oc=oc: h.activation(out=ew[:, :], in_=psw[:, 0:n], func=AF.Exp, scale=-1.0, bias=cw0[:, oc:oc + 1]), reads=[pkw] + ckeys, writes=[FK["ew"]])
                        S.op("dve", lambda h: h.tensor_scalar(out=ew[:, :], in0=ew[:, :], scalar1=1.0, scalar2=None, op0=ALU.add), reads=[FK["ew"]], writes=[FK["ew"]])
                        S.op("act", lambda h: h.activation(out=ew[:, :], in_=ew[:, :], func=AF.Ln), reads=[FK["ew"]], writes=[FK["ew"]])
                        S.op("act", lambda h: h.activation(out=ew[:, :], in_=ew[:, :], func=AF.Exp, scale=-1.0, bias=cnh[:, 0:1]), reads=[FK["ew"]] + ckeys, writes=[FK["ew"]])
                        S.op("act", lambda h, psa=psa, oc=oc: h.activation(out=aa[:, :], in_=psa[:, 0:n], func=AF.Sigmoid, bias=ca0[:, oc:oc + 1]), reads=[pka] + ckeys, writes=[FK["a"]])
                        cp("act", Gt[:, oc, :], psg[:, 0:n], [pkg], [("Gt", oc)])
                        S.op("dve", lambda h, kq=kq, oc=oc: h.tensor_scalar(out=kk[:, :], in0=kq, scalar1=ckk[:, oc:oc + 1], scalar2=None, op0=ALU.mult), reads=["DT"] + ckeys, writes=[FK["kk"]])
                        S.op("act", lambda h: h.activation(out=t1[:, :], in_=kk[:, :], func=AF.Square), reads=[FK["kk"]], writes=[FK["t1"]])
                        ps, pk = psum()
                        S.op("pe", lambda h, ps=ps: h.matmul(ps[:, 0:n], lhsT=onesbd[:, :], rhs=t1[:, :], start=True, stop=True), reads=["onesbd", FK["t1"]], writes=[pk])
                        S.op("act", lambda h, ps=ps: h.activation(out=t1[:, :], in_=ps[:, 0:n], func=AF.Sqrt), reads=[pk], writes=[FK["t1"]])
                        S.op("dve", lambda h: h.tensor_scalar(out=t1[:, :], in0=t1[:, :], scalar1=1e-12, scalar2=None, op0=ALU.max), reads=[FK["t1"]], writes=[FK["t1"]])
                        S.op("dve", lambda h: h.reciprocal(out=t1[:, :], in_=t1[:, :]), reads=[FK["t1"]], writes=[FK["t1"]])
                        S.op("dve", lambda h: h.tensor_tensor(out=kk[:, :], in0=kk[:, :], in1=t1[:, :], op=ALU.mult), reads=[FK["kk"], FK["t1"]], writes=[FK["kk"]])
                        S.op("dve", lambda h, oc=oc: h.tensor_scalar(out=t1[:, :], in0=aa[:, :], scalar1=cka[:, oc:oc + 1], scalar2=cka[:, oc:oc + 1], op0=ALU.mult, op1=ALU.subtract), reads=[FK["a"], FK["t1"]] + ckeys, writes=[FK["t1"]])
                        S.op("dve", lambda h: h.tensor_scalar(out=t1[:, :], in0=t1[:, :], scalar1=1.0, scalar2=None, op0=ALU.add), reads=[FK["t1"]], writes=[FK["t1"]])
                        S.op("dve", lambda h, kq=kq: h.tensor_tensor(out=km[:, :], in0=t1[:, :], in1=kq, op=ALU.mult), reads=[FK["t1"], "DT"], writes=[FK["km"]])
                        S.op("dve", lambda h: h.tensor_tensor(out=be[:, :], in0=kk[:, :], in1=aa[:, :], op=ALU.mult), reads=[FK["kk"], FK["a"]], writes=[FK["be"]])
                        S.op("dve", lambda h, rr=rr: h.tensor_tensor(out=t1[:, :], in0=rr, in1=km[:, :], op=ALU.mult), reads=["DT", FK["km"], FK["t1"]], writes=[FK["t1"]])
                        S.op("dve", lambda h, oc=oc: h.tensor_scalar(out=t1[:, :], in0=t1[:, :], scalar1=crk[:, oc:oc + 1], scalar2=None, op0=ALU.mult), reads=[FK["t1"]] + ckeys, writes=[FK["t1"]])
                        ps, pk = psum()
                        S.op("pe", lambda h, ps=ps: h.matmul(ps[:, 0:n], lhsT=onesbd[:, :], rhs=t1[:, :], start=True, stop=True), reads=["onesbd", FK["t1"]], writes=[pk])
                        S.op("dve", lambda h, ps=ps, vv=vv, oc=oc: h.tensor_tensor(out=BON[:, oc, :], in0=ps[:, 0:n], in1=vv, op=ALU.mult), reads=[pk, "DT"], writes=[("BON", oc)])
                        for ch in range(NCH):
                            S.op("dve", lambda h, ch=ch: h.tensor_tensor_scan(out=cs[:, ch * 64:(ch + 1) * 64], data0=onesf[:, 0:64], data1=ew[:, ch * 64:(ch + 1) * 64], initial=0.0, op0=ALU.mult, op1=ALU.add),
                                 reads=[FK["ew"], "onesf"], writes=[FK["cs"]])
                        S.op("act", lambda h: h.activation(out=wi[:, :], in_=cs[:, :], func=AF.Exp, scale=-1.0), reads=[FK["cs"]], writes=[FK["wi"]])
                        S.op("act", lambda h: h.activation(out=wv[:, :], in_=cs[:, :], func=AF.Exp), reads=[FK["cs"]], writes=[FK["wv"]])
                        S.op("dve", lambda h: h.tensor_tensor(out=we[:, :], in0=cs[:, :], in1=ew[:, :], op=ALU.subtract), reads=[FK["cs"], FK["ew"]], writes=[FK["we"]])
                        S.op("act", lambda h: h.activation(out=we[:, :], in_=we[:, :], func=AF.Exp, scale=-1.0), reads=[FK["we"]], writes=[FK["we"]])
                        v3 = lambda a_: a_.rearrange("p (c t) -> p c t", c=NCH)
                        S.op("dve", lambda h: h.tensor_scalar(out=t1[:, :], in0=kk[:, :], scalar1=-1.0, scalar2=None, op0=ALU.mult), reads=[FK["kk"], FK["t1"]], writes=[FK["t1"]])
                        S.op("dve", lambda h, oc=oc: h.tensor_tensor(out=AR[:, oc, :, 0, :], in0=v3(t1[:, :]), in1=v3(we[:, :]), op=ALU.mult), reads=[FK["t1"], FK["we"]], writes=[("AR", oc, 0)])
                        S.op("dve", lambda h, oc=oc, rr=rr: h.tensor_tensor(out=AR[:, oc, :, 1, :], in0=v3(rr), in1=v3(wi[:, :]), op=ALU.mult), reads=["DT", FK["wi"]], writes=[("AR", oc, 1)])
                        S.op("dve", lambda h, oc=oc: h.tensor_tensor(out=BK[:, oc, :, 0, :], in0=v3(be[:, :]), in1=v3(wv[:, :]), op=ALU.mult), reads=[FK["be"], FK["wv"]], writes=[("BK", oc, 0)])
                        S.op("dve", lambda h, oc=oc: h.tensor_tensor(out=BK[:, oc, :, 1, :], in0=v3(km[:, :]), in1=v3(wv[:, :]), op=ALU.mult), reads=[FK["km"], FK["wv"]], writes=[("BK", oc, 1)])
                        S.op("dve", lambda h, oc=oc: h.tensor_copy(out=WCb[:, oc, :], in_=v3(wi[:, :])[:, :, 63]), reads=[FK["wi"]], writes=[("WCb", oc)])
                        S.op("act", lambda h, oc=oc, vv=vv: h.activation(out=VB[:, oc, 64:64 + NB], in_=vv, func=AF.Copy), reads=["DT"], writes=[("VB", oc)])
                    if RWSUB < 3:
                        continue
                    pre_ = block_pre(MG, mgk, ML, "ML")
                    ark, bkk, kbk, atk, xvk, xvvk, gtk, wmk, xtk, PF = (pre_[k_] for k_ in ("ark", "bkk", "kbk", "atk", "xvk", "xvvk", "gtk", "wmk", "xtk", "PF"))
                    if RWSUB < 7:
                        continue
                    for ch in range(NCH):
                        psu, pku = psum()
                        for h_ in range(4):
                            oc, pb = h_ // 2, (h_ % 2) * 64
                            S.op("pe", lambda h, psu=psu, ch=ch, h_=h_, oc=oc, pb=pb: h.matmul(psu[0:64, h_ * 64:(h_ + 1) * 64], lhsT=Wms[:, ch, h_ % 2, oc, :], rhs=HB[:, ch, oc, :], start=True, stop=True),
                                 reads=wmk + [("HB", ch)], writes=[pku])
                        S.op("dve", lambda h, psu=psu, ch=ch: h.tensor_tensor(out=Xf[:, :, :], in0=psu[0:64, 0:256].rearrange("p (a v) -> p a v", a=4), in1=XTs[:, ch * 4:ch * 4 + 4, :], op=ALU.add),
                             reads=[pku] + xtk, writes=["Xf"])
                        S.op("act", lambda h, ch=ch: h.activation(out=X[0:64, ch, :, :], in_=Xf[:, :, :], func=AF.Copy), reads=["Xf"], writes=[("Xu", ch)])
                        psh, pkh = psum()
                        for h_ in range(4):
                            oc, pb = h_ // 2, (h_ % 2) * 64
                            S.op("pe", lambda h, psh=psh, ch=ch, h_=h_, oc=oc, pb=pb: h.matmul(psh[pb:pb + 64, oc * 64:(oc + 1) * 64], lhsT=KB[:, ch, oc, (h_ % 2) * 64:(h_ % 2) * 64 + 64], rhs=X[:, ch, h_, :], start=True, stop=True),
                                 reads=kbk + xvk + [("Xu", ch)], writes=[pkh])
                        S.op("dve", lambda h, psh=psh: h.tensor_tensor(out=Hf2[:, :, :], in0=psh[:, 0:128].rearrange("p (o v) -> p o v", o=2), in1=Hf[:, :, :], op=ALU.add), reads=[pkh, "Hf"], writes=["Hf2"])
                        S.op("dve", lambda h, ch=ch: h.tensor_tensor(out=Hf[:, :, :], in0=Hf2[:, :, :], in1=WCb[:, :, ch:ch + 1].broadcast_to([128, 2, 64]), op=ALU.mult),
                             reads=["Hf2", ("WCb", 0), ("WCb", 1)], writes=["Hf"])
                        S.op("act", lambda h, ch=ch: h.activation(out=HB[:, ch + 1, :, :], in_=Hf[:, :, :], func=AF.Copy), reads=["Hf"], writes=[("HB", ch + 1)])
                        for hh in range(2):
                            S.op("act", lambda h, ch=ch, hh=hh: h.activation(out=HBh[hh * 64:(hh + 1) * 64, ch + 1, hh, :, :], in_=Hf[hh * 64:(hh + 1) * 64, :, :], func=AF.Copy),
                                 reads=["Hf", ("zinit", 4)], writes=[("HBh", ch + 1, hh)])
                    if RWSUB < 8:
                        continue
                    hbk = [("HB", c_) for c_ in range(NCH + 1)]
                    hbhk = [("HBh", c_, hh_) for c_ in range(NCH + 1) for hh_ in range(2)]
                    xuk = [("Xu", c_) for c_ in range(NCH)]
                    for oc in range(2):
                        ps, pk = psum()
                        for ch in range(NCH):
                            for hh in range(2):
                                h_, pb = oc * 2 + hh, hh * 64
                                S.op("pe", lambda h, ps=ps, ch=ch, oc=oc, pb=pb, hh=hh: h.matmul(ps[pb:pb + 64, ch * 64:(ch + 1) * 64], lhsT=HBh[:, ch, hh, oc, :], rhs=AR[:, oc, ch, 1, :], start=True, stop=False),
                                     reads=hbhk + ark, writes=[pk])
                                S.op("pe", lambda h, ps=ps, ch=ch, h_=h_, pb=pb: h.matmul(ps[pb:pb + 64, ch * 64:(ch + 1) * 64], lhsT=X[:, ch, h_, :], rhs=GTs[:, ch * 4 + h_, 64:128], start=False, stop=True),
                                     reads=xuk + xvk + gtk, writes=[pk])
                        cp("act", Ysb[:, oc, :], ps[:, 0:NB], [pk], [("Ysb", oc)])
                        yv = Ysb[:, oc, :]
                        ps, pk = psum()
                        S.op("pe", lambda h, ps=ps, yv=yv: h.matmul(ps[:, 0:n], lhsT=onesbd[:, :], rhs=yv, start=True, stop=True), reads=[("Ysb", oc), "onesbd"], writes=[pk])
                        S.op("act", lambda h, ps=ps: h.activation(out=MB[:, :], in_=ps[:, 0:n], func=AF.Copy, scale=1.0 / 64), reads=[pk], writes=["MB"])
                        S.op("dve", lambda h, yv=yv: h.tensor_tensor(out=yv, in0=yv, in1=MB[:, :], op=ALU.subtract), reads=["MB", ("Ysb", oc)], writes=[("Ysb", oc)])
                        S.op("act", lambda h, yv=yv: h.activation(out=SQ2[:, :], in_=yv, func=AF.Square), reads=[("Ysb", oc)], writes=["SQ2"])
                        ps, pk = psum()
                        S.op("pe", lambda h, ps=ps: h.matmul(ps[:, 0:n], lhsT=onesbd[:, :], rhs=SQ2[:, :], start=True, stop=True), reads=["SQ2", "onesbd"], writes=[pk])
                        S.op("act", lambda h, ps=ps: h.activation(out=MB[:, :], in_=ps[:, 0:n], func=AF.Sqrt, scale=1.0 / 64, bias=cge[:, 0:1]), reads=[pk] + ckeys, writes=["MB"])
                        S.op("dve", lambda h: h.reciprocal(out=MB[:, :], in_=MB[:, :]), reads=["MB"], writes=["MB"])
                        S.op("dve", lambda h, yv=yv, oc=oc: h.scalar_tensor_tensor(out=yv, in0=MB[:, :], scalar=clg[:, oc:oc + 1], in1=yv, op0=ALU.mult, op1=ALU.mult), reads=["MB", ("Ysb", oc)] + ckeys, writes=[("Ysb", oc)])
                        S.op("dve", lambda h, yv=yv, oc=oc: h.scalar_tensor_tensor(out=yv, in0=BON[:, oc, :], scalar=clb[:, oc:oc + 1], in1=yv, op0=ALU.add, op1=ALU.add), reads=[("Ysb", oc), ("BON", oc)] + ckeys, writes=[("Ysb", oc)])
                        S.op("dve", lambda h, yv=yv, oc=oc, blk0=blk0: h.tensor_tensor(out=yT[:, oc, blk0:blk0 + NB], in0=yv, in1=Gt[:, oc, :], op=ALU.mult), reads=[("Ysb", oc), ("Gt", oc)], writes=[("yT", FK["a"], oc, blk)])
                    S.op("act", lambda h: h.activation(out=HB[:, 0, :, :], in_=Hf[:, :, :], func=AF.Copy), reads=["Hf"] + hbk, writes=[("HB", 0)])
                    for hh in range(2):
                        S.op("act", lambda h, hh=hh: h.activation(out=HBh[hh * 64:(hh + 1) * 64, 0, hh, :, :], in_=Hf[hh * 64:(hh + 1) * 64, :, :], func=AF.Copy),
                             reads=["Hf"] + hbhk, writes=[("HBh", 0, hh)])

                for oc in range(2 if RWSUB >= 0 else 0):
                    ps, pk = psum()
                    S.op("pe", lambda h, ps=ps, oc=oc: h.transpose(out=ps[0:64, 0:128], in_=Hf[:, oc, :], identity=ident[:, :]), reads=["Hf", "ident"], writes=[pk])
                    cp("act", sto[:, 2 * oc:2 * oc + 2, :], ps[0:64, 0:128].rearrange("p (a k) -> p a k", a=2), [pk], [("sto", oc)])
                    S.dma("sp", o_prw[l, 2 * oc:2 * oc + 2].rearrange("a v k -> v a k"), sto[:, 2 * oc:2 * oc + 2, :], reads=[("sto", oc)])
                if RWSUB >= 0:
                    S.dma("sp", o_psh[l].rearrange("(c p) -> p c", p=128), zlast[:, :, 0], reads=["zlast"], allow_slow_non_contiguous=True)

                if stage >= 13 and SRW >= 1:
                    S.flush()
                    n = NS
                    ZBs = ZB[:, :, 0:24].rearrange("p c (b t) -> p c b t", b=4)
                    DTs = ZB[:, :, 32:48]
                    FS = [ZB[:, k_, 64:80] for k_ in range(8)] + [ZB[:, 0, 96:112], ZB[:, 1, 96:112]]
                    Gts, BONs, WCs = ZB[:, 2:4, 96:112], ZB[:, 4:6, 96:112], ZB[:, 6:8, 96:100]
                    rmask, rtmp = ZB[:, 0, 128:132], ZB[:, 1, 128:132]
                    cmask = DT[:, 0, :].rearrange("p (b t) -> p b t", b=4)
                    ctmp = DT[:, 6, :].rearrange("p (b t) -> p b t", b=4)
                    BDh, MLs, MGs = DT[:, 1, 0:64], DT[0:64, 1, 64:128], DT[:, 1, 128:256]
                    Hfs = DT[:, 2:4, :].rearrange("p r (b o v) -> p (r b) o v", b=2, o=2)
                    Hfs2 = DT[:, 4:6, :].rearrange("p r (b o v) -> p (r b) o v", b=2, o=2)
                    hst = [DT[0:64, 7, 0:128].rearrange("p (a k) -> p a k", a=2), DT[0:64, 7, 128:256].rearrange("p (a k) -> p a k", a=2)]
                    Yss, MBs, SQs = Ysb[:, :, 0:64], MB[:, 0:64], SQ2[:, 0:64]
                    LAs, SGs = LA[:, 0:NS], SG[:, 0:NS]
                    HBs, HBhs = HB[:, 0:4, :, :], HBh[:, 0:4, :, :, :]
                    Wmb = BON[:, :, :].bitcast(BF16).rearrange("p r (b j t) -> p (r b) j t", b=2, j=4)
                    KBb = Gt[:, :, :].bitcast(BF16).rearrange("p r (b o k) -> p (r b) o k", b=2, o=2)
                    ARbv = [tmpG[b_ // 2][:, (b_ % 2) * 64:(b_ % 2) * 64 + 64].bitcast(BF16).rearrange("p (o t) -> p o t", o=2) for b_ in range(4)]
                    S.op("pool", lambda h: h.tensor_copy(out=rtmp, in_=onesf[:, 0:4]), reads=["onesf"], writes=["rtmp"])
                    for h0 in (0, 64):
                        S.op("pool", lambda h, h0=h0: h.affine_select(out=rmask[h0:h0 + 64, :], in_=rtmp[h0:h0 + 64, :], pattern=[[-4, 4]], compare_op=ALU.is_ge, fill=0.0, base=0, channel_multiplier=1), reads=["rtmp"], writes=[("rm1", h0)])
                    for h0 in (0, 64):
                        S.op("pool", lambda h, h0=h0: h.affine_select(out=rtmp[h0:h0 + 64, :], in_=rmask[h0:h0 + 64, :], pattern=[[4, 4]], compare_op=ALU.is_ge, fill=0.0, base=3, channel_multiplier=-1), reads=[("rm1", 0), ("rm1", 64)], writes=[("rm2", h0)])
                    S.op("pool", lambda h: h.tensor_copy(out=rmask, in_=rtmp), reads=[("rm2", 0), ("rm2", 64)], writes=["rmask"])
                    S.op("pool", lambda h: h.memset(DT[:, 0, :], 1.0), writes=["cm0"])
                    S.op("pool", lambda h: h.affine_select(out=ctmp, in_=cmask, pattern=[[-4, 4], [1, 64]], compare_op=ALU.is_ge, fill=0.0, base=0, channel_multiplier=0), reads=["cm0"], writes=["cm1"])
                    S.op("pool", lambda h: h.affine_select(out=cmask, in_=ctmp, pattern=[[4, 4], [-1, 64]], compare_op=ALU.is_ge, fill=0.0, base=3, channel_multiplier=0), reads=["cm1", "cm0"], writes=["cmask"])
                    S.op("dve", lambda h: h.tensor_scalar(out=BDh, in0=cmask[:, 0, :], scalar1=rmask[:, 0:1], scalar2=None, op0=ALU.mult), reads=["cmask", "rmask"], writes=["BDh"])
                    for b in range(1, 4):
                        S.op("dve", lambda h, b=b: h.scalar_tensor_tensor(out=BDh, in0=cmask[:, b, :], scalar=rmask[:, b:b + 1], in1=BDh, op0=ALU.mult, op1=ALU.add), reads=["cmask", "rmask", "BDh"], writes=["BDh"])
                    for hf in range(2):
                        S.op("dve", lambda h, hf=hf: h.tensor_tensor(out=MGs[:, hf * 64:(hf + 1) * 64], in0=MG[:, hf * 64:(hf + 1) * 64], in1=BDh, op=ALU.mult), reads=["BDh"], writes=[("MGs", hf)])
                    S.op("dve", lambda h: h.tensor_tensor(out=MLs, in0=ML[:, :], in1=DT[0:64, 1, 0:64], op=ALU.mult), reads=["BDh"], writes=["MLs"])
                    for _once in (0,):
                        if SRW < 2:
                            break
                        for b in range(4):
                            for oc in range(2):
                                hs_ = hst[(b * 2 + oc) % 2]
                                hk_ = ("hst", (b * 2 + oc) % 2)
                                S.dma("sp", hs_, srw[l, b, 2 * oc:2 * oc + 2].rearrange("a v k -> v a k"), writes=[hk_])
                                ps, pk = psum()
                                S.op("pe", lambda h, ps=ps, hs_=hs_: h.transpose(out=ps[:, 0:64], in_=hs_.rearrange("p a k -> p (a k)"), identity=ident[0:64, 0:64]), reads=[hk_, "ident"], writes=[pk])
                                S.op("act", lambda h, ps=ps, b=b, oc=oc: h.activation(out=Hfs[:, b, oc, :], in_=ps[:, 0:64], func=AF.Copy), reads=[pk], writes=[("Hfs", b, oc)])
                            hfk = [("Hfs", b, 0), ("Hfs", b, 1)]
                            S.op("act", lambda h, b=b: h.activation(out=HBs[:, b, :, :], in_=Hfs[:, b, :, :], func=AF.Copy), reads=hfk, writes=[("HBs", b)])
                            for hh in range(2):
                                S.op("act", lambda h, b=b, hh=hh: h.activation(out=HBhs[hh * 64:(hh + 1) * 64, b, hh, :, :], in_=Hfs[hh * 64:(hh + 1) * 64, b, :, :], func=AF.Copy), reads=hfk, writes=[("HBhs", b, hh)])
                        if SRW < 3:
                            break
                        rmsnorm_fm(xTs, 0, n, gcol, xn, "xns", scr)
                        xnk = [("xns", c) for c in range(KC)]
                        for b in range(4):
                            S.dma("sp", ZBs[:, :, b, 0], ssh[l, b].rearrange("(c p) -> p c", p=128), writes=[("zs0", b)], allow_slow_non_contiguous=True)
                        for oc in range(8):
                            ps, pk = psum()
                            for c in range(KC):
                                S.op("pe", lambda h, ps=ps, c=c, oc=oc: h.matmul(ps[:, 0:n], lhsT=wrw[:, c, oc * 128:(oc + 1) * 128], rhs=xn[:, c, 0:n], start=(c == 0), stop=(c == KC - 1)), reads=xnk, writes=[pk])
                            S.op("act", lambda h, ps=ps, oc=oc: h.activation(out=ZBs[:, oc, :, 1:5], in_=ps[:, 0:n].rearrange("p (b t) -> p b t", b=4), func=AF.Copy), reads=[pk], writes=[("zsn", oc)])
                        zk_ = [("zs0", b) for b in range(4)] + [("zsn", oc) for oc in range(8)]
                        for b in range(4):
                            S.dma("sp", o_ssh[l, b].rearrange("(c p) -> p c", p=128), ZBs[:, :, b, 4], reads=zk_, allow_slow_non_contiguous=True)
                        for oc in range(8):
                            dv = DTs[:, oc, :].rearrange("p (b t) -> p b t", b=4)
                            S.op("dve", lambda h, oc=oc, dv=dv: h.tensor_tensor(out=dv, in0=ZBs[:, oc, :, 0:4], in1=ZBs[:, oc, :, 1:5], op=ALU.subtract), reads=zk_, writes=[("DTs", oc)])
                            S.op("dve", lambda h, oc=oc, dv=dv: h.tensor_scalar(out=dv, in0=dv, scalar1=mu[:, oc:oc + 1], scalar2=None, op0=ALU.mult), reads=[("DTs", oc)], writes=[("DTs", oc)])
                            S.op("dve", lambda h, oc=oc, dv=dv: h.tensor_tensor(out=dv, in0=dv, in1=ZBs[:, oc, :, 1:5], op=ALU.add), reads=[("DTs", oc)] + zk_, writes=[("DTs", oc)])
                        if SRW < 4:
                            break
                        dk = lambda i_: ("DTs", i_)
                        S.op("act", lambda h: h.activation(out=LAs[0:64, :], in_=DTs[0:64, 6, :], func=AF.Tanh), reads=[dk(6)], writes=["LAs0"])
                        S.op("act", lambda h: h.activation(out=LAs[64:128, :], in_=DTs[64:128, 6, :], func=AF.Copy), reads=[dk(6)], writes=["LAs1"])
                        S.op("act", lambda h: h.activation(out=SGs, in_=DTs[:, 7, :], func=AF.Sigmoid), reads=[dk(7)], writes=["SGs"])
                        for zt_, zn_ in ((AR, "p a b c d -> p (a b c d)"), (BK, "p a b c d -> p (a b c d)"), (VB, "p a b -> p (a b)")):
                            S.op("pool", lambda h, zt_=zt_, zn_=zn_: h.memset(zt_[:].rearrange(zn_), 0.0), writes=[("z0", id(zt_))])
                        z0k = [("z0", id(AR)), ("z0", id(BK)), ("z0", id(VB))]
                        ew, aa, kk, t1, km, cs, wi, wv, we, be = FS
                        fsk = lambda i_: ("FS", i_)
                        for oc in range(2):
                            rr, kq, vv = DTs[:, oc, :], DTs[:, 2 + oc, :], DTs[:, 4 + oc, :]
                            psw, pkw = psum()
                            S.op("pe", lambda h, psw=psw, oc=oc: h.matmul(psw[:, 0:n], lhsT=WA[0:64, oc * 128:(oc + 1) * 128], rhs=LAs[0:64, :], start=True, stop=True), reads=["LAs0"], writes=[pkw])
                            psa, pka = psum()
                            S.op("pe", lambda h, psa=psa, oc=oc: h.matmul(psa[:, 0:n], lhsT=WA[64:128, oc * 128:(oc + 1) * 128], rhs=LAs[64:128, :], start=True, stop=True), reads=["LAs1"], writes=[pka])
                            psg, pkg = psum()
                            S.op("pe", lambda h, psg=psg, oc=oc: h.matmul(psg[:, 0:n], lhsT=GU[:, oc * 128:(oc + 1) * 128], rhs=SGs, start=True, stop=True), reads=["SGs"], writes=[pkg])
                            S.op("act", lambda h, psw=psw, oc=oc: h.activation(out=ew, in_=psw[:, 0:n], func=AF.Exp, scale=-1.0, bias=cw0[:, oc:oc + 1]), reads=[pkw], writes=[fsk(0)])
                            S.op("dve", lambda h: h.tensor_scalar(out=ew, in0=ew, scalar1=1.0, scalar2=None, op0=ALU.add), reads=[fsk(0)], writes=[fsk(0)])
                            S.op("act", lambda h: h.activation(out=ew, in_=ew, func=AF.Ln), reads=[fsk(0)], writes=[fsk(0)])
                            S.op("act", lambda h: h.activation(out=ew, in_=ew, func=AF.Exp, scale=-1.0, bias=cnh[:, 0:1]), reads=[fsk(0)], writes=[fsk(0)])
                            S.op("act", lambda h, psa=psa, oc=oc: h.activation(out=aa, in_=psa[:, 0:n], func=AF.Sigmoid, bias=ca0[:, oc:oc + 1]), reads=[pka], writes=[fsk(1)])
                            S.op("act", lambda h, psg=psg, oc=oc: h.activation(out=Gts[:, oc, :], in_=psg[:, 0:n], func=AF.Copy), reads=[pkg], writes=[("Gts", oc)])
                            S.op("dve", lambda h, kq=kq, oc=oc: h.tensor_scalar(out=kk, in0=kq, scalar1=ckk[:, oc:oc + 1], scalar2=None, op0=ALU.mult), reads=[dk(2 + oc)], writes=[fsk(2)])
                            S.op("act", lambda h: h.activation(out=t1, in_=kk, func=AF.Square), reads=[fsk(2)], writes=[fsk(3)])
                            ps, pk = psum()
                            S.op("pe", lambda h, ps=ps: h.matmul(ps[:, 0:n], lhsT=onesbd[:, :], rhs=t1, start=True, stop=True), reads=[fsk(3)], writes=[pk])
                            S.op("act", lambda h, ps=ps: h.activation(out=t1, in_=ps[:, 0:n], func=AF.Sqrt), reads=[pk], writes=[fsk(3)])
                            S.op("dve", lambda h: h.tensor_scalar(out=t1, in0=t1, scalar1=1e-12, scalar2=None, op0=ALU.max), reads=[fsk(3)], writes=[fsk(3)])
                            S.op("dve", lambda h: h.reciprocal(out=t1, in_=t1), reads=[fsk(3)], writes=[fsk(3)])
                            S.op("dve", lambda h: h.tensor_tensor(out=kk, in0=kk, in1=t1, op=ALU.mult), reads=[fsk(2), fsk(3)], writes=[fsk(2)])
                            S.op("dve", lambda h, oc=oc: h.tensor_scalar(out=t1, in0=aa, scalar1=cka[:, oc:oc + 1], scalar2=cka[:, oc:oc + 1], op0=ALU.mult, op1=ALU.subtract), reads=[fsk(1), fsk(3)], writes=[fsk(3)])
                            S.op("dve", lambda h: h.tensor_scalar(out=t1, in0=t1, scalar1=1.0, scalar2=None, op0=ALU.add), reads=[fsk(3)], writes=[fsk(3)])
                            S.op("dve", lambda h, kq=kq: h.tensor_tensor(out=km, in0=t1, in1=kq, op=ALU.mult), reads=[fsk(3), dk(2 + oc)], writes=[fsk(4)])
                            S.op("dve", lambda h: h.tensor_tensor(out=be, in0=kk, in1=aa, op=ALU.mult), reads=[fsk(2), fsk(1)], writes=[fsk(9)])
                            S.op("dve", lambda h, rr=rr: h.tensor_tensor(out=t1, in0=rr, in1=km, op=ALU.mult), reads=[dk(oc), fsk(4), fsk(3)], writes=[fsk(3)])
                            S.op("dve", lambda h, oc=oc: h.tensor_scalar(out=t1, in0=t1, scalar1=crk[:, oc:oc + 1], scalar2=None, op0=ALU.mult), reads=[fsk(3)], writes=[fsk(3)])
                            ps, pk = psum()
                            S.op("pe", lambda h, ps=ps: h.matmul(ps[:, 0:n], lhsT=onesbd[:, :], rhs=t1, start=True, stop=True), reads=[fsk(3)], writes=[pk])
                            S.op("dve", lambda h, ps=ps, vv=vv, oc=oc: h.tensor_tensor(out=BONs[:, oc, :], in0=ps[:, 0:n], in1=vv, op=ALU.mult), reads=[pk, dk(4 + oc)], writes=[("BONs", oc)])
                            for b in range(4):
                                S.op("dve", lambda h, b=b: h.tensor_tensor_scan(out=cs[:, 4 * b:4 * b + 4], data0=onesf[:, 0:4], data1=ew[:, 4 * b:4 * b + 4], initial=0.0, op0=ALU.mult, op1=ALU.add), reads=[fsk(0)], writes=[fsk(5)])
                            S.op("act", lambda h: h.activation(out=wi, in_=cs, func=AF.Exp, scale=-1.0), reads=[fsk(5)], writes=[fsk(6)])
                            S.op("act", lambda h: h.activation(out=wv, in_=cs, func=AF.Exp), reads=[fsk(5)], writes=[fsk(7)])
                            S.op("dve", lambda h: h.tensor_tensor(out=we, in0=cs, in1=ew, op=ALU.subtract), reads=[fsk(5), fsk(0)], writes=[fsk(8)])
                            S.op("act", lambda h: h.activation(out=we, in_=we, func=AF.Exp, scale=-1.0), reads=[fsk(8)], writes=[fsk(8)])
                            S.op("dve", lambda h: h.tensor_scalar(out=t1, in0=kk, scalar1=-1.0, scalar2=None, op0=ALU.mult), reads=[fsk(2), fsk(3)], writes=[fsk(3)])
                            S.op("dve", lambda h, oc=oc: h.tensor_tensor(out=AR[:, oc, 0, 0, 0:n], in0=t1, in1=we, op=ALU.mult), reads=[fsk(3), fsk(8)] + z0k, writes=[("AR", oc, 0)])
                            S.op("dve", lambda h, oc=oc, rr=rr: h.tensor_tensor(out=AR[:, oc, 0, 1, 0:n], in0=rr, in1=wi, op=ALU.mult), reads=[dk(oc), fsk(6)] + z0k, writes=[("AR", oc, 1)])
                            S.op("dve", lambda h, oc=oc: h.tensor_tensor(out=BK[:, oc, 0, 0, 0:n], in0=be, in1=wv, op=ALU.mult), reads=[fsk(9), fsk(7)] + z0k, writes=[("BK", oc, 0)])
                            S.op("dve", lambda h, oc=oc: h.tensor_tensor(out=BK[:, oc, 0, 1, 0:n], in0=km, in1=wv, op=ALU.mult), reads=[fsk(4), fsk(7)] + z0k, writes=[("BK", oc, 1)])
                            S.op("dve", lambda h, oc=oc: h.tensor_copy(out=WCs[:, oc, :], in_=wi.rearrange("p (b t) -> p b t", b=4)[:, :, 3]), reads=[fsk(6)], writes=[("WCs", oc)])
                            S.op("act", lambda h, oc=oc, vv=vv: h.activation(out=VB[:, oc, 64:64 + n], in_=vv, func=AF.Copy), reads=[dk(4 + oc)] + z0k, writes=[("VB", oc)])
                        if SRW < 5:
                            break
                        pre_ = block_pre(MGs, [("MGs", 0), ("MGs", 1)], MLs, "MLs")
                        ark, bkk, kbk, atk, xvk, xvvk, gtk, wmk, xtk, PF = (pre_[k_] for k_ in ("ark", "bkk", "kbk", "atk", "xvk", "xvvk", "gtk", "wmk", "xtk", "PF"))
                        for b in range(4):
                            S.op("dve", lambda h, b=b: h.tensor_tensor(out=Wmb[:, b, :, :], in0=Wms[:, 0, :, :, :].rearrange("p a o t -> p (a o) t"), in1=cmask[:, b, :].unsqueeze(1).broadcast_to([128, 4, 64]), op=ALU.mult),
                                 reads=wmk + ["cmask"], writes=[("Wmb", b)])
                            S.op("dve", lambda h, b=b: h.tensor_tensor(out=ARbv[b], in0=AR[:, :, 0, 1, :], in1=cmask[:, b, :].unsqueeze(1).broadcast_to([128, 2, 64]), op=ALU.mult),
                                 reads=ark + ["cmask", ("tmpG", b // 2)], writes=[("ARb", b), ("tmpG", b // 2)])
                            S.op("dve", lambda h, b=b: h.tensor_scalar(out=KBb[:, b, :, :], in0=KB[:, 0, :, :], scalar1=rmask[:, b:b + 1], scalar2=None, op0=ALU.mult),
                                 reads=kbk + ["rmask"], writes=[("KBb", b)])
                        if SRW < 6:
                            break
                        psu, pku = psum()
                        for h_ in range(4):
                            oc, hh = h_ // 2, h_ % 2
                            for b in range(4):
                                S.op("pe", lambda h, psu=psu, h_=h_, oc=oc, hh=hh, b=b: h.matmul(psu[0:64, h_ * 64:(h_ + 1) * 64], lhsT=Wmb[:, b, hh * 2 + oc, :], rhs=HBs[:, b, oc, :], start=(b == 0), stop=(b == 3)),
                                     reads=[("Wmb", b), ("HBs", b)], writes=[pku])
                        S.op("dve", lambda h, psu=psu: h.tensor_tensor(out=Xf[:, :, :], in0=psu[0:64, 0:256].rearrange("p (a v) -> p a v", a=4), in1=XTs[:, 0:4, :], op=ALU.add), reads=[pku] + xtk, writes=["Xf"])
                        S.op("act", lambda h: h.activation(out=X[0:64, 0, :, :], in_=Xf[:, :, :], func=AF.Copy), reads=["Xf"], writes=[("Xu", 0)])
                        for b in range(4):
                            psh, pkh = psum()
                            for h_ in range(4):
                                oc, hh = h_ // 2, h_ % 2
                                pb = hh * 64
                                S.op("pe", lambda h, psh=psh, b=b, h_=h_, oc=oc, hh=hh, pb=pb: h.matmul(psh[pb:pb + 64, oc * 64:(oc + 1) * 64], lhsT=KBb[:, b, oc, hh * 64:(hh + 1) * 64], rhs=X[:, 0, h_, :], start=True, stop=True),
                                     reads=[("KBb", b), ("Xu", 0)] + xvk, writes=[pkh])
                            S.op("dve", lambda h, psh=psh, b=b: h.tensor_tensor(out=Hfs2[:, b, :, :], in0=psh[:, 0:128].rearrange("p (o v) -> p o v", o=2), in1=Hfs[:, b, :, :], op=ALU.add),
                                 reads=[pkh, ("Hfs", b, 0), ("Hfs", b, 1), ("HBs", b), ("HBhs", b, 0), ("HBhs", b, 1)], writes=[("Hfs2", b)])
                            S.op("dve", lambda h, b=b: h.tensor_tensor(out=Hfs2[:, b, :, :], in0=Hfs2[:, b, :, :], in1=WCs[:, :, b:b + 1].broadcast_to([128, 2, 64]), op=ALU.mult),
                                 reads=[("Hfs2", b), ("WCs", 0), ("WCs", 1)], writes=[("Hfs2", b)])
                            for oc in range(2):
                                ps, pk = psum()
                                S.op("pe", lambda h, ps=ps, b=b, oc=oc: h.transpose(out=ps[0:64, 0:128], in_=Hfs2[:, b, oc, :], identity=ident[:, :]), reads=[("Hfs2", b), "ident"], writes=[pk])
                                sk_ = ("sto", (b * 2 + oc) % 2)
                                S.op("act", lambda h, ps=ps, b=b, oc=oc: h.activation(out=sto[:, ((b * 2 + oc) % 2) * 2:((b * 2 + oc) % 2) * 2 + 2, :], in_=ps[0:64, 0:128].rearrange("p (a k) -> p a k", a=2), func=AF.Copy), reads=[pk], writes=[sk_])
                                S.dma("sp", o_srw[l, b, 2 * oc:2 * oc + 2].rearrange("a v k -> v a k"), sto[:, ((b * 2 + oc) % 2) * 2:((b * 2 + oc) % 2) * 2 + 2, :], reads=[sk_])
                        if SRW < 7:
                            break
                        for oc in range(2):
                            ps, pk = psum()
                            for hh in range(2):
                                h_, pb = oc * 2 + hh, hh * 64
                                for b in range(4):
                                    S.op("pe", lambda h, ps=ps, b=b, oc=oc, hh=hh, pb=pb: h.matmul(ps[pb:pb + 64, 0:64], lhsT=HBhs[:, b, hh, oc, :], rhs=ARbv[b][:, oc, :], start=(b == 0), stop=False),
                                         reads=[("HBhs", b, hh), ("ARb", b)], writes=[pk])
                                S.op("pe", lambda h, ps=ps, h_=h_, pb=pb: h.matmul(ps[pb:pb + 64, 0:64], lhsT=X[:, 0, h_, :], rhs=GTs[:, h_, 64:128], start=False, stop=True), reads=[("Xu", 0)] + xvk + gtk, writes=[pk])
                            yv = Yss[:, oc, 0:n]
                            S.op("act", lambda h, ps=ps, oc=oc: h.activation(out=Yss[:, oc, :], in_=ps[:, 0:64], func=AF.Copy), reads=[pk], writes=[("Yss", oc)])
                            ps, pk = psum()
                            S.op("pe", lambda h, ps=ps, yv=yv: h.matmul(ps[:, 0:n], lhsT=onesbd[:, :], rhs=yv, start=True, stop=True), reads=[("Yss", oc)], writes=[pk])
                            S.op("act", lambda h, ps=ps: h.activation(out=MBs[:, 0:n], in_=ps[:, 0:n], func=AF.Copy, scale=1.0 / 64), reads=[pk], writes=["MBs"])
                            S.op("dve", lambda h, yv=yv: h.tensor_tensor(out=yv, in0=yv, in1=MBs[:, 0:n], op=ALU.subtract), reads=["MBs", ("Yss", oc)], writes=[("Yss", oc)])
                            S.op("act", lambda h, yv=yv: h.activation(out=SQs[:, 0:n], in_=yv, func=AF.Square), reads=[("Yss", oc)], writes=["SQs"])
                            ps, pk = psum()
                            S.op("pe", lambda h, ps=ps: h.matmul(ps[:, 0:n], lhsT=onesbd[:, :], rhs=SQs[:, 0:n], start=True, stop=True), reads=["SQs"], writes=[pk])
                            S.op("act", lambda h, ps=ps: h.activation(out=MBs[:, 0:n], in_=ps[:, 0:n], func=AF.Sqrt, scale=1.0 / 64, bias=cge[:, 0:1]), reads=[pk], writes=["MBs"])
                            S.op("dve", lambda h: h.reciprocal(out=MBs[:, 0:n], in_=MBs[:, 0:n]), reads=["MBs"], writes=["MBs"])
                            S.op("dve", lambda h, yv=yv, oc=oc: h.scalar_tensor_tensor(out=yv, in0=MBs[:, 0:n], scalar=clg[:, oc:oc + 1], in1=yv, op0=ALU.mult, op1=ALU.mult), reads=["MBs", ("Yss", oc)], writes=[("Yss", oc)])
                            S.op("dve", lambda h, yv=yv, oc=oc: h.scalar_tensor_tensor(out=yv, in0=BONs[:, oc, :], scalar=clb[:, oc:oc + 1], in1=yv, op0=ALU.add, op1=ALU.add), reads=[("Yss", oc), ("BONs", oc)], writes=[("Yss", oc)])
                            S.op("dve", lambda h, yv=yv, oc=oc: h.tensor_tensor(out=yTs[:, oc, :], in0=yv, in1=Gts[:, oc, :], op=ALU.mult), reads=[("Yss", oc), ("Gts", oc)], writes=[("yTs", "a", oc)])
                        if "yas" in DBG and l == 0:
                            for oc in range(2):
                                S.dma("sp", DBG["yas"][oc], yTs[:, oc, :], reads=[("yTs", "a", oc)])
                if "ya" in DBG and os.environ.get("NOYA") is None:
                    for oc in range(2):
                        S.dma("sp", DBG["ya"][oc], yT[:, oc, :], reads=[("yT", FK["a"], oc, b_) for b_ in range(T // NB)])
                S.flush()

        if stage >= 10:
            with contextlib.ExitStack() as ph:
                wout = sb("wout", [128, KC, D], BF16, ph)
                load_w(wout, W["w_out"][l], 0, D, "wout")
                wk_ = [("wout", c) for c in range(KC)]
                for blk in range(4):
                    for oc in range(KC):
                        ps, pk = psum()
                        for c in range(KC):
                            S.op("pe", lambda h, ps=ps, c=c, oc=oc, blk=blk: h.matmul(ps[:, :], lhsT=wout[:, c, oc * 128:(oc + 1) * 128], rhs=yT[:, c, blk * 512:(blk + 1) * 512], start=(c == 0), stop=(c == KC - 1)),
                                 reads=wk_, writes=[pk])
                        S.op("dve", lambda h, ps=ps, oc=oc, blk=blk: h.tensor_tensor(out=xT[:, oc, blk * 512:(blk + 1) * 512], in0=ps[:, :], in1=xT[:, oc, blk * 512:(blk + 1) * 512], op=ALU.add),
                             reads=[pk], writes=[("x", oc, blk)])
                for oc in range(KC):
                    ps, pk = psum()
                    for c in range(KC):
                        S.op("pe", lambda h, ps=ps, c=c, oc=oc: h.matmul(ps[:, 0:NS], lhsT=wout[:, c, oc * 128:(oc + 1) * 128], rhs=yTs[:, c, :], start=(c == 0), stop=(c == KC - 1)), reads=wk_, writes=[pk])
                    S.op("dve", lambda h, ps=ps, oc=oc: h.tensor_tensor(out=xTs[:, oc, :], in0=ps[:, 0:NS], in1=xTs[:, oc, :], op=ALU.add), reads=[pk], writes=[("xs", oc)])
                if "x1" in DBG and l == 0:
                    for c in range(KC):
                        S.dma("sp", DBG["x1"][c], xT[:, c, :], reads=[("x", c, b_) for b_ in range(4)])
                        S.dma("sp", DBG["x1s"][c], xTs[:, c, :], reads=[("xs", c)])
                S.flush()
        lay.close()

        xa = contextlib.ExitStack()
        mkT = sb("mkT", [128, KC, 256], BF16, xa)
        mvb = sb("mvb", [128, 2, D], BF16, xa)
        if stage >= 9:
            with contextlib.ExitStack() as ph:
                wxk = sb("wxk", [128, KC, D], BF16, ph)
                wxv = sb("wxv", [128, KC, D], BF16, ph)
                gcol = sb("gcol", [128, KC], F32, ph)
                epsc = sb("epsc", [128, 1], F32, ph)
                gxk = sb("gxk", [128, 256], F32, ph)
                sq = sb("sq", [128, KC, 256], BF16, ph)
                rbc = sb("rbc", [128, 256], F32, ph)
                mn = sb("mn", [128, KC, 256], BF16, ph)
                tsq = sb("tsq", [128, 512], F32, ph)
                tkm = [sb(f"tkm{i}", [128, 512], F32, ph) for i in range(2)]
                tvm = [sb(f"tvm{i}", [128, 512], F32, ph) for i in range(2)]
                ssm = sb("ssm", [128, 8], F32, ph)
                scr = dict(sq=sq, rbc=rbc, eps=epsc)
                memT = sb("memT", [128, KC, 256], F32, ph)
                mtm = [sb(f"mtm{i}", [128, D], F32, ph) for i in range(2)]
                for mt in range(2):
                    S.dma("sp", mtm[mt][:, :], mem[mt * 128:(mt + 1) * 128, :], writes=[("mtm", mt)])
                    for half in range(2):
                        ps, pk = psum()
                        for q in range(4):
                            c = half * 4 + q
                            S.op("pe", lambda h, ps=ps, q=q, c=c, mt=mt: h.transpose(out=ps[:, q * 128:(q + 1) * 128], in_=mtm[mt][:, c * 128:(c + 1) * 128], identity=ident[:, :]),
                                 reads=[("mtm", mt), "ident"], writes=[pk])
                        S.op("act", lambda h, ps=ps, half=half, mt=mt: h.activation(out=memT[:, half * 4:half * 4 + 4, mt * 128:(mt + 1) * 128], in_=ps[:, :].rearrange("p (q t) -> p q t", q=4), func=AF.Copy),
                             reads=[pk], writes=[("memT", mt, half)])
                load_w(wxk, W["w_xk"][l], 0, D, "wxk")
                load_w(wxv, W["w_xv"][l], 0, D, "wxv")
                load_cols(gcol, W["g_mem"][l], KC, "gcol")
                S.op("pool", lambda h: h.memset(epsc[:], RMS_EPS), writes=["eps"])
                S.dma("sp", gxk[:, :], W["xk_norm"][l].partition_broadcast(128), writes=["gxk"])
                rmsnorm_fm(memT, 0, 256, gcol, mn, "mn", scr, src_keys=[("memT", mt_, hf_) for mt_ in range(2) for hf_ in range(2)])
                mnk = [("mn", c) for c in range(KC)]
                it = 0
                for mt in range(2):
                    for hf in range(2):
                        psk, pkk = psum()
                        for c in range(KC):
                            S.op("pe", lambda h, psk=psk, c=c, mt=mt, hf=hf: h.matmul(psk[:, :], lhsT=mn[:, c, mt * 128:(mt + 1) * 128], rhs=wxk[:, c, hf * 512:(hf + 1) * 512], start=(c == 0), stop=(c == KC - 1)),
                                 reads=mnk + [("wxk", c_) for c_ in range(KC)], writes=[pkk])
                        psv, pkv = psum()
                        for c in range(KC):
                            S.op("pe", lambda h, psv=psv, c=c, mt=mt, hf=hf: h.matmul(psv[:, :], lhsT=mn[:, c, mt * 128:(mt + 1) * 128], rhs=wxv[:, c, hf * 512:(hf + 1) * 512], start=(c == 0), stop=(c == KC - 1)),
                                 reads=mnk + [("wxv", c_) for c_ in range(KC)], writes=[pkv])
                        tk_, tv_ = tkm[it % 2], tvm[it % 2]
                        kk_, vk_, sk_ = ("tkm", it % 2), ("tvm", it % 2), ("ssm", it % 4)
                        ssv = ssm[:, (it % 4) * 2:(it % 4) * 2 + 2]
                        it += 1
                        S.op("act", lambda h, psk=psk: h.activation(out=tsq[:, :], in_=psk[:, :], func=AF.Square), reads=[pkk], writes=["tsq"])
                        S.op("dve", lambda h, ssv=ssv: h.tensor_reduce(out=ssv, in_=tsq[:, :].rearrange("p (h d) -> p h d", h=2), axis=AX.X, op=ALU.add), reads=["tsq"], writes=[sk_])
                        S.op("act", lambda h, ssv=ssv: h.activation(out=ssv, in_=ssv, func=AF.Sqrt, scale=1.0 / 256, bias=epsc[:, 0:1]), reads=[sk_, "eps"], writes=[sk_])
                        S.op("dve", lambda h, ssv=ssv: h.reciprocal(out=ssv, in_=ssv), reads=[sk_], writes=[sk_])
                        S.op("dve", lambda h, psk=psk, tk_=tk_, ssv=ssv: h.tensor_tensor(out=tk_[:, :].rearrange("p (h d) -> p h d", h=2), in0=psk[:, :].rearrange("p (h d) -> p h d", h=2),
                                                                                   in1=ssv.unsqueeze(2).broadcast_to([128, 2, 256]), op=ALU.mult), reads=[pkk, sk_], writes=[kk_])
                        S.op("dve", lambda h, tk_=tk_: h.tensor_tensor(out=tk_[:, :].rearrange("p (h d) -> p h d", h=2), in0=tk_[:, :].rearrange("p (h d) -> p h d", h=2),
                                                                     in1=gxk[:, :].unsqueeze(1).broadcast_to([128, 2, 256]), op=ALU.mult), reads=[kk_, "gxk"], writes=[kk_])
                        S.dma("sp", o_pmk[l, mt * 128:(mt + 1) * 128, hf * 512:(hf + 1) * 512], tk_[:, :], reads=[kk_])
                        ps, pk = psum()
                        for q in range(4):
                            S.op("pe", lambda h, ps=ps, q=q, tk_=tk_: h.transpose(out=ps[:, q * 128:(q + 1) * 128], in_=tk_[:, q * 128:(q + 1) * 128], identity=ident[:, :]), reads=[kk_, "ident"], writes=[pk])
                        S.op("act", lambda h, ps=ps, hf=hf, mt=mt: h.activation(out=mkT[:, hf * 4:hf * 4 + 4, mt * 128:(mt + 1) * 128], in_=ps[:, :].rearrange("p (q t) -> p q t", q=4), func=AF.Copy),
                             reads=[pk], writes=[("mkT", hf, mt)])
                        S.op("act", lambda h, psv=psv, tv_=tv_: h.activation(out=tv_[:, :], in_=psv[:, :], func=AF.Copy), reads=[pkv], writes=[vk_])
                        S.dma("sp", o_pmv[l, mt * 128:(mt + 1) * 128, hf * 512:(hf + 1) * 512], tv_[:, :], reads=[vk_])
                        S.op("pool", lambda h, tv_=tv_, hf=hf, mt=mt: h.tensor_copy(out=mvb[:, mt, hf * 512:(hf + 1) * 512], in_=tv_[:, :]), reads=[vk_], writes=[("mvb", hf, mt)])
                S.flush()

        if stage >= 11:
            with contextlib.ExitStack() as ph:
                TBX = 256
                wq = sb("wq", [128, KC, D], BF16, ph)
                wo = sb("wo", [128, KC, D], BF16, ph)
                gcol = sb("gcol", [128, KC], F32, ph)
                epsc = sb("epsc", [128, 1], F32, ph)
                xqs = sb("xqs", [128, 2], F32, ph)
                sq = sb("sq", [128, KC, TBX], BF16, ph)
                rbc = sb("rbc", [128, TBX], F32, ph)
                xn = sb("xn", [128, KC, TBX], BF16, ph)
                qraw = sb("qraw", [128, KC, TBX], F32, ph)
                qsq = sb("qsq", [128, KC, TBX], BF16, ph)
                qT = sb("qT", [128, KC, TBX], BF16, ph)
                rq = sb("rq", [128, TBX], F32, ph)
                PTx = [sb(f"PTx{i}", [128, TBX], BF16, ph) for i in range(4)]
                oT = sb("oT", [128, KC, TBX], BF16, ph)
                rden = sb("rden", [128, TBX], F32, ph)
                otmp = [sb(f"otmp{i}", [128, TBX], F32, ph) for i in range(2)]
                scr = dict(sq=sq, rbc=rbc, eps=epsc)
                load_w(wq, W["w_xq"][l], 0, D, "wq")
                load_w(wo, W["w_xo"][l], 0, D, "wo")
                wqk = [("wq", c) for c in range(KC)]
                wok = [("wo", c) for c in range(KC)]
                load_cols(gcol, W["g_x"][l], KC, "gcol")
                load_cols(xqs, W["xq_norm"][l], 2, "xqs")
                S.op("dve", lambda h: h.tensor_scalar(out=xqs[:, :], in0=xqs[:, :], scalar1=1.0 / 16, scalar2=None, op0=ALU.mult), reads=["xqs"], writes=["xqs"])
                S.op("pool", lambda h: h.memset(epsc[:], RMS_EPS), writes=["eps"])
                pti = 0
                oti = 0
                for blk in range(T // TBX):
                    b0 = blk * TBX
                    xk_ = [("x", c, blk) for c in range(KC)]
                    rmsnorm_fm(xT, b0, TBX, gcol, xn, "xn", scr, src_keys=xk_)
                    xnk = [("xn", c) for c in range(KC)]
                    for i in range(KC):
                        ps, pk = psum()
                        for c in range(KC):
                            S.op("pe", lambda h, ps=ps, c=c, i=i: h.matmul(ps[:, 0:TBX], lhsT=wq[:, c, i * 128:(i + 1) * 128], rhs=xn[:, c, :], start=(c == 0), stop=(c == KC - 1)), reads=xnk + wqk, writes=[pk])
                        S.op("act", lambda h, ps=ps, i=i: h.activation(out=qraw[:, i, :], in_=ps[:, 0:TBX], func=AF.Copy), reads=[pk], writes=[("qraw", i)])
                        S.op("act", lambda h, ps=ps, i=i: h.activation(out=qsq[:, i, :], in_=ps[:, 0:TBX], func=AF.Square), reads=[pk], writes=[("qsq", i)])
                    for hd in range(4):
                        ps, pk = psum()
                        for dc in range(2):
                            S.op("pe", lambda h, ps=ps, hd=hd, dc=dc: h.matmul(ps[:, 0:TBX], lhsT=onesb[:, :], rhs=qsq[:, hd * 2 + dc, :], start=(dc == 0), stop=(dc == 1)), reads=[("qsq", hd * 2 + dc), "onesb"], writes=[pk])
                        S.op("act", lambda h, ps=ps: h.activation(out=rq[:, :], in_=ps[:, 0:TBX], func=AF.Sqrt, scale=1.0 / 256, bias=epsc[:, 0:1]), reads=[pk, "eps"], writes=["rq"])
                        S.op("dve", lambda h: h.reciprocal(out=rq[:, :], in_=rq[:, :]), reads=["rq"], writes=["rq"])
                        for dc in range(2):
                            i = hd * 2 + dc
                            S.op("dve", lambda h, i=i, dc=dc: h.scalar_tensor_tensor(out=qT[:, i, :], in0=qraw[:, i, :], scalar=xqs[:, dc:dc + 1], in1=rq[:, :], op0=ALU.mult, op1=ALU.mult),
                                 reads=[("qraw", i), "xqs", "rq"], writes=[("qT", i)])
                    for hd in range(4):
                        pts = []
                        for mt in range(2):
                            ps, pk = psum()
                            for dc in range(2):
                                S.op("pe", lambda h, ps=ps, hd=hd, dc=dc, mt=mt: h.matmul(ps[:, 0:TBX], lhsT=mkT[:, hd * 2 + dc, mt * 128:(mt + 1) * 128], rhs=qT[:, hd * 2 + dc, :], start=(dc == 0), stop=(dc == 1)),
                                     reads=[("qT", hd * 2 + dc)], writes=[pk])
                            pt, ptk = PTx[pti % 4], ("PTx", pti % 4)
                            pti += 1
                            S.op("act", lambda h, ps=ps, pt=pt: h.activation(out=pt[:, :], in_=ps[:, 0:TBX], func=AF.Exp), reads=[pk], writes=[ptk])
                            pts.append((pt, ptk))
                        ps, pk = psum()
                        for mt in range(2):
                            S.op("pe", lambda h, ps=ps, mt=mt, pt=pts[mt][0]: h.matmul(ps[:, 0:TBX], lhsT=onesb[:, :], rhs=pt[:, :], start=(mt == 0), stop=(mt == 1)), reads=[pts[mt][1], "onesb"], writes=[pk])
                        S.op("dve", lambda h, ps=ps: h.reciprocal(out=rden[:, :], in_=ps[:, 0:TBX]), reads=[pk], writes=["rden"])
                        for dc in range(2):
                            i = hd * 2 + dc
                            ps, pk = psum()
                            for mt in range(2):
                                S.op("pe", lambda h, ps=ps, mt=mt, i=i, pt=pts[mt][0]: h.matmul(ps[:, 0:TBX], lhsT=mvb[:, mt, i * 128:(i + 1) * 128], rhs=pt[:, :], start=(mt == 0), stop=(mt == 1)), reads=[pts[mt][1]], writes=[pk])
                            ot, otk = otmp[oti % 2], ("otmp", oti % 2)
                            oti += 1
                            S.op("dve", lambda h, ps=ps, ot=ot: h.tensor_tensor(out=ot[:, :], in0=ps[:, 0:TBX], in1=rden[:, :], op=ALU.mult), reads=[pk, "rden"], writes=[otk])
                            S.op("act", lambda h, ot=ot, i=i: h.activation(out=oT[:, i, :], in_=ot[:, :], func=AF.Copy), reads=[otk], writes=[("oT", i)])
                    otks = [("oT", i) for i in range(KC)]
                    for oc in range(KC):
                        ps, pk = psum()
                        for c in range(KC):
                            S.op("pe", lambda h, ps=ps, c=c, oc=oc: h.matmul(ps[:, 0:TBX], lhsT=wo[:, c, oc * 128:(oc + 1) * 128], rhs=oT[:, c, :], start=(c == 0), stop=(c == KC - 1)), reads=otks + wok, writes=[pk])
                        S.op("dve", lambda h, ps=ps, oc=oc, b0=b0: h.tensor_tensor(out=xT[:, oc, b0:b0 + TBX], in0=ps[:, 0:TBX], in1=xT[:, oc, b0:b0 + TBX], op=ALU.add), reads=[pk], writes=[("x", oc, blk)])

                if stage >= 15:
                    cks = [sb(f"cks{i}", [128, D], F32, ph) for i in range(2)]
                    cvs = [sb(f"cvs{i}", [128, D], F32, ph) for i in range(2)]
                    mkTs = sb("mkTs", [128, KC, 256], BF16, ph)
                    mvs = sb("mvs", [128, 2, D], BF16, ph)
                    rmsnorm_fm(xTs, 0, NS, gcol, xn, "xns", scr)
                    xnk = [("xns", c) for c in range(KC)]
                    for i in range(KC):
                        ps, pk = psum()
                        for c in range(KC):
                            S.op("pe", lambda h, ps=ps, c=c, i=i: h.matmul(ps[:, 0:NS], lhsT=wq[:, c, i * 128:(i + 1) * 128], rhs=xn[:, c, 0:NS], start=(c == 0), stop=(c == KC - 1)), reads=xnk + wqk, writes=[pk])
                        S.op("act", lambda h, ps=ps, i=i: h.activation(out=qraw[:, i, 0:NS], in_=ps[:, 0:NS], func=AF.Copy), reads=[pk], writes=[("qraws", i)])
                        S.op("act", lambda h, ps=ps, i=i: h.activation(out=qsq[:, i, 0:NS], in_=ps[:, 0:NS], func=AF.Square), reads=[pk], writes=[("qsqs", i)])
                    for hd in range(4):
                        ps, pk = psum()
                        for dc in range(2):
                            S.op("pe", lambda h, ps=ps, hd=hd, dc=dc: h.matmul(ps[:, 0:NS], lhsT=onesb[:, :], rhs=qsq[:, hd * 2 + dc, 0:NS], start=(dc == 0), stop=(dc == 1)), reads=[("qsqs", hd * 2 + dc), "onesb"], writes=[pk])
                        S.op("act", lambda h, ps=ps: h.activation(out=rq[:, 0:NS], in_=ps[:, 0:NS], func=AF.Sqrt, scale=1.0 / 256, bias=epsc[:, 0:1]), reads=[pk, "eps"], writes=["rqs"])
                        S.op("dve", lambda h: h.reciprocal(out=rq[:, 0:NS], in_=rq[:, 0:NS]), reads=["rqs"], writes=["rqs"])
                        for dc in range(2):
                            i = hd * 2 + dc
                            S.op("dve", lambda h, i=i, dc=dc: h.scalar_tensor_tensor(out=qT[:, i, 0:NS], in0=qraw[:, i, 0:NS], scalar=xqs[:, dc:dc + 1], in1=rq[:, 0:NS], op0=ALU.mult, op1=ALU.mult),
                                 reads=[("qraws", i), "xqs", "rqs"], writes=[("qTs", i)])
                    for b in range(4):
                        for mt in range(2):
                            S.dma("sp", cks[mt][:, :], cmk[l, b, mt * 128:(mt + 1) * 128, :], writes=[("cks", mt)])
                            S.dma("sp", cvs[mt][:, :], cmv[l, b, mt * 128:(mt + 1) * 128, :], writes=[("cvs", mt)])
                            for half in range(2):
                                ps, pk = psum()
                                for q in range(4):
                                    c = half * 4 + q
                                    S.op("pe", lambda h, ps=ps, q=q, c=c, mt=mt: h.transpose(out=ps[:, q * 128:(q + 1) * 128], in_=cks[mt][:, c * 128:(c + 1) * 128], identity=ident[:, :]), reads=[("cks", mt), "ident"], writes=[pk])
                                S.op("act", lambda h, ps=ps, half=half, mt=mt: h.activation(out=mkTs[:, half * 4:half * 4 + 4, mt * 128:(mt + 1) * 128], in_=ps[:, :].rearrange("p (q t) -> p q t", q=4), func=AF.Copy),
                                     reads=[pk], writes=[("mkTs", mt, half)])
                            S.op("pool", lambda h, mt=mt: h.tensor_copy(out=mvs[:, mt, :], in_=cvs[mt][:, :]), reads=[("cvs", mt)], writes=[("mvs", mt)])
                        mkk = [("mkTs", mt, half) for mt in range(2) for half in range(2)]
                        for hd in range(4):
                            pts = []
                            for mt in range(2):
                                ps, pk = psum()
                                for dc in range(2):
                                    S.op("pe", lambda h, ps=ps, hd=hd, dc=dc, mt=mt, b=b: h.matmul(ps[:, 0:4], lhsT=mkTs[:, hd * 2 + dc, mt * 128:(mt + 1) * 128], rhs=qT[:, hd * 2 + dc, 4 * b:4 * b + 4], start=(dc == 0), stop=(dc == 1)),
                                         reads=[("qTs", hd * 2 + dc)] + mkk, writes=[pk])
                                pt, ptk = PTx[pti % 4], ("PTx", pti % 4)
                                pti += 1
                                S.op("act", lambda h, ps=ps, pt=pt: h.activation(out=pt[:, 0:4], in_=ps[:, 0:4], func=AF.Exp), reads=[pk], writes=[ptk])
                                pts.append((pt, ptk))
                            ps, pk = psum()
                            for mt in range(2):
                                S.op("pe", lambda h, ps=ps, mt=mt, pt=pts[mt][0]: h.matmul(ps[:, 0:4], lhsT=onesb[:, :], rhs=pt[:, 0:4], start=(mt == 0), stop=(mt == 1)), reads=[pts[mt][1], "onesb"], writes=[pk])
                            S.op("dve", lambda h, ps=ps: h.reciprocal(out=rden[:, 0:4], in_=ps[:, 0:4]), reads=[pk], writes=["rdens"])
                            for dc in range(2):
                                i = hd * 2 + dc
                                ps, pk = psum()
                                for mt in range(2):
                                    S.op("pe", lambda h, ps=ps, mt=mt, i=i, pt=pts[mt][0]: h.matmul(ps[:, 0:4], lhsT=mvs[:, mt, i * 128:(i + 1) * 128], rhs=pt[:, 0:4], start=(mt == 0), stop=(mt == 1)), reads=[pts[mt][1], ("mvs", mt)], writes=[pk])
                                ot, otk = otmp[oti % 2], ("otmp", oti % 2)
                                oti += 1
                                S.op("dve", lambda h, ps=ps, ot=ot: h.tensor_tensor(out=ot[:, 0:4], in0=ps[:, 0:4], in1=rden[:, 0:4], op=ALU.mult), reads=[pk, "rdens"], writes=[otk])
                                S.op("act", lambda h, ot=ot, i=i, b=b: h.activation(out=oT[:, i, 4 * b:4 * b + 4], in_=ot[:, 0:4], func=AF.Copy), reads=[otk], writes=[("oTs", i, b)])
                    otks = [("oTs", i, b) for i in range(KC) for b in range(4)]
                    for oc in range(KC):
                        ps, pk = psum()
                        for c in range(KC):
                            S.op("pe", lambda h, ps=ps, c=c, oc=oc: h.matmul(ps[:, 0:NS], lhsT=wo[:, c, oc * 128:(oc + 1) * 128], rhs=oT[:, c, 0:NS], start=(c == 0), stop=(c == KC - 1)), reads=otks + wok, writes=[pk])
                        S.op("dve", lambda h, ps=ps, oc=oc: h.tensor_tensor(out=xTs[:, oc, :], in0=ps[:, 0:NS], in1=xTs[:, oc, :], op=ALU.add), reads=[pk], writes=[("xs", oc)])
                    if "x2s" in DBG and l == 0:
                        for c in range(KC):
                            S.dma("sp", DBG["x2s"][c], xTs[:, c, :], reads=[("xs", c)])
                if "x2" in DBG and l == 0:
                    for c in range(KC):
                        S.dma("sp", DBG["x2"][c], xT[:, c, :], reads=[("x", c, b_) for b_ in range(T // TBX)])
                S.flush()
        xa.close()

        if stage >= 12:
            with contextlib.ExitStack() as ph:
                TBF, NJ = 512, 22
                wdn = sb("wdn", [128, NJ, D], BF16, ph)
                gcol = sb("gcol", [128, KC], F32, ph)
                epsc = sb("epsc", [128, 1], F32, ph)
                cw = sb("cw", [128, 2 * NJ, 3], F32, ph)
                cb = sb("cb", [128, 2 * NJ], F32, ph)
                act_ = sb("act", [128, NJ, TBF], BF16, ph)
                wsl = [sb(f"wsl{i}", [128, KC, 256], BF16, ph) for i in range(3)]
                sq = sb("sq", [128, KC, TBF], BF16, ph)
                rbc = sb("rbc", [128, TBF], F32, ph)
                xn = sb("xn", [128, KC, TBF], BF16, ph)
                hb = [sb(f"hb{i}", [128, 2, TBF + 2], F32, ph) for i in range(2)]
                halo = sb("halo", [128, 2 * NJ, 2], F32, ph)
                ta = sb("ta", [128, 2, TBF], F32, ph)
                tb = sb("tb", [128, 2, TBF], F32, ph)
                sg = sb("sg", [128, TBF], F32, ph)
                stg = [sb(f"stg{i}", [2, 512], F32, ph) for i in range(2)]
                scr = dict(sq=sq, rbc=rbc, eps=epsc)
                load_w(wdn, W["w_down"][l], 0, D, "wdn")
                wdk = [("wdn", c) for c in range(NJ)]
                load_cols(gcol, W["g_ffn"][l], KC, "gcol")
                for tap in range(3):
                    S.dma("sp", cw[:, :, tap], W["ffn_conv_w"][l][tap].rearrange("(j p) -> p j", p=128), writes=[("cw", tap)], allow_slow_non_contiguous=True)
                cwk = [("cw", tap) for tap in range(3)]
                load_cols(cb, W["ffn_conv_b"][l], 2 * NJ, "cb")
                S.op("pool", lambda h: h.memset(epsc[:], RMS_EPS), writes=["eps"])
                S.op("pool", lambda h: h.memset(halo[:].rearrange("p a b -> p (a b)"), 0.0), writes=["halo0"])
                S.flush()
                wi_ = 0
                for blk in range(T // TBF):
                    b0 = blk * TBF
                    xk_ = [("x", c, blk) for c in range(KC)]
                    rmsnorm_fm(xT, b0, TBF, gcol, xn, "xn", scr, src_keys=xk_)
                    xnk = [("xn", c) for c in range(KC)]
                    for j in range(NJ):
                        w_ = wsl[wi_ % 3]
                        wk2 = [("wsl", wi_ % 3, 0), ("wsl", wi_ % 3, 1)]
                        wi_ += 1
                        for g in range(2):
                            S.dma("pool", w_[:, :, g * 128:(g + 1) * 128], W["w_up"][l][:, g * D_FF + j * 128:g * D_FF + (j + 1) * 128].rearrange("(c p) n -> p c n", p=128), writes=[wk2[g]])
                        h_ = hb[j % 2]
                        for g in range(2):
                            jj = j + NJ * g
                            hk = ("hb", j % 2, g)
                            ps, pk = psum()
                            for c in range(KC):
                                S.op("pe", lambda h, ps=ps, c=c, g=g, w_=w_: h.matmul(ps[:, :], lhsT=w_[:, c, g * 128:(g + 1) * 128], rhs=xn[:, c, :], start=(c == 0), stop=(c == KC - 1)), reads=xnk + [wk2[g]], writes=[pk])
                            S.op("dve", lambda h, h_=h_, g=g, jj=jj: h.tensor_copy(out=h_[:, g, 0:2], in_=halo[:, jj, :]), reads=[("halo", jj)], writes=[hk])
                            S.op("act", lambda h, ps=ps, h_=h_, g=g: h.activation(out=h_[:, g, 2:TBF + 2], in_=ps[:, :], func=AF.Copy), reads=[pk, hk], writes=[hk])
                            S.op("dve", lambda h, h_=h_, g=g, jj=jj: h.tensor_copy(out=halo[:, jj, :], in_=h_[:, g, TBF:TBF + 2]), reads=[hk], writes=[("halo", jj)])
                            S.op("dve", lambda h, h_=h_, g=g, jj=jj: h.tensor_scalar(out=ta[:, g, :], in0=h_[:, g, 0:TBF], scalar1=cw[:, jj, 0:1], scalar2=cb[:, jj:jj + 1], op0=ALU.mult, op1=ALU.add),
                                 reads=[hk, "cb"] + cwk, writes=[("ta", g)])
                            S.op("dve", lambda h, h_=h_, g=g, jj=jj: h.scalar_tensor_tensor(out=tb[:, g, :], in0=h_[:, g, 1:TBF + 1], scalar=cw[:, jj, 1:2], in1=ta[:, g, :], op0=ALU.mult, op1=ALU.add),
                                 reads=[hk, ("ta", g)] + cwk, writes=[("tb", g)])
                            S.op("dve", lambda h, h_=h_, g=g, jj=jj: h.scalar_tensor_tensor(out=ta[:, g, :], in0=h_[:, g, 2:TBF + 2], scalar=cw[:, jj, 2:3], in1=tb[:, g, :], op0=ALU.mult, op1=ALU.add),
                                 reads=[hk, ("tb", g)] + cwk, writes=[("ta", g)])
                        S.op("act", lambda h: h.activation(out=sg[:, :], in_=ta[:, 0, :], func=AF.Silu), reads=[("ta", 0)], writes=["sg"])
                        S.op("dve", lambda h, j=j: h.tensor_tensor(out=act_[:, j, :], in0=sg[:, :], in1=ta[:, 1, :], op=ALU.mult), reads=["sg", ("ta", 1)], writes=[("act", j)])
                    actk = [("act", j) for j in range(NJ)]
                    for oc in range(KC):
                        ps, pk = psum()
                        for j in range(NJ):
                            S.op("pe", lambda h, ps=ps, j=j, oc=oc: h.matmul(ps[:, :], lhsT=wdn[:, j, oc * 128:(oc + 1) * 128], rhs=act_[:, j, :], start=(j == 0), stop=(j == NJ - 1)), reads=actk + wdk, writes=[pk])
                        S.op("dve", lambda h, ps=ps, oc=oc, b0=b0: h.tensor_tensor(out=xT[:, oc, b0:b0 + TBF], in0=ps[:, :], in1=xT[:, oc, b0:b0 + TBF], op=ALU.add), reads=[pk], writes=[("x", oc, blk)])

                if stage >= 15:
                    sfs = [sb(f"sfs{i}", [8, 512], F32, ph) for i in range(2)]
                    halos = sb("halos", [128, 2 * NJ, 4, 2], F32, ph)
                    haloo = sb("haloo", [128, 2 * NJ, 4, 2], F32, ph)
                    hbS = [sb(f"hbS{i}", [128, 2, 4, 6], F32, ph) for i in range(2)]
                    taS = sb("taS", [128, 2, NS], F32, ph)
                    tbS = sb("tbS", [128, 2, NS], F32, ph)
                    for g11 in range(11):
                        sf_ = sfs[g11 % 2]
                        S.dma("sp", sf_[:, :], sff[l][:, g11 * 512:(g11 + 1) * 512], writes=[("sfs", g11 % 2)])
                        ps, pk = psum()
                        for q in range(4):
                            S.op("pe", lambda h, ps=ps, q=q, sf_=sf_: h.transpose(out=ps[:, q * 8:(q + 1) * 8], in_=sf_[0:8, q * 128:(q + 1) * 128], identity=ident[0:8, 0:8]), reads=[("sfs", g11 % 2), "ident"], writes=[pk])
                        S.op("act", lambda h, ps=ps, g11=g11: h.activation(out=halos[:, g11 * 4:(g11 + 1) * 4, :, :].rearrange("p a b t -> p a (b t)"), in_=ps[:, 0:32].rearrange("p (a k) -> p a k", a=4), func=AF.Copy), reads=[pk], writes=[("halos", g11)])
                    rmsnorm_fm(xTs, 0, NS, gcol, xn, "xns", scr)
                    xnk = [("xns", c) for c in range(KC)]
                    for j in range(NJ):
                        w_ = wsl[wi_ % 3]
                        wk2 = [("wsl", wi_ % 3, 0), ("wsl", wi_ % 3, 1)]
                        wi_ += 1
                        for g in range(2):
                            S.dma("pool", w_[:, :, g * 128:(g + 1) * 128], W["w_up"][l][:, g * D_FF + j * 128:g * D_FF + (j + 1) * 128].rearrange("(c p) n -> p c n", p=128), writes=[wk2[g]])
                        hS = hbS[j % 2]
                        for g in range(2):
                            jj = j + NJ * g
                            hk = ("hbS", j % 2, g)
                            ps, pk = psum()
                            for c in range(KC):
                                S.op("pe", lambda h, ps=ps, c=c, g=g, w_=w_: h.matmul(ps[:, 0:NS], lhsT=w_[:, c, g * 128:(g + 1) * 128], rhs=xn[:, c, 0:NS], start=(c == 0), stop=(c == KC - 1)), reads=xnk + [wk2[g]], writes=[pk])
                            S.op("dve", lambda h, hS=hS, g=g, jj=jj: h.tensor_copy(out=hS[:, g, :, 0:2], in_=halos[:, jj, :, :]), reads=[("halos", jj // 4)], writes=[hk])
                            S.op("act", lambda h, ps=ps, hS=hS, g=g: h.activation(out=hS[:, g, :, 2:6], in_=ps[:, 0:NS].rearrange("p (b t) -> p b t", b=4), func=AF.Copy), reads=[pk, hk], writes=[hk])
                            S.op("dve", lambda h, hS=hS, g=g, jj=jj: h.tensor_copy(out=haloo[:, jj, :, :], in_=hS[:, g, :, 4:6]), reads=[hk], writes=[("haloo", jj)])
                            tav = taS[:, g, :].rearrange("p (b t) -> p b t", b=4)
                            tbv = tbS[:, g, :].rearrange("p (b t) -> p b t", b=4)
                            S.op("dve", lambda h, hS=hS, g=g, jj=jj, tav=tav: h.tensor_scalar(out=tav, in0=hS[:, g, :, 0:4], scalar1=cw[:, jj, 0:1], scalar2=cb[:, jj:jj + 1], op0=ALU.mult, op1=ALU.add), reads=[hk, "cb"] + cwk, writes=[("taS", g)])
                            S.op("dve", lambda h, hS=hS, g=g, jj=jj, tav=tav, tbv=tbv: h.scalar_tensor_tensor(out=tbv, in0=hS[:, g, :, 1:5], scalar=cw[:, jj, 1:2], in1=tav, op0=ALU.mult, op1=ALU.add), reads=[hk, ("taS", g)] + cwk, writes=[("tbS", g)])
                            S.op("dve", lambda h, hS=hS, g=g, jj=jj, tav=tav, tbv=tbv: h.scalar_tensor_tensor(out=tav, in0=hS[:, g, :, 2:6], scalar=cw[:, jj, 2:3], in1=tbv, op0=ALU.mult, op1=ALU.add), reads=[hk, ("tbS", g)] + cwk, writes=[("taS", g)])
                        S.op("act", lambda h: h.activation(out=sg[:, 0:NS], in_=taS[:, 0, :], func=AF.Silu), reads=[("taS", 0)], writes=["sgs"])
                        S.op("dve", lambda h, j=j: h.tensor_tensor(out=act_[:, j, 0:NS], in0=sg[:, 0:NS], in1=taS[:, 1, :], op=ALU.mult), reads=["sgs", ("taS", 1)], writes=[("acts", j)])
                    actk = [("acts", j) for j in range(NJ)]
                    for oc in range(KC):
                        ps, pk = psum()
                        for j in range(NJ):
                            S.op("pe", lambda h, ps=ps, j=j, oc=oc: h.matmul(ps[:, 0:NS], lhsT=wdn[:, j, oc * 128:(oc + 1) * 128], rhs=act_[:, j, 0:NS], start=(j == 0), stop=(j == NJ - 1)), reads=actk + wdk, writes=[pk])
                        S.op("dve", lambda h, ps=ps, oc=oc: h.tensor_tensor(out=xTs[:, oc, :], in0=ps[:, 0:NS], in1=xTs[:, oc, :], op=ALU.add), reads=[pk], writes=[("xs", oc)])
                    for g11 in range(11):
                        ps, pk = psum()
                        for q in range(4):
                            jj = g11 * 4 + q
                            S.op("pe", lambda h, ps=ps, q=q, jj=jj: h.transpose(out=ps[0:8, q * 128:(q + 1) * 128], in_=haloo[:, jj, :, :].rearrange("p b t -> p (b t)"), identity=ident[:, :]), reads=[("haloo", jj), "ident"], writes=[pk])
                        sf_ = sfs[g11 % 2]
                        S.op("act", lambda h, ps=ps, sf_=sf_: h.activation(out=sf_[:, :], in_=ps[0:8, :], func=AF.Copy), reads=[pk], writes=[("sfo", g11 % 2)])
                        S.dma("sp", o_sff[l][:, g11 * 512:(g11 + 1) * 512], sf_[:, :], reads=[("sfo", g11 % 2)])
                    if "x3s" in DBG and l == 0:
                        for c in range(KC):
                            S.dma("sp", DBG["x3s"][c], xTs[:, c, :], reads=[("xs", c)])
                for g11 in range(11):
                    ps, pk = psum()
                    for q in range(4):
                        jj = g11 * 4 + q
                        S.op("pe", lambda h, ps=ps, q=q, jj=jj: h.transpose(out=ps[0:2, q * 128:(q + 1) * 128], in_=halo[:, jj, :], identity=ident[:, :]), reads=[("halo", jj), "ident"], writes=[pk])
                    sg_ = stg[g11 % 2]
                    S.op("act", lambda h, ps=ps, sg_=sg_: h.activation(out=sg_[:, :], in_=ps[0:2, :], func=AF.Copy), reads=[pk], writes=[("stg", g11 % 2)])
                    S.dma("sp", o_pff[l][:, g11 * 512:(g11 + 1) * 512], sg_[:, :], reads=[("stg", g11 % 2)])
                if "x3" in DBG and l == 0:
                    for c in range(KC):
                        S.dma("sp", DBG["x3"][c], xT[:, c, :], reads=[("x", c, b_) for b_ in range(T // TBF)])
                S.flush()

    if stage >= 12:
        with contextlib.ExitStack() as ph:
            ysb = [sb(f"ysb{i}", [128, D], F32, ph) for i in range(2)]
            for tt in range(NT):
                y_ = ysb[tt % 2]
                for half in range(2):
                    ps, pk = psum()
                    for q in range(4):
                        S.op("pe", lambda h, ps=ps, q=q, half=half, tt=tt: h.transpose(out=ps[:, q * 128:(q + 1) * 128], in_=xT[:, half * 4 + q, tt * 128:(tt + 1) * 128], identity=ident[:, :]), reads=["ident"], writes=[pk])
                    if half == 0:
                        S.op("act", lambda h, ps=ps, y_=y_: h.activation(out=y_[:, 0:512], in_=ps[:, :], func=AF.Copy), reads=[pk], writes=[("ysb", tt % 2, 0)])
                    else:
                        S.op("dve", lambda h, ps=ps, y_=y_: h.tensor_copy(out=y_[:, 512:1024], in_=ps[:, :]), reads=[pk], writes=[("ysb", tt % 2, 1)])
                S.dma("sp", o_yp[tt * 128:(tt + 1) * 128, :], y_[:, :], reads=[("ysb", tt % 2, 0), ("ysb", tt % 2, 1)])
            yss = sb("yss", [NS, D], F32, ph)
            for half in range(2):
                ps, pk = psum()
                for q in range(4):
                    S.op("pe", lambda h, ps=ps, q=q, half=half: h.transpose(out=ps[0:NS, q * 128:(q + 1) * 128], in_=xTs[:, half * 4 + q, :], identity=ident[:, :]), reads=["ident"], writes=[pk])
                S.op("act", lambda h, ps=ps, half=half: h.activation(out=yss[:, half * 512:(half + 1) * 512], in_=ps[0:NS, :], func=AF.Copy), reads=[pk], writes=[("yss", half)])
            S.dma("sp", o_ys[:, :], yss[:, :], reads=[("yss", 0), ("yss", 1)])
            S.flush()

    S.flush()
    st.close()
    return nc


def _core_inputs(inp, b):
    sl = slice(4 * b, 4 * b + 4)
    m = {
        "xp": np.ascontiguousarray(inp["x_prompt"][b]),
        "xs": np.ascontiguousarray(inp["x_sample"][sl].reshape(NS, D)),
        "mem": np.ascontiguousarray(inp["mem_prompt"][b]),
    }
    m["scv"] = np.ascontiguousarray(inp["state_conv"][:, sl])
    m["srw"] = np.ascontiguousarray(inp["state_rwkv"][:, sl])
    for l_ in range(DEPTH):
        m[f"fk{l_}"] = inp["cache_fox_k"][l_].reshape(-1, 512)
        m[f"fv{l_}"] = inp["cache_fox_v"][l_].reshape(-1, 512)
        m[f"flf{l_}"] = inp["cache_fox_logf"][l_].reshape(-1, 1024)
    m["ptab"] = np.ascontiguousarray(inp["page_table"][sl]).astype(np.int32)
    m["cmk"] = np.ascontiguousarray(inp["cache_mem_k"][:, sl]).reshape(DEPTH, 4, 256, D)
    m["cmv"] = np.ascontiguousarray(inp["cache_mem_v"][:, sl]).reshape(DEPTH, 4, 256, D)
    m["sff"] = np.ascontiguousarray(inp["state_ffn"][:, sl]).reshape(DEPTH, 8, 2 * D_FF)
    m["ssh"] = np.ascontiguousarray(inp["state_rwkv_shift"][:, sl, 0])
    for k in ("g_mix", "w_in", "fox_q_norm", "fox_k_norm", "fox_b_f", "conv_w", "conv_b", "conv_ln_g", "conv_ln_b",
              "rwkv_mu", "rwkv_w0", "rwkv_w_up", "rwkv_a0", "rwkv_a_up", "rwkv_g_up", "rwkv_k_k", "rwkv_k_a", "rwkv_ln_g", "rwkv_ln_b",
              "g_mem", "w_xk", "w_xv", "xk_norm", "w_out", "g_x", "w_xq", "xq_norm", "w_xo", "g_ffn", "w_up", "ffn_conv_w", "ffn_conv_b", "w_down"):
        m[k] = np.ascontiguousarray(inp[k])
    m["rwkv_r_k"] = np.ascontiguousarray(inp["rwkv_r_k"].reshape(DEPTH, 256))
    return m


def run(inputs, cores=tuple(range(NCORES)), stage=99):
    nc = build(stage)
    in_maps = [_core_inputs(inputs, b) for b in cores]
    res = run_bass_kernel_spmd(nc, in_maps, core_ids=list(range(len(cores))), trace=bool(os.environ.get("KTRACE")))
    if os.environ.get("KTRACE"):
        print("KTRACE exec_time_ns", res.exec_time_ns)
    return res.results


def kernel(**inputs):
    inputs = {k: np.asarray(v) for k, v in inputs.items()}
    r = run(inputs)
    B, DB = 8, 32
    f = np.float32
    out = dict(
        y_prompt=np.zeros((B, T, D), f), y_sample=np.zeros((DB, 4, D), f),
        p_fox_k=np.zeros((DEPTH, B, T, 8, 64), f), p_fox_v=np.zeros((DEPTH, B, T, 8, 64), f), p_fox_logf=np.zeros((DEPTH, B, T, 8), f),
        p_rwkv=np.zeros((DEPTH, B, 4, 64, 64), f), p_rwkv_shift=np.zeros((DEPTH, B, 1, 1024), f),
        p_conv=np.zeros((DEPTH, B, 30, 256), f), p_ffn=np.zeros((DEPTH, B, 2, 5632), f),
        p_mem_k=np.zeros((DEPTH, B, 256, 4, 256), f), p_mem_v=np.zeros((DEPTH, B, 256, 4, 256), f),
        s_fox_k=np.zeros((DEPTH, DB, 4, 8, 64), f), s_fox_v=np.zeros((DEPTH, DB, 4, 8, 64), f), s_fox_logf=np.zeros((DEPTH, DB, 4, 8), f),
        s_rwkv=np.zeros((DEPTH, DB, 4, 64, 64), f), s_rwkv_shift=np.zeros((DEPTH, DB, 1, 1024), f),
        s_conv=np.zeros((DEPTH, DB, 30, 256), f), s_ffn=np.zeros((DEPTH, DB, 2, 5632), f),
    )
    for b in range(NCORES):
        rb = r[b]
        sl = slice(4 * b, 4 * b + 4)
        out["p_fox_k"][:, b] = rb["p_fox_k"].reshape(DEPTH, T, 8, 64)
        out["p_fox_v"][:, b] = rb["p_fox_v"].reshape(DEPTH, T, 8, 64)
        out["p_fox_logf"][:, b] = rb["p_fox_logf"]
        out["s_fox_k"][:, sl] = rb["s_fox_k"].reshape(DEPTH, 4, 4, 8, 64)
        out["s_fox_v"][:, sl] = rb["s_fox_v"].reshape(DEPTH, 4, 4, 8, 64)
        out["s_fox_logf"][:, sl] = rb["s_fox_logf"].reshape(DEPTH, 4, 4, 8)
        out["p_conv"][:, b] = rb["p_conv"]
        out["s_conv"][:, sl] = rb["s_conv"]
        out["s_ffn"][:, sl] = rb["s_ffn"].reshape(DEPTH, 4, 2, 2 * D_FF)
        out["y_sample"][sl] = rb["y_sample"].reshape(4, 4, D)
        out["p_rwkv"][:, b] = rb["p_rwkv"]
        out["p_ffn"][:, b] = rb["p_ffn"]
        out["y_prompt"][b] = rb["y_prompt"]
        out["p_mem_k"][:, b] = rb["p_mem_k"].reshape(DEPTH, 256, 4, 256)
        out["p_mem_v"][:, b] = rb["p_mem_v"].reshape(DEPTH, 256, 4, 256)
        out["p_rwkv_shift"][:, b, 0] = rb["p_rwkv_shift"]
        out["s_rwkv"][:, sl] = rb["s_rwkv"]
        out["s_rwkv_shift"][:, sl, 0] = rb["s_rwkv_shift"]
    order = ["y_prompt", "y_sample", "p_fox_k", "p_fox_v", "p_fox_logf", "p_rwkv", "p_rwkv_shift", "p_conv", "p_ffn",
             "p_mem_k", "p_mem_v", "s_fox_k", "s_fox_v", "s_fox_logf", "s_rwkv", "s_rwkv_shift", "s_conv", "s_ffn"]
    return tuple(out[k] for k in order)
```
